# Optimizing a Trainium2 kernel written in Bass

```python
import math
import jax, jax.numpy as jnp
from jax import lax
import numpy as np

D_MODEL = 1024
BATCH = 16
SEQ = 256
DEPTH = 2
DEC_BATCH = 4
DEC_SEQ = 4096
PAST_LEN = 256

GRID_W = 64
EXPAND = 2
D_INNER = EXPAND * D_MODEL
A_WIDTH = D_INNER // 2
A_HEAD = 64
A_HEADS = A_WIDTH // A_HEAD
DECAY_LORA = 64
ICLR_LORA = 64
B_WIDTH = D_INNER // 2
B_HD = 64
B_HEADS = B_WIDTH // (2 * B_HD)
B_QK = B_HEADS * 2 * B_HD
C_WIDTH = D_INNER
CONV_W = 3
N_EVEN = (DEPTH + 1) // 2
N_ODD = DEPTH // 2
A_SHIFT_SIZES = (A_WIDTH, A_WIDTH, A_WIDTH, DECAY_LORA, DECAY_LORA, ICLR_LORA, ICLR_LORA)
A_SHIFT_COLS = sum(A_SHIFT_SIZES)
EVEN_SIZES = (A_SHIFT_COLS, A_WIDTH, B_QK, B_QK, B_WIDTH, B_WIDTH)
EVEN_IN = sum(EVEN_SIZES)
ODD_IN = 4 * C_WIDTH
Q_BLOCK = 128
ROPE_BASE = 10000.0
NORM_EPS = 1e-6
GN_EPS = 64e-5

kernel_name = 'hybrid_rwkv7_diffattn_shortconv_diffusion_step'


def _split(x, sizes):
    return jnp.split(x, [int(i) for i in np.cumsum(sizes)[:-1]], axis=-1)


def _rms(x, w):
    xf = x.astype(jnp.float32)
    y = xf * lax.rsqrt(jnp.mean(xf * xf, axis=-1, keepdims=True) + NORM_EPS)
    return (y * w.astype(jnp.float32)).astype(x.dtype)


def _ada(cvec, w, b):
    m = jax.nn.silu(cvec) @ w + b
    shift, scale, gate = jnp.split(m, 3, axis=-1)
    return shift[:, None, :], scale[:, None, :], gate[:, None, :]


def _centred_shift(p):
    zero = jnp.zeros_like(p[:, :1])
    prev = jnp.concatenate([zero, p[:, :-1]], axis=1)
    nxt = jnp.concatenate([p[:, 1:], zero], axis=1)
    return 0.5 * (prev + nxt)


def _axial_rope_angles(n_rows):
    row = jnp.repeat(jnp.arange(n_rows, dtype=jnp.float32), GRID_W)
    col = jnp.tile(jnp.arange(GRID_W, dtype=jnp.float32), n_rows)
    half = B_HD // 2
    inv = ROPE_BASE ** (-jnp.arange(0, half, 2, dtype=jnp.float32) / half)
    ang_row = (row[:, None] * inv)[:, None, None, :]
    ang_col = (col[:, None] * inv)[:, None, None, :]
    return ang_row, ang_col


def _rope_chunk(x, ang):
    x1, x2 = jnp.split(x, 2, axis=-1)
    cos, sin = jnp.cos(ang), jnp.sin(ang)
    return jnp.concatenate([x1 * cos - x2 * sin, x2 * cos + x1 * sin], axis=-1)


def _apply_axial_rope(x, ang_row, ang_col):
    xr, xc = jnp.split(x.astype(jnp.float32), 2, axis=-1)
    return jnp.concatenate([_rope_chunk(xr, ang_row), _rope_chunk(xc, ang_col)], axis=-1).astype(x.dtype)


def _rwkv_scan(r, w, k, v, kk, a, s0, reverse):
    def step(S, inp):
        r_t, w_t, k_t, v_t, kk_t, a_t = inp
        s_kk = jnp.einsum('bhvk,bhk->bhv', S, kk_t)
        S = (S * w_t[:, :, None, :] - s_kk[..., None] * (kk_t * a_t)[:, :, None, :]
             + v_t[..., None] * k_t[:, :, None, :])
        return S, jnp.einsum('bhvk,bhk->bhv', S, r_t)
    xs = tuple(jnp.moveaxis(t, 1, 0) for t in (r, w, k, v, kk, a))
    s_fin, o = lax.scan(step, s0, xs, reverse=reverse)
    return jnp.moveaxis(o, 0, 1), s_fin


def _rwkv_branch(pa, ga, s0_f, s0_b, mu, w0, w_up, a0, a_up, k_k, k_a, r_k, lnx_w, lnx_b):
    bsz, t_len, _ = pa.shape
    f32 = jnp.float32
    heads = lambda t: t.reshape(bsz, t_len, A_HEADS, A_HEAD)
    p = pa.astype(f32)
    p = p + mu * (_centred_shift(p) - p)
    r, k, v, wl_f, wl_b, al_f, al_b = _split(p, A_SHIFT_SIZES)
    kk = heads(k * k_k)
    kk = kk / jnp.maximum(jnp.sqrt(jnp.sum(kk * kk, axis=-1, keepdims=True)), 1e-12)
    r_h, v_h = heads(r), heads(v)
    outs, bonuses, states = [], [], []
    for d, (wl, al, s0) in enumerate(((wl_f, al_f, s0_f), (wl_b, al_b, s0_b))):
        wlog = -jax.nn.softplus(-(w0[d] + jnp.tanh(wl) @ w_up[d])) - 0.5
        decay = jnp.exp(-jnp.exp(wlog))
        a = jax.nn.sigmoid(a0[d] + al @ a_up[d])
        k_d = heads(k * (1.0 + (a - 1.0) * k_a))
        o_d, s_d = _rwkv_scan(r_h, heads(decay), k_d, v_h, kk, heads(a), s0.astype(f32), reverse=(d == 1))
        outs.append(o_d)
        bonuses.append(jnp.sum(r_h * k_d * r_k, axis=-1, keepdims=True) * v_h)
        states.append(s_d.astype(pa.dtype))
    o = outs[0] + outs[1]
    mean = jnp.mean(o, axis=-1, keepdims=True)
    var = jnp.mean(jnp.square(o - mean), axis=-1, keepdims=True)
    o = ((o - mean) * lax.rsqrt(var + GN_EPS) * lnx_w.reshape(A_HEADS, A_HEAD)
         + lnx_b.reshape(A_HEADS, A_HEAD))
    y = (o + bonuses[0] + bonuses[1]).reshape(bsz, t_len, A_WIDTH).astype(ga.dtype) * jax.nn.silu(ga)
    return y, states[0], states[1]


def _diff_attend_block(qb, k, v, lam):
    s = jnp.einsum('bqhcd,bkhcd->bhcqk', qb, k) * (B_HD ** -0.5)
    p = jax.nn.softmax(s, axis=-1)
    attn = p[:, :, 0] - lam * p[:, :, 1]
    return jnp.einsum('bhqk,bkhe->bqhe', attn, v)


def _diff_attend(q, k, v, lam):
    bsz, t_len = q.shape[:2]
    nb = t_len // Q_BLOCK
    qb = jnp.moveaxis(q.reshape(bsz, nb, Q_BLOCK, B_HEADS, 2, B_HD), 1, 0)
    o = lax.map(lambda blk: _diff_attend_block(blk, k, v, lam), qb)
    return jnp.moveaxis(o, 0, 1).reshape(bsz, t_len, B_HEADS, 2 * B_HD)


def _diff_branch(pq, pk, pv, gb, q_norm, k_norm, lam_vec, subln_w, lam_init, rope, ctx_k, ctx_v):
    bsz, t_len, _ = pq.shape
    f32 = jnp.float32
    q = _rms(pq.reshape(bsz, t_len, B_HEADS, 2, B_HD), q_norm)
    k = _rms(pk.reshape(bsz, t_len, B_HEADS, 2, B_HD), k_norm)
    v = pv.reshape(bsz, t_len, B_HEADS, 2 * B_HD)
    if rope is None:
        keys, vals = k, v
    else:
        ang_row, ang_col = rope
        q = _apply_axial_rope(q, ang_row, ang_col)
        keys = jnp.concatenate([_apply_axial_rope(k, ang_row, ang_col), ctx_k.astype(k.dtype)], axis=1)
        vals = jnp.concatenate([v, ctx_v.astype(v.dtype)], axis=1)
    lf = lam_vec.astype(f32)
    lam = jnp.exp(jnp.sum(lf[0] * lf[1])) - jnp.exp(jnp.sum(lf[2] * lf[3])) + lam_init
    o = _diff_attend(q.astype(f32), keys.astype(f32), vals.astype(f32), lam)
    o = _rms(o, subln_w) * (1.0 - lam_init)
    y = o.reshape(bsz, t_len, B_WIDTH).astype(pq.dtype) * jax.nn.silu(gb)
    return y, k, v


def _even_mixer(h, rope, s0_f, s0_b, ctx_k, ctx_v, w_in, w_out, rw, dp):
    pa, ga, pq, pk, pv, gb = _split(h @ w_in, EVEN_SIZES)
    ya, s_f, s_b = _rwkv_branch(pa, ga, s0_f, s0_b, *rw)
    yb, k_c, v_c = _diff_branch(pq, pk, pv, gb, *dp, rope, ctx_k, ctx_v)
    return jnp.concatenate([ya, yb], axis=-1) @ w_out, s_f, s_b, k_c, v_c


def _dwconv3(u, w, b):
    pad = (CONV_W - 1) // 2
    out = lax.conv_general_dilated(u, w[:, None, :].astype(u.dtype), window_strides=(1,),
                                   padding=((pad, pad),), dimension_numbers=('NWC', 'WIO', 'NWC'),
                                   feature_group_count=u.shape[-1])
    return out + b


def _odd_mixer(h, w_in, conv_w, conv_b, w_out):
    bg, cg, u, z = jnp.split(h @ w_in, 4, axis=-1)
    y = bg * _dwconv3(cg * u, conv_w, conv_b) * jax.nn.silu(z)
    return y @ w_out


def setup_inputs(seed: int = 0) -> dict:
    key = jax.random.key(seed)
    ks = iter(jax.random.split(key, 40))
    nrm = lambda shape, s=1.0: s * jax.random.normal(next(ks), shape, jnp.float32)
    return {
        'x_prompt': nrm((BATCH, SEQ, D_MODEL)),
        'x_sample': nrm((DEC_BATCH, DEC_SEQ, D_MODEL)),
        'state_rwkv_fwd': nrm((DEC_BATCH, N_EVEN, A_HEADS, A_HEAD, A_HEAD)),
        'state_rwkv_bwd': nrm((DEC_BATCH, N_EVEN, A_HEADS, A_HEAD, A_HEAD)),
        'cache_diff_k': nrm((DEC_BATCH, N_EVEN, PAST_LEN, B_HEADS, 2, B_HD)),
        'cache_diff_v': nrm((DEC_BATCH, N_EVEN, PAST_LEN, B_HEADS, 2 * B_HD)),
        'c': nrm((DEC_BATCH, D_MODEL)),
        'c_ctx': nrm((D_MODEL,)),
        'norm_w': 1.0 + nrm((DEPTH, D_MODEL), 0.1),
        'ada_w': nrm((DEPTH, D_MODEL, 3 * D_MODEL), 0.5 * D_MODEL ** -0.5),
        'ada_b': nrm((DEPTH, 3 * D_MODEL), 0.02),
        'e_w_in': nrm((N_EVEN, D_MODEL, EVEN_IN), D_MODEL ** -0.5),
        'e_w_out': nrm((N_EVEN, A_WIDTH + B_WIDTH, D_MODEL), (A_WIDTH + B_WIDTH) ** -0.5),
        'e_mu': jax.random.uniform(next(ks), (N_EVEN, A_SHIFT_COLS), jnp.float32),
        'e_w0': -2.0 + nrm((N_EVEN, 2, A_WIDTH), 0.5),
        'e_w_up': nrm((N_EVEN, 2, DECAY_LORA, A_WIDTH), 0.1),
        'e_a0': nrm((N_EVEN, 2, A_WIDTH), 0.1),
        'e_a_up': nrm((N_EVEN, 2, ICLR_LORA, A_WIDTH), 0.1),
        'e_k_k': 0.85 + nrm((N_EVEN, A_WIDTH), 0.05),
        'e_k_a': 1.0 + nrm((N_EVEN, A_WIDTH), 0.05),
        'e_r_k': nrm((N_EVEN, A_HEADS, A_HEAD), 0.1),
        'e_lnx_w': 1.0 + nrm((N_EVEN, A_WIDTH), 0.1),
        'e_lnx_b': nrm((N_EVEN, A_WIDTH), 0.02),
        'e_q_norm': 1.0 + nrm((N_EVEN, B_HD), 0.1),
        'e_k_norm': 1.0 + nrm((N_EVEN, B_HD), 0.1),
        'e_lambda': nrm((N_EVEN, 4, B_HD), 0.1),
        'e_subln': 1.0 + nrm((N_EVEN, 2 * B_HD), 0.1),
        'o_w_in': nrm((N_ODD, D_MODEL, ODD_IN), D_MODEL ** -0.5),
        'o_conv_w': nrm((N_ODD, CONV_W, C_WIDTH), CONV_W ** -0.5),
        'o_conv_b': nrm((N_ODD, C_WIDTH), 0.02),
        'o_w_out': nrm((N_ODD, C_WIDTH, D_MODEL), C_WIDTH ** -0.5),
    }


def reference(x_prompt, x_sample, state_rwkv_fwd, state_rwkv_bwd, cache_diff_k, cache_diff_v, c, c_ctx,
              norm_w, ada_w, ada_b,
              e_w_in, e_w_out, e_mu, e_w0, e_w_up, e_a0, e_a_up, e_k_k, e_k_a, e_r_k, e_lnx_w, e_lnx_b,
              e_q_norm, e_k_norm, e_lambda, e_subln,
              o_w_in, o_conv_w, o_conv_b, o_w_out):
    n_rows = x_sample.shape[1] // GRID_W
    rope = _axial_rope_angles(n_rows)
    xp, xs = x_prompt, x_sample
    new_sf, new_sb, new_k, new_v = [], [], [], []
    for layer in range(DEPTH):
        sh_p, sc_p, g_p = _ada(c_ctx[None, :], ada_w[layer], ada_b[layer])
        sh_s, sc_s, g_s = _ada(c, ada_w[layer], ada_b[layer])
        hp = _rms(xp, norm_w[layer]) * (1.0 + sc_p) + sh_p
        hs = _rms(xs, norm_w[layer]) * (1.0 + sc_s) + sh_s
        i = layer // 2
        if layer % 2 == 0:
            lam_init = 0.8 - 0.6 * math.exp(-0.3 * layer)
            rw = (e_mu[i], e_w0[i], e_w_up[i], e_a0[i], e_a_up[i], e_k_k[i], e_k_a[i], e_r_k[i],
                  e_lnx_w[i], e_lnx_b[i])
            dp = (e_q_norm[i], e_k_norm[i], e_lambda[i], e_subln[i], lam_init)
            zeros = jnp.zeros((xp.shape[0], A_HEADS, A_HEAD, A_HEAD), jnp.float32)
            op, s_f, s_b, k_c, v_c = _even_mixer(hp, None, zeros, zeros, None, None,
                                                 e_w_in[i], e_w_out[i], rw, dp)
            os_, _, _, _, _ = _even_mixer(hs, rope, state_rwkv_fwd[:, i], state_rwkv_bwd[:, i],
                                          cache_diff_k[:, i], cache_diff_v[:, i],
                                          e_w_in[i], e_w_out[i], rw, dp)
            new_sf.append(s_f)
            new_sb.append(s_b)
            new_k.append(k_c)
            new_v.append(v_c)
        else:
            op = _odd_mixer(hp, o_w_in[i], o_conv_w[i], o_conv_b[i], o_w_out[i])
            os_ = _odd_mixer(hs, o_w_in[i], o_conv_w[i], o_conv_b[i], o_w_out[i])
        xp = xp + (g_p * op).astype(xp.dtype)
        xs = xs + (g_s * os_).astype(xs.dtype)
    return (xp, xs, jnp.stack(new_sf, axis=1), jnp.stack(new_sb, axis=1),
            jnp.stack(new_k, axis=1), jnp.stack(new_v, axis=1))
```

```python
import contextlib
import numpy as np
import concourse.bass as bass
import concourse.mybir as mybir
from concourse.bass_utils import run_bass_kernel_spmd

F32 = mybir.dt.float32
BF16 = mybir.dt.bfloat16
AF = mybir.ActivationFunctionType
ALU = mybir.AluOpType
AX = mybir.AxisListType

ENGS = ("pe", "act", "dve", "pool", "sp")
N_DMA_SEMS = 32
BIG = 1 << 40

D = 1024
NT_OWN, NT_OTH, NT_PR = 17, 15, 4
NT_S = NT_OWN + NT_OTH
NT = NT_S + NT_PR
T_ALL = NT * 128
T_OP = (NT_OWN + NT_PR) * 128
HT_COLS = 4612
EVEN_IN = 8448
NORM_EPS = 1e-6
GN_EPS = 64e-5
LAM_INIT = 0.8 - 0.6
DECAY_C = -0.6065306597126334


def tile_col0(i):
    if i < NT_S:
        return 1 + 128 * i
    if i < NT_S + 2:
        return 4098 + 128 * (i - NT_S)
    return 4355 + 128 * (i - NT_S - 2)


def op_index(i):
    return i if i < NT_OWN else NT_OWN + (i - NT_S)


OUT_TILES = list(range(NT_OWN)) + list(range(NT_S, NT))


class Prog:
    def __init__(self, nc):
        self.nc = nc
        self.streams = {e: [] for e in ENGS}
        self.count = {e: 0 for e in ENGS}
        self.seen = {e: {} for e in ENGS}
        self.regions = {}
        self.dma_cnt = [0] * N_DMA_SEMS
        self.dma_rr = 0
        self.out_tokens = []

    def _deps(self, reads, writes):
        deps = []
        for (name, lo, hi) in reads:
            for rec in self.regions.get(name, ()):
                if rec[0] < hi and lo < rec[1]:
                    if rec[2] is not None:
                        deps.append(rec[2])
                    if name.startswith("ps"):
                        deps.extend(rec[3].values())
        for (name, lo, hi) in writes:
            for rec in self.regions.get(name, ()):
                if rec[0] < hi and lo < rec[1]:
                    if rec[2] is not None:
                        deps.append(rec[2])
                    deps.extend(rec[3].values())
        return deps

    def _commit(self, reads, writes, token, rkey):
        for (name, lo, hi) in reads:
            lst = self.regions.setdefault(name, [])
            hit = False
            for rec in lst:
                if rec[0] < hi and lo < rec[1]:
                    rec[3][rkey] = token
                    hit = True
            if not hit:
                lst.append([lo, hi, None, {rkey: token}])
        for (name, lo, hi) in writes:
            lst = self.regions.setdefault(name, [])
            keep = [rec for rec in lst if not (lo <= rec[0] and rec[1] <= hi)]
            keep.append([lo, hi, token, {}])
            self.regions[name] = keep

    def _waits(self, eng, deps):
        best = {}
        for (k, v) in deps:
            if v > best.get(k, 0):
                best[k] = v
        out = []
        for k, v in best.items():
            if eng == "pe" and k == "pe":
                continue
            if self.seen[eng].get(k, 0) >= v:
                continue
            self.seen[eng][k] = v
            out.append((k, v))
        return out

    @staticmethod
    def _norm(regs):
        out = []
        for r in regs:
            if isinstance(r, str):
                out.append((r, 0, BIG))
            elif isinstance(r, Buf):
                out.append(r.reg)
            else:
                out.append(r)
        return out

    def op(self, eng, fn, reads=(), writes=()):
        reads = self._norm(reads)
        writes = self._norm(writes)
        waits = self._waits(eng, self._deps(reads, writes))
        self.count[eng] += 1
        token = (eng, self.count[eng])
        self.streams[eng].append(("op", waits, fn))
        self._commit(reads, writes, token, eng)
        return token

    def dma(self, fn, reads=(), writes=(), is_output=False, queue="sp"):
        queue = "sp"
        reads = self._norm(reads)
        writes = self._norm(writes)
        deps = self._deps(reads, writes)
        s = self.dma_rr
        self.dma_rr = (self.dma_rr + 1) % N_DMA_SEMS
        key = "dma%d" % s
        if self.dma_cnt[s] > 0:
            deps.append((key, self.dma_cnt[s]))
        waits = self._waits(queue, deps)
        self.dma_cnt[s] += 16
        token = (key, self.dma_cnt[s])
        self.streams[queue].append(("dma", waits, fn, s))
        self._commit(reads, writes, token, key)
        if is_output:
            self.out_tokens.append(token)
        return token

    def run(self):
        nc = self.nc
        fin = self._waits("sp", list(self.out_tokens))
        self.streams["sp"].append(("wait", fin))
        with contextlib.ExitStack() as st:
            sems = {}
            for e in ENGS:
                sems[e] = st.enter_context(nc.semaphore("s_" + e))
            for i in range(N_DMA_SEMS):
                sems["dma%d" % i] = st.enter_context(nc.semaphore("s_dma%d" % i))
            block = st.enter_context(nc.Block())

            def player(ename):
                def play(eng):
                    for item in self.streams[ename]:
                        for (k, v) in item[1]:
                            eng.wait_ge(sems[k], v)
                        if item[0] == "op":
                            item[2](eng).then_inc(sems[ename], 1)
                        elif item[0] == "dma":
                            item[2](eng).then_inc(sems["dma%d" % item[3]], 16)
                return play

            block.tensor(player("pe"))
            block.scalar(player("act"))
            block.vector(player("dve"))
            block.gpsimd(player("pool"))
            block.sync(player("sp"))


class Buf:
    def __init__(self, ap, reg):
        self.ap = ap
        self.reg = reg

    def cols(self, a, b):
        name, lo, hi = self.reg
        if name == "ar":
            if self.ap.dtype == BF16:
                r = (name, lo + a // 2, lo + (b + 1) // 2)
            else:
                r = (name, lo + a, lo + b)
        else:
            r = self.reg
        return Buf(self.ap[:, a:b], r)

    def v(self, s, **kw):
        return self.ap.rearrange(s, **kw)


class Arena:
    def __init__(self, ap, size):
        self.ap = ap
        self.size = size
        self.top = 0

    def f32(self, n):
        lo = self.top
        self.top += n
        assert self.top <= self.size, ("arena overflow", self.top, self.size)
        return Buf(self.ap[:, lo:lo + n], ("ar", lo, lo + n))

    def bf(self, n):
        nf = (n + 1) // 2
        lo = self.top
        self.top += nf
        assert self.top <= self.size, ("arena overflow", self.top, self.size)
        return Buf(self.ap[:, lo:lo + nf].bitcast(BF16)[:, 0:n], ("ar", lo, lo + nf))

    def mark(self):
        return self.top

    def release(self, m):
        self.top = m


class DramBuf:
    def __init__(self, name, ap):
        self.name = name
        self.ap = ap

    def rows(self, a, b):
        return (self.name, a, b)


ARENA_F32 = 52800


class K:
    pass


def build_program(debug=()):
    nc = bass.Bass("TRN2", target_bir_lowering=False)
    k = K()
    k.nc = nc
    k.debug = debug
    ins = {}
    outs = {}

    def din(name, shape, dt=F32):
        ins[name] = nc.dram_tensor(name, list(shape), dt, kind="ExternalInput").ap()
        return ins[name]

    def dout(name, shape, dt=F32):
        outs[name] = nc.dram_tensor(name, list(shape), dt, kind="ExternalOutput").ap()
        return outs[name]

    def dscr(name, shape, dt=F32):
        kind = "ExternalOutput" if name in debug else "Internal"
        t = nc.dram_tensor(name, list(shape), dt, kind=kind).ap()
        if name in debug:
            outs[name] = t
        return DramBuf(name, t)

    x_all = din("x_all", [T_ALL, D])
    cv = din("cv", [128, 16])
    normw_fm = din("normw_fm", [128, 16])
    ada_w = din("ada_w", [2, D, 3072])
    adab_fm = din("adab_fm", [128, 48])
    adab_g = din("adab_g", [128, 2048])
    w_in0 = din("w_in0", [D, EVEN_IN])
    mu_b = din("mu_b", [128, 3328])
    qkw_b = din("qkw_b", [128, 1024])
    rope_cs = din("rope_cs", [4096, 128])
    rw_prm = din("rw_prm", [128, 5120])
    rw_up = din("rw_up", [65, 4096])
    rw_const = din("rw_const", [128, 2050])
    st_in = din("st_in", [128, 1024])
    ctx_k = din("ctx_k", [256, 1024])
    ctx_v = din("ctx_v", [256, 1024])
    at_prm = din("at_prm", [128, 257])
    w_out0 = din("w_out0", [2048, 1024])
    w_in1 = din("w_in1", [D, 8192])
    w_out1 = din("w_out1", [2048, 1024])
    cv_prm = din("cv_prm", [128, 64])
    OB = dscr("OB", [T_OP, 1024])
    k.X1S = dscr("X1S", [T_OP, 1024])
    YT = dscr("YT", [2048, 2816], BF16)
    PA = dscr("PA", [T_ALL, 3328])
    GA = dscr("GA", [T_OP, 1024])
    QT = dscr("QT", [8, 128, T_OP], BF16)
    KT = dscr("KT", [8, 128, T_ALL + 256], BF16)
    VS = dscr("VS", [T_ALL + 256, 1024], BF16)
    GBT = dscr("GBT", [1024, T_OP])
    kc_out = dout("kc_out", [512, 1024])
    vc_out = dout("vc_out", [512, 1024])
    st_out = dout("st_out", [2, 2, 128, 512])
    y_out = dout("y_out", [T_OP, 1024])
    if "YB" in debug:
        dout("YB", [1024, T_OP])
    if "X1" in debug:
        dout("X1", [T_OP, 1024])
    if "YA" in debug:
        dout("YA", [T_OP, 1024])
    if "dumpQ" in debug:
        dout("dQ", [128, 12288])

    st = contextlib.ExitStack()
    with st:
        ar_t = st.enter_context(nc.sbuf_tensor("ar", [128, ARENA_F32], F32))
        psb = [st.enter_context(nc.psum_tensor("ps%d" % i, [128, 512], F32)) for i in range(8)]
        k.ps = [Buf(psb[i][:, :], ("ps%d" % i, 0, BIG)) for i in range(8)]
        A = Arena(ar_t, ARENA_F32)
        P = Prog(nc)
        k.P, k.A = P, A

        ident = A.f32(128)
        identb = A.bf(128)
        P.op("pool", lambda e: e.memset(ident.ap, 1.0), writes=[ident])
        P.op("pool", lambda e: e.affine_select(out=ident.ap, in_=ident.ap, pattern=[[-1, 128]],
                                               compare_op=ALU.is_equal, fill=0.0, base=0, channel_multiplier=1),
             reads=[ident], writes=[ident])
        P.op("dve", lambda e: e.tensor_copy(identb.ap, ident.ap), reads=[ident], writes=[identb])
        k.ident, k.identb = ident, identb

        cvt = A.f32(16)
        nwt = A.f32(16)
        abf = A.f32(48)
        modS = A.f32(32)
        modB = A.f32(32)
        gB = A.f32(4096)
        k.modS, k.modB, k.gB = modS, modB, gB
        m0 = A.mark()
        abg = A.f32(2048)
        screp = A.f32(16 * 128)
        P.dma(lambda e: e.dma_start(out=cvt.ap, in_=cv[:, :]), writes=[cvt])
        P.dma(lambda e: e.dma_start(out=nwt.ap, in_=normw_fm[:, :]), writes=[nwt])
        P.dma(lambda e: e.dma_start(out=abf.ap, in_=adab_fm[:, :]), writes=[abf])
        P.dma(lambda e: e.dma_start(out=abg.ap, in_=adab_g[:, :]), writes=[abg])
        P.op("act", lambda e: e.activation(out=cvt.ap, in_=cvt.ap, func=AF.Silu), reads=[cvt], writes=[cvt])
        P.op("dve", lambda e: e.tensor_copy(screp.v("p (a b) -> p a b", b=128),
                                            cvt.ap.unsqueeze(2).broadcast_to([128, 16, 128])),
             reads=[cvt], writes=[screp])
        wst = [A.f32(8 * 512) for _ in range(2)]
        for l in range(2 if "noada" not in debug else 0):
            for g in range(6):
                wb = wst[(l * 6 + g) % 2]
                P.dma(lambda e, wb=wb, l=l, g=g: e.dma_start(
                    out=wb.v("p (k c) -> p k c", c=512),
                    in_=ada_w[l, :, 512 * g:512 * g + 512].rearrange("(k p) c -> p k c", p=128)), writes=[wb])
                if g < 4:
                    pb = k.ps[g % 2]
                    for j in range(4):
                        for kk in range(8):
                            P.op("pe", lambda e, pb=pb, wb=wb, j=j, kk=kk: e.matmul(
                                pb.ap[:, 2 * j:2 * j + 2], wb.ap[:, kk * 512 + 128 * j: kk * 512 + 128 * j + 128],
                                cvt.ap[:, 2 * kk:2 * kk + 2], start=(kk == 0), stop=(kk == 7)),
                                reads=[wb, cvt], writes=[pb])
                    for v in range(2):
                        if g < 2:
                            dst = modB.ap[:, l * 16 + v * 8 + 4 * g: l * 16 + v * 8 + 4 * g + 4]
                            P.op("dve", lambda e, pb=pb, dst=dst, v=v, l=l, g=g: e.tensor_tensor(
                                out=dst, in0=pb.ap[:, 0:8].rearrange("p (j v) -> p j v", v=2)[:, :, v],
                                in1=abf.ap[:, l * 24 + 4 * g: l * 24 + 4 * g + 4], op=ALU.add),
                                reads=[pb, abf], writes=[modB])
                        else:
                            gg = g - 2
                            dst = modS.ap[:, l * 16 + v * 8 + 4 * gg: l * 16 + v * 8 + 4 * gg + 4]
                            P.op("dve", lambda e, pb=pb, dst=dst, v=v, l=l, g=g: e.tensor_tensor(
                                out=dst, in0=pb.ap[:, 0:8].rearrange("p (j v) -> p j v", v=2)[:, :, v],
                                in1=abf.ap[:, l * 24 + 4 * g: l * 24 + 4 * g + 4], op=ALU.add),
                                reads=[pb, abf], writes=[modS])
                            P.op("dve", lambda e, dst=dst, l=l, gg=gg: e.scalar_tensor_tensor(
                                out=dst, in0=dst, scalar=1.0, in1=nwt.ap[:, l * 8 + 4 * gg: l * 8 + 4 * gg + 4],
                                op0=ALU.add, op1=ALU.mult), reads=[modS, nwt], writes=[modS])
                else:
                    gg = g - 4
                    for v in range(2):
                        pb = k.ps[2 + v]
                        for kk in range(8):
                            P.op("pe", lambda e, pb=pb, wb=wb, v=v, kk=kk: e.matmul(
                                pb.ap, screp.ap[:, (2 * kk + v) * 128:(2 * kk + v) * 128 + 128],
                                wb.ap[:, kk * 512: kk * 512 + 512], start=(kk == 0), stop=(kk == 7)),
                                reads=[wb, screp], writes=[pb])
                        dst = gB.cols((l * 2 + v) * 1024 + 512 * gg, (l * 2 + v) * 1024 + 512 * gg + 512)
                        P.op("dve", lambda e, pb=pb, dst=dst, l=l, gg=gg: e.tensor_tensor(
                            out=dst.ap, in0=pb.ap, in1=abg.ap[:, l * 1024 + 512 * gg: l * 1024 + 512 * gg + 512],
                            op=ALU.add), reads=[pb, abg], writes=[dst])
        A.release(m0)

        mH = A.mark()
        hT = A.bf(8 * HT_COLS)
        k.hT = hT
        hT3 = hT.v("p (k c) -> p k c", c=HT_COLS)
        for zc in (0, 4097, 4354, 4611):
            P.op("pool", lambda e, zc=zc: e.memset(hT3[:, :, zc:zc + 1], 0.0), writes=[hT])
        mA = A.mark()
        emit_norm_phase(k, x_src=lambda i: x_all[128 * i:128 * i + 128, :], tiles=(list(range(NT)) if "noA1" not in debug else []), layer=0,
                        hT3=hT3, hT=hT, col0=tile_col0, variant=lambda i: 0 if i < NT_S else 1)
        A.release(mA)

        if "noA2" not in debug:
            emit_inproj0(k, ins, outs, PA, GA, QT, KT, VS, GBT, hT3)

        if "hT" in debug:
            hdbg = dout("hT_dbg", [128, 8 * HT_COLS], BF16)
            P.dma(lambda e: e.dma_start(out=hdbg[:, :], in_=hT.ap), reads=[hT], is_output=True)
        A.release(mH)
        if "noR" not in debug:
            emit_rwkv(k, ins, outs, PA, GA, OB, YT)
        if "noT" not in debug:
            emit_attn(k, ins, outs, QT, KT, VS, GBT, YT)
        if "noO" not in debug:
            emit_tail(k, ins, outs, YT, x_all)
        P.run()
    return nc, ins, outs


def emit_norm_phase(k, x_src, tiles, layer, hT3, hT, col0, variant, x_keep=None, cols_total=HT_COLS, src_reg=None):
    P, A = k.P, k.A
    xt = [A.f32(1024) for _ in range(2)]
    xn = [A.f32(1024) for _ in range(2)]
    junk = A.bf(1024)
    ss = [A.f32(1) for _ in range(2)]
    for n, i in enumerate(tiles):
        xb, xnb, ssb = xt[n % 2], xn[n % 2], ss[n % 2]
        src = x_src(i)
        if src is not None:
            P.dma(lambda e, xb=xb, src=src: e.dma_start(out=xb.ap, in_=src), writes=[xb],
                  reads=([src_reg(i)] if src_reg is not None else []))
        else:
            xb = x_keep(i)
        P.op("act", lambda e, xb=xb, ssb=ssb: e.activation(out=junk.ap, in_=xb.ap, func=AF.Square, accum_out=ssb.ap),
             reads=[xb], writes=[junk, ssb])
        P.op("act", lambda e, ssb=ssb: e.activation(out=ssb.ap, in_=ssb.ap, func=AF.Ln, scale=1.0 / D, bias=NORM_EPS),
             reads=[ssb], writes=[ssb])
        P.op("act", lambda e, ssb=ssb: e.activation(out=ssb.ap, in_=ssb.ap, func=AF.Exp, scale=-0.5),
             reads=[ssb], writes=[ssb])
        P.op("dve", lambda e, xb=xb, xnb=xnb, ssb=ssb: e.tensor_scalar(
            out=xnb.ap, in0=xb.ap, scalar1=ssb.ap[:, 0:1], scalar2=None, op0=ALU.mult), reads=[xb, ssb], writes=[xnb])
        v = variant(i)
        c0 = col0(i)
        for half in range(2):
            pb = k.ps[(2 * n + half) % 4]
            for q in range(4):
                kk = 4 * half + q
                P.op("pe", lambda e, pb=pb, xnb=xnb, kk=kk, q=q: e.transpose(
                    pb.ap[:, 128 * q:128 * q + 128], xnb.ap[:, 128 * kk:128 * kk + 128], k.ident.ap),
                    reads=[xnb, k.ident], writes=[pb])
            for q in range(4):
                kk = 4 * half + q
                mi = layer * 16 + v * 8 + kk
                hreg = ("ar", hT.reg[1] + (kk * cols_total + c0) // 2, hT.reg[1] + (kk * cols_total + c0 + 129) // 2)
                P.op("act", lambda e, pb=pb, kk=kk, q=q, mi=mi, c0=c0: e.activation(
                    out=hT3[:, kk, c0:c0 + 128], in_=pb.ap[:, 128 * q:128 * q + 128], func=AF.Identity,
                    scale=k.modS.ap[:, mi:mi + 1], bias=k.modB.ap[:, mi:mi + 1]),
                    reads=[pb, k.modS, k.modB], writes=[hreg])


def emit_inproj0(k, ins, outs, PA, GA, QT, KT, VS, GBT, hT3):
    P, A, nc = k.P, k.A, k.nc
    hT = k.hT
    w_in0 = ins["w_in0"]
    m0 = A.mark()
    mub = A.f32(3328)
    qkw = A.f32(1024)
    P.dma(lambda e: e.dma_start(out=mub.ap, in_=ins["mu_b"][:, :]), writes=[mub])
    P.dma(lambda e: e.dma_start(out=qkw.ap, in_=ins["qkw_b"][:, :]), writes=[qkw])
    wst = [A.f32(8 * 512) for _ in range(2)]
    wbf = [A.bf(8 * 512) for _ in range(2)]
    hd = [A.bf(8 * 128) for _ in range(2)]
    ev = [A.f32(512) for _ in range(6)]
    rs8 = [A.f32(8) for _ in range(2)]
    ropet = [A.f32(128) for _ in range(2)]
    tb = [A.bf(512) for _ in range(2)]
    evb = [A.bf(512) for _ in range(2)]

    def hreg(i):
        c0 = tile_col0(i)
        return ("ar", hT.reg[1], hT.reg[2])

    groups = []
    for c in range(0, 3072, 512):
        groups.append((c, 512, "A"))
    groups.append((3072, 256, "A"))
    for c in range(3328, 4352, 512):
        groups.append((c, 512, "ga"))
    for c in range(4352, 5376, 512):
        groups.append((c, 512, "pq"))
    for c in range(5376, 6400, 512):
        groups.append((c, 512, "pk"))
    for c in range(6400, 7424, 512):
        groups.append((c, 512, "pv"))
    cnt = 0
    only = [d[5:] for d in k.debug if d.startswith("only_")]
    if only:
        groups = [g for g in groups if g[2] in only]
    for gi, (cs, wdt, kind) in enumerate(groups):
        ws, wb = wst[gi % 2], wbf[gi % 2]
        P.dma(lambda e, ws=ws, cs=cs, wdt=wdt: e.dma_start(
            out=ws.v("p (k c) -> p k c", c=512)[:, :, 0:wdt],
            in_=w_in0[:, cs:cs + wdt].rearrange("(k p) c -> p k c", p=128)), writes=[ws])
        P.op("pool", lambda e, ws=ws, wb=wb: e.tensor_copy(wb.ap, ws.ap), reads=[ws], writes=[wb])
        if kind == "A":
            tiles = list(range(NT)) if cs >= 1024 else OUT_TILES
        elif kind in ("ga", "pq"):
            tiles = OUT_TILES
        else:
            tiles = list(range(NT))
        for i in tiles:
            c0 = tile_col0(i)
            sample = i < NT_S
            cnt += 1
            p1 = k.ps[4 + (cnt % 2) * 2]
            p2 = k.ps[5 + (cnt % 2) * 2]
            for kk in range(8):
                P.op("pe", lambda e, p1=p1, wb=wb, kk=kk, c0=c0, wdt=wdt: e.matmul(
                    p1.ap[:, 0:wdt], hT3[:, kk, c0:c0 + 128], wb.ap[:, kk * 512:kk * 512 + wdt],
                    start=(kk == 0), stop=(kk == 7)), reads=[wb, hT], writes=[p1])
            if kind == "A":
                hdb = hd[cnt % 2]
                hd3 = hdb.v("p (k c) -> p k c", c=128)
                P.op("pool", lambda e, hd3=hd3, c0=c0: e.tensor_tensor(
                    out=hd3, in0=hT3[:, :, c0 - 1:c0 + 127], in1=hT3[:, :, c0 + 1:c0 + 129], op=ALU.add),
                    reads=[hT], writes=[hdb])
                P.op("dve", lambda e, hd3=hd3, c0=c0: e.scalar_tensor_tensor(
                    out=hd3, in0=hd3, scalar=0.5, in1=hT3[:, :, c0:c0 + 128], op0=ALU.mult, op1=ALU.subtract),
                    reads=[hT, hdb], writes=[hdb])
                for kk in range(8):
                    P.op("pe", lambda e, p2=p2, wb=wb, kk=kk, hd3=hd3, wdt=wdt: e.matmul(
                        p2.ap[:, 0:wdt], hd3[:, kk, :], wb.ap[:, kk * 512:kk * 512 + wdt],
                        start=(kk == 0), stop=(kk == 7)), reads=[wb, hdb], writes=[p2])
                e1, e2 = ev[(cnt % 2) * 2], ev[(cnt % 2) * 2 + 1]
                P.op("act", lambda e, p2=p2, e1=e1, wdt=wdt: e.activation(out=e1.ap[:, 0:wdt], in_=p2.ap[:, 0:wdt], func=AF.Identity),
                     reads=[p2], writes=[e1])
                P.op("dve", lambda e, e1=e1, cs=cs, wdt=wdt: e.tensor_tensor(
                    out=e1.ap[:, 0:wdt], in0=e1.ap[:, 0:wdt], in1=mub.ap[:, cs:cs + wdt], op=ALU.mult),
                    reads=[e1, mub], writes=[e1])
                P.op("dve", lambda e, e1=e1, e2=e2, p1=p1, wdt=wdt: e.tensor_tensor(
                    out=e2.ap[:, 0:wdt], in0=e1.ap[:, 0:wdt], in1=p1.ap[:, 0:wdt], op=ALU.add),
                    reads=[e1, p1], writes=[e2])
                P.dma(lambda e, e2=e2, i=i, cs=cs, wdt=wdt: e.dma_start(
                    out=PA.ap[128 * i:128 * i + 128, cs:cs + wdt], in_=e2.ap[:, 0:wdt]),
                    reads=[e2], writes=[PA.rows(128 * i, 128 * i + 128)], queue="pool")
            elif kind == "ga":
                e1 = ev[cnt % 2]
                o = op_index(i)
                P.op("act", lambda e, p1=p1, e1=e1: e.activation(out=e1.ap, in_=p1.ap, func=AF.Silu), reads=[p1], writes=[e1])
                P.dma(lambda e, e1=e1, o=o, cs=cs: e.dma_start(
                    out=GA.ap[128 * o:128 * o + 128, cs - 3328:cs - 3328 + 512], in_=e1.ap),
                    reads=[e1], writes=[GA.rows(128 * o, 128 * o + 128)], queue="pool")
            elif kind == "pv":
                e1, eb = ev[cnt % 2], evb[cnt % 2]
                P.op("act", lambda e, p1=p1, e1=e1: e.activation(out=e1.ap, in_=p1.ap, func=AF.Identity), reads=[p1], writes=[e1])
                P.op("dve", lambda e, e1=e1, eb=eb: e.tensor_copy(eb.ap, e1.ap), reads=[e1], writes=[eb])
                P.dma(lambda e, eb=eb, i=i, cs=cs: e.dma_start(
                    out=VS.ap[128 * i:128 * i + 128, cs - 6400:cs - 6400 + 512], in_=eb.ap),
                    reads=[eb], writes=[VS.rows(128 * i, 128 * i + 128)], queue="pool")
                if not sample:
                    pi = i - NT_S
                    P.dma(lambda e, e1=e1, pi=pi, cs=cs: e.dma_start(
                        out=outs["vc_out"][128 * pi:128 * pi + 128, cs - 6400:cs - 6400 + 512], in_=e1.ap),
                        reads=[e1], is_output=True, queue="pool")
            else:
                isq = kind == "pq"
                base = 4352 if isq else 5376
                h0 = (cs - base) // 128
                wof = 0 if isq else 512
                e1, e2, e3 = ev[(cnt % 2) * 3], ev[(cnt % 2) * 3 + 1], ev[(cnt % 2) * 3 + 2]
                r8 = rs8[cnt % 2]
                P.op("act", lambda e, p1=p1, e1=e1: e.activation(out=e1.ap, in_=p1.ap, func=AF.Square), reads=[p1], writes=[e1])
                P.op("dve", lambda e, e1=e1, r8=r8: e.tensor_reduce(
                    out=r8.ap, in_=e1.v("p (g d) -> p g d", d=64), axis=AX.X, op=ALU.add), reads=[e1], writes=[r8])
                P.op("act", lambda e, r8=r8: e.activation(out=r8.ap, in_=r8.ap, func=AF.Ln, scale=1.0 / 64, bias=NORM_EPS),
                     reads=[r8], writes=[r8])
                P.op("act", lambda e, r8=r8: e.activation(out=r8.ap, in_=r8.ap, func=AF.Exp, scale=-0.5), reads=[r8], writes=[r8])
                P.op("dve", lambda e, p1=p1, e2=e2, r8=r8: e.tensor_tensor(
                    out=e2.v("p (g d) -> p g d", d=64), in0=p1.v("p (g d) -> p g d", d=64),
                    in1=r8.ap.unsqueeze(2).broadcast_to([128, 8, 64]), op=ALU.mult), reads=[p1, r8], writes=[e2])
                P.op("dve", lambda e, e2=e2, wof=wof: e.tensor_tensor(
                    out=e2.ap, in0=e2.ap, in1=qkw.ap[:, wof:wof + 512], op=ALU.mult), reads=[e2, qkw], writes=[e2])
                tbb = tb[cnt % 2]
                if (not sample) and (not isq):
                    pi = i - NT_S
                    P.dma(lambda e, e2=e2, pi=pi, cs=cs: e.dma_start(
                        out=outs["kc_out"][128 * pi:128 * pi + 128, cs - 5376:cs - 5376 + 512], in_=e2.ap),
                        reads=[e2], is_output=True, queue="pool")
                if sample:
                    rt = ropet[cnt % 2]
                    P.dma(lambda e, rt=rt, i=i: e.dma_start(out=rt.ap, in_=ins["rope_cs"][128 * i:128 * i + 128, :]), writes=[rt])
                    cosb = rt.ap[:, 0:64].unsqueeze(1).broadcast_to([128, 8, 64])
                    P.op("dve", lambda e, e1=e1, e2=e2, cosb=cosb: e.tensor_tensor(
                        out=e1.v("p (g d) -> p g d", d=64), in0=e2.v("p (g d) -> p g d", d=64), in1=cosb, op=ALU.mult),
                        reads=[e2, rt], writes=[e1])
                    for hf in range(2):
                        sinb = rt.ap[:, 64:128].rearrange("p (c h d) -> p c h d", c=2, h=2)[:, :, hf, :] \
                            .unsqueeze(1).broadcast_to([128, 8, 2, 16])
                        P.op("pool", lambda e, e2=e2, e3=e3, hf=hf, sinb=sinb: e.tensor_tensor(
                            out=e3.v("p (g c h d) -> p g c h d", c=2, h=2, d=16)[:, :, :, hf, :],
                            in0=e2.v("p (g c h d) -> p g c h d", c=2, h=2, d=16)[:, :, :, 1 - hf, :],
                            in1=sinb, op=ALU.mult), reads=[e2, rt], writes=[e3])
                    P.op("dve", lambda e, e1=e1, e3=e3, tbb=tbb: e.tensor_tensor(out=tbb.ap, in0=e1.ap, in1=e3.ap, op=ALU.add),
                         reads=[e1, e3], writes=[tbb])
                else:
                    P.op("dve", lambda e, e2=e2, tbb=tbb: e.tensor_copy(tbb.ap, e2.ap), reads=[e2], writes=[tbb])
                pt = k.ps[cnt % 2]
                ptb = pt.ap.bitcast(BF16)
                for hh in range(4):
                    P.op("pe", lambda e, ptb=ptb, tbb=tbb, hh=hh: e.transpose(
                        ptb[:, 128 * hh:128 * hh + 128], tbb.ap[:, 128 * hh:128 * hh + 128], k.identb.ap),
                        reads=[tbb, k.identb], writes=[pt])
                eb = evb[cnt % 2]
                P.op("act", lambda e, ptb=ptb, eb=eb: e.activation(out=eb.ap, in_=ptb[:, 0:512], func=AF.Identity),
                     reads=[pt], writes=[eb])
                if isq:
                    o = op_index(i)
                    P.dma(lambda e, eb=eb, o=o, h0=h0: e.dma_start(
                        out=QT.ap[h0:h0 + 4, :, 128 * o:128 * o + 128].rearrange("h p t -> p h t"),
                        in_=eb.v("p (h t) -> p h t", t=128)), reads=[eb], writes=[QT.name], queue="pool")
                else:
                    P.dma(lambda e, eb=eb, i=i, h0=h0: e.dma_start(
                        out=KT.ap[h0:h0 + 4, :, 128 * i:128 * i + 128].rearrange("h p t -> p h t"),
                        in_=eb.v("p (h t) -> p h t", t=128)), reads=[eb], writes=[KT.name], queue="pool")
    wins = [(1 + 512 * w_, 512, 512 * w_) for w_ in range(4)] + [(2049, 128, 2048), (4098, 256, 2176), (4355, 256, 2432)]
    if (not only) or ("gb" in only):
        for g2 in range(2):
            gi = len(groups) + g2
            ws, wb = wst[gi % 2], wbf[gi % 2]
            cs = 7424 + 512 * g2
            P.dma(lambda e, ws=ws, cs=cs: e.dma_start(
                out=ws.v("p (k c) -> p k c", c=512), in_=w_in0[:, cs:cs + 512].rearrange("(k p) c -> p k c", p=128)), writes=[ws])
            P.op("pool", lambda e, ws=ws, wb=wb: e.tensor_copy(wb.ap, ws.ap), reads=[ws], writes=[wb])
            for q in range(4):
                for (c0, nw, o0) in wins:
                    cnt += 1
                    p1 = k.ps[4 + cnt % 4]
                    e1 = ev[cnt % 2]
                    for kk in range(8):
                        P.op("pe", lambda e, p1=p1, wb=wb, kk=kk, c0=c0, nw=nw, q=q: e.matmul(
                            p1.ap[:, 0:nw], wb.ap[:, kk * 512 + 128 * q:kk * 512 + 128 * q + 128], hT3[:, kk, c0:c0 + nw],
                            start=(kk == 0), stop=(kk == 7)), reads=[wb, hT], writes=[p1])
                    P.op("act", lambda e, p1=p1, e1=e1, nw=nw: e.activation(out=e1.ap[:, 0:nw], in_=p1.ap[:, 0:nw], func=AF.Silu),
                         reads=[p1], writes=[e1])
                    r0 = 512 * g2 + 128 * q
                    P.dma(lambda e, e1=e1, r0=r0, o0=o0, nw=nw: e.dma_start(out=GBT.ap[r0:r0 + 128, o0:o0 + nw], in_=e1.ap[:, 0:nw]),
                          reads=[e1], writes=[GBT.name])
    if (not only) or ("ctx" in only):
        for t_ in range(2):
            for which in range(2):
                cnt += 1
                e1a, e1b = ev[(cnt % 2) * 2], ev[(cnt % 2) * 2 + 1]
                src = ins["ctx_k"] if which == 0 else ins["ctx_v"]
                for hf in range(2):
                    ebuf = e1a if hf == 0 else e1b
                    P.dma(lambda e, ebuf=ebuf, src=src, t_=t_, hf=hf: e.dma_start(
                        out=ebuf.ap, in_=src[128 * t_:128 * t_ + 128, 512 * hf:512 * hf + 512]), writes=[ebuf])
                    tbb = tb[(cnt + hf) % 2]
                    P.op("dve", lambda e, ebuf=ebuf, tbb=tbb: e.tensor_copy(tbb.ap, ebuf.ap), reads=[ebuf], writes=[tbb])
                    if which == 1:
                        P.dma(lambda e, tbb=tbb, t_=t_, hf=hf: e.dma_start(
                            out=VS.ap[T_ALL + 128 * t_:T_ALL + 128 * t_ + 128, 512 * hf:512 * hf + 512], in_=tbb.ap),
                            reads=[tbb], writes=[VS.rows(T_ALL + 128 * t_, T_ALL + 128 * t_ + 128)])
                    else:
                        pt = k.ps[(cnt + hf) % 2]
                        ptb = pt.ap.bitcast(BF16)
                        for hh in range(4):
                            P.op("pe", lambda e, ptb=ptb, tbb=tbb, hh=hh: e.transpose(
                                ptb[:, 128 * hh:128 * hh + 128], tbb.ap[:, 128 * hh:128 * hh + 128], k.identb.ap),
                                reads=[tbb, k.identb], writes=[pt])
                        eb = evb[(cnt + hf) % 2]
                        P.op("act", lambda e, ptb=ptb, eb=eb: e.activation(out=eb.ap, in_=ptb[:, 0:512], func=AF.Identity),
                             reads=[pt], writes=[eb])
                        P.dma(lambda e, eb=eb, t_=t_, hf=hf: e.dma_start(
                            out=KT.ap[4 * hf:4 * hf + 4, :, T_ALL + 128 * t_:T_ALL + 128 * t_ + 128].rearrange("h p t -> p h t"),
                            in_=eb.v("p (h t) -> p h t", t=128)), reads=[eb], writes=[KT.name])
    A.release(m0)


def TT(P, eng, out, in0, in1, op, r, w):
    P.op(eng, lambda e: e.tensor_tensor(out=out, in0=in0, in1=in1, op=op), reads=r, writes=w)


def ACTF(P, out, in_, func, r, w, **kw):
    P.op("act", lambda e: e.activation(out=out, in_=in_, func=func, **kw), reads=r, writes=w)


def MM(P, out, lhsT, rhs, start, stop, r, w):
    P.op("pe", lambda e: e.matmul(out, lhsT, rhs, start=start, stop=stop), reads=r, writes=w)


def TR(P, out, in_, ident, r, w):
    P.op("pe", lambda e: e.transpose(out, in_, ident), reads=r, writes=w)


def CP(P, eng, out, in_, r, w):
    P.op(eng, lambda e: e.tensor_copy(out, in_), reads=r, writes=w)


def STT(P, out, in0, scalar, in1, op0, op1, r, w):
    P.op("dve", lambda e: e.scalar_tensor_tensor(out=out, in0=in0, scalar=scalar, in1=in1, op0=op0, op1=op1),
         reads=r, writes=w)


def RED(P, out, in_, r, w):
    P.op("dve", lambda e: e.tensor_reduce(out=out, in_=in_, axis=AX.X, op=ALU.add), reads=r, writes=w)


def g64(ap):
    return ap.rearrange("p (g d) -> p g d", d=64)


def b64(ap16):
    return ap16.unsqueeze(2).broadcast_to([128, 16, 64])


def emit_rwkv(k, ins, outs, PA, GA, OB, YT):
    P, A = k.P, k.A
    ps = k.ps
    psb = [p_.ap.bitcast(BF16) for p_ in ps]
    m0 = A.mark()
    prm = A.f32(5120)
    kkb, kab, rkb, lwb, lbb = [prm.cols(1024 * i, 1024 * i + 1024) for i in range(5)]
    wup = A.f32(4096)
    CM = A.f32(768)
    IND = A.f32(2)
    MSK = A.f32(1024)
    lT = A.f32(256)
    H = [A.f32(512) for _ in range(2)]
    Hb = [A.bf(512) for _ in range(2)]
    gC = A.f32(16)
    sbst = A.f32(21 * 16)
    P.dma(lambda e: e.dma_start(out=prm.ap, in_=ins["rw_prm"][:, :]), writes=[prm])
    P.dma(lambda e: e.dma_start(out=wup.ap[0:65, :], in_=ins["rw_up"][:, :]), writes=[wup])
    P.dma(lambda e: e.dma_start(out=CM.ap, in_=ins["rw_const"][:, 0:768]), writes=[CM])
    P.dma(lambda e: e.dma_start(out=IND.ap, in_=ins["rw_const"][:, 768:770]), writes=[IND])
    P.dma(lambda e: e.dma_start(out=MSK.ap, in_=ins["rw_const"][:, 770:1794]), writes=[MSK])
    P.op("pool", lambda e: e.memset(lT.ap, 1.0), writes=[lT])
    pa_one = A.f32(3328)
    pa2 = [pa_one, pa_one]
    LT = A.f32(128)
    SW, AA, KAP, KD, BE, X1, X2 = [A.f32(1024) for _ in range(7)]
    E = [A.f32(1024) for _ in range(2)]
    s16, sbc = A.f32(16), A.f32(16)
    RTIL, KATIL, BTIL, KTIL, BH, KH, VB = [A.bf(1024) for _ in range(7)]
    QRS, BKS = A.bf(2048), A.bf(2048)
    SC = A.bf(16 * 512)
    Nb = [A.bf(2048) for _ in range(2)]
    Ntb = [A.bf(2048) for _ in range(2)]
    Tb = [A.bf(2048) for _ in range(2)]
    Wsb = A.bf(1024)
    Usbc = [A.bf(1024) for _ in range(2)]
    VBc = [A.bf(1024) for _ in range(2)]
    Osb = A.f32(1024)
    obuf, gabuf = X2, E[0]
    for zb in (Wsb, Usbc[0], Usbc[1], VBc[0], VBc[1]):
        P.op("pool", lambda e, zb=zb: e.memset(zb.ap, 0.0), writes=[zb])
    ytb = A.bf(1024)
    lT3 = lT.v("p (a t) -> p a t", t=128)
    QRS4 = QRS.v("p (j x t) -> p j x t", x=2, t=128)
    BKS4 = BKS.v("p (j x t) -> p j x t", x=2, t=128)
    SC3 = SC.v("p (h c) -> p h c", c=512)

    def visit(i, d, state_only, chunks, finalize):
        n = visit.n
        visit.n += 1
        pa = pa2[n % 2]
        c_lo = 1024 if state_only else 0
        P.dma(lambda e: e.dma_start(out=pa.ap[:, c_lo:3328], in_=PA.ap[128 * i:128 * i + 128, c_lo:3328]),
              reads=[PA.rows(128 * i, 128 * i + 128)], writes=[pa])
        par, pak, pav = pa.ap[:, 0:1024], pa.ap[:, 1024:2048], pa.ap[:, 2048:3072]
        Hd, Hbd = H[d], Hb[d]
        Hd3 = Hd.v("p (j v) -> p j v", v=64)
        Hb3 = Hbd.v("p (j v) -> p j v", v=64)
        CP(P, "pool", VB.ap, pav, [pa], [VB])
        CP(P, "pool", VBc[0].ap[0:64, :], pav[0:64, :], [pa], [VBc[0]])
        CP(P, "pool", VBc[1].ap[64:128, :], pav[64:128, :], [pa], [VBc[1]])
        ACTF(P, LT.ap[:, 0:64], pa.ap[:, 3072 + 64 * d:3136 + 64 * d], AF.Tanh, [pa], [LT])
        ACTF(P, LT.ap[:, 64:128], pa.ap[:, 3200 + 64 * d:3264 + 64 * d], AF.Identity, [pa], [LT])
        TR(P, ps[0].ap[0:64, 0:128], LT.ap[:, 0:64], k.ident.ap, [LT, k.ident], [ps[0]])
        TR(P, ps[0].ap[0:64, 128:256], LT.ap[:, 64:128], k.ident.ap, [LT, k.ident], [ps[0]])
        ACTF(P, lT3[0:64, :, :], ps[0].ap[0:64, 0:256].rearrange("p (a t) -> p a t", t=128), AF.Identity, [ps[0]], [lT])
        for hf in range(2):
            MM(P, ps[hf].ap, lT3[0:65, 0, :], wup.ap[0:65, 1024 * d + 512 * hf:1024 * d + 512 * hf + 512], True, True,
               [lT, wup], [ps[hf]])
            MM(P, ps[2 + hf].ap, lT3[0:65, 1, :], wup.ap[0:65, 2048 + 1024 * d + 512 * hf:2048 + 1024 * d + 512 * hf + 512],
               True, True, [lT, wup], [ps[2 + hf]])
        for hf in range(2):
            ACTF(P, SW.ap[:, 512 * hf:512 * hf + 512], ps[hf].ap, AF.Sigmoid, [ps[hf]], [SW])
            ACTF(P, AA.ap[:, 512 * hf:512 * hf + 512], ps[2 + hf].ap, AF.Sigmoid, [ps[2 + hf]], [AA])
        TT(P, "dve", X1.ap, pak, kkb.ap, ALU.mult, [pa, kkb], [X1])
        ACTF(P, X2.ap, X1.ap, AF.Square, [X1], [X2])
        RED(P, s16.ap, g64(X2.ap), [X2], [s16])
        ACTF(P, s16.ap, s16.ap, AF.Ln, [s16], [s16], bias=1e-24)
        ACTF(P, s16.ap, s16.ap, AF.Exp, [s16], [s16], scale=-0.5)
        TT(P, "dve", g64(KAP.ap), g64(X1.ap), b64(s16.ap), ALU.mult, [X1, s16], [KAP])
        STT(P, X2.ap, AA.ap, -1.0, kab.ap, ALU.add, ALU.mult, [AA, kab], [X2])
        STT(P, KD.ap, X2.ap, 1.0, pak, ALU.add, ALU.mult, [X2, pa], [KD])
        TT(P, "pool", BE.ap, KAP.ap, AA.ap, ALU.mult, [KAP, AA], [BE])
        if not state_only:
            TT(P, "pool", X1.ap, par, rkb.ap, ALU.mult, [pa, rkb], [X1])
            TT(P, "pool", X1.ap, X1.ap, KD.ap, ALU.mult, [X1, KD], [X1])
            RED(P, sbc.ap, g64(X1.ap), [X1], [sbc])
        cm = CM.v("p (d m t) -> p d m t", d=2, m=3)
        for hf in range(2):
            MM(P, ps[4 + hf].ap, cm[:, d, 0, :], SW.ap[:, 512 * hf:512 * hf + 512], True, True, [CM, SW], [ps[4 + hf]])
            MM(P, ps[6 + hf].ap, cm[:, d, 1, :], SW.ap[:, 512 * hf:512 * hf + 512], True, True, [CM, SW], [ps[6 + hf]])
        for hf in range(2):
            ACTF(P, E[0].ap[:, 512 * hf:512 * hf + 512], ps[4 + hf].ap, AF.Exp, [ps[4 + hf]], [E[0]])
            ACTF(P, E[1].ap[:, 512 * hf:512 * hf + 512], ps[4 + hf].ap, AF.Exp, [ps[4 + hf]], [E[1]], scale=-1.0)
        if not state_only:
            TT(P, "dve", RTIL.ap, par, E[0].ap, ALU.mult, [pa, E[0]], [RTIL])
        TT(P, "pool", BTIL.ap, BE.ap, E[1].ap, ALU.mult, [BE, E[1]], [BTIL])
        TT(P, "dve", KTIL.ap, KD.ap, E[1].ap, ALU.mult, [KD, E[1]], [KTIL])
        for hf in range(2):
            MM(P, ps[hf].ap, cm[:, d, 2, :], SW.ap[:, 512 * hf:512 * hf + 512], True, True, [CM, SW], [ps[hf]])
        for hf in range(2):
            ACTF(P, E[0].ap[:, 512 * hf:512 * hf + 512], ps[6 + hf].ap, AF.Exp, [ps[6 + hf]], [E[0]])
            ACTF(P, E[1].ap[:, 512 * hf:512 * hf + 512], ps[hf].ap, AF.Exp, [ps[hf]], [E[1]])
        TT(P, "dve", KATIL.ap, KAP.ap, E[0].ap, ALU.mult, [KAP, E[0]], [KATIL])
        TT(P, "pool", BH.ap, BE.ap, E[1].ap, ALU.mult, [BE, E[1]], [BH])
        TT(P, "pool", KH.ap, KD.ap, E[1].ap, ALU.mult, [KD, E[1]], [KH])
        for j in range(8):
            MM(P, ps[2].ap[:, 2 * j:2 * j + 2], SW.ap[:, 128 * j:128 * j + 128], IND.ap, True, True, [SW, IND], [ps[2]])
        ACTF(P, gC.ap, ps[2].ap[:, 0:16], AF.Exp, [ps[2]], [gC])
        gC3 = gC.v("p (j c) -> p j c", c=2)
        quants = [(KATIL, QRS4, 0), (BTIL, BKS4, 0), (KTIL, BKS4, 1)]
        if not state_only:
            quants.append((RTIL, QRS4, 1))
        for qi, (src, dst4, x) in enumerate(quants):
            dstbuf = QRS if dst4 is QRS4 else BKS
            pi_ = 4 + qi % 4
            for j in range(8):
                TR(P, psb[pi_][:, 128 * j:128 * j + 128], src.ap[:, 128 * j:128 * j + 128], k.identb.ap,
                   [src, k.identb], [ps[pi_]])
            dview = dst4[:, :, x, :]
            sview = psb[pi_].rearrange("p (j t) -> p j t", t=128)
            if qi % 2 == 0:
                ACTF(P, dview, sview, AF.Identity, [ps[pi_]], [dstbuf])
            else:
                CP(P, "dve", dview, sview, [ps[pi_]], [dstbuf])
        msk = MSK.ap[:, 512 * d:512 * d + 512]
        ncol = 128 if state_only else 256
        for h in range(16):
            j, e = h // 2, h % 2
            pb = ps[h % 4]
            rhs = QRS4[64 * e:64 * e + 64, j, :, :] if not state_only else QRS4[64 * e:64 * e + 64, j, 0:1, :]
            MM(P, pb.ap[:, 0:ncol], BKS4[64 * e:64 * e + 64, j, 0, :], rhs, True, True, [BKS, QRS], [pb])
            MM(P, pb.ap[:, 256:256 + ncol], BKS4[64 * e:64 * e + 64, j, 1, :], rhs, True, True, [BKS, QRS], [pb])
            TT(P, "dve", SC3[:, h, :], pb.ap, msk, ALU.mult, [pb, MSK], [SC])
        CP(P, "pool", Nb[0].v("p (h t) -> p h t", t=128), SC3[:, :, 0:128], [SC], [Nb[0]])
        TT(P, "dve", Tb[0].v("p (h t) -> p h t", t=128),
           k.identb.ap.unsqueeze(1).broadcast_to([128, 16, 128]), SC3[:, :, 0:128], ALU.subtract, [SC, k.identb], [Tb[0]])
        for hh in range(2):
            pi_ = 4 + hh
            for q in range(8):
                h = 8 * hh + q
                TR(P, psb[pi_][:, 128 * q:128 * q + 128], SC3[:, h, 0:128], k.identb.ap, [SC, k.identb], [ps[pi_]])
            ACTF(P, Ntb[0].ap[:, 1024 * hh:1024 * hh + 1024], psb[pi_], AF.Identity, [ps[pi_]], [Ntb[0]])
        cur = 0
        for lvl in range(1, 6):
            nxt = 1 - cur
            last = lvl == 5
            for hq in range(4):
                sl = slice(512 * hq, 512 * hq + 512)
                pbt = ps[hq % 4]
                for q in range(4):
                    h = 4 * hq + q
                    c1 = slice(128 * h, 128 * h + 128)
                    MM(P, pbt.ap[:, 128 * q:128 * q + 128], Nb[cur].ap[:, c1], Ntb[cur].ap[:, c1], True, True,
                       [Nb[cur], Ntb[cur]], [pbt])
                ACTF(P, Ntb[nxt].ap[:, sl], pbt.ap, AF.Identity, [pbt], [Ntb[nxt]])
                if not last:
                    pbn = ps[4 + hq % 4]
                    for q in range(4):
                        h = 4 * hq + q
                        c1 = slice(128 * h, 128 * h + 128)
                        MM(P, pbn.ap[:, 128 * q:128 * q + 128], Ntb[cur].ap[:, c1], Nb[cur].ap[:, c1], True, True,
                           [Nb[cur], Ntb[cur]], [pbn])
                    CP(P, "dve", Nb[nxt].ap[:, sl], pbn.ap, [pbn], [Nb[nxt]])
            for hq in range(4):
                sl = slice(512 * hq, 512 * hq + 512)
                pb = ps[4 + hq % 4] if last else ps[hq % 4]
                for q in range(4):
                    h = 4 * hq + q
                    c1 = slice(128 * h, 128 * h + 128)
                    MM(P, pb.ap[:, 128 * q:128 * q + 128], k.identb.ap, Tb[cur].ap[:, c1], True, False,
                       [k.identb, Tb[cur]], [pb])
                    MM(P, pb.ap[:, 128 * q:128 * q + 128], Ntb[nxt].ap[:, c1], Tb[cur].ap[:, c1], False, True,
                       [Ntb[nxt], Tb[cur]], [pb])
                if hq % 2 == 0:
                    ACTF(P, Tb[nxt].ap[:, sl], pb.ap, AF.Identity, [pb], [Tb[nxt]])
                else:
                    CP(P, "dve", Tb[nxt].ap[:, sl], pb.ap, [pb], [Tb[nxt]])
            cur = nxt
        MTb = Tb[cur]
        MT3 = MTb.v("p (h t) -> p h t", t=128)
        for flag in ("xa", "xb", "xc", "xd"):
            if flag in k.debug:
                for h in range(16):
                    j, e = h // 2, h % 2
                    es = slice(64 * e, 64 * e + 64)
                    pb = ps[h // 8]
                    oc = slice(64 * (h % 8), 64 * (h % 8) + 64)
                    if flag == "xa":
                        MM(P, pb.ap[:, oc], BKS4[es, j, 0, :], BKS4[es, j, 1, 0:64], True, True, [BKS], [pb])
                    elif flag == "xb":
                        MM(P, pb.ap[:, oc], QRS4[es, j, 0, :], BKS4[es, j, 1, 0:64], True, True, [BKS, QRS], [pb])
                    elif flag == "xc":
                        MM(P, pb.ap[:, oc], QRS4[es, j, 0, :], Hb3[es, j, :], True, True, [QRS, Hbd], [pb])
                    elif flag == "xd":
                        MM(P, pb.ap[:, oc], BKS4[es, j, 0, :], Hb3[es, j, :], True, True, [BKS, Hbd], [pb])
                for hf in range(2):
                    ACTF(P, Wsb.ap[:, 512 * hf:512 * hf + 512], ps[hf].ap, AF.Identity, [ps[hf]], [Wsb])
        if "rwstop6" in k.debug:
            return
        for ch in chunks:
            rs = slice(64 * ch, 64 * ch + 64)
            Uc, Vc = Usbc[ch], VBc[ch]
            for h in range(16):
                j, e = h // 2, h % 2
                es = slice(64 * e, 64 * e + 64)
                pb = ps[h // 8]
                oc = slice(64 * (h % 8), 64 * (h % 8) + 64)
                MM(P, pb.ap[:, oc], QRS4[es, j, 0, :], Hb3[es, j, :], True, False, [QRS, Hbd], [pb])
                MM(P, pb.ap[:, oc], SC3[:, h, 256:384], VB.ap[:, 64 * h:64 * h + 64], False, True, [SC, VB], [pb])
            for hf in range(2):
                ACTF(P, Wsb.ap[rs, 512 * hf:512 * hf + 512], ps[hf].ap[rs, :], AF.Identity, [ps[hf]], [Wsb])
            if "rwstop7" in k.debug:
                return
            for h in range(16):
                pb = ps[2 + h // 8]
                oc = slice(64 * (h % 8), 64 * (h % 8) + 64)
                MM(P, pb.ap[:, oc], MT3[:, h, :], Wsb.ap[:, 64 * h:64 * h + 64], True, True, [MTb, Wsb], [pb])
            for hf in range(2):
                ACTF(P, Uc.ap[rs, 512 * hf:512 * hf + 512], ps[2 + hf].ap[rs, :], AF.Identity, [ps[2 + hf]], [Uc], scale=-1.0)
            if not state_only:
                for h in range(16):
                    j, e = h // 2, h % 2
                    es = slice(64 * e, 64 * e + 64)
                    pb = ps[4 + h // 8]
                    oc = slice(64 * (h % 8), 64 * (h % 8) + 64)
                    MM(P, pb.ap[:, oc], QRS4[es, j, 1, :], Hb3[es, j, :], True, False, [QRS, Hbd], [pb])
                    MM(P, pb.ap[:, oc], SC3[:, h, 128:256], Uc.ap[:, 64 * h:64 * h + 64], False, False, [SC, Uc], [pb])
                    MM(P, pb.ap[:, oc], SC3[:, h, 384:512], VB.ap[:, 64 * h:64 * h + 64], False, True, [SC, VB], [pb])
                for hf in range(2):
                    CP(P, "dve", Osb.ap[rs, 512 * hf:512 * hf + 512], ps[4 + hf].ap[rs, :], [ps[4 + hf]], [Osb])
            for h in range(16):
                j = h // 2
                pb = ps[6 + h // 8]
                oc = slice(64 * (h % 8), 64 * (h % 8) + 64)
                MM(P, pb.ap[:, oc], BH.ap[:, 128 * j:128 * j + 128], Uc.ap[:, 64 * h:64 * h + 64], True, False, [BH, Uc], [pb])
                MM(P, pb.ap[:, oc], KH.ap[:, 128 * j:128 * j + 128], Vc.ap[:, 64 * h:64 * h + 64], False, True, [KH, Vc], [pb])
            for e in range(2):
                es = slice(64 * e, 64 * e + 64)
                TT(P, "dve", Hd3[es, :, :], Hd3[es, :, :], gC3[es, :, ch].unsqueeze(2).broadcast_to([64, 8, 64]), ALU.mult,
                   [Hd, gC], [Hd])
                for hf in range(2):
                    src = ps[6 + hf].ap[es, :].rearrange("p (j e v) -> p j e v", e=2, v=64)[:, :, e, :]
                    TT(P, "dve", Hd3[es, 4 * hf:4 * hf + 4, :], Hd3[es, 4 * hf:4 * hf + 4, :], src, ALU.add,
                       [Hd, ps[6 + hf]], [Hd])
            CP(P, "pool", Hbd.ap, Hd.ap, [Hd], [Hbd])
        if state_only:
            return
        o = op_index(i)
        if not finalize:
            P.dma(lambda e: e.dma_start(out=OB.ap[128 * o:128 * o + 128, :], in_=Osb.ap), reads=[Osb],
                  writes=[OB.rows(128 * o, 128 * o + 128)])
            CP(P, "pool", sbst.ap[:, 16 * o:16 * o + 16], sbc.ap, [sbc], [sbst])
            return
        P.dma(lambda e: e.dma_start(out=obuf.ap, in_=OB.ap[128 * o:128 * o + 128, :]),
              reads=[OB.rows(128 * o, 128 * o + 128)], writes=[obuf])
        P.dma(lambda e: e.dma_start(out=gabuf.ap, in_=GA.ap[128 * o:128 * o + 128, :]),
              reads=[GA.rows(128 * o, 128 * o + 128)], writes=[gabuf])
        TT(P, "dve", X1.ap, Osb.ap, obuf.ap, ALU.add, [Osb, obuf], [X1])
        RED(P, s16.ap, g64(X1.ap), [X1], [s16])
        P.op("dve", lambda e: e.tensor_scalar(out=s16.ap, in0=s16.ap, scalar1=-1.0 / 64, scalar2=None, op0=ALU.mult),
             reads=[s16], writes=[s16])
        TT(P, "dve", g64(X1.ap), g64(X1.ap), b64(s16.ap), ALU.add, [X1, s16], [X1])
        ACTF(P, X2.ap, X1.ap, AF.Square, [X1], [X2])
        RED(P, s16.ap, g64(X2.ap), [X2], [s16])
        ACTF(P, s16.ap, s16.ap, AF.Ln, [s16], [s16], scale=1.0 / 64, bias=GN_EPS)
        ACTF(P, s16.ap, s16.ap, AF.Exp, [s16], [s16], scale=-0.5)
        TT(P, "dve", g64(X1.ap), g64(X1.ap), b64(s16.ap), ALU.mult, [X1, s16], [X1])
        TT(P, "pool", X1.ap, X1.ap, lwb.ap, ALU.mult, [X1, lwb], [X1])
        TT(P, "pool", X1.ap, X1.ap, lbb.ap, ALU.add, [X1, lbb], [X1])
        TT(P, "dve", sbc.ap, sbc.ap, sbst.ap[:, 16 * o:16 * o + 16], ALU.add, [sbc, sbst], [sbc])
        TT(P, "dve", g64(X2.ap), g64(pav), b64(sbc.ap), ALU.mult, [pa, sbc], [X2])
        TT(P, "pool", X1.ap, X1.ap, X2.ap, ALU.add, [X1, X2], [X1])
        TT(P, "dve", X1.ap, X1.ap, gabuf.ap, ALU.mult, [X1, gabuf], [X1])
        if "YA" in k.debug:
            P.dma(lambda e: e.dma_start(out=outs["YA"][128 * o:128 * o + 128, :], in_=X1.ap), reads=[X1], is_output=True)
        for hf in range(2):
            pb = ps[4 + hf]
            for q in range(4):
                kk_ = 4 * hf + q
                TR(P, pb.ap[:, 128 * q:128 * q + 128], X1.ap[:, 128 * kk_:128 * kk_ + 128], k.ident.ap, [X1, k.ident], [pb])
            ACTF(P, ytb.ap[:, 512 * hf:512 * hf + 512], pb.ap, AF.Identity, [pb], [ytb])
        P.dma(lambda e: e.dma_start(out=YT.ap[0:1024, 128 * o:128 * o + 128].rearrange("(kk p) t -> p kk t", p=128),
                                    in_=ytb.v("p (kk t) -> p kk t", t=128)), reads=[ytb], writes=[YT.name])

    visit.n = 0

    def init_state(d, from_input):
        if from_input:
            P.dma(lambda e: e.dma_start(out=H[d].ap, in_=ins["st_in"][:, 512 * d:512 * d + 512]), writes=[H[d]])
        else:
            P.op("pool", lambda e: e.memset(H[d].ap, 0.0), writes=[H[d]])
        CP(P, "pool", Hb[d].ap, H[d].ap, [H[d]], [Hb[d]])

    only = [x for x in k.debug if x.startswith("rw_")]
    do_sample = (not only) or ("rw_sample" in only)
    do_prompt = (not only) or ("rw_prompt" in only)
    if do_sample:
        init_state(1, True)
        for i in range(NT_S - 1, NT_OWN - 1, -1):
            visit(i, 1, True, (1, 0), False)
        for i in range(NT_OWN - 1, -1, -1):
            visit(i, 1, False, (1, 0), False)
        init_state(0, True)
        for i in range(NT_OWN):
            visit(i, 0, False, (0, 1), True)
    if do_prompt:
        for sq in range(2):
            t0 = NT_S + 2 * sq
            init_state(1, False)
            for i in (t0 + 1, t0):
                visit(i, 1, False, (1, 0), False)
            P.dma(lambda e, sq=sq: e.dma_start(out=outs["st_out"][sq, 1, :, :], in_=H[1].ap), reads=[H[1]], is_output=True)
            init_state(0, False)
            for i in (t0, t0 + 1):
                visit(i, 0, False, (0, 1), True)
            P.dma(lambda e, sq=sq: e.dma_start(out=outs["st_out"][sq, 0, :, :], in_=H[0].ap), reads=[H[0]], is_output=True)
    A.release(m0)


def emit_attn(k, ins, outs, QT, KT, VS, GBT, YT):
    P, A = k.P, k.A
    ps = k.ps
    m0 = A.mark()
    prm = A.f32(257)
    lam4 = A.f32(4)
    neglam = A.f32(1)
    subw = A.f32(1)
    onesb = A.bf(128)
    P.dma(lambda e: e.dma_start(out=prm.ap, in_=ins["at_prm"][:, :]), writes=[prm])
    P.op("pool", lambda e: e.memset(onesb.ap, 1.0), writes=[onesb])
    tmp = A.f32(128)
    l4 = prm.ap[:, 0:256].rearrange("p (a d) -> p a d", d=64)
    TT(P, "dve", tmp.v("p (a d) -> p a d", d=64), l4[:, 0:4:2, :], l4[:, 1:4:2, :], ALU.mult, [prm], [tmp])
    P.op("dve", lambda e: e.tensor_reduce(out=lam4.ap[:, 0:2], in_=tmp.v("p (a d) -> p a d", d=64), axis=AX.X, op=ALU.add),
         reads=[tmp], writes=[lam4])
    ACTF(P, lam4.ap[:, 0:2], lam4.ap[:, 0:2], AF.Exp, [lam4], [lam4])
    STT(P, neglam.ap, lam4.ap[:, 1:2], -LAM_INIT, lam4.ap[:, 0:1], ALU.add, ALU.subtract, [lam4], [neglam])
    P.op("dve", lambda e: e.tensor_scalar(out=subw.ap, in0=prm.ap[:, 256:257], scalar1=1.0 - LAM_INIT, scalar2=None, op0=ALU.mult),
         reads=[prm], writes=[subw])
    NKMAX = 34
    kT = [A.bf(NKMAX * 128) for _ in range(2)]
    vv = [A.bf(NKMAX * 128) for _ in range(2)]
    qT = [A.bf(2176) for _ in range(2)]
    pT = [A.bf(512) for _ in range(4)]
    r0b, t0b, t1b, gbt = A.f32(512), A.f32(512), A.f32(512), A.f32(512)
    sqb = A.bf(512)
    yb = A.bf(512)
    sample_keys = [128 * i for i in range(NT_S)] + [T_ALL, T_ALL + 128]
    groups = [(0, 128 * NT_OWN, sample_keys)]
    for sq in range(2):
        groups.append((128 * NT_OWN + 256 * sq, 256, [128 * NT_S + 256 * sq, 128 * NT_S + 256 * sq + 128]))
    only = [x for x in k.debug if x.startswith("at_")]
    if "at_prompt" in only:
        groups = groups[1:]
    if "at_sample" in only:
        groups = groups[:1]
    it = 0
    for (q0, nq, keys) in groups:
        nk = len(keys)
        for h in range(8):
            it += 1
            kTb, vb, qTb = kT[it % 2], vv[it % 2], qT[it % 2]
            runs = []
            for ki, kr in enumerate(keys):
                if runs and runs[-1][1] + runs[-1][2] == kr:
                    runs[-1][2] += 128
                else:
                    runs.append([ki, kr, 128])
            for (ki, kr, ln) in runs:
                P.dma(lambda e, kTb=kTb, ki=ki, kr=kr, ln=ln, h=h: e.dma_start(
                    out=kTb.ap[:, 128 * ki:128 * ki + ln], in_=KT.ap[h, :, kr:kr + ln]), reads=[KT.name], writes=[kTb])
                P.dma(lambda e, vb=vb, ki=ki, kr=kr, ln=ln, h=h: e.dma_start(
                    out=vb.ap[:, 128 * ki:128 * ki + ln].rearrange("p (t c) -> p t c", c=128),
                    in_=VS.ap[kr:kr + ln, 128 * h:128 * h + 128].rearrange("(t p) c -> p t c", p=128)),
                    reads=[VS.rows(kr, kr + ln)], writes=[vb])
            P.dma(lambda e, qTb=qTb, q0=q0, nq=nq, h=h: e.dma_start(out=qTb.ap[:, 0:nq], in_=QT.ap[h, :, q0:q0 + nq]),
                  reads=[QT.name], writes=[qTb])
            for qs in range(0, nq, 512):
                qn = min(512, nq - qs)
                for ki in range(nk):
                    for c in range(2):
                        sb_ = ps[2 * (ki % 2) + c]
                        cs_ = slice(64 * c, 64 * c + 64)
                        MM(P, sb_.ap[:, 0:qn], kTb.ap[cs_, 128 * ki:128 * ki + 128], qTb.ap[cs_, qs:qs + qn], True, True,
                           [kTb, qTb], [sb_])
                    for c in range(2):
                        sb_ = ps[2 * (ki % 2) + c]
                        pb_ = pT[2 * (ki % 2) + c]
                        ACTF(P, pb_.ap[:, 0:qn], sb_.ap[:, 0:qn], AF.Exp, [sb_], [pb_], scale=0.125)
                    for c in range(2):
                        pb_ = pT[2 * (ki % 2) + c]
                        MM(P, ps[4 + c].ap[:, 0:qn], vb.ap[:, 128 * ki:128 * ki + 128], pb_.ap[:, 0:qn], ki == 0, ki == nk - 1,
                           [vb, pb_], [ps[4 + c]])
                        MM(P, ps[6 + c].ap[:, 0:qn], onesb.ap, pb_.ap[:, 0:qn], ki == 0, ki == nk - 1, [onesb, pb_], [ps[6 + c]])
                P.op("dve", lambda e, qn=qn: e.reciprocal(out=r0b.ap[:, 0:qn], in_=ps[6].ap[:, 0:qn]), reads=[ps[6]], writes=[r0b])
                TT(P, "dve", t0b.ap[:, 0:qn], ps[4].ap[:, 0:qn], r0b.ap[:, 0:qn], ALU.mult, [ps[4], r0b], [t0b])
                P.op("dve", lambda e, qn=qn: e.reciprocal(out=r0b.ap[:, 0:qn], in_=ps[7].ap[:, 0:qn]), reads=[ps[7]], writes=[r0b])
                TT(P, "dve", t1b.ap[:, 0:qn], ps[5].ap[:, 0:qn], r0b.ap[:, 0:qn], ALU.mult, [ps[5], r0b], [t1b])
                STT(P, t0b.ap[:, 0:qn], t1b.ap[:, 0:qn], neglam.ap[:, 0:1], t0b.ap[:, 0:qn], ALU.mult, ALU.add,
                    [t1b, neglam, t0b], [t0b])
                ACTF(P, sqb.ap[:, 0:qn], t0b.ap[:, 0:qn], AF.Square, [t0b], [sqb])
                MM(P, ps[0].ap[:, 0:qn], onesb.ap, sqb.ap[:, 0:qn], True, True, [onesb, sqb], [ps[0]])
                ACTF(P, t1b.ap[:, 0:qn], ps[0].ap[:, 0:qn], AF.Ln, [ps[0]], [t1b], scale=1.0 / 128, bias=NORM_EPS)
                ACTF(P, t1b.ap[:, 0:qn], t1b.ap[:, 0:qn], AF.Exp, [t1b], [t1b], scale=-0.5)
                P.dma(lambda e, qs=qs, qn=qn, q0=q0, h=h: e.dma_start(
                    out=gbt.ap[:, 0:qn], in_=GBT.ap[128 * h:128 * h + 128, q0 + qs:q0 + qs + qn]), reads=[GBT.name], writes=[gbt])
                STT(P, t0b.ap[:, 0:qn], t0b.ap[:, 0:qn], subw.ap[:, 0:1], t1b.ap[:, 0:qn], ALU.mult, ALU.mult,
                    [t0b, subw, t1b], [t0b])
                TT(P, "dve", yb.ap[:, 0:qn], t0b.ap[:, 0:qn], gbt.ap[:, 0:qn], ALU.mult, [t0b, gbt], [yb])
                if "YB" in k.debug:
                    TT(P, "dve", t1b.ap[:, 0:qn], t0b.ap[:, 0:qn], gbt.ap[:, 0:qn], ALU.mult, [t0b, gbt], [t1b])
                    P.dma(lambda e, qs=qs, qn=qn, q0=q0, h=h: e.dma_start(
                        out=outs["YB"][128 * h:128 * h + 128, q0 + qs:q0 + qs + qn], in_=t1b.ap[:, 0:qn]), reads=[t1b], is_output=True)
                P.dma(lambda e, qs=qs, qn=qn, q0=q0, h=h: e.dma_start(
                    out=YT.ap[1024 + 128 * h:1024 + 128 * h + 128, q0 + qs:q0 + qs + qn], in_=yb.ap[:, 0:qn]),
                    reads=[yb], writes=[YT.name])
    A.release(m0)


L1_COLS = 2694


def l1_col0(o):
    if o < NT_OWN:
        return 1 + 128 * o
    if o < NT_OWN + 2:
        return 2179 + 128 * (o - NT_OWN)
    return 2436 + 128 * (o - NT_OWN - 2)


def emit_tail(k, ins, outs, YT, x_all):
    P, A, nc = k.P, k.A, k.nc
    ps = k.ps
    NO = NT_OWN + NT_PR
    m0 = A.mark()
    X1S = k.X1S
    hT1 = A.bf(8 * L1_COLS)
    hT13 = hT1.v("p (k c) -> p k c", c=L1_COLS)
    P.op("pool", lambda e: e.memset(hT1.ap, 0.0), writes=[hT1])
    m1 = A.mark()

    def outproj(w_dram, yt_cols, layer, x_get, x_put, tag):
        mm = A.mark()
        wst = [A.f32(1024) for _ in range(2)]
        wb = A.bf(16 * 1024)
        for kk in range(16):
            wsb = wst[kk % 2]
            P.dma(lambda e, wsb=wsb, kk=kk: e.dma_start(out=wsb.ap[:, 0:1024], in_=w_dram[128 * kk:128 * kk + 128, :]), writes=[wsb])
            CP(P, "pool" if kk % 2 else "dve", wb.ap[:, 1024 * kk:1024 * kk + 1024], wsb.ap[:, 0:1024], [wsb],
               [wb.cols(1024 * kk, 1024 * kk + 1024)])
        ytb = [A.bf(16 * 128) for _ in range(2)]
        xt = [A.f32(1024) for _ in range(2)]
        for o in range(NO):
            yb_ = ytb[o % 2]
            c0 = yt_cols(o)
            P.dma(lambda e, yb_=yb_, c0=c0: e.dma_start(
                out=yb_.v("p (kk t) -> p kk t", t=128), in_=YT.ap[:, c0:c0 + 128].rearrange("(kk p) t -> p kk t", p=128)),
                reads=[YT.name], writes=[yb_])
            v = 0 if o < NT_OWN else 1
            g = k.gB.ap[:, (layer * 2 + v) * 1024:(layer * 2 + v) * 1024 + 1024]
            xin = x_get(o, xt[o % 2])
            xo = x_put(o, xt[o % 2])
            for hf in range(2):
                pb = ps[2 * (o % 2) + hf]
                for kk in range(16):
                    MM(P, pb.ap, yb_.ap[:, 128 * kk:128 * kk + 128], wb.ap[:, 1024 * kk + 512 * hf:1024 * kk + 512 * hf + 512],
                       kk == 0, kk == 15, [yb_, wb], [pb])
                sl = slice(512 * hf, 512 * hf + 512)
                TT(P, "dve", xo.ap[:, sl], pb.ap, g[:, sl], ALU.mult, [pb, k.gB], [xo])
                TT(P, "pool" if hf else "dve", xo.ap[:, sl], xo.ap[:, sl], xin.ap[:, sl], ALU.add, [xo, xin], [xo])
            if tag == "final":
                P.dma(lambda e, xo=xo, o=o: e.dma_start(out=outs["y_out"][128 * o:128 * o + 128, :], in_=xo.ap), reads=[xo],
                      is_output=True)
            else:
                P.dma(lambda e, xo=xo, o=o: e.dma_start(out=X1S.ap[128 * o:128 * o + 128, :], in_=xo.ap), reads=[xo],
                      writes=[X1S.rows(128 * o, 128 * o + 128)])
            if tag != "final" and "X1" in k.debug:
                P.dma(lambda e, xo=xo, o=o: e.dma_start(out=outs["X1"][128 * o:128 * o + 128, :], in_=xo.ap), reads=[xo],
                      is_output=True)
        A.release(mm)

    def x_get0(o, buf):
        i = o if o < NT_OWN else NT_S + (o - NT_OWN)
        P.dma(lambda e, buf=buf, i=i: e.dma_start(out=buf.ap, in_=x_all[128 * i:128 * i + 128, :]), writes=[buf])
        return buf

    xo2 = [A.f32(1024) for _ in range(2)]
    outproj(ins["w_out0"], lambda o: 128 * o, 0, x_get0, lambda o, buf: xo2[o % 2], "l0")
    if "stopX1" in k.debug:
        A.release(m0)
        return
    mn = A.mark()
    emit_norm_phase(k, x_src=lambda o: X1S.ap[128 * o:128 * o + 128, :], tiles=list(range(NO)), layer=1, hT3=hT13, hT=hT1,
                    col0=l1_col0, variant=lambda o: 0 if o < NT_OWN else 1, cols_total=L1_COLS,
                    src_reg=lambda o: X1S.rows(128 * o, 128 * o + 128))
    A.release(mn)
    mm = A.mark()
    cvp = A.f32(64)
    P.dma(lambda e: e.dma_start(out=cvp.ap, in_=ins["cv_prm"][:, :]), writes=[cvp])
    wst = [A.f32(4 * 1024) for _ in range(2)]
    wbf = [A.bf(4 * 1024) for _ in range(2)]
    NB = L1_COLS + 2
    cgu = A.f32(NB)
    bgr = A.f32(NB)
    zs = A.f32(NB)
    cvb = A.f32(NB)
    cgt = A.f32(512)
    yrow = A.bf(NB)
    P.op("pool", lambda e: e.memset(cgu.ap, 0.0), writes=[cgu])
    w_in1 = ins["w_in1"]
    wins = [(c, min(512, L1_COLS - c)) for c in range(0, L1_COLS, 512)]
    for j in range(16):
        ws, wb = wst[j % 2], wbf[j % 2]
        for q in range(4):
            P.dma(lambda e, ws=ws, q=q, j=j: e.dma_start(
                out=ws.ap[:, 1024 * q:1024 * q + 1024].rearrange("p (k c) -> p k c", c=128),
                in_=w_in1[:, 2048 * q + 128 * j:2048 * q + 128 * j + 128].rearrange("(k p) c -> p k c", p=128)),
                writes=[ws.cols(1024 * q, 1024 * q + 1024)])
        CP(P, "pool", wb.ap, ws.ap, [ws], [wb])
        for (c0, nw) in wins:
            for q in range(4):
                pb = ps[4 + q]
                for kk in range(8):
                    MM(P, pb.ap[:, 0:nw], wb.ap[:, 1024 * q + 128 * kk:1024 * q + 128 * kk + 128], hT13[:, kk, c0:c0 + nw],
                       kk == 0, kk == 7, [wb, hT1], [pb])
            ACTF(P, bgr.ap[:, 1 + c0:1 + c0 + nw], ps[4].ap[:, 0:nw], AF.Identity, [ps[4]], [bgr])
            ACTF(P, cgt.ap[:, 0:nw], ps[5].ap[:, 0:nw], AF.Identity, [ps[5]], [cgt])
            TT(P, "dve", cgu.ap[:, 1 + c0:1 + c0 + nw], cgt.ap[:, 0:nw], ps[6].ap[:, 0:nw], ALU.mult, [cgt, ps[6]], [cgu])
            ACTF(P, zs.ap[:, 1 + c0:1 + c0 + nw], ps[7].ap[:, 0:nw], AF.Silu, [ps[7]], [zs])
        n = L1_COLS
        w0, w1, w2, bb = [cvp.ap[:, 16 * t_ + j:16 * t_ + j + 1] for t_ in range(4)]
        P.op("dve", lambda e, w1=w1, bb=bb, n=n: e.tensor_scalar(out=cvb.ap[:, 1:1 + n], in0=cgu.ap[:, 1:1 + n], scalar1=w1,
                                                                scalar2=bb, op0=ALU.mult, op1=ALU.add), reads=[cgu, cvp], writes=[cvb])
        STT(P, cvb.ap[:, 1:1 + n], cgu.ap[:, 0:n], w0, cvb.ap[:, 1:1 + n], ALU.mult, ALU.add, [cgu, cvp, cvb], [cvb])
        STT(P, cvb.ap[:, 1:1 + n], cgu.ap[:, 2:2 + n], w2, cvb.ap[:, 1:1 + n], ALU.mult, ALU.add, [cgu, cvp, cvb], [cvb])
        TT(P, "pool", cvb.ap[:, 1:1 + n], cvb.ap[:, 1:1 + n], bgr.ap[:, 1:1 + n], ALU.mult, [cvb, bgr], [cvb])
        TT(P, "dve", yrow.ap[:, 1:1 + n], cvb.ap[:, 1:1 + n], zs.ap[:, 1:1 + n], ALU.mult, [cvb, zs], [yrow])
        P.dma(lambda e, j=j, n=n: e.dma_start(out=YT.ap[128 * j:128 * j + 128, 0:n], in_=yrow.ap[:, 1:1 + n]), reads=[yrow],
              writes=[YT.name])
    A.release(mm)
    def x_get1(o, buf):
        P.dma(lambda e, buf=buf, o=o: e.dma_start(out=buf.ap, in_=X1S.ap[128 * o:128 * o + 128, :]),
              reads=[X1S.rows(128 * o, 128 * o + 128)], writes=[buf])
        return buf

    outproj(ins["w_out1"], l1_col0, 1, x_get1, lambda o, buf: xo2[o % 2], "final")
    A.release(m0)


def prep_core_inputs(q, inp):
    b, mir = q // 2, q % 2
    f = (lambda a: a[::-1]) if mir else (lambda a: a)
    d = {}
    xs = f(inp["x_sample"][b])
    xp0 = f(inp["x_prompt"][2 * q])
    xp1 = f(inp["x_prompt"][2 * q + 1])
    d["x_all"] = np.ascontiguousarray(np.concatenate([xs, xp0, xp1], 0))
    cvv = np.stack([inp["c"][b], inp["c_ctx"]], -1)
    d["cv"] = np.ascontiguousarray(cvv.reshape(8, 128, 2).transpose(1, 0, 2).reshape(128, 16))
    d["normw_fm"] = np.ascontiguousarray(inp["norm_w"].reshape(2, 8, 128).transpose(2, 0, 1).reshape(128, 16))
    d["ada_w"] = inp["ada_w"]
    d["adab_fm"] = np.ascontiguousarray(inp["ada_b"].reshape(2, 24, 128).transpose(2, 0, 1).reshape(128, 48))
    d["adab_g"] = np.ascontiguousarray(np.broadcast_to(inp["ada_b"][:, 2048:].reshape(1, 2048), (128, 2048)))
    w = inp["e_w_in"][0]
    mu = inp["e_mu"][0]
    if mir:
        perm = np.arange(EVEN_IN)
        perm[3072:3136], perm[3136:3200] = np.arange(3136, 3200), np.arange(3072, 3136)
        perm[3200:3264], perm[3264:3328] = np.arange(3264, 3328), np.arange(3200, 3264)
        w = w[:, perm]
        mu = mu[perm[:3328]]
    d["w_in0"] = np.ascontiguousarray(w)
    d["mu_b"] = np.ascontiguousarray(np.broadcast_to(mu.reshape(1, -1), (128, 3328)))
    qk = np.concatenate([np.tile(inp["e_q_norm"][0], 8), np.tile(inp["e_k_norm"][0], 8)])
    d["qkw_b"] = np.ascontiguousarray(np.broadcast_to(qk.reshape(1, -1), (128, 1024)))
    t = np.arange(4096)
    row = (t // 64).astype(np.float32)
    col = (t % 64).astype(np.float32)
    inv = (10000.0 ** (-np.arange(0, 32, 2, dtype=np.float32) / 32)).astype(np.float32)
    ar, ac = row[:, None] * inv, col[:, None] * inv
    cos = np.concatenate([np.cos(ar), np.cos(ar), np.cos(ac), np.cos(ac)], 1)
    sin = np.concatenate([-np.sin(ar), np.sin(ar), -np.sin(ac), np.sin(ac)], 1)
    d["rope_cs"] = np.ascontiguousarray(f(np.concatenate([cos, sin], 1).astype(np.float32)))
    dirs = (1, 0) if mir else (0, 1)
    rep = lambda v: np.broadcast_to(np.asarray(v, np.float32).reshape(1, -1), (128, v.size))
    d["rw_prm"] = np.ascontiguousarray(np.concatenate(
        [rep(inp["e_k_k"][0]), rep(inp["e_k_a"][0]), rep(inp["e_r_k"][0].reshape(-1)),
         rep(inp["e_lnx_w"][0]), rep(inp["e_lnx_b"][0])], 1))
    up = np.zeros((65, 4096), np.float32)
    for fd, td in enumerate(dirs):
        up[:64, 1024 * fd:1024 * fd + 1024] = inp["e_w_up"][0, td]
        up[64, 1024 * fd:1024 * fd + 1024] = inp["e_w0"][0, td]
        up[:64, 2048 + 1024 * fd:2048 + 1024 * fd + 1024] = inp["e_a_up"][0, td]
        up[64, 2048 + 1024 * fd:2048 + 1024 * fd + 1024] = inp["e_a0"][0, td]
    d["rw_up"] = up
    d["rw_const"] = rwkv_consts()
    sts = (inp["state_rwkv_fwd"][b, 0], inp["state_rwkv_bwd"][b, 0])
    st = np.zeros((128, 1024), np.float32)
    for fd, td in enumerate(dirs):
        hh = sts[td].reshape(8, 2, 64, 64).transpose(1, 3, 0, 2).reshape(128, 512)
        st[:, 512 * fd:512 * fd + 512] = hh
    d["st_in"] = st
    d["ctx_k"] = np.ascontiguousarray(inp["cache_diff_k"][b, 0].reshape(256, 1024))
    d["ctx_v"] = np.ascontiguousarray(inp["cache_diff_v"][b, 0].reshape(256, 1024))
    d["at_prm"] = np.ascontiguousarray(np.concatenate(
        [rep(inp["e_lambda"][0].reshape(-1)), inp["e_subln"][0].reshape(128, 1)], 1).astype(np.float32))
    d["w_out0"] = inp["e_w_out"][0]
    d["w_in1"] = inp["o_w_in"][0]
    d["w_out1"] = inp["o_w_out"][0]
    cw = inp["o_conv_w"][0]
    if mir:
        cw = cw[::-1]
    cvp = np.concatenate([cw.reshape(3, 16, 128), inp["o_conv_b"][0].reshape(1, 16, 128)], 0)
    d["cv_prm"] = np.ascontiguousarray(cvp.transpose(2, 0, 1).reshape(128, 64))
    return d


def rwkv_consts():
    idx = np.arange(128)
    ch, pos = idx // 64, idx % 64
    same = ch[:, None] == ch[None, :]
    out = np.zeros((128, 2050), np.float32)
    for dd in range(2):
        before = (pos[:, None] < pos[None, :]) if dd == 0 else (pos[:, None] > pos[None, :])
        eq = pos[:, None] == pos[None, :]
        incl = same & (before | eq)
        excl = same & before
        suf = same & before.T
        for m, mat in enumerate((incl, excl, suf)):
            out[:, (dd * 3 + m) * 128:(dd * 3 + m) * 128 + 128] = DECAY_C * mat
        msk = np.concatenate([excl, incl], 1).astype(np.float32)
        out[:, 770 + 512 * dd:770 + 512 * dd + 256] = msk
        out[:, 770 + 512 * dd + 256:770 + 512 * dd + 512] = msk
        out[:, 1794 + 128 * dd:1794 + 128 * dd + 128] = excl.T
    out[:, 768] = DECAY_C * (ch == 0)
    out[:, 769] = DECAY_C * (ch == 1)
    return out


def assemble(q, r, outs6):
    y_p, y_s, n_sf, n_sb, n_k, n_v = outs6
    b, mir = q // 2, q % 2
    S = y_p.shape[1]
    yo = np.asarray(r["y_out"])
    if mir:
        y_s[b, 2048:] = yo[:2048][::-1]
    else:
        y_s[b, :2048] = yo[:2048]
    so = np.asarray(r["st_out"])
    for j in range(2):
        yp = yo[2176 + 256 * j:2176 + 256 * j + 256]
        kc = np.asarray(r["kc_out"])[256 * j:256 * j + 256]
        vc = np.asarray(r["vc_out"])[256 * j:256 * j + 256]
        if mir:
            yp, kc, vc = yp[::-1], kc[::-1], vc[::-1]
        y_p[2 * q + j] = yp
        n_k[2 * q + j, 0] = kc.reshape(S, 8, 2, 64)
        n_v[2 * q + j, 0] = vc.reshape(S, 8, 128)
        for fd in range(2):
            st = so[j, fd].reshape(2, 64, 8, 64).transpose(2, 0, 3, 1).reshape(16, 64, 64)
            td = (1 - fd) if mir else fd
            (n_sf if td == 0 else n_sb)[2 * q + j, 0] = st


def kernel(**inputs):
    inp = {k_: np.asarray(v) for k_, v in inputs.items()}
    nc, ins, outs = build_program()
    in_maps = []
    for q in range(8):
        d = prep_core_inputs(q, inp)
        in_maps.append({n: d[n] for n in ins})
    res = run_bass_kernel_spmd(nc, in_maps, core_ids=list(range(8)))
    B, S = inp["x_prompt"].shape[0], inp["x_prompt"].shape[1]
    outs6 = (np.zeros(inp["x_prompt"].shape, np.float32), np.zeros(inp["x_sample"].shape, np.float32),
             np.zeros((B, 1, 16, 64, 64), np.float32), np.zeros((B, 1, 16, 64, 64), np.float32),
             np.zeros((B, 1, S, 8, 2, 64), np.float32), np.zeros((B, 1, S, 8, 128), np.float32))
    for q in range(8):
        assemble(q, res.results[q], outs6)
    return outs6
```

```python
import contextlib
import numpy as np
import concourse.bass as bass
import concourse.mybir as mybir
from concourse.bass_utils import run_bass_kernel_spmd

F32 = mybir.dt.float32
BF16 = mybir.dt.bfloat16
AF = mybir.ActivationFunctionType
ALU = mybir.AluOpType
AX = mybir.AxisListType

ENGS = ("pe", "act", "dve", "pool", "sp")
N_DMA_SEMS = 32
BIG = 1 << 40

D = 1024
NT_OWN, NT_OTH, NT_PR = 17, 15, 4
NT_S = NT_OWN + NT_OTH
NT = NT_S + NT_PR
T_ALL = NT * 128
T_OP = (NT_OWN + NT_PR) * 128
HT_COLS = 4612
EVEN_IN = 8448
NORM_EPS = 1e-6
GN_EPS = 64e-5
LAM_INIT = 0.8 - 0.6
DECAY_C = -0.6065306597126334


def tile_col0(i):
    if i < NT_S:
        return 1 + 128 * i
    if i < NT_S + 2:
        return 4098 + 128 * (i - NT_S)
    return 4355 + 128 * (i - NT_S - 2)


def op_index(i):
    return i if i < NT_OWN else NT_OWN + (i - NT_S)


OUT_TILES = list(range(NT_OWN)) + list(range(NT_S, NT))


class Prog:
    def __init__(self, nc):
        self.nc = nc
        self.streams = {e: [] for e in ENGS}
        self.count = {e: 0 for e in ENGS}
        self.seen = {e: {} for e in ENGS}
        self.regions = {}
        self.dma_cnt = [0] * N_DMA_SEMS
        self.dma_rr = 0
        self.out_tokens = []

    def _deps(self, reads, writes):
        deps = []
        for (name, lo, hi) in reads:
            for rec in self.regions.get(name, ()):
                if rec[0] < hi and lo < rec[1]:
                    if rec[2] is not None:
                        deps.append(rec[2])
                    if name.startswith("ps"):
                        deps.extend(rec[3].values())
        for (name, lo, hi) in writes:
            for rec in self.regions.get(name, ()):
                if rec[0] < hi and lo < rec[1]:
                    if rec[2] is not None:
                        deps.append(rec[2])
                    deps.extend(rec[3].values())
        return deps

    def _commit(self, reads, writes, token, rkey):
        for (name, lo, hi) in reads:
            lst = self.regions.setdefault(name, [])
            hit = False
            for rec in lst:
                if rec[0] < hi and lo < rec[1]:
                    rec[3][rkey] = token
                    hit = True
            if not hit:
                lst.append([lo, hi, None, {rkey: token}])
        for (name, lo, hi) in writes:
            lst = self.regions.setdefault(name, [])
            keep = []
            for rec in lst:
                if rec[1] <= lo or hi <= rec[0]:
                    keep.append(rec)
                    continue
                if rec[0] < lo:
                    keep.append([rec[0], lo, rec[2], dict(rec[3])])
                if hi < rec[1]:
                    keep.append([hi, rec[1], rec[2], dict(rec[3])])
            keep.append([lo, hi, token, {}])
            self.regions[name] = keep

    def _waits(self, eng, deps):
        best = {}
        for (k, v) in deps:
            if v > best.get(k, 0):
                best[k] = v
        out = []
        for k, v in best.items():
            if eng == "pe" and k == "pe":
                continue
            if self.seen[eng].get(k, 0) >= v:
                continue
            self.seen[eng][k] = v
            out.append((k, v))
        return out

    @staticmethod
    def _norm(regs):
        out = []
        for r in regs:
            if isinstance(r, str):
                out.append((r, 0, BIG))
            elif isinstance(r, Buf):
                out.append(r.reg)
            else:
                out.append(r)
        return out

    def op(self, eng, fn, reads=(), writes=()):
        reads = self._norm(reads)
        writes = self._norm(writes)
        waits = self._waits(eng, self._deps(reads, writes))
        self.count[eng] += 1
        token = (eng, self.count[eng])
        self.streams[eng].append(("op", waits, fn))
        self._commit(reads, writes, token, eng)
        return token

    def dma(self, fn, reads=(), writes=(), is_output=False, queue="sp"):
        queue = "sp"
        reads = self._norm(reads)
        writes = self._norm(writes)
        deps = self._deps(reads, writes)
        s = self.dma_rr
        self.dma_rr = (self.dma_rr + 1) % N_DMA_SEMS
        key = "dma%d" % s
        if self.dma_cnt[s] > 0:
            deps.append((key, self.dma_cnt[s]))
        waits = self._waits(queue, deps)
        self.dma_cnt[s] += 16
        token = (key, self.dma_cnt[s])
        self.streams[queue].append(("dma", waits, fn, s))
        self._commit(reads, writes, token, key)
        if is_output:
            self.out_tokens.append(token)
        return token

    def run(self):
        nc = self.nc
        fin = self._waits("sp", list(self.out_tokens))
        self.streams["sp"].append(("wait", fin))
        with contextlib.ExitStack() as st:
            sems = {}
            for e in ENGS:
                sems[e] = st.enter_context(nc.semaphore("s_" + e))
            for i in range(N_DMA_SEMS):
                sems["dma%d" % i] = st.enter_context(nc.semaphore("s_dma%d" % i))
            block = st.enter_context(nc.Block())

            def player(ename):
                def play(eng):
                    for item in self.streams[ename]:
                        for (k, v) in item[1]:
                            eng.wait_ge(sems[k], v)
                        if item[0] == "op":
                            item[2](eng).then_inc(sems[ename], 1)
                        elif item[0] == "dma":
                            item[2](eng).then_inc(sems["dma%d" % item[3]], 16)
                return play

            block.tensor(player("pe"))
            block.scalar(player("act"))
            block.vector(player("dve"))
            block.gpsimd(player("pool"))
            block.sync(player("sp"))


class Buf:
    def __init__(self, ap, reg):
        self.ap = ap
        self.reg = reg

    def cols(self, a, b):
        name, lo, hi = self.reg
        if name == "ar":
            if self.ap.dtype == BF16:
                r = (name, lo + a // 2, lo + (b + 1) // 2)
            else:
                r = (name, lo + a, lo + b)
        else:
            r = self.reg
        return Buf(self.ap[:, a:b], r)

    def v(self, s, **kw):
        return self.ap.rearrange(s, **kw)


class Arena:
    def __init__(self, ap, size):
        self.ap = ap
        self.size = size
        self.top = 0

    def f32(self, n):
        lo = self.top
        self.top += n
        assert self.top <= self.size, ("arena overflow", self.top, self.size)
        return Buf(self.ap[:, lo:lo + n], ("ar", lo, lo + n))

    def bf(self, n):
        nf = (n + 1) // 2
        lo = self.top
        self.top += nf
        assert self.top <= self.size, ("arena overflow", self.top, self.size)
        return Buf(self.ap[:, lo:lo + nf].bitcast(BF16)[:, 0:n], ("ar", lo, lo + nf))

    def mark(self):
        return self.top

    def release(self, m):
        self.top = m


class DramBuf:
    def __init__(self, name, ap):
        self.name = name
        self.ap = ap

    def rows(self, a, b):
        return (self.name, a, b)


ARENA_F32 = 52800


class K:
    pass


def build_program(debug=()):
    nc = bass.Bass("TRN2", target_bir_lowering=False)
    k = K()
    k.nc = nc
    k.debug = debug
    ins = {}
    outs = {}

    def din(name, shape, dt=F32):
        ins[name] = nc.dram_tensor(name, list(shape), dt, kind="ExternalInput").ap()
        return ins[name]

    def dout(name, shape, dt=F32):
        outs[name] = nc.dram_tensor(name, list(shape), dt, kind="ExternalOutput").ap()
        return outs[name]

    def dscr(name, shape, dt=F32):
        kind = "ExternalOutput" if name in debug else "Internal"
        t = nc.dram_tensor(name, list(shape), dt, kind=kind).ap()
        if name in debug:
            outs[name] = t
        return DramBuf(name, t)

    x_all = din("x_all", [T_ALL, D])
    cv = din("cv", [128, 16])
    normw_fm = din("normw_fm", [128, 16])
    ada_w = din("ada_w", [2, D, 3072])
    adab_fm = din("adab_fm", [128, 48])
    adab_g = din("adab_g", [128, 2048])
    w_in0 = din("w_in0", [D, EVEN_IN])
    mu_b = din("mu_b", [128, 3328])
    qkw_b = din("qkw_b", [128, 1024])
    rope_cs = din("rope_cs", [4096, 128])
    rw_prm = din("rw_prm", [128, 5120])
    rw_up = din("rw_up", [65, 4096])
    rw_const = din("rw_const", [128, 2050])
    st_in = din("st_in", [128, 1024])
    ctx_k = din("ctx_k", [256, 1024])
    ctx_v = din("ctx_v", [256, 1024])
    at_prm = din("at_prm", [128, 257])
    w_out0 = din("w_out0", [2048, 1024])
    w_in1 = din("w_in1", [D, 8192])
    w_out1 = din("w_out1", [2048, 1024])
    cv_prm = din("cv_prm", [128, 64])
    OB = dscr("OB", [T_OP, 1024])
    k.X1S = dscr("X1S", [T_OP, 1024])
    YT = dscr("YT", [2048, 2816], BF16)
    PA = dscr("PA", [T_ALL, 3328])
    GA = dscr("GA", [T_OP, 1024])
    QT = dscr("QT", [8, 128, T_OP], BF16)
    KT = dscr("KT", [8, 128, T_ALL + 256], BF16)
    VS = dscr("VS", [T_ALL + 256, 1024], BF16)
    GBT = dscr("GBT", [1024, T_OP])
    kc_out = dout("kc_out", [512, 1024])
    vc_out = dout("vc_out", [512, 1024])
    st_out = dout("st_out", [2, 2, 128, 512])
    y_out = dout("y_out", [T_OP, 1024])
    if "YB" in debug:
        dout("YB", [1024, T_OP])
    if "X1" in debug:
        dout("X1", [T_OP, 1024])
    if "YA" in debug:
        dout("YA", [T_OP, 1024])
    if "dumpQ" in debug:
        dout("dQ", [128, 12288])

    st = contextlib.ExitStack()
    with st:
        ar_t = st.enter_context(nc.sbuf_tensor("ar", [128, ARENA_F32], F32))
        psb = [st.enter_context(nc.psum_tensor("ps%d" % i, [128, 512], F32)) for i in range(8)]
        k.ps = [Buf(psb[i][:, :], ("ps%d" % i, 0, BIG)) for i in range(8)]
        A = Arena(ar_t, ARENA_F32)
        P = Prog(nc)
        k.P, k.A = P, A

        ident = A.f32(128)
        identb = A.bf(128)
        P.op("pool", lambda e: e.memset(ident.ap, 1.0), writes=[ident])
        P.op("pool", lambda e: e.affine_select(out=ident.ap, in_=ident.ap, pattern=[[-1, 128]],
                                               compare_op=ALU.is_equal, fill=0.0, base=0, channel_multiplier=1),
             reads=[ident], writes=[ident])
        P.op("dve", lambda e: e.tensor_copy(identb.ap, ident.ap), reads=[ident], writes=[identb])
        k.ident, k.identb = ident, identb

        cvt = A.f32(16)
        nwt = A.f32(16)
        abf = A.f32(48)
        modS = A.f32(32)
        modB = A.f32(32)
        gB = A.f32(4096)
        k.modS, k.modB, k.gB = modS, modB, gB
        m0 = A.mark()
        abg = A.f32(2048)
        screp = A.f32(16 * 128)
        P.dma(lambda e: e.dma_start(out=cvt.ap, in_=cv[:, :]), writes=[cvt])
        P.dma(lambda e: e.dma_start(out=nwt.ap, in_=normw_fm[:, :]), writes=[nwt])
        P.dma(lambda e: e.dma_start(out=abf.ap, in_=adab_fm[:, :]), writes=[abf])
        P.dma(lambda e: e.dma_start(out=abg.ap, in_=adab_g[:, :]), writes=[abg])
        P.op("act", lambda e: e.activation(out=cvt.ap, in_=cvt.ap, func=AF.Silu), reads=[cvt], writes=[cvt])
        P.op("dve", lambda e: e.tensor_copy(screp.v("p (a b) -> p a b", b=128),
                                            cvt.ap.unsqueeze(2).broadcast_to([128, 16, 128])),
             reads=[cvt], writes=[screp])
        wst = [A.f32(8 * 512) for _ in range(2)]
        for l in range(2 if "noada" not in debug else 0):
            for g in range(6):
                wb = wst[(l * 6 + g) % 2]
                P.dma(lambda e, wb=wb, l=l, g=g: e.dma_start(
                    out=wb.v("p (k c) -> p k c", c=512),
                    in_=ada_w[l, :, 512 * g:512 * g + 512].rearrange("(k p) c -> p k c", p=128)), writes=[wb])
                if g < 4:
                    pb = k.ps[g % 2]
                    for j in range(4):
                        for kk in range(8):
                            P.op("pe", lambda e, pb=pb, wb=wb, j=j, kk=kk: e.matmul(
                                pb.ap[:, 2 * j:2 * j + 2], wb.ap[:, kk * 512 + 128 * j: kk * 512 + 128 * j + 128],
                                cvt.ap[:, 2 * kk:2 * kk + 2], start=(kk == 0), stop=(kk == 7)),
                                reads=[wb, cvt], writes=[pb])
                    for v in range(2):
                        if g < 2:
                            dst = modB.ap[:, l * 16 + v * 8 + 4 * g: l * 16 + v * 8 + 4 * g + 4]
                            P.op("dve", lambda e, pb=pb, dst=dst, v=v, l=l, g=g: e.tensor_tensor(
                                out=dst, in0=pb.ap[:, 0:8].rearrange("p (j v) -> p j v", v=2)[:, :, v],
                                in1=abf.ap[:, l * 24 + 4 * g: l * 24 + 4 * g + 4], op=ALU.add),
                                reads=[pb, abf], writes=[modB])
                        else:
                            gg = g - 2
                            dst = modS.ap[:, l * 16 + v * 8 + 4 * gg: l * 16 + v * 8 + 4 * gg + 4]
                            P.op("dve", lambda e, pb=pb, dst=dst, v=v, l=l, g=g: e.tensor_tensor(
                                out=dst, in0=pb.ap[:, 0:8].rearrange("p (j v) -> p j v", v=2)[:, :, v],
                                in1=abf.ap[:, l * 24 + 4 * g: l * 24 + 4 * g + 4], op=ALU.add),
                                reads=[pb, abf], writes=[modS])
                            P.op("dve", lambda e, dst=dst, l=l, gg=gg: e.scalar_tensor_tensor(
                                out=dst, in0=dst, scalar=1.0, in1=nwt.ap[:, l * 8 + 4 * gg: l * 8 + 4 * gg + 4],
                                op0=ALU.add, op1=ALU.mult), reads=[modS, nwt], writes=[modS])
                else:
                    gg = g - 4
                    for v in range(2):
                        pb = k.ps[2 + v]
                        for kk in range(8):
                            P.op("pe", lambda e, pb=pb, wb=wb, v=v, kk=kk: e.matmul(
                                pb.ap, screp.ap[:, (2 * kk + v) * 128:(2 * kk + v) * 128 + 128],
                                wb.ap[:, kk * 512: kk * 512 + 512], start=(kk == 0), stop=(kk == 7)),
                                reads=[wb, screp], writes=[pb])
                        dst = gB.cols((l * 2 + v) * 1024 + 512 * gg, (l * 2 + v) * 1024 + 512 * gg + 512)
                        P.op("dve", lambda e, pb=pb, dst=dst, l=l, gg=gg: e.tensor_tensor(
                            out=dst.ap, in0=pb.ap, in1=abg.ap[:, l * 1024 + 512 * gg: l * 1024 + 512 * gg + 512],
                            op=ALU.add), reads=[pb, abg], writes=[dst])
        A.release(m0)

        mH = A.mark()
        hT = A.bf(8 * HT_COLS)
        k.hT = hT
        hT3 = hT.v("p (k c) -> p k c", c=HT_COLS)
        for zc in (0, 4097, 4354, 4611):
            P.op("pool", lambda e, zc=zc: e.memset(hT3[:, :, zc:zc + 1], 0.0), writes=[hT])
        mA = A.mark()
        emit_norm_phase(k, x_src=lambda i: x_all[128 * i:128 * i + 128, :], tiles=(list(range(NT)) if "noA1" not in debug else []), layer=0,
                        hT3=hT3, hT=hT, col0=tile_col0, variant=lambda i: 0 if i < NT_S else 1)
        A.release(mA)

        if "noA2" not in debug:
            emit_inproj0(k, ins, outs, PA, GA, QT, KT, VS, GBT, hT3)

        if "hT" in debug:
            hdbg = dout("hT_dbg", [128, 8 * HT_COLS], BF16)
            P.dma(lambda e: e.dma_start(out=hdbg[:, :], in_=hT.ap), reads=[hT], is_output=True)
        A.release(mH)
        if "noR" not in debug:
            emit_rwkv(k, ins, outs, PA, GA, OB, YT)
        if "noT" not in debug:
            emit_attn(k, ins, outs, QT, KT, VS, GBT, YT)
        if "noO" not in debug:
            emit_tail(k, ins, outs, YT, x_all)
        P.run()
    return nc, ins, outs


def emit_norm_phase(k, x_src, tiles, layer, hT3, hT, col0, variant, x_keep=None, cols_total=HT_COLS, src_reg=None):
    P, A = k.P, k.A
    xt = [A.f32(1024) for _ in range(2)]
    xn = [A.f32(1024) for _ in range(2)]
    junk = A.bf(1024)
    ss = [A.f32(1) for _ in range(2)]
    for n, i in enumerate(tiles):
        xb, xnb, ssb = xt[n % 2], xn[n % 2], ss[n % 2]
        src = x_src(i)
        if src is not None:
            P.dma(lambda e, xb=xb, src=src: e.dma_start(out=xb.ap, in_=src), writes=[xb],
                  reads=([src_reg(i)] if src_reg is not None else []))
        else:
            xb = x_keep(i)
        P.op("act", lambda e, xb=xb, ssb=ssb: e.activation(out=junk.ap, in_=xb.ap, func=AF.Square, accum_out=ssb.ap),
             reads=[xb], writes=[junk, ssb])
        P.op("act", lambda e, ssb=ssb: e.activation(out=ssb.ap, in_=ssb.ap, func=AF.Ln, scale=1.0 / D, bias=NORM_EPS),
             reads=[ssb], writes=[ssb])
        P.op("act", lambda e, ssb=ssb: e.activation(out=ssb.ap, in_=ssb.ap, func=AF.Exp, scale=-0.5),
             reads=[ssb], writes=[ssb])
        P.op("dve", lambda e, xb=xb, xnb=xnb, ssb=ssb: e.tensor_scalar(
            out=xnb.ap, in0=xb.ap, scalar1=ssb.ap[:, 0:1], scalar2=None, op0=ALU.mult), reads=[xb, ssb], writes=[xnb])
        v = variant(i)
        c0 = col0(i)
        for half in range(2):
            pb = k.ps[(2 * n + half) % 4]
            for q in range(4):
                kk = 4 * half + q
                P.op("pe", lambda e, pb=pb, xnb=xnb, kk=kk, q=q: e.transpose(
                    pb.ap[:, 128 * q:128 * q + 128], xnb.ap[:, 128 * kk:128 * kk + 128], k.ident.ap),
                    reads=[xnb, k.ident], writes=[pb])
            for q in range(4):
                kk = 4 * half + q
                mi = layer * 16 + v * 8 + kk
                hreg = ("ar", hT.reg[1] + (kk * cols_total + c0) // 2, hT.reg[1] + (kk * cols_total + c0 + 129) // 2)
                P.op("act", lambda e, pb=pb, kk=kk, q=q, mi=mi, c0=c0: e.activation(
                    out=hT3[:, kk, c0:c0 + 128], in_=pb.ap[:, 128 * q:128 * q + 128], func=AF.Identity,
                    scale=k.modS.ap[:, mi:mi + 1], bias=k.modB.ap[:, mi:mi + 1]),
                    reads=[pb, k.modS, k.modB], writes=[hreg])


def emit_inproj0(k, ins, outs, PA, GA, QT, KT, VS, GBT, hT3):
    P, A, nc = k.P, k.A, k.nc
    hT = k.hT
    w_in0 = ins["w_in0"]
    m0 = A.mark()
    mub = A.f32(3328)
    qkw = A.f32(1024)
    P.dma(lambda e: e.dma_start(out=mub.ap, in_=ins["mu_b"][:, :]), writes=[mub])
    P.dma(lambda e: e.dma_start(out=qkw.ap, in_=ins["qkw_b"][:, :]), writes=[qkw])
    wst = [A.f32(8 * 512) for _ in range(2)]
    wbf = [A.bf(8 * 512) for _ in range(2)]
    hd = [A.bf(8 * 128) for _ in range(2)]
    ev = [A.f32(512) for _ in range(6)]
    rs8 = [A.f32(8) for _ in range(2)]
    ropet = [A.f32(128) for _ in range(2)]
    tb = [A.bf(512) for _ in range(2)]
    evb = [A.bf(512) for _ in range(2)]

    def hreg(i):
        c0 = tile_col0(i)
        return ("ar", hT.reg[1], hT.reg[2])

    groups = []
    for c in range(0, 3072, 512):
        groups.append((c, 512, "A"))
    groups.append((3072, 256, "A"))
    for c in range(3328, 4352, 512):
        groups.append((c, 512, "ga"))
    for c in range(4352, 5376, 512):
        groups.append((c, 512, "pq"))
    for c in range(5376, 6400, 512):
        groups.append((c, 512, "pk"))
    for c in range(6400, 7424, 512):
        groups.append((c, 512, "pv"))
    cnt = 0
    only = [d[5:] for d in k.debug if d.startswith("only_")]
    if only:
        groups = [g for g in groups if g[2] in only]
    for gi, (cs, wdt, kind) in enumerate(groups):
        ws, wb = wst[gi % 2], wbf[gi % 2]
        P.dma(lambda e, ws=ws, cs=cs, wdt=wdt: e.dma_start(
            out=ws.v("p (k c) -> p k c", c=512)[:, :, 0:wdt],
            in_=w_in0[:, cs:cs + wdt].rearrange("(k p) c -> p k c", p=128)), writes=[ws])
        P.op("pool", lambda e, ws=ws, wb=wb: e.tensor_copy(wb.ap, ws.ap), reads=[ws], writes=[wb])
        if kind == "A":
            tiles = list(range(NT)) if cs >= 1024 else OUT_TILES
        elif kind in ("ga", "pq"):
            tiles = OUT_TILES
        else:
            tiles = list(range(NT))
        for i in tiles:
            c0 = tile_col0(i)
            sample = i < NT_S
            cnt += 1
            p1 = k.ps[4 + (cnt % 2) * 2]
            p2 = k.ps[5 + (cnt % 2) * 2]
            for kk in range(8):
                P.op("pe", lambda e, p1=p1, wb=wb, kk=kk, c0=c0, wdt=wdt: e.matmul(
                    p1.ap[:, 0:wdt], hT3[:, kk, c0:c0 + 128], wb.ap[:, kk * 512:kk * 512 + wdt],
                    start=(kk == 0), stop=(kk == 7)), reads=[wb, hT], writes=[p1])
            if kind == "A":
                hdb = hd[cnt % 2]
                hd3 = hdb.v("p (k c) -> p k c", c=128)
                P.op("pool", lambda e, hd3=hd3, c0=c0: e.tensor_tensor(
                    out=hd3, in0=hT3[:, :, c0 - 1:c0 + 127], in1=hT3[:, :, c0 + 1:c0 + 129], op=ALU.add),
                    reads=[hT], writes=[hdb])
                P.op("dve", lambda e, hd3=hd3, c0=c0: e.scalar_tensor_tensor(
                    out=hd3, in0=hd3, scalar=0.5, in1=hT3[:, :, c0:c0 + 128], op0=ALU.mult, op1=ALU.subtract),
                    reads=[hT, hdb], writes=[hdb])
                for kk in range(8):
                    P.op("pe", lambda e, p2=p2, wb=wb, kk=kk, hd3=hd3, wdt=wdt: e.matmul(
                        p2.ap[:, 0:wdt], hd3[:, kk, :], wb.ap[:, kk * 512:kk * 512 + wdt],
                        start=(kk == 0), stop=(kk == 7)), reads=[wb, hdb], writes=[p2])
                e1, e2 = ev[(cnt % 2) * 2], ev[(cnt % 2) * 2 + 1]
                P.op("act", lambda e, p2=p2, e1=e1, wdt=wdt: e.activation(out=e1.ap[:, 0:wdt], in_=p2.ap[:, 0:wdt], func=AF.Identity),
                     reads=[p2], writes=[e1])
                P.op("dve", lambda e, e1=e1, cs=cs, wdt=wdt: e.tensor_tensor(
                    out=e1.ap[:, 0:wdt], in0=e1.ap[:, 0:wdt], in1=mub.ap[:, cs:cs + wdt], op=ALU.mult),
                    reads=[e1, mub], writes=[e1])
                P.op("dve", lambda e, e1=e1, e2=e2, p1=p1, wdt=wdt: e.tensor_tensor(
                    out=e2.ap[:, 0:wdt], in0=e1.ap[:, 0:wdt], in1=p1.ap[:, 0:wdt], op=ALU.add),
                    reads=[e1, p1], writes=[e2])
                P.dma(lambda e, e2=e2, i=i, cs=cs, wdt=wdt: e.dma_start(
                    out=PA.ap[128 * i:128 * i + 128, cs:cs + wdt], in_=e2.ap[:, 0:wdt]),
                    reads=[e2], writes=[PA.rows(128 * i, 128 * i + 128)], queue="pool")
            elif kind == "ga":
                e1 = ev[cnt % 2]
                o = op_index(i)
                P.op("act", lambda e, p1=p1, e1=e1: e.activation(out=e1.ap, in_=p1.ap, func=AF.Silu), reads=[p1], writes=[e1])
                P.dma(lambda e, e1=e1, o=o, cs=cs: e.dma_start(
                    out=GA.ap[128 * o:128 * o + 128, cs - 3328:cs - 3328 + 512], in_=e1.ap),
                    reads=[e1], writes=[GA.rows(128 * o, 128 * o + 128)], queue="pool")
            elif kind == "pv":
                e1, eb = ev[cnt % 2], evb[cnt % 2]
                P.op("act", lambda e, p1=p1, e1=e1: e.activation(out=e1.ap, in_=p1.ap, func=AF.Identity), reads=[p1], writes=[e1])
                P.op("dve", lambda e, e1=e1, eb=eb: e.tensor_copy(eb.ap, e1.ap), reads=[e1], writes=[eb])
                P.dma(lambda e, eb=eb, i=i, cs=cs: e.dma_start(
                    out=VS.ap[128 * i:128 * i + 128, cs - 6400:cs - 6400 + 512], in_=eb.ap),
                    reads=[eb], writes=[VS.rows(128 * i, 128 * i + 128)], queue="pool")
                if not sample:
                    pi = i - NT_S
                    P.dma(lambda e, e1=e1, pi=pi, cs=cs: e.dma_start(
                        out=outs["vc_out"][128 * pi:128 * pi + 128, cs - 6400:cs - 6400 + 512], in_=e1.ap),
                        reads=[e1], is_output=True, queue="pool")
            else:
                isq = kind == "pq"
                base = 4352 if isq else 5376
                h0 = (cs - base) // 128
                wof = 0 if isq else 512
                e1, e2, e3 = ev[(cnt % 2) * 3], ev[(cnt % 2) * 3 + 1], ev[(cnt % 2) * 3 + 2]
                r8 = rs8[cnt % 2]
                P.op("act", lambda e, p1=p1, e1=e1: e.activation(out=e1.ap, in_=p1.ap, func=AF.Square), reads=[p1], writes=[e1])
                P.op("dve", lambda e, e1=e1, r8=r8: e.tensor_reduce(
                    out=r8.ap, in_=e1.v("p (g d) -> p g d", d=64), axis=AX.X, op=ALU.add), reads=[e1], writes=[r8])
                P.op("act", lambda e, r8=r8: e.activation(out=r8.ap, in_=r8.ap, func=AF.Ln, scale=1.0 / 64, bias=NORM_EPS),
                     reads=[r8], writes=[r8])
                P.op("act", lambda e, r8=r8: e.activation(out=r8.ap, in_=r8.ap, func=AF.Exp, scale=-0.5), reads=[r8], writes=[r8])
                P.op("dve", lambda e, p1=p1, e2=e2, r8=r8: e.tensor_tensor(
                    out=e2.v("p (g d) -> p g d", d=64), in0=p1.v("p (g d) -> p g d", d=64),
                    in1=r8.ap.unsqueeze(2).broadcast_to([128, 8, 64]), op=ALU.mult), reads=[p1, r8], writes=[e2])
                P.op("dve", lambda e, e2=e2, wof=wof: e.tensor_tensor(
                    out=e2.ap, in0=e2.ap, in1=qkw.ap[:, wof:wof + 512], op=ALU.mult), reads=[e2, qkw], writes=[e2])
                tbb = tb[cnt % 2]
                if (not sample) and (not isq):
                    pi = i - NT_S
                    P.dma(lambda e, e2=e2, pi=pi, cs=cs: e.dma_start(
                        out=outs["kc_out"][128 * pi:128 * pi + 128, cs - 5376:cs - 5376 + 512], in_=e2.ap),
                        reads=[e2], is_output=True, queue="pool")
                if sample:
                    rt = ropet[cnt % 2]
                    P.dma(lambda e, rt=rt, i=i: e.dma_start(out=rt.ap, in_=ins["rope_cs"][128 * i:128 * i + 128, :]), writes=[rt])
                    cosb = rt.ap[:, 0:64].unsqueeze(1).broadcast_to([128, 8, 64])
                    P.op("dve", lambda e, e1=e1, e2=e2, cosb=cosb: e.tensor_tensor(
                        out=e1.v("p (g d) -> p g d", d=64), in0=e2.v("p (g d) -> p g d", d=64), in1=cosb, op=ALU.mult),
                        reads=[e2, rt], writes=[e1])
                    for hf in range(2):
                        sinb = rt.ap[:, 64:128].rearrange("p (c h d) -> p c h d", c=2, h=2)[:, :, hf, :] \
                            .unsqueeze(1).broadcast_to([128, 8, 2, 16])
                        P.op("pool", lambda e, e2=e2, e3=e3, hf=hf, sinb=sinb: e.tensor_tensor(
                            out=e3.v("p (g c h d) -> p g c h d", c=2, h=2, d=16)[:, :, :, hf, :],
                            in0=e2.v("p (g c h d) -> p g c h d", c=2, h=2, d=16)[:, :, :, 1 - hf, :],
                            in1=sinb, op=ALU.mult), reads=[e2, rt], writes=[e3])
                    P.op("dve", lambda e, e1=e1, e3=e3, tbb=tbb: e.tensor_tensor(out=tbb.ap, in0=e1.ap, in1=e3.ap, op=ALU.add),
                         reads=[e1, e3], writes=[tbb])
                else:
                    P.op("dve", lambda e, e2=e2, tbb=tbb: e.tensor_copy(tbb.ap, e2.ap), reads=[e2], writes=[tbb])
                pt = k.ps[cnt % 2]
                ptb = pt.ap.bitcast(BF16)
                for hh in range(4):
                    P.op("pe", lambda e, ptb=ptb, tbb=tbb, hh=hh: e.transpose(
                        ptb[:, 128 * hh:128 * hh + 128], tbb.ap[:, 128 * hh:128 * hh + 128], k.identb.ap),
                        reads=[tbb, k.identb], writes=[pt])
                eb = evb[cnt % 2]
                P.op("act", lambda e, ptb=ptb, eb=eb: e.activation(out=eb.ap, in_=ptb[:, 0:512], func=AF.Identity),
                     reads=[pt], writes=[eb])
                if isq:
                    o = op_index(i)
                    P.dma(lambda e, eb=eb, o=o, h0=h0: e.dma_start(
                        out=QT.ap[h0:h0 + 4, :, 128 * o:128 * o + 128].rearrange("h p t -> p h t"),
                        in_=eb.v("p (h t) -> p h t", t=128)), reads=[eb], writes=[QT.name], queue="pool")
                else:
                    P.dma(lambda e, eb=eb, i=i, h0=h0: e.dma_start(
                        out=KT.ap[h0:h0 + 4, :, 128 * i:128 * i + 128].rearrange("h p t -> p h t"),
                        in_=eb.v("p (h t) -> p h t", t=128)), reads=[eb], writes=[KT.name], queue="pool")
    wins = [(1 + 512 * w_, 512, 512 * w_) for w_ in range(4)] + [(2049, 128, 2048), (4098, 256, 2176), (4355, 256, 2432)]
    if (not only) or ("gb" in only):
        for g2 in range(2):
            gi = len(groups) + g2
            ws, wb = wst[gi % 2], wbf[gi % 2]
            cs = 7424 + 512 * g2
            P.dma(lambda e, ws=ws, cs=cs: e.dma_start(
                out=ws.v("p (k c) -> p k c", c=512), in_=w_in0[:, cs:cs + 512].rearrange("(k p) c -> p k c", p=128)), writes=[ws])
            P.op("pool", lambda e, ws=ws, wb=wb: e.tensor_copy(wb.ap, ws.ap), reads=[ws], writes=[wb])
            for q in range(4):
                for (c0, nw, o0) in wins:
                    cnt += 1
                    p1 = k.ps[4 + cnt % 4]
                    e1 = ev[cnt % 2]
                    for kk in range(8):
                        P.op("pe", lambda e, p1=p1, wb=wb, kk=kk, c0=c0, nw=nw, q=q: e.matmul(
                            p1.ap[:, 0:nw], wb.ap[:, kk * 512 + 128 * q:kk * 512 + 128 * q + 128], hT3[:, kk, c0:c0 + nw],
                            start=(kk == 0), stop=(kk == 7)), reads=[wb, hT], writes=[p1])
                    P.op("act", lambda e, p1=p1, e1=e1, nw=nw: e.activation(out=e1.ap[:, 0:nw], in_=p1.ap[:, 0:nw], func=AF.Silu),
                         reads=[p1], writes=[e1])
                    r0 = 512 * g2 + 128 * q
                    P.dma(lambda e, e1=e1, r0=r0, o0=o0, nw=nw: e.dma_start(out=GBT.ap[r0:r0 + 128, o0:o0 + nw], in_=e1.ap[:, 0:nw]),
                          reads=[e1], writes=[GBT.name])
    if (not only) or ("ctx" in only):
        for t_ in range(2):
            for which in range(2):
                cnt += 1
                e1a, e1b = ev[(cnt % 2) * 2], ev[(cnt % 2) * 2 + 1]
                src = ins["ctx_k"] if which == 0 else ins["ctx_v"]
                for hf in range(2):
                    ebuf = e1a if hf == 0 else e1b
                    P.dma(lambda e, ebuf=ebuf, src=src, t_=t_, hf=hf: e.dma_start(
                        out=ebuf.ap, in_=src[128 * t_:128 * t_ + 128, 512 * hf:512 * hf + 512]), writes=[ebuf])
                    tbb = tb[(cnt + hf) % 2]
                    P.op("dve", lambda e, ebuf=ebuf, tbb=tbb: e.tensor_copy(tbb.ap, ebuf.ap), reads=[ebuf], writes=[tbb])
                    if which == 1:
                        P.dma(lambda e, tbb=tbb, t_=t_, hf=hf: e.dma_start(
                            out=VS.ap[T_ALL + 128 * t_:T_ALL + 128 * t_ + 128, 512 * hf:512 * hf + 512], in_=tbb.ap),
                            reads=[tbb], writes=[VS.rows(T_ALL + 128 * t_, T_ALL + 128 * t_ + 128)])
                    else:
                        pt = k.ps[(cnt + hf) % 2]
                        ptb = pt.ap.bitcast(BF16)
                        for hh in range(4):
                            P.op("pe", lambda e, ptb=ptb, tbb=tbb, hh=hh: e.transpose(
                                ptb[:, 128 * hh:128 * hh + 128], tbb.ap[:, 128 * hh:128 * hh + 128], k.identb.ap),
                                reads=[tbb, k.identb], writes=[pt])
                        eb = evb[(cnt + hf) % 2]
                        P.op("act", lambda e, ptb=ptb, eb=eb: e.activation(out=eb.ap, in_=ptb[:, 0:512], func=AF.Identity),
                             reads=[pt], writes=[eb])
                        P.dma(lambda e, eb=eb, t_=t_, hf=hf: e.dma_start(
                            out=KT.ap[4 * hf:4 * hf + 4, :, T_ALL + 128 * t_:T_ALL + 128 * t_ + 128].rearrange("h p t -> p h t"),
                            in_=eb.v("p (h t) -> p h t", t=128)), reads=[eb], writes=[KT.name])
    A.release(m0)


def TT(P, eng, out, in0, in1, op, r, w):
    P.op(eng, lambda e: e.tensor_tensor(out=out, in0=in0, in1=in1, op=op), reads=r, writes=w)


def ACTF(P, out, in_, func, r, w, **kw):
    P.op("act", lambda e: e.activation(out=out, in_=in_, func=func, **kw), reads=r, writes=w)


def MM(P, out, lhsT, rhs, start, stop, r, w):
    P.op("pe", lambda e: e.matmul(out, lhsT, rhs, start=start, stop=stop), reads=r, writes=w)


def TR(P, out, in_, ident, r, w):
    P.op("pe", lambda e: e.transpose(out, in_, ident), reads=r, writes=w)


def CP(P, eng, out, in_, r, w):
    P.op(eng, lambda e: e.tensor_copy(out, in_), reads=r, writes=w)


def STT(P, out, in0, scalar, in1, op0, op1, r, w):
    P.op("dve", lambda e: e.scalar_tensor_tensor(out=out, in0=in0, scalar=scalar, in1=in1, op0=op0, op1=op1),
         reads=r, writes=w)


def RED(P, out, in_, r, w):
    P.op("dve", lambda e: e.tensor_reduce(out=out, in_=in_, axis=AX.X, op=ALU.add), reads=r, writes=w)


def g64(ap):
    return ap.rearrange("p (g d) -> p g d", d=64)


def b64(ap16):
    return ap16.unsqueeze(2).broadcast_to([128, 16, 64])


def emit_rwkv(k, ins, outs, PA, GA, OB, YT):
    P, A = k.P, k.A
    ps = k.ps
    psb = [p_.ap.bitcast(BF16) for p_ in ps]
    m0 = A.mark()
    prm = A.f32(5120)
    kkb, kab, rkb, lwb, lbb = [prm.cols(1024 * i, 1024 * i + 1024) for i in range(5)]
    wup = A.f32(4096)
    CM = A.f32(768)
    IND = A.f32(2)
    MSK = A.f32(1024)
    lT = A.f32(256)
    H = [A.f32(512) for _ in range(2)]
    Hb = [A.bf(512) for _ in range(2)]
    gC = A.f32(16)
    sbst = A.f32(21 * 16)
    P.dma(lambda e: e.dma_start(out=prm.ap, in_=ins["rw_prm"][:, :]), writes=[prm])
    P.dma(lambda e: e.dma_start(out=wup.ap[0:65, :], in_=ins["rw_up"][:, :]), writes=[wup])
    P.dma(lambda e: e.dma_start(out=CM.ap, in_=ins["rw_const"][:, 0:768]), writes=[CM])
    P.dma(lambda e: e.dma_start(out=IND.ap, in_=ins["rw_const"][:, 768:770]), writes=[IND])
    P.dma(lambda e: e.dma_start(out=MSK.ap, in_=ins["rw_const"][:, 770:1794]), writes=[MSK])
    P.op("pool", lambda e: e.memset(lT.ap, 1.0), writes=[lT])
    pa_one = A.f32(3328)
    pa2 = [pa_one, pa_one]
    LT = A.f32(128)
    SW, AA, KAP, KD, BE, X1, X2 = [A.f32(1024) for _ in range(7)]
    E = [A.f32(1024) for _ in range(2)]
    s16, sbc = A.f32(16), A.f32(16)
    RTIL, KATIL, BTIL, KTIL, BH, KH, VB = [A.bf(1024) for _ in range(7)]
    QRS, BKS = A.bf(2048), A.bf(2048)
    SC = A.bf(16 * 512)
    Nb = [A.bf(2048) for _ in range(2)]
    Ntb = [A.bf(2048) for _ in range(2)]
    Tb = [A.bf(2048) for _ in range(2)]
    Wsb = A.bf(1024)
    Usbc = [A.bf(1024) for _ in range(2)]
    VBc = [A.bf(1024) for _ in range(2)]
    Osb = A.f32(1024)
    obuf, gabuf = X2, E[0]
    for zb in (Wsb, Usbc[0], Usbc[1], VBc[0], VBc[1]):
        P.op("pool", lambda e, zb=zb: e.memset(zb.ap, 0.0), writes=[zb])
    ytb = A.bf(1024)
    lT3 = lT.v("p (a t) -> p a t", t=128)
    QRS4 = QRS.v("p (j x t) -> p j x t", x=2, t=128)
    BKS4 = BKS.v("p (j x t) -> p j x t", x=2, t=128)
    SC3 = SC.v("p (h c) -> p h c", c=512)

    def visit(i, d, state_only, chunks, finalize):
        n = visit.n
        visit.n += 1
        pa = pa2[n % 2]
        c_lo = 1024 if state_only else 0
        P.dma(lambda e: e.dma_start(out=pa.ap[:, c_lo:3328], in_=PA.ap[128 * i:128 * i + 128, c_lo:3328]),
              reads=[PA.rows(128 * i, 128 * i + 128)], writes=[pa])
        par, pak, pav = pa.ap[:, 0:1024], pa.ap[:, 1024:2048], pa.ap[:, 2048:3072]
        Hd, Hbd = H[d], Hb[d]
        Hd3 = Hd.v("p (j v) -> p j v", v=64)
        Hb3 = Hbd.v("p (j v) -> p j v", v=64)
        CP(P, "pool", VB.ap, pav, [pa], [VB])
        CP(P, "pool", VBc[0].ap[0:64, :], pav[0:64, :], [pa], [VBc[0]])
        CP(P, "pool", VBc[1].ap[64:128, :], pav[64:128, :], [pa], [VBc[1]])
        ACTF(P, LT.ap[:, 0:64], pa.ap[:, 3072 + 64 * d:3136 + 64 * d], AF.Tanh, [pa], [LT])
        ACTF(P, LT.ap[:, 64:128], pa.ap[:, 3200 + 64 * d:3264 + 64 * d], AF.Identity, [pa], [LT])
        TR(P, ps[0].ap[0:64, 0:128], LT.ap[:, 0:64], k.ident.ap, [LT, k.ident], [ps[0]])
        TR(P, ps[0].ap[0:64, 128:256], LT.ap[:, 64:128], k.ident.ap, [LT, k.ident], [ps[0]])
        ACTF(P, lT3[0:64, :, :], ps[0].ap[0:64, 0:256].rearrange("p (a t) -> p a t", t=128), AF.Identity, [ps[0]], [lT])
        for hf in range(2):
            MM(P, ps[hf].ap, lT3[0:65, 0, :], wup.ap[0:65, 1024 * d + 512 * hf:1024 * d + 512 * hf + 512], True, True,
               [lT, wup], [ps[hf]])
            MM(P, ps[2 + hf].ap, lT3[0:65, 1, :], wup.ap[0:65, 2048 + 1024 * d + 512 * hf:2048 + 1024 * d + 512 * hf + 512],
               True, True, [lT, wup], [ps[2 + hf]])
        for hf in range(2):
            ACTF(P, SW.ap[:, 512 * hf:512 * hf + 512], ps[hf].ap, AF.Sigmoid, [ps[hf]], [SW])
            ACTF(P, AA.ap[:, 512 * hf:512 * hf + 512], ps[2 + hf].ap, AF.Sigmoid, [ps[2 + hf]], [AA])
        TT(P, "dve", X1.ap, pak, kkb.ap, ALU.mult, [pa, kkb], [X1])
        ACTF(P, X2.ap, X1.ap, AF.Square, [X1], [X2])
        RED(P, s16.ap, g64(X2.ap), [X2], [s16])
        ACTF(P, s16.ap, s16.ap, AF.Ln, [s16], [s16], bias=1e-24)
        ACTF(P, s16.ap, s16.ap, AF.Exp, [s16], [s16], scale=-0.5)
        TT(P, "dve", g64(KAP.ap), g64(X1.ap), b64(s16.ap), ALU.mult, [X1, s16], [KAP])
        STT(P, X2.ap, AA.ap, -1.0, kab.ap, ALU.add, ALU.mult, [AA, kab], [X2])
        STT(P, KD.ap, X2.ap, 1.0, pak, ALU.add, ALU.mult, [X2, pa], [KD])
        TT(P, "pool", BE.ap, KAP.ap, AA.ap, ALU.mult, [KAP, AA], [BE])
        if not state_only:
            TT(P, "pool", X1.ap, par, rkb.ap, ALU.mult, [pa, rkb], [X1])
            TT(P, "pool", X1.ap, X1.ap, KD.ap, ALU.mult, [X1, KD], [X1])
            RED(P, sbc.ap, g64(X1.ap), [X1], [sbc])
        cm = CM.v("p (d m t) -> p d m t", d=2, m=3)
        for hf in range(2):
            MM(P, ps[4 + hf].ap, cm[:, d, 0, :], SW.ap[:, 512 * hf:512 * hf + 512], True, True, [CM, SW], [ps[4 + hf]])
            MM(P, ps[6 + hf].ap, cm[:, d, 1, :], SW.ap[:, 512 * hf:512 * hf + 512], True, True, [CM, SW], [ps[6 + hf]])
        for hf in range(2):
            ACTF(P, E[0].ap[:, 512 * hf:512 * hf + 512], ps[4 + hf].ap, AF.Exp, [ps[4 + hf]], [E[0]])
            ACTF(P, E[1].ap[:, 512 * hf:512 * hf + 512], ps[4 + hf].ap, AF.Exp, [ps[4 + hf]], [E[1]], scale=-1.0)
        if not state_only:
            TT(P, "dve", RTIL.ap, par, E[0].ap, ALU.mult, [pa, E[0]], [RTIL])
        TT(P, "pool", BTIL.ap, BE.ap, E[1].ap, ALU.mult, [BE, E[1]], [BTIL])
        TT(P, "dve", KTIL.ap, KD.ap, E[1].ap, ALU.mult, [KD, E[1]], [KTIL])
        for hf in range(2):
            MM(P, ps[hf].ap, cm[:, d, 2, :], SW.ap[:, 512 * hf:512 * hf + 512], True, True, [CM, SW], [ps[hf]])
        for hf in range(2):
            ACTF(P, E[0].ap[:, 512 * hf:512 * hf + 512], ps[6 + hf].ap, AF.Exp, [ps[6 + hf]], [E[0]])
            ACTF(P, E[1].ap[:, 512 * hf:512 * hf + 512], ps[hf].ap, AF.Exp, [ps[hf]], [E[1]])
        TT(P, "dve", KATIL.ap, KAP.ap, E[0].ap, ALU.mult, [KAP, E[0]], [KATIL])
        TT(P, "pool", BH.ap, BE.ap, E[1].ap, ALU.mult, [BE, E[1]], [BH])
        TT(P, "pool", KH.ap, KD.ap, E[1].ap, ALU.mult, [KD, E[1]], [KH])
        for j in range(8):
            MM(P, ps[2].ap[:, 2 * j:2 * j + 2], SW.ap[:, 128 * j:128 * j + 128], IND.ap, True, True, [SW, IND], [ps[2]])
        ACTF(P, gC.ap, ps[2].ap[:, 0:16], AF.Exp, [ps[2]], [gC])
        gC3 = gC.v("p (j c) -> p j c", c=2)
        quants = [(KATIL, QRS4, 0), (BTIL, BKS4, 0), (KTIL, BKS4, 1)]
        if not state_only:
            quants.append((RTIL, QRS4, 1))
        for qi, (src, dst4, x) in enumerate(quants):
            dstbuf = QRS if dst4 is QRS4 else BKS
            pi_ = 4 + qi % 4
            for j in range(8):
                TR(P, psb[pi_][:, 128 * j:128 * j + 128], src.ap[:, 128 * j:128 * j + 128], k.identb.ap,
                   [src, k.identb], [ps[pi_]])
            dview = dst4[:, :, x, :]
            sview = psb[pi_].rearrange("p (j t) -> p j t", t=128)
            if qi % 2 == 0:
                ACTF(P, dview, sview, AF.Identity, [ps[pi_]], [dstbuf])
            else:
                CP(P, "dve", dview, sview, [ps[pi_]], [dstbuf])
        msk = MSK.ap[:, 512 * d:512 * d + 512]
        ncol = 128 if state_only else 256
        for h in range(16):
            j, e = h // 2, h % 2
            pb = ps[h % 4]
            rhs = QRS4[64 * e:64 * e + 64, j, :, :] if not state_only else QRS4[64 * e:64 * e + 64, j, 0:1, :]
            MM(P, pb.ap[:, 0:ncol], BKS4[64 * e:64 * e + 64, j, 0, :], rhs, True, True, [BKS, QRS], [pb])
            MM(P, pb.ap[:, 256:256 + ncol], BKS4[64 * e:64 * e + 64, j, 1, :], rhs, True, True, [BKS, QRS], [pb])
            TT(P, "dve", SC3[:, h, :], pb.ap, msk, ALU.mult, [pb, MSK], [SC])
        CP(P, "pool", Nb[0].v("p (h t) -> p h t", t=128), SC3[:, :, 0:128], [SC], [Nb[0]])
        TT(P, "dve", Tb[0].v("p (h t) -> p h t", t=128),
           k.identb.ap.unsqueeze(1).broadcast_to([128, 16, 128]), SC3[:, :, 0:128], ALU.subtract, [SC, k.identb], [Tb[0]])
        for hh in range(2):
            pi_ = 4 + hh
            for q in range(8):
                h = 8 * hh + q
                TR(P, psb[pi_][:, 128 * q:128 * q + 128], SC3[:, h, 0:128], k.identb.ap, [SC, k.identb], [ps[pi_]])
            ACTF(P, Ntb[0].ap[:, 1024 * hh:1024 * hh + 1024], psb[pi_], AF.Identity, [ps[pi_]], [Ntb[0]])
        cur = 0
        for lvl in range(1, 6):
            nxt = 1 - cur
            last = lvl == 5
            for hq in range(4):
                sl = slice(512 * hq, 512 * hq + 512)
                pbt = ps[hq % 4]
                for q in range(4):
                    h = 4 * hq + q
                    c1 = slice(128 * h, 128 * h + 128)
                    MM(P, pbt.ap[:, 128 * q:128 * q + 128], Nb[cur].ap[:, c1], Ntb[cur].ap[:, c1], True, True,
                       [Nb[cur], Ntb[cur]], [pbt])
                ACTF(P, Ntb[nxt].ap[:, sl], pbt.ap, AF.Identity, [pbt], [Ntb[nxt]])
                if not last:
                    pbn = ps[4 + hq % 4]
                    for q in range(4):
                        h = 4 * hq + q
                        c1 = slice(128 * h, 128 * h + 128)
                        MM(P, pbn.ap[:, 128 * q:128 * q + 128], Ntb[cur].ap[:, c1], Nb[cur].ap[:, c1], True, True,
                           [Nb[cur], Ntb[cur]], [pbn])
                    CP(P, "dve", Nb[nxt].ap[:, sl], pbn.ap, [pbn], [Nb[nxt]])
            for hq in range(4):
                sl = slice(512 * hq, 512 * hq + 512)
                pb = ps[4 + hq % 4] if last else ps[hq % 4]
                for q in range(4):
                    h = 4 * hq + q
                    c1 = slice(128 * h, 128 * h + 128)
                    MM(P, pb.ap[:, 128 * q:128 * q + 128], k.identb.ap, Tb[cur].ap[:, c1], True, False,
                       [k.identb, Tb[cur]], [pb])
                    MM(P, pb.ap[:, 128 * q:128 * q + 128], Ntb[nxt].ap[:, c1], Tb[cur].ap[:, c1], False, True,
                       [Ntb[nxt], Tb[cur]], [pb])
                if hq % 2 == 0:
                    ACTF(P, Tb[nxt].ap[:, sl], pb.ap, AF.Identity, [pb], [Tb[nxt]])
                else:
                    CP(P, "dve", Tb[nxt].ap[:, sl], pb.ap, [pb], [Tb[nxt]])
            cur = nxt
        MTb = Tb[cur]
        MT3 = MTb.v("p (h t) -> p h t", t=128)
        for flag in ("xa", "xb", "xc", "xd"):
            if flag in k.debug:
                for h in range(16):
                    j, e = h // 2, h % 2
                    es = slice(64 * e, 64 * e + 64)
                    pb = ps[h // 8]
                    oc = slice(64 * (h % 8), 64 * (h % 8) + 64)
                    if flag == "xa":
                        MM(P, pb.ap[:, oc], BKS4[es, j, 0, :], BKS4[es, j, 1, 0:64], True, True, [BKS], [pb])
                    elif flag == "xb":
                        MM(P, pb.ap[:, oc], QRS4[es, j, 0, :], BKS4[es, j, 1, 0:64], True, True, [BKS, QRS], [pb])
                    elif flag == "xc":
                        MM(P, pb.ap[:, oc], QRS4[es, j, 0, :], Hb3[es, j, :], True, True, [QRS, Hbd], [pb])
                    elif flag == "xd":
                        MM(P, pb.ap[:, oc], BKS4[es, j, 0, :], Hb3[es, j, :], True, True, [BKS, Hbd], [pb])
                for hf in range(2):
                    ACTF(P, Wsb.ap[:, 512 * hf:512 * hf + 512], ps[hf].ap, AF.Identity, [ps[hf]], [Wsb])
        if "rwstop6" in k.debug:
            return
        for ch in chunks:
            rs = slice(64 * ch, 64 * ch + 64)
            Uc, Vc = Usbc[ch], VBc[ch]
            for h in range(16):
                j, e = h // 2, h % 2
                es = slice(64 * e, 64 * e + 64)
                pb = ps[h // 8]
                oc = slice(64 * (h % 8), 64 * (h % 8) + 64)
                MM(P, pb.ap[:, oc], QRS4[es, j, 0, :], Hb3[es, j, :], True, False, [QRS, Hbd], [pb])
                MM(P, pb.ap[:, oc], SC3[:, h, 256:384], VB.ap[:, 64 * h:64 * h + 64], False, True, [SC, VB], [pb])
            for hf in range(2):
                ACTF(P, Wsb.ap[rs, 512 * hf:512 * hf + 512], ps[hf].ap[rs, :], AF.Identity, [ps[hf]], [Wsb])
            if "rwstop7" in k.debug:
                return
            for h in range(16):
                pb = ps[2 + h // 8]
                oc = slice(64 * (h % 8), 64 * (h % 8) + 64)
                MM(P, pb.ap[:, oc], MT3[:, h, :], Wsb.ap[:, 64 * h:64 * h + 64], True, True, [MTb, Wsb], [pb])
            for hf in range(2):
                ACTF(P, Uc.ap[rs, 512 * hf:512 * hf + 512], ps[2 + hf].ap[rs, :], AF.Identity, [ps[2 + hf]], [Uc], scale=-1.0)
            if not state_only:
                for h in range(16):
                    j, e = h // 2, h % 2
                    es = slice(64 * e, 64 * e + 64)
                    pb = ps[4 + h // 8]
                    oc = slice(64 * (h % 8), 64 * (h % 8) + 64)
                    MM(P, pb.ap[:, oc], QRS4[es, j, 1, :], Hb3[es, j, :], True, False, [QRS, Hbd], [pb])
                    MM(P, pb.ap[:, oc], SC3[:, h, 128:256], Uc.ap[:, 64 * h:64 * h + 64], False, False, [SC, Uc], [pb])
                    MM(P, pb.ap[:, oc], SC3[:, h, 384:512], VB.ap[:, 64 * h:64 * h + 64], False, True, [SC, VB], [pb])
                for hf in range(2):
                    CP(P, "dve", Osb.ap[rs, 512 * hf:512 * hf + 512], ps[4 + hf].ap[rs, :], [ps[4 + hf]], [Osb])
            for h in range(16):
                j = h // 2
                pb = ps[6 + h // 8]
                oc = slice(64 * (h % 8), 64 * (h % 8) + 64)
                MM(P, pb.ap[:, oc], BH.ap[:, 128 * j:128 * j + 128], Uc.ap[:, 64 * h:64 * h + 64], True, False, [BH, Uc], [pb])
                MM(P, pb.ap[:, oc], KH.ap[:, 128 * j:128 * j + 128], Vc.ap[:, 64 * h:64 * h + 64], False, True, [KH, Vc], [pb])
            for e in range(2):
                es = slice(64 * e, 64 * e + 64)
                TT(P, "dve", Hd3[es, :, :], Hd3[es, :, :], gC3[es, :, ch].unsqueeze(2).broadcast_to([64, 8, 64]), ALU.mult,
                   [Hd, gC], [Hd])
                for hf in range(2):
                    src = ps[6 + hf].ap[es, :].rearrange("p (j e v) -> p j e v", e=2, v=64)[:, :, e, :]
                    TT(P, "dve", Hd3[es, 4 * hf:4 * hf + 4, :], Hd3[es, 4 * hf:4 * hf + 4, :], src, ALU.add,
                       [Hd, ps[6 + hf]], [Hd])
            CP(P, "pool", Hbd.ap, Hd.ap, [Hd], [Hbd])
        if state_only:
            return
        o = op_index(i)
        if not finalize:
            P.dma(lambda e: e.dma_start(out=OB.ap[128 * o:128 * o + 128, :], in_=Osb.ap), reads=[Osb],
                  writes=[OB.rows(128 * o, 128 * o + 128)])
            CP(P, "pool", sbst.ap[:, 16 * o:16 * o + 16], sbc.ap, [sbc], [sbst])
            return
        P.dma(lambda e: e.dma_start(out=obuf.ap, in_=OB.ap[128 * o:128 * o + 128, :]),
              reads=[OB.rows(128 * o, 128 * o + 128)], writes=[obuf])
        P.dma(lambda e: e.dma_start(out=gabuf.ap, in_=GA.ap[128 * o:128 * o + 128, :]),
              reads=[GA.rows(128 * o, 128 * o + 128)], writes=[gabuf])
        TT(P, "dve", X1.ap, Osb.ap, obuf.ap, ALU.add, [Osb, obuf], [X1])
        RED(P, s16.ap, g64(X1.ap), [X1], [s16])
        P.op("dve", lambda e: e.tensor_scalar(out=s16.ap, in0=s16.ap, scalar1=-1.0 / 64, scalar2=None, op0=ALU.mult),
             reads=[s16], writes=[s16])
        TT(P, "dve", g64(X1.ap), g64(X1.ap), b64(s16.ap), ALU.add, [X1, s16], [X1])
        ACTF(P, X2.ap, X1.ap, AF.Square, [X1], [X2])
        RED(P, s16.ap, g64(X2.ap), [X2], [s16])
        ACTF(P, s16.ap, s16.ap, AF.Ln, [s16], [s16], scale=1.0 / 64, bias=GN_EPS)
        ACTF(P, s16.ap, s16.ap, AF.Exp, [s16], [s16], scale=-0.5)
        TT(P, "dve", g64(X1.ap), g64(X1.ap), b64(s16.ap), ALU.mult, [X1, s16], [X1])
        TT(P, "pool", X1.ap, X1.ap, lwb.ap, ALU.mult, [X1, lwb], [X1])
        TT(P, "pool", X1.ap, X1.ap, lbb.ap, ALU.add, [X1, lbb], [X1])
        TT(P, "dve", sbc.ap, sbc.ap, sbst.ap[:, 16 * o:16 * o + 16], ALU.add, [sbc, sbst], [sbc])
        TT(P, "dve", g64(X2.ap), g64(pav), b64(sbc.ap), ALU.mult, [pa, sbc], [X2])
        TT(P, "pool", X1.ap, X1.ap, X2.ap, ALU.add, [X1, X2], [X1])
        TT(P, "dve", X1.ap, X1.ap, gabuf.ap, ALU.mult, [X1, gabuf], [X1])
        if "YA" in k.debug:
            P.dma(lambda e: e.dma_start(out=outs["YA"][128 * o:128 * o + 128, :], in_=X1.ap), reads=[X1], is_output=True)
        for hf in range(2):
            pb = ps[4 + hf]
            for q in range(4):
                kk_ = 4 * hf + q
                TR(P, pb.ap[:, 128 * q:128 * q + 128], X1.ap[:, 128 * kk_:128 * kk_ + 128], k.ident.ap, [X1, k.ident], [pb])
            ACTF(P, ytb.ap[:, 512 * hf:512 * hf + 512], pb.ap, AF.Identity, [pb], [ytb])
        P.dma(lambda e: e.dma_start(out=YT.ap[0:1024, 128 * o:128 * o + 128].rearrange("(kk p) t -> p kk t", p=128),
                                    in_=ytb.v("p (kk t) -> p kk t", t=128)), reads=[ytb], writes=[YT.name])

    visit.n = 0

    def init_state(d, from_input):
        if from_input:
            P.dma(lambda e: e.dma_start(out=H[d].ap, in_=ins["st_in"][:, 512 * d:512 * d + 512]), writes=[H[d]])
        else:
            P.op("pool", lambda e: e.memset(H[d].ap, 0.0), writes=[H[d]])
        CP(P, "pool", Hb[d].ap, H[d].ap, [H[d]], [Hb[d]])

    only = [x for x in k.debug if x.startswith("rw_")]
    do_sample = (not only) or ("rw_sample" in only)
    do_prompt = (not only) or ("rw_prompt" in only)
    if do_sample:
        init_state(1, True)
        for i in range(NT_S - 1, NT_OWN - 1, -1):
            visit(i, 1, True, (1, 0), False)
        for i in range(NT_OWN - 1, -1, -1):
            visit(i, 1, False, (1, 0), False)
        init_state(0, True)
        for i in range(NT_OWN):
            visit(i, 0, False, (0, 1), True)
    if do_prompt:
        for sq in range(2):
            t0 = NT_S + 2 * sq
            init_state(1, False)
            for i in (t0 + 1, t0):
                visit(i, 1, False, (1, 0), False)
            P.dma(lambda e, sq=sq: e.dma_start(out=outs["st_out"][sq, 1, :, :], in_=H[1].ap), reads=[H[1]], is_output=True)
            init_state(0, False)
            for i in (t0, t0 + 1):
                visit(i, 0, False, (0, 1), True)
            P.dma(lambda e, sq=sq: e.dma_start(out=outs["st_out"][sq, 0, :, :], in_=H[0].ap), reads=[H[0]], is_output=True)
    A.release(m0)


def emit_attn(k, ins, outs, QT, KT, VS, GBT, YT):
    P, A = k.P, k.A
    ps = k.ps
    m0 = A.mark()
    prm = A.f32(257)
    lam4 = A.f32(4)
    neglam = A.f32(1)
    subw = A.f32(1)
    onesb = A.bf(128)
    P.dma(lambda e: e.dma_start(out=prm.ap, in_=ins["at_prm"][:, :]), writes=[prm])
    P.op("pool", lambda e: e.memset(onesb.ap, 1.0), writes=[onesb])
    tmp = A.f32(128)
    l4 = prm.ap[:, 0:256].rearrange("p (a d) -> p a d", d=64)
    TT(P, "dve", tmp.v("p (a d) -> p a d", d=64), l4[:, 0:4:2, :], l4[:, 1:4:2, :], ALU.mult, [prm], [tmp])
    P.op("dve", lambda e: e.tensor_reduce(out=lam4.ap[:, 0:2], in_=tmp.v("p (a d) -> p a d", d=64), axis=AX.X, op=ALU.add),
         reads=[tmp], writes=[lam4])
    ACTF(P, lam4.ap[:, 0:2], lam4.ap[:, 0:2], AF.Exp, [lam4], [lam4])
    STT(P, neglam.ap, lam4.ap[:, 1:2], -LAM_INIT, lam4.ap[:, 0:1], ALU.add, ALU.subtract, [lam4], [neglam])
    P.op("dve", lambda e: e.tensor_scalar(out=subw.ap, in0=prm.ap[:, 256:257], scalar1=1.0 - LAM_INIT, scalar2=None, op0=ALU.mult),
         reads=[prm], writes=[subw])
    NKMAX = 34
    kT = [A.bf(NKMAX * 128) for _ in range(2)]
    vv = [A.bf(NKMAX * 128) for _ in range(2)]
    qT = [A.bf(2176) for _ in range(2)]
    pT = [A.bf(512) for _ in range(4)]
    r0b, t0b, t1b, gbt = A.f32(512), A.f32(512), A.f32(512), A.f32(512)
    sqb = A.bf(512)
    yb = A.bf(512)
    sample_keys = [128 * i for i in range(NT_S)] + [T_ALL, T_ALL + 128]
    groups = [(0, 128 * NT_OWN, sample_keys)]
    for sq in range(2):
        groups.append((128 * NT_OWN + 256 * sq, 256, [128 * NT_S + 256 * sq, 128 * NT_S + 256 * sq + 128]))
    only = [x for x in k.debug if x.startswith("at_")]
    if "at_prompt" in only:
        groups = groups[1:]
    if "at_sample" in only:
        groups = groups[:1]
    it = 0
    for (q0, nq, keys) in groups:
        nk = len(keys)
        for h in range(1 if "at_h1" in k.debug else 8):
            it += 1
            kTb, vb, qTb = kT[it % 2], vv[it % 2], qT[it % 2]
            runs = []
            for ki, kr in enumerate(keys):
                if runs and runs[-1][1] + runs[-1][2] == kr:
                    runs[-1][2] += 128
                else:
                    runs.append([ki, kr, 128])
            for (ki, kr, ln) in runs:
                P.dma(lambda e, kTb=kTb, ki=ki, kr=kr, ln=ln, h=h: e.dma_start(
                    out=kTb.ap[:, 128 * ki:128 * ki + ln], in_=KT.ap[h, :, kr:kr + ln]), reads=[KT.name], writes=[kTb])
                P.dma(lambda e, vb=vb, ki=ki, kr=kr, ln=ln, h=h: e.dma_start(
                    out=vb.ap[:, 128 * ki:128 * ki + ln].rearrange("p (t c) -> p t c", c=128),
                    in_=VS.ap[kr:kr + ln, 128 * h:128 * h + 128].rearrange("(t p) c -> p t c", p=128)),
                    reads=[VS.rows(kr, kr + ln)], writes=[vb])
            P.dma(lambda e, qTb=qTb, q0=q0, nq=nq, h=h: e.dma_start(out=qTb.ap[:, 0:nq], in_=QT.ap[h, :, q0:q0 + nq]),
                  reads=[QT.name], writes=[qTb])
            for qs in range(0, nq, 512):
                qn = min(512, nq - qs)
                def emit_qk(ki):
                    for c in range(2):
                        sb_ = ps[2 * (ki % 2) + c]
                        cs_ = slice(64 * c, 64 * c + 64)
                        MM(P, sb_.ap[:, 0:qn], kTb.ap[cs_, 128 * ki:128 * ki + 128], qTb.ap[cs_, qs:qs + qn], True, True,
                           [kTb, qTb], [sb_])

                emit_qk(0)
                for ki in range(nk):
                    if ki + 1 < nk:
                        emit_qk(ki + 1)
                    for c in range(2):
                        sb_ = ps[2 * (ki % 2) + c]
                        pb_ = pT[2 * (ki % 2) + c]
                        ACTF(P, pb_.ap[:, 0:qn], sb_.ap[:, 0:qn], AF.Exp, [sb_], [pb_], scale=0.125)
                    for c in range(2):
                        pb_ = pT[2 * (ki % 2) + c]
                        MM(P, ps[4 + c].ap[:, 0:qn], vb.ap[:, 128 * ki:128 * ki + 128], pb_.ap[:, 0:qn], ki == 0, ki == nk - 1,
                           [vb, pb_], [ps[4 + c]])
                        MM(P, ps[6 + c].ap[:, 0:qn], onesb.ap, pb_.ap[:, 0:qn], ki == 0, ki == nk - 1, [onesb, pb_], [ps[6 + c]])
                P.op("dve", lambda e, qn=qn: e.reciprocal(out=r0b.ap[:, 0:qn], in_=ps[6].ap[:, 0:qn]), reads=[ps[6]], writes=[r0b])
                TT(P, "dve", t0b.ap[:, 0:qn], ps[4].ap[:, 0:qn], r0b.ap[:, 0:qn], ALU.mult, [ps[4], r0b], [t0b])
                P.op("dve", lambda e, qn=qn: e.reciprocal(out=r0b.ap[:, 0:qn], in_=ps[7].ap[:, 0:qn]), reads=[ps[7]], writes=[r0b])
                TT(P, "dve", t1b.ap[:, 0:qn], ps[5].ap[:, 0:qn], r0b.ap[:, 0:qn], ALU.mult, [ps[5], r0b], [t1b])
                STT(P, t0b.ap[:, 0:qn], t1b.ap[:, 0:qn], neglam.ap[:, 0:1], t0b.ap[:, 0:qn], ALU.mult, ALU.add,
                    [t1b, neglam, t0b], [t0b])
                ACTF(P, sqb.ap[:, 0:qn], t0b.ap[:, 0:qn], AF.Square, [t0b], [sqb])
                MM(P, ps[0].ap[:, 0:qn], onesb.ap, sqb.ap[:, 0:qn], True, True, [onesb, sqb], [ps[0]])
                ACTF(P, t1b.ap[:, 0:qn], ps[0].ap[:, 0:qn], AF.Ln, [ps[0]], [t1b], scale=1.0 / 128, bias=NORM_EPS)
                ACTF(P, t1b.ap[:, 0:qn], t1b.ap[:, 0:qn], AF.Exp, [t1b], [t1b], scale=-0.5)
                P.dma(lambda e, qs=qs, qn=qn, q0=q0, h=h: e.dma_start(
                    out=gbt.ap[:, 0:qn], in_=GBT.ap[128 * h:128 * h + 128, q0 + qs:q0 + qs + qn]), reads=[GBT.name], writes=[gbt])
                STT(P, t0b.ap[:, 0:qn], t0b.ap[:, 0:qn], subw.ap[:, 0:1], t1b.ap[:, 0:qn], ALU.mult, ALU.mult,
                    [t0b, subw, t1b], [t0b])
                TT(P, "dve", yb.ap[:, 0:qn], t0b.ap[:, 0:qn], gbt.ap[:, 0:qn], ALU.mult, [t0b, gbt], [yb])
                if "YB" in k.debug:
                    TT(P, "dve", t1b.ap[:, 0:qn], t0b.ap[:, 0:qn], gbt.ap[:, 0:qn], ALU.mult, [t0b, gbt], [t1b])
                    P.dma(lambda e, qs=qs, qn=qn, q0=q0, h=h: e.dma_start(
                        out=outs["YB"][128 * h:128 * h + 128, q0 + qs:q0 + qs + qn], in_=t1b.ap[:, 0:qn]), reads=[t1b], is_output=True)
                P.dma(lambda e, qs=qs, qn=qn, q0=q0, h=h: e.dma_start(
                    out=YT.ap[1024 + 128 * h:1024 + 128 * h + 128, q0 + qs:q0 + qs + qn], in_=yb.ap[:, 0:qn]),
                    reads=[yb], writes=[YT.name])
    A.release(m0)


L1_COLS = 2694


def l1_col0(o):
    if o < NT_OWN:
        return 1 + 128 * o
    if o < NT_OWN + 2:
        return 2179 + 128 * (o - NT_OWN)
    return 2436 + 128 * (o - NT_OWN - 2)


def emit_tail(k, ins, outs, YT, x_all):
    P, A, nc = k.P, k.A, k.nc
    ps = k.ps
    NO = NT_OWN + NT_PR
    m0 = A.mark()
    X1S = k.X1S
    hT1 = A.bf(8 * L1_COLS)
    hT13 = hT1.v("p (k c) -> p k c", c=L1_COLS)
    P.op("pool", lambda e: e.memset(hT1.ap, 0.0), writes=[hT1])
    m1 = A.mark()

    def outproj(w_dram, yt_cols, layer, x_get, x_put, tag):
        mm = A.mark()
        wst = [A.f32(1024) for _ in range(2)]
        wb = A.bf(16 * 1024)
        for kk in range(16):
            wsb = wst[kk % 2]
            P.dma(lambda e, wsb=wsb, kk=kk: e.dma_start(out=wsb.ap[:, 0:1024], in_=w_dram[128 * kk:128 * kk + 128, :]), writes=[wsb])
            CP(P, "pool" if kk % 2 else "dve", wb.ap[:, 1024 * kk:1024 * kk + 1024], wsb.ap[:, 0:1024], [wsb],
               [wb.cols(1024 * kk, 1024 * kk + 1024)])
        ytb = [A.bf(16 * 128) for _ in range(2)]
        xt = [A.f32(1024) for _ in range(2)]
        for o in range(NO):
            yb_ = ytb[o % 2]
            c0 = yt_cols(o)
            P.dma(lambda e, yb_=yb_, c0=c0: e.dma_start(
                out=yb_.v("p (kk t) -> p kk t", t=128), in_=YT.ap[:, c0:c0 + 128].rearrange("(kk p) t -> p kk t", p=128)),
                reads=[YT.name], writes=[yb_])
            v = 0 if o < NT_OWN else 1
            g = k.gB.ap[:, (layer * 2 + v) * 1024:(layer * 2 + v) * 1024 + 1024]
            xin = x_get(o, xt[o % 2])
            xo = x_put(o, xt[o % 2])
            for hf in range(2):
                pb = ps[2 * (o % 2) + hf]
                for kk in range(16):
                    MM(P, pb.ap, yb_.ap[:, 128 * kk:128 * kk + 128], wb.ap[:, 1024 * kk + 512 * hf:1024 * kk + 512 * hf + 512],
                       kk == 0, kk == 15, [yb_, wb], [pb])
                sl = slice(512 * hf, 512 * hf + 512)
                TT(P, "dve", xo.ap[:, sl], pb.ap, g[:, sl], ALU.mult, [pb, k.gB], [xo])
                TT(P, "pool" if hf else "dve", xo.ap[:, sl], xo.ap[:, sl], xin.ap[:, sl], ALU.add, [xo, xin], [xo])
            if tag == "final":
                P.dma(lambda e, xo=xo, o=o: e.dma_start(out=outs["y_out"][128 * o:128 * o + 128, :], in_=xo.ap), reads=[xo],
                      is_output=True)
            else:
                P.dma(lambda e, xo=xo, o=o: e.dma_start(out=X1S.ap[128 * o:128 * o + 128, :], in_=xo.ap), reads=[xo],
                      writes=[X1S.rows(128 * o, 128 * o + 128)])
            if tag != "final" and "X1" in k.debug:
                P.dma(lambda e, xo=xo, o=o: e.dma_start(out=outs["X1"][128 * o:128 * o + 128, :], in_=xo.ap), reads=[xo],
                      is_output=True)
        A.release(mm)

    def x_get0(o, buf):
        i = o if o < NT_OWN else NT_S + (o - NT_OWN)
        P.dma(lambda e, buf=buf, i=i: e.dma_start(out=buf.ap, in_=x_all[128 * i:128 * i + 128, :]), writes=[buf])
        return buf

    xo2 = [A.f32(1024) for _ in range(2)]
    outproj(ins["w_out0"], lambda o: 128 * o, 0, x_get0, lambda o, buf: xo2[o % 2], "l0")
    if "stopX1" in k.debug:
        A.release(m0)
        return
    mn = A.mark()
    emit_norm_phase(k, x_src=lambda o: X1S.ap[128 * o:128 * o + 128, :], tiles=list(range(NO)), layer=1, hT3=hT13, hT=hT1,
                    col0=l1_col0, variant=lambda o: 0 if o < NT_OWN else 1, cols_total=L1_COLS,
                    src_reg=lambda o: X1S.rows(128 * o, 128 * o + 128))
    A.release(mn)
    mm = A.mark()
    cvp = A.f32(64)
    P.dma(lambda e: e.dma_start(out=cvp.ap, in_=ins["cv_prm"][:, :]), writes=[cvp])
    wst = [A.f32(4 * 1024) for _ in range(2)]
    wbf = [A.bf(4 * 1024) for _ in range(2)]
    NB = L1_COLS + 2
    cgu = A.f32(NB)
    bgr = A.f32(NB)
    zs = A.f32(NB)
    cvb = A.f32(NB)
    cgt = A.f32(512)
    yrow = A.bf(NB)
    P.op("pool", lambda e: e.memset(cgu.ap, 0.0), writes=[cgu])
    w_in1 = ins["w_in1"]
    wins = [(c, min(512, L1_COLS - c)) for c in range(0, L1_COLS, 512)]
    for j in range(16):
        ws, wb = wst[j % 2], wbf[j % 2]
        for q in range(4):
            P.dma(lambda e, ws=ws, q=q, j=j: e.dma_start(
                out=ws.ap[:, 1024 * q:1024 * q + 1024].rearrange("p (k c) -> p k c", c=128),
                in_=w_in1[:, 2048 * q + 128 * j:2048 * q + 128 * j + 128].rearrange("(k p) c -> p k c", p=128)),
                writes=[ws.cols(1024 * q, 1024 * q + 1024)])
        CP(P, "pool", wb.ap, ws.ap, [ws], [wb])
        for (c0, nw) in wins:
            for q in range(4):
                pb = ps[4 + q]
                for kk in range(8):
                    MM(P, pb.ap[:, 0:nw], wb.ap[:, 1024 * q + 128 * kk:1024 * q + 128 * kk + 128], hT13[:, kk, c0:c0 + nw],
                       kk == 0, kk == 7, [wb, hT1], [pb])
            ACTF(P, bgr.ap[:, 1 + c0:1 + c0 + nw], ps[4].ap[:, 0:nw], AF.Identity, [ps[4]], [bgr])
            ACTF(P, cgt.ap[:, 0:nw], ps[5].ap[:, 0:nw], AF.Identity, [ps[5]], [cgt])
            TT(P, "dve", cgu.ap[:, 1 + c0:1 + c0 + nw], cgt.ap[:, 0:nw], ps[6].ap[:, 0:nw], ALU.mult, [cgt, ps[6]], [cgu])
            ACTF(P, zs.ap[:, 1 + c0:1 + c0 + nw], ps[7].ap[:, 0:nw], AF.Silu, [ps[7]], [zs])
        n = L1_COLS
        w0, w1, w2, bb = [cvp.ap[:, 16 * t_ + j:16 * t_ + j + 1] for t_ in range(4)]
        P.op("dve", lambda e, w1=w1, bb=bb, n=n: e.tensor_scalar(out=cvb.ap[:, 1:1 + n], in0=cgu.ap[:, 1:1 + n], scalar1=w1,
                                                                scalar2=bb, op0=ALU.mult, op1=ALU.add), reads=[cgu, cvp], writes=[cvb])
        STT(P, cvb.ap[:, 1:1 + n], cgu.ap[:, 0:n], w0, cvb.ap[:, 1:1 + n], ALU.mult, ALU.add, [cgu, cvp, cvb], [cvb])
        STT(P, cvb.ap[:, 1:1 + n], cgu.ap[:, 2:2 + n], w2, cvb.ap[:, 1:1 + n], ALU.mult, ALU.add, [cgu, cvp, cvb], [cvb])
        TT(P, "pool", cvb.ap[:, 1:1 + n], cvb.ap[:, 1:1 + n], bgr.ap[:, 1:1 + n], ALU.mult, [cvb, bgr], [cvb])
        TT(P, "dve", yrow.ap[:, 1:1 + n], cvb.ap[:, 1:1 + n], zs.ap[:, 1:1 + n], ALU.mult, [cvb, zs], [yrow])
        P.dma(lambda e, j=j, n=n: e.dma_start(out=YT.ap[128 * j:128 * j + 128, 0:n], in_=yrow.ap[:, 1:1 + n]), reads=[yrow],
              writes=[YT.name])
    A.release(mm)
    def x_get1(o, buf):
        P.dma(lambda e, buf=buf, o=o: e.dma_start(out=buf.ap, in_=X1S.ap[128 * o:128 * o + 128, :]),
              reads=[X1S.rows(128 * o, 128 * o + 128)], writes=[buf])
        return buf

    outproj(ins["w_out1"], l1_col0, 1, x_get1, lambda o, buf: xo2[o % 2], "final")
    A.release(m0)


def prep_core_inputs(q, inp):
    b, mir = q // 2, q % 2
    f = (lambda a: a[::-1]) if mir else (lambda a: a)
    d = {}
    xs = f(inp["x_sample"][b])
    xp0 = f(inp["x_prompt"][2 * q])
    xp1 = f(inp["x_prompt"][2 * q + 1])
    d["x_all"] = np.ascontiguousarray(np.concatenate([xs, xp0, xp1], 0))
    cvv = np.stack([inp["c"][b], inp["c_ctx"]], -1)
    d["cv"] = np.ascontiguousarray(cvv.reshape(8, 128, 2).transpose(1, 0, 2).reshape(128, 16))
    d["normw_fm"] = np.ascontiguousarray(inp["norm_w"].reshape(2, 8, 128).transpose(2, 0, 1).reshape(128, 16))
    d["ada_w"] = inp["ada_w"]
    d["adab_fm"] = np.ascontiguousarray(inp["ada_b"].reshape(2, 24, 128).transpose(2, 0, 1).reshape(128, 48))
    d["adab_g"] = np.ascontiguousarray(np.broadcast_to(inp["ada_b"][:, 2048:].reshape(1, 2048), (128, 2048)))
    w = inp["e_w_in"][0]
    mu = inp["e_mu"][0]
    if mir:
        perm = np.arange(EVEN_IN)
        perm[3072:3136], perm[3136:3200] = np.arange(3136, 3200), np.arange(3072, 3136)
        perm[3200:3264], perm[3264:3328] = np.arange(3264, 3328), np.arange(3200, 3264)
        w = w[:, perm]
        mu = mu[perm[:3328]]
    d["w_in0"] = np.ascontiguousarray(w)
    d["mu_b"] = np.ascontiguousarray(np.broadcast_to(mu.reshape(1, -1), (128, 3328)))
    qk = np.concatenate([np.tile(inp["e_q_norm"][0], 8), np.tile(inp["e_k_norm"][0], 8)])
    d["qkw_b"] = np.ascontiguousarray(np.broadcast_to(qk.reshape(1, -1), (128, 1024)))
    t = np.arange(4096)
    row = (t // 64).astype(np.float32)
    col = (t % 64).astype(np.float32)
    inv = (10000.0 ** (-np.arange(0, 32, 2, dtype=np.float32) / 32)).astype(np.float32)
    ar, ac = row[:, None] * inv, col[:, None] * inv
    cos = np.concatenate([np.cos(ar), np.cos(ar), np.cos(ac), np.cos(ac)], 1)
    sin = np.concatenate([-np.sin(ar), np.sin(ar), -np.sin(ac), np.sin(ac)], 1)
    d["rope_cs"] = np.ascontiguousarray(f(np.concatenate([cos, sin], 1).astype(np.float32)))
    dirs = (1, 0) if mir else (0, 1)
    rep = lambda v: np.broadcast_to(np.asarray(v, np.float32).reshape(1, -1), (128, v.size))
    d["rw_prm"] = np.ascontiguousarray(np.concatenate(
        [rep(inp["e_k_k"][0]), rep(inp["e_k_a"][0]), rep(inp["e_r_k"][0].reshape(-1)),
         rep(inp["e_lnx_w"][0]), rep(inp["e_lnx_b"][0])], 1))
    up = np.zeros((65, 4096), np.float32)
    for fd, td in enumerate(dirs):
        up[:64, 1024 * fd:1024 * fd + 1024] = inp["e_w_up"][0, td]
        up[64, 1024 * fd:1024 * fd + 1024] = inp["e_w0"][0, td]
        up[:64, 2048 + 1024 * fd:2048 + 1024 * fd + 1024] = inp["e_a_up"][0, td]
        up[64, 2048 + 1024 * fd:2048 + 1024 * fd + 1024] = inp["e_a0"][0, td]
    d["rw_up"] = up
    d["rw_const"] = rwkv_consts()
    sts = (inp["state_rwkv_fwd"][b, 0], inp["state_rwkv_bwd"][b, 0])
    st = np.zeros((128, 1024), np.float32)
    for fd, td in enumerate(dirs):
        hh = sts[td].reshape(8, 2, 64, 64).transpose(1, 3, 0, 2).reshape(128, 512)
        st[:, 512 * fd:512 * fd + 512] = hh
    d["st_in"] = st
    d["ctx_k"] = np.ascontiguousarray(inp["cache_diff_k"][b, 0].reshape(256, 1024))
    d["ctx_v"] = np.ascontiguousarray(inp["cache_diff_v"][b, 0].reshape(256, 1024))
    d["at_prm"] = np.ascontiguousarray(np.concatenate(
        [rep(inp["e_lambda"][0].reshape(-1)), inp["e_subln"][0].reshape(128, 1)], 1).astype(np.float32))
    d["w_out0"] = inp["e_w_out"][0]
    d["w_in1"] = inp["o_w_in"][0]
    d["w_out1"] = inp["o_w_out"][0]
    cw = inp["o_conv_w"][0]
    if mir:
        cw = cw[::-1]
    cvp = np.concatenate([cw.reshape(3, 16, 128), inp["o_conv_b"][0].reshape(1, 16, 128)], 0)
    d["cv_prm"] = np.ascontiguousarray(cvp.transpose(2, 0, 1).reshape(128, 64))
    return d


def rwkv_consts():
    idx = np.arange(128)
    ch, pos = idx // 64, idx % 64
    same = ch[:, None] == ch[None, :]
    out = np.zeros((128, 2050), np.float32)
    for dd in range(2):
        before = (pos[:, None] < pos[None, :]) if dd == 0 else (pos[:, None] > pos[None, :])
        eq = pos[:, None] == pos[None, :]
        incl = same & (before | eq)
        excl = same & before
        suf = same & before.T
        for m, mat in enumerate((incl, excl, suf)):
            out[:, (dd * 3 + m) * 128:(dd * 3 + m) * 128 + 128] = DECAY_C * mat
        msk = np.concatenate([excl, incl], 1).astype(np.float32)
        out[:, 770 + 512 * dd:770 + 512 * dd + 256] = msk
        out[:, 770 + 512 * dd + 256:770 + 512 * dd + 512] = msk
        out[:, 1794 + 128 * dd:1794 + 128 * dd + 128] = excl.T
    out[:, 768] = DECAY_C * (ch == 0)
    out[:, 769] = DECAY_C * (ch == 1)
    return out


def assemble(q, r, outs6):
    y_p, y_s, n_sf, n_sb, n_k, n_v = outs6
    b, mir = q // 2, q % 2
    S = y_p.shape[1]
    yo = np.asarray(r["y_out"])
    if mir:
        y_s[b, 2048:] = yo[:2048][::-1]
    else:
        y_s[b, :2048] = yo[:2048]
    so = np.asarray(r["st_out"])
    for j in range(2):
        yp = yo[2176 + 256 * j:2176 + 256 * j + 256]
        kc = np.asarray(r["kc_out"])[256 * j:256 * j + 256]
        vc = np.asarray(r["vc_out"])[256 * j:256 * j + 256]
        if mir:
            yp, kc, vc = yp[::-1], kc[::-1], vc[::-1]
        y_p[2 * q + j] = yp
        n_k[2 * q + j, 0] = kc.reshape(S, 8, 2, 64)
        n_v[2 * q + j, 0] = vc.reshape(S, 8, 128)
        for fd in range(2):
            st = so[j, fd].reshape(2, 64, 8, 64).transpose(2, 0, 3, 1).reshape(16, 64, 64)
            td = (1 - fd) if mir else fd
            (n_sf if td == 0 else n_sb)[2 * q + j, 0] = st


def kernel(**inputs):
    inp = {k_: np.asarray(v) for k_, v in inputs.items()}
    nc, ins, outs = build_program()
    in_maps = []
    for q in range(8):
        d = prep_core_inputs(q, inp)
        in_maps.append({n: d[n] for n in ins})
    res = run_bass_kernel_spmd(nc, in_maps, core_ids=list(range(8)))
    B, S = inp["x_prompt"].shape[0], inp["x_prompt"].shape[1]
    outs6 = (np.zeros(inp["x_prompt"].shape, np.float32), np.zeros(inp["x_sample"].shape, np.float32),
             np.zeros((B, 1, 16, 64, 64), np.float32), np.zeros((B, 1, 16, 64, 64), np.float32),
             np.zeros((B, 1, S, 8, 2, 64), np.float32), np.zeros((B, 1, S, 8, 128), np.float32))
    for q in range(8):
        assemble(q, res.results[q], outs6)
    return outs6
```

```python
import contextlib
import numpy as np
import concourse.bass as bass
import concourse.mybir as mybir
from concourse.bass_utils import run_bass_kernel_spmd

F32 = mybir.dt.float32
BF16 = mybir.dt.bfloat16
AF = mybir.ActivationFunctionType
ALU = mybir.AluOpType
AX = mybir.AxisListType

ENGS = ("pe", "act", "dve", "pool", "sp")
N_DMA_SEMS = 32
BIG = 1 << 40

D = 1024
NT_OWN, NT_OTH, NT_PR = 17, 15, 4
NT_S = NT_OWN + NT_OTH
NT = NT_S + NT_PR
T_ALL = NT * 128
T_OP = (NT_OWN + NT_PR) * 128
HT_COLS = 4612
EVEN_IN = 8448
NORM_EPS = 1e-6
GN_EPS = 64e-5
LAM_INIT = 0.8 - 0.6
DECAY_C = -0.6065306597126334


def tile_col0(i):
    if i < NT_S:
        return 1 + 128 * i
    if i < NT_S + 2:
        return 4098 + 128 * (i - NT_S)
    return 4355 + 128 * (i - NT_S - 2)


def op_index(i):
    return i if i < NT_OWN else NT_OWN + (i - NT_S)


OUT_TILES = list(range(NT_OWN)) + list(range(NT_S, NT))


class Prog:
    def __init__(self, nc):
        self.nc = nc
        self.streams = {e: [] for e in ENGS}
        self.count = {e: 0 for e in ENGS}
        self.seen = {e: {} for e in ENGS}
        self.regions = {}
        self.dma_cnt = [0] * N_DMA_SEMS
        self.dma_rr = 0
        self.out_tokens = []

    def _deps(self, reads, writes):
        deps = []
        for (name, lo, hi) in reads:
            for rec in self.regions.get(name, ()):
                if rec[0] < hi and lo < rec[1]:
                    if rec[2] is not None:
                        deps.append(rec[2])
                    if name.startswith("ps"):
                        deps.extend(rec[3].values())
        for (name, lo, hi) in writes:
            for rec in self.regions.get(name, ()):
                if rec[0] < hi and lo < rec[1]:
                    if rec[2] is not None:
                        deps.append(rec[2])
                    deps.extend(rec[3].values())
        return deps

    def _commit(self, reads, writes, token, rkey):
        for (name, lo, hi) in reads:
            lst = self.regions.setdefault(name, [])
            hit = False
            for rec in lst:
                if rec[0] < hi and lo < rec[1]:
                    rec[3][rkey] = token
                    hit = True
            if not hit:
                lst.append([lo, hi, None, {rkey: token}])
        for (name, lo, hi) in writes:
            lst = self.regions.setdefault(name, [])
            keep = []
            for rec in lst:
                if rec[1] <= lo or hi <= rec[0]:
                    keep.append(rec)
                    continue
                if rec[0] < lo:
                    keep.append([rec[0], lo, rec[2], dict(rec[3])])
                if hi < rec[1]:
                    keep.append([hi, rec[1], rec[2], dict(rec[3])])
            keep.append([lo, hi, token, {}])
            self.regions[name] = keep

    def _waits(self, eng, deps):
        best = {}
        for (k, v) in deps:
            if v > best.get(k, 0):
                best[k] = v
        out = []
        for k, v in best.items():
            if eng == "pe" and k == "pe":
                continue
            if self.seen[eng].get(k, 0) >= v:
                continue
            self.seen[eng][k] = v
            out.append((k, v))
        return out

    @staticmethod
    def _norm(regs):
        out = []
        for r in regs:
            if isinstance(r, str):
                out.append((r, 0, BIG))
            elif isinstance(r, Buf):
                out.append(r.reg)
            else:
                out.append(r)
        return out

    def op(self, eng, fn, reads=(), writes=()):
        reads = self._norm(reads)
        writes = self._norm(writes)
        waits = self._waits(eng, self._deps(reads, writes))
        self.count[eng] += 1
        token = (eng, self.count[eng])
        self.streams[eng].append(("op", waits, fn))
        self._commit(reads, writes, token, eng)
        return token

    def dma(self, fn, reads=(), writes=(), is_output=False, queue="sp"):
        queue = "sp"
        reads = self._norm(reads)
        writes = self._norm(writes)
        deps = self._deps(reads, writes)
        s = self.dma_rr
        self.dma_rr = (self.dma_rr + 1) % N_DMA_SEMS
        key = "dma%d" % s
        if self.dma_cnt[s] > 0:
            deps.append((key, self.dma_cnt[s]))
        waits = self._waits(queue, deps)
        self.dma_cnt[s] += 16
        token = (key, self.dma_cnt[s])
        self.streams[queue].append(("dma", waits, fn, s))
        self._commit(reads, writes, token, key)
        if is_output:
            self.out_tokens.append(token)
        return token

    def run(self):
        nc = self.nc
        fin = self._waits("sp", list(self.out_tokens))
        self.streams["sp"].append(("wait", fin))
        with contextlib.ExitStack() as st:
            sems = {}
            for e in ENGS:
                sems[e] = st.enter_context(nc.semaphore("s_" + e))
            for i in range(N_DMA_SEMS):
                sems["dma%d" % i] = st.enter_context(nc.semaphore("s_dma%d" % i))
            block = st.enter_context(nc.Block())

            def player(ename):
                def play(eng):
                    for item in self.streams[ename]:
                        for (k, v) in item[1]:
                            eng.wait_ge(sems[k], v)
                        if item[0] == "op":
                            item[2](eng).then_inc(sems[ename], 1)
                        elif item[0] == "dma":
                            item[2](eng).then_inc(sems["dma%d" % item[3]], 16)
                return play

            block.tensor(player("pe"))
            block.scalar(player("act"))
            block.vector(player("dve"))
            block.gpsimd(player("pool"))
            block.sync(player("sp"))


class Buf:
    def __init__(self, ap, reg):
        self.ap = ap
        self.reg = reg

    def cols(self, a, b):
        name, lo, hi = self.reg
        if name == "ar":
            if self.ap.dtype == BF16:
                r = (name, lo + a // 2, lo + (b + 1) // 2)
            else:
                r = (name, lo + a, lo + b)
        else:
            r = self.reg
        return Buf(self.ap[:, a:b], r)

    def v(self, s, **kw):
        return self.ap.rearrange(s, **kw)


class Arena:
    def __init__(self, ap, size):
        self.ap = ap
        self.size = size
        self.top = 0

    def f32(self, n):
        lo = self.top
        self.top += n
        assert self.top <= self.size, ("arena overflow", self.top, self.size)
        return Buf(self.ap[:, lo:lo + n], ("ar", lo, lo + n))

    def bf(self, n):
        nf = (n + 1) // 2
        lo = self.top
        self.top += nf
        assert self.top <= self.size, ("arena overflow", self.top, self.size)
        return Buf(self.ap[:, lo:lo + nf].bitcast(BF16)[:, 0:n], ("ar", lo, lo + nf))

    def mark(self):
        return self.top

    def release(self, m):
        self.top = m


class DramBuf:
    def __init__(self, name, ap):
        self.name = name
        self.ap = ap

    def rows(self, a, b):
        return (self.name, a, b)


ARENA_F32 = 52800


class K:
    pass


def build_program(debug=()):
    nc = bass.Bass("TRN2", target_bir_lowering=False)
    k = K()
    k.nc = nc
    k.debug = debug
    ins = {}
    outs = {}

    def din(name, shape, dt=F32):
        ins[name] = nc.dram_tensor(name, list(shape), dt, kind="ExternalInput").ap()
        return ins[name]

    def dout(name, shape, dt=F32):
        outs[name] = nc.dram_tensor(name, list(shape), dt, kind="ExternalOutput").ap()
        return outs[name]

    def dscr(name, shape, dt=F32):
        kind = "ExternalOutput" if name in debug else "Internal"
        t = nc.dram_tensor(name, list(shape), dt, kind=kind).ap()
        if name in debug:
            outs[name] = t
        return DramBuf(name, t)

    x_all = din("x_all", [T_ALL, D])
    cv = din("cv", [128, 16])
    normw_fm = din("normw_fm", [128, 16])
    ada_w = din("ada_w", [2, D, 3072])
    adab_fm = din("adab_fm", [128, 48])
    adab_g = din("adab_g", [128, 2048])
    w_in0 = din("w_in0", [D, EVEN_IN])
    mu_b = din("mu_b", [128, 3328])
    qkw_b = din("qkw_b", [128, 1024])
    rope_cs = din("rope_cs", [4096, 128])
    rw_prm = din("rw_prm", [128, 5120])
    rw_up = din("rw_up", [65, 4096])
    rw_const = din("rw_const", [128, 2050])
    st_in = din("st_in", [128, 1024])
    ctx_k = din("ctx_k", [256, 1024])
    ctx_v = din("ctx_v", [256, 1024])
    at_prm = din("at_prm", [128, 257])
    w_out0 = din("w_out0", [2048, 1024])
    w_in1 = din("w_in1", [D, 8192])
    w_out1 = din("w_out1", [2048, 1024])
    cv_prm = din("cv_prm", [128, 64])
    OB = dscr("OB", [T_OP, 1024])
    k.X1S = dscr("X1S", [T_OP, 1024])
    YT = dscr("YT", [2048, 2816], BF16)
    PA = dscr("PA", [T_ALL, 3328])
    GA = dscr("GA", [T_OP, 1024])
    QT = dscr("QT", [8, 128, T_OP], BF16)
    KT = dscr("KT", [8, 128, T_ALL + 256], BF16)
    VS = dscr("VS", [T_ALL + 256, 1024], BF16)
    GBT = dscr("GBT", [1024, T_OP])
    kc_out = dout("kc_out", [512, 1024])
    vc_out = dout("vc_out", [512, 1024])
    st_out = dout("st_out", [2, 2, 128, 512])
    y_out = dout("y_out", [T_OP, 1024])
    if "YB" in debug:
        dout("YB", [1024, T_OP])
    if "X1" in debug:
        dout("X1", [T_OP, 1024])
    if "YA" in debug:
        dout("YA", [T_OP, 1024])
    if "dumpQ" in debug:
        dout("dQ", [128, 12288])

    st = contextlib.ExitStack()
    with st:
        ar_t = st.enter_context(nc.sbuf_tensor("ar", [128, ARENA_F32], F32))
        psb = [st.enter_context(nc.psum_tensor("ps%d" % i, [128, 512], F32)) for i in range(8)]
        k.ps = [Buf(psb[i][:, :], ("ps%d" % i, 0, BIG)) for i in range(8)]
        A = Arena(ar_t, ARENA_F32)
        P = Prog(nc)
        k.P, k.A = P, A

        ident = A.f32(128)
        identb = A.bf(128)
        P.op("pool", lambda e: e.memset(ident.ap, 1.0), writes=[ident])
        P.op("pool", lambda e: e.affine_select(out=ident.ap, in_=ident.ap, pattern=[[-1, 128]],
                                               compare_op=ALU.is_equal, fill=0.0, base=0, channel_multiplier=1),
             reads=[ident], writes=[ident])
        P.op("dve", lambda e: e.tensor_copy(identb.ap, ident.ap), reads=[ident], writes=[identb])
        k.ident, k.identb = ident, identb

        cvt = A.f32(16)
        nwt = A.f32(16)
        abf = A.f32(48)
        modS = A.f32(32)
        modB = A.f32(32)
        gB = A.f32(4096)
        k.modS, k.modB, k.gB = modS, modB, gB
        m0 = A.mark()
        abg = A.f32(2048)
        screp = A.f32(16 * 128)
        P.dma(lambda e: e.dma_start(out=cvt.ap, in_=cv[:, :]), writes=[cvt])
        P.dma(lambda e: e.dma_start(out=nwt.ap, in_=normw_fm[:, :]), writes=[nwt])
        P.dma(lambda e: e.dma_start(out=abf.ap, in_=adab_fm[:, :]), writes=[abf])
        P.dma(lambda e: e.dma_start(out=abg.ap, in_=adab_g[:, :]), writes=[abg])
        P.op("act", lambda e: e.activation(out=cvt.ap, in_=cvt.ap, func=AF.Silu), reads=[cvt], writes=[cvt])
        P.op("dve", lambda e: e.tensor_copy(screp.v("p (a b) -> p a b", b=128),
                                            cvt.ap.unsqueeze(2).broadcast_to([128, 16, 128])),
             reads=[cvt], writes=[screp])
        wst = [A.f32(8 * 512) for _ in range(2)]
        for l in range(2 if "noada" not in debug else 0):
            for g in range(6):
                wb = wst[(l * 6 + g) % 2]
                P.dma(lambda e, wb=wb, l=l, g=g: e.dma_start(
                    out=wb.v("p (k c) -> p k c", c=512),
                    in_=ada_w[l, :, 512 * g:512 * g + 512].rearrange("(k p) c -> p k c", p=128)), writes=[wb])
                if g < 4:
                    pb = k.ps[g % 2]
                    for j in range(4):
                        for kk in range(8):
                            P.op("pe", lambda e, pb=pb, wb=wb, j=j, kk=kk: e.matmul(
                                pb.ap[:, 2 * j:2 * j + 2], wb.ap[:, kk * 512 + 128 * j: kk * 512 + 128 * j + 128],
                                cvt.ap[:, 2 * kk:2 * kk + 2], start=(kk == 0), stop=(kk == 7)),
                                reads=[wb, cvt], writes=[pb])
                    for v in range(2):
                        if g < 2:
                            dst = modB.ap[:, l * 16 + v * 8 + 4 * g: l * 16 + v * 8 + 4 * g + 4]
                            P.op("dve", lambda e, pb=pb, dst=dst, v=v, l=l, g=g: e.tensor_tensor(
                                out=dst, in0=pb.ap[:, 0:8].rearrange("p (j v) -> p j v", v=2)[:, :, v],
                                in1=abf.ap[:, l * 24 + 4 * g: l * 24 + 4 * g + 4], op=ALU.add),
                                reads=[pb, abf], writes=[modB])
                        else:
                            gg = g - 2
                            dst = modS.ap[:, l * 16 + v * 8 + 4 * gg: l * 16 + v * 8 + 4 * gg + 4]
                            P.op("dve", lambda e, pb=pb, dst=dst, v=v, l=l, g=g: e.tensor_tensor(
                                out=dst, in0=pb.ap[:, 0:8].rearrange("p (j v) -> p j v", v=2)[:, :, v],
                                in1=abf.ap[:, l * 24 + 4 * g: l * 24 + 4 * g + 4], op=ALU.add),
                                reads=[pb, abf], writes=[modS])
                            P.op("dve", lambda e, dst=dst, l=l, gg=gg: e.scalar_tensor_tensor(
                                out=dst, in0=dst, scalar=1.0, in1=nwt.ap[:, l * 8 + 4 * gg: l * 8 + 4 * gg + 4],
                                op0=ALU.add, op1=ALU.mult), reads=[modS, nwt], writes=[modS])
                else:
                    gg = g - 4
                    for v in range(2):
                        pb = k.ps[2 + v]
                        for kk in range(8):
                            P.op("pe", lambda e, pb=pb, wb=wb, v=v, kk=kk: e.matmul(
                                pb.ap, screp.ap[:, (2 * kk + v) * 128:(2 * kk + v) * 128 + 128],
                                wb.ap[:, kk * 512: kk * 512 + 512], start=(kk == 0), stop=(kk == 7)),
                                reads=[wb, screp], writes=[pb])
                        dst = gB.cols((l * 2 + v) * 1024 + 512 * gg, (l * 2 + v) * 1024 + 512 * gg + 512)
                        P.op("dve", lambda e, pb=pb, dst=dst, l=l, gg=gg: e.tensor_tensor(
                            out=dst.ap, in0=pb.ap, in1=abg.ap[:, l * 1024 + 512 * gg: l * 1024 + 512 * gg + 512],
                            op=ALU.add), reads=[pb, abg], writes=[dst])
        A.release(m0)

        mH = A.mark()
        hT = A.bf(8 * HT_COLS)
        k.hT = hT
        hT3 = hT.v("p (k c) -> p k c", c=HT_COLS)
        for zc in (0, 4097, 4354, 4611):
            P.op("pool", lambda e, zc=zc: e.memset(hT3[:, :, zc:zc + 1], 0.0), writes=[hT])
        mA = A.mark()
        emit_norm_phase(k, x_src=lambda i: x_all[128 * i:128 * i + 128, :], tiles=(list(range(NT)) if "noA1" not in debug else []), layer=0,
                        hT3=hT3, hT=hT, col0=tile_col0, variant=lambda i: 0 if i < NT_S else 1)
        A.release(mA)

        if "noA2" not in debug:
            emit_inproj0(k, ins, outs, PA, GA, QT, KT, VS, GBT, hT3)

        if "hT" in debug:
            hdbg = dout("hT_dbg", [128, 8 * HT_COLS], BF16)
            P.dma(lambda e: e.dma_start(out=hdbg[:, :], in_=hT.ap), reads=[hT], is_output=True)
        A.release(mH)
        if "noR" not in debug:
            emit_rwkv(k, ins, outs, PA, GA, OB, YT)
        if "noT" not in debug:
            emit_attn(k, ins, outs, QT, KT, VS, GBT, YT)
        if "noO" not in debug:
            emit_tail(k, ins, outs, YT, x_all)
        P.run()
    return nc, ins, outs


def emit_norm_phase(k, x_src, tiles, layer, hT3, hT, col0, variant, x_keep=None, cols_total=HT_COLS, src_reg=None):
    P, A = k.P, k.A
    xt = [A.f32(1024) for _ in range(2)]
    xn = [A.f32(1024) for _ in range(2)]
    junk = A.bf(1024)
    ss = [A.f32(1) for _ in range(2)]
    for n, i in enumerate(tiles):
        xb, xnb, ssb = xt[n % 2], xn[n % 2], ss[n % 2]
        src = x_src(i)
        if src is not None:
            P.dma(lambda e, xb=xb, src=src: e.dma_start(out=xb.ap, in_=src), writes=[xb],
                  reads=([src_reg(i)] if src_reg is not None else []))
        else:
            xb = x_keep(i)
        P.op("act", lambda e, xb=xb, ssb=ssb: e.activation(out=junk.ap, in_=xb.ap, func=AF.Square, accum_out=ssb.ap),
             reads=[xb], writes=[junk, ssb])
        P.op("act", lambda e, ssb=ssb: e.activation(out=ssb.ap, in_=ssb.ap, func=AF.Ln, scale=1.0 / D, bias=NORM_EPS),
             reads=[ssb], writes=[ssb])
        P.op("act", lambda e, ssb=ssb: e.activation(out=ssb.ap, in_=ssb.ap, func=AF.Exp, scale=-0.5),
             reads=[ssb], writes=[ssb])
        P.op("dve", lambda e, xb=xb, xnb=xnb, ssb=ssb: e.tensor_scalar(
            out=xnb.ap, in0=xb.ap, scalar1=ssb.ap[:, 0:1], scalar2=None, op0=ALU.mult), reads=[xb, ssb], writes=[xnb])
        v = variant(i)
        c0 = col0(i)
        for half in range(2):
            pb = k.ps[(2 * n + half) % 4]
            for q in range(4):
                kk = 4 * half + q
                P.op("pe", lambda e, pb=pb, xnb=xnb, kk=kk, q=q: e.transpose(
                    pb.ap[:, 128 * q:128 * q + 128], xnb.ap[:, 128 * kk:128 * kk + 128], k.ident.ap),
                    reads=[xnb, k.ident], writes=[pb])
            for q in range(4):
                kk = 4 * half + q
                mi = layer * 16 + v * 8 + kk
                hreg = ("ar", hT.reg[1] + (kk * cols_total + c0) // 2, hT.reg[1] + (kk * cols_total + c0 + 129) // 2)
                P.op("act", lambda e, pb=pb, kk=kk, q=q, mi=mi, c0=c0: e.activation(
                    out=hT3[:, kk, c0:c0 + 128], in_=pb.ap[:, 128 * q:128 * q + 128], func=AF.Identity,
                    scale=k.modS.ap[:, mi:mi + 1], bias=k.modB.ap[:, mi:mi + 1]),
                    reads=[pb, k.modS, k.modB], writes=[hreg])


def emit_inproj0(k, ins, outs, PA, GA, QT, KT, VS, GBT, hT3):
    P, A, nc = k.P, k.A, k.nc
    hT = k.hT
    w_in0 = ins["w_in0"]
    m0 = A.mark()
    mub = A.f32(3328)
    qkw = A.f32(1024)
    P.dma(lambda e: e.dma_start(out=mub.ap, in_=ins["mu_b"][:, :]), writes=[mub])
    P.dma(lambda e: e.dma_start(out=qkw.ap, in_=ins["qkw_b"][:, :]), writes=[qkw])
    wst = [A.f32(8 * 512) for _ in range(2)]
    wbf = [A.bf(8 * 512) for _ in range(2)]
    hd = [A.bf(8 * 128) for _ in range(2)]
    ev = [A.f32(512) for _ in range(6)]
    evs = [A.f32(512) for _ in range(4)]
    rs8 = [A.f32(8) for _ in range(4)]
    ropet = [A.f32(128) for _ in range(4)]
    tb = [A.bf(512) for _ in range(4)]
    evb = [A.bf(512) for _ in range(4)]

    def hreg(i):
        c0 = tile_col0(i)
        return ("ar", hT.reg[1], hT.reg[2])

    groups = []
    for c in range(0, 3072, 512):
        groups.append((c, 512, "A"))
    groups.append((3072, 256, "A"))
    for c in range(3328, 4352, 512):
        groups.append((c, 512, "ga"))
    for c in range(4352, 5376, 512):
        groups.append((c, 512, "pq"))
    for c in range(5376, 6400, 512):
        groups.append((c, 512, "pk"))
    for c in range(6400, 7424, 512):
        groups.append((c, 512, "pv"))
    cnt = 0
    only = [d[5:] for d in k.debug if d.startswith("only_")]
    if only:
        groups = [g for g in groups if g[2] in only]
    for gi, (cs, wdt, kind) in enumerate(groups):
        ws, wb = wst[gi % 2], wbf[gi % 2]
        P.dma(lambda e, ws=ws, cs=cs, wdt=wdt: e.dma_start(
            out=ws.v("p (k c) -> p k c", c=512)[:, :, 0:wdt],
            in_=w_in0[:, cs:cs + wdt].rearrange("(k p) c -> p k c", p=128)), writes=[ws])
        P.op("pool", lambda e, ws=ws, wb=wb: e.tensor_copy(wb.ap, ws.ap), reads=[ws], writes=[wb])
        if kind == "A":
            tiles = list(range(NT)) if cs >= 1024 else OUT_TILES
        elif kind in ("ga", "pq"):
            tiles = OUT_TILES
        else:
            tiles = list(range(NT))
        def emit_hd(i_, slot):
            hdb_ = hd[slot % 2]
            hd3_ = hdb_.v("p (k c) -> p k c", c=128)
            c0_ = tile_col0(i_)
            P.op("pool", lambda e, hd3_=hd3_, c0_=c0_: e.tensor_tensor(
                out=hd3_, in0=hT3[:, :, c0_ - 1:c0_ + 127], in1=hT3[:, :, c0_ + 1:c0_ + 129], op=ALU.add),
                reads=[hT], writes=[hdb_])
            P.op("dve", lambda e, hd3_=hd3_, c0_=c0_: e.scalar_tensor_tensor(
                out=hd3_, in0=hd3_, scalar=0.5, in1=hT3[:, :, c0_:c0_ + 128], op0=ALU.mult, op1=ALU.subtract),
                reads=[hT, hdb_], writes=[hdb_])

        if kind == "A":
            emit_hd(tiles[0], 0)
        for ti, i in enumerate(tiles):
            c0 = tile_col0(i)
            sample = i < NT_S
            cnt += 1
            if kind == "A":
                p1 = k.ps[4 + (cnt % 2) * 2]
                p2 = k.ps[5 + (cnt % 2) * 2]
            else:
                p1 = k.ps[4 + cnt % 4]
                p2 = None
            if kind == "A" and ti + 1 < len(tiles):
                emit_hd(tiles[ti + 1], ti + 1)
            for kk in range(8):
                P.op("pe", lambda e, p1=p1, wb=wb, kk=kk, c0=c0, wdt=wdt: e.matmul(
                    p1.ap[:, 0:wdt], hT3[:, kk, c0:c0 + 128], wb.ap[:, kk * 512:kk * 512 + wdt],
                    start=(kk == 0), stop=(kk == 7)), reads=[wb, hT], writes=[p1])
            if kind == "A":
                hdb = hd[ti % 2]
                hd3 = hdb.v("p (k c) -> p k c", c=128)
                for kk in range(8):
                    P.op("pe", lambda e, p2=p2, wb=wb, kk=kk, hd3=hd3, wdt=wdt: e.matmul(
                        p2.ap[:, 0:wdt], hd3[:, kk, :], wb.ap[:, kk * 512:kk * 512 + wdt],
                        start=(kk == 0), stop=(kk == 7)), reads=[wb, hdb], writes=[p2])
                e1, e2 = ev[cnt % 2], evs[cnt % 4]
                P.op("act", lambda e, p2=p2, e1=e1, wdt=wdt: e.activation(out=e1.ap[:, 0:wdt], in_=p2.ap[:, 0:wdt], func=AF.Identity),
                     reads=[p2], writes=[e1])
                P.op("pool", lambda e, e1=e1, cs=cs, wdt=wdt: e.tensor_tensor(
                    out=e1.ap[:, 0:wdt], in0=e1.ap[:, 0:wdt], in1=mub.ap[:, cs:cs + wdt], op=ALU.mult),
                    reads=[e1, mub], writes=[e1])
                P.op("dve", lambda e, e1=e1, e2=e2, p1=p1, wdt=wdt: e.tensor_tensor(
                    out=e2.ap[:, 0:wdt], in0=e1.ap[:, 0:wdt], in1=p1.ap[:, 0:wdt], op=ALU.add),
                    reads=[e1, p1], writes=[e2])
                P.dma(lambda e, e2=e2, i=i, cs=cs, wdt=wdt: e.dma_start(
                    out=PA.ap[128 * i:128 * i + 128, cs:cs + wdt], in_=e2.ap[:, 0:wdt]),
                    reads=[e2], writes=[PA.rows(128 * i, 128 * i + 128)], queue="pool")
            elif kind == "ga":
                e1 = evs[cnt % 4]
                o = op_index(i)
                P.op("act", lambda e, p1=p1, e1=e1: e.activation(out=e1.ap, in_=p1.ap, func=AF.Silu), reads=[p1], writes=[e1])
                P.dma(lambda e, e1=e1, o=o, cs=cs: e.dma_start(
                    out=GA.ap[128 * o:128 * o + 128, cs - 3328:cs - 3328 + 512], in_=e1.ap),
                    reads=[e1], writes=[GA.rows(128 * o, 128 * o + 128)], queue="pool")
            elif kind == "pv":
                e1, eb = evs[cnt % 4], evb[cnt % 4]
                P.op("act", lambda e, p1=p1, e1=e1: e.activation(out=e1.ap, in_=p1.ap, func=AF.Identity), reads=[p1], writes=[e1])
                P.op("dve", lambda e, e1=e1, eb=eb: e.tensor_copy(eb.ap, e1.ap), reads=[e1], writes=[eb])
                P.dma(lambda e, eb=eb, i=i, cs=cs: e.dma_start(
                    out=VS.ap[128 * i:128 * i + 128, cs - 6400:cs - 6400 + 512], in_=eb.ap),
                    reads=[eb], writes=[VS.rows(128 * i, 128 * i + 128)], queue="pool")
                if not sample:
                    pi = i - NT_S
                    P.dma(lambda e, e1=e1, pi=pi, cs=cs: e.dma_start(
                        out=outs["vc_out"][128 * pi:128 * pi + 128, cs - 6400:cs - 6400 + 512], in_=e1.ap),
                        reads=[e1], is_output=True, queue="pool")
            else:
                isq = kind == "pq"
                base = 4352 if isq else 5376
                h0 = (cs - base) // 128
                wof = 0 if isq else 512
                e1, e2, e3 = ev[(cnt % 2) * 3], ev[(cnt % 2) * 3 + 1], ev[(cnt % 2) * 3 + 2]
                if cnt % 4 >= 2:
                    e1, e3 = evs[0], evs[1]
                    e2 = evs[2 + cnt % 2]
                r8 = rs8[cnt % 4]
                P.op("act", lambda e, p1=p1, e1=e1: e.activation(out=e1.ap, in_=p1.ap, func=AF.Square), reads=[p1], writes=[e1])
                P.op("dve", lambda e, e1=e1, r8=r8: e.tensor_reduce(
                    out=r8.ap, in_=e1.v("p (g d) -> p g d", d=64), axis=AX.X, op=ALU.add), reads=[e1], writes=[r8])
                P.op("act", lambda e, r8=r8: e.activation(out=r8.ap, in_=r8.ap, func=AF.Ln, scale=1.0 / 64, bias=NORM_EPS),
                     reads=[r8], writes=[r8])
                P.op("act", lambda e, r8=r8: e.activation(out=r8.ap, in_=r8.ap, func=AF.Exp, scale=-0.5), reads=[r8], writes=[r8])
                P.op("dve", lambda e, p1=p1, e2=e2, r8=r8: e.tensor_tensor(
                    out=e2.v("p (g d) -> p g d", d=64), in0=p1.v("p (g d) -> p g d", d=64),
                    in1=r8.ap.unsqueeze(2).broadcast_to([128, 8, 64]), op=ALU.mult), reads=[p1, r8], writes=[e2])
                P.op("dve", lambda e, e2=e2, wof=wof: e.tensor_tensor(
                    out=e2.ap, in0=e2.ap, in1=qkw.ap[:, wof:wof + 512], op=ALU.mult), reads=[e2, qkw], writes=[e2])
                tbb = tb[cnt % 4]
                if (not sample) and (not isq):
                    pi = i - NT_S
                    P.dma(lambda e, e2=e2, pi=pi, cs=cs: e.dma_start(
                        out=outs["kc_out"][128 * pi:128 * pi + 128, cs - 5376:cs - 5376 + 512], in_=e2.ap),
                        reads=[e2], is_output=True, queue="pool")
                if sample:
                    rt = ropet[cnt % 4]
                    P.dma(lambda e, rt=rt, i=i: e.dma_start(out=rt.ap, in_=ins["rope_cs"][128 * i:128 * i + 128, :]), writes=[rt])
                    cosb = rt.ap[:, 0:64].unsqueeze(1).broadcast_to([128, 8, 64])
                    P.op("dve", lambda e, e1=e1, e2=e2, cosb=cosb: e.tensor_tensor(
                        out=e1.v("p (g d) -> p g d", d=64), in0=e2.v("p (g d) -> p g d", d=64), in1=cosb, op=ALU.mult),
                        reads=[e2, rt], writes=[e1])
                    for hf in range(2):
                        sinb = rt.ap[:, 64:128].rearrange("p (c h d) -> p c h d", c=2, h=2)[:, :, hf, :] \
                            .unsqueeze(1).broadcast_to([128, 8, 2, 16])
                        P.op("pool", lambda e, e2=e2, e3=e3, hf=hf, sinb=sinb: e.tensor_tensor(
                            out=e3.v("p (g c h d) -> p g c h d", c=2, h=2, d=16)[:, :, :, hf, :],
                            in0=e2.v("p (g c h d) -> p g c h d", c=2, h=2, d=16)[:, :, :, 1 - hf, :],
                            in1=sinb, op=ALU.mult), reads=[e2, rt], writes=[e3])
                    P.op("dve", lambda e, e1=e1, e3=e3, tbb=tbb: e.tensor_tensor(out=tbb.ap, in0=e1.ap, in1=e3.ap, op=ALU.add),
                         reads=[e1, e3], writes=[tbb])
                else:
                    P.op("dve", lambda e, e2=e2, tbb=tbb: e.tensor_copy(tbb.ap, e2.ap), reads=[e2], writes=[tbb])
                pt = k.ps[cnt % 4]
                ptb = pt.ap.bitcast(BF16)
                for hh in range(4):
                    P.op("pe", lambda e, ptb=ptb, tbb=tbb, hh=hh: e.transpose(
                        ptb[:, 128 * hh:128 * hh + 128], tbb.ap[:, 128 * hh:128 * hh + 128], k.identb.ap),
                        reads=[tbb, k.identb], writes=[pt])
                eb = evb[cnt % 4]
                P.op("act", lambda e, ptb=ptb, eb=eb: e.activation(out=eb.ap, in_=ptb[:, 0:512], func=AF.Identity),
                     reads=[pt], writes=[eb])
                if isq:
                    o = op_index(i)
                    P.dma(lambda e, eb=eb, o=o, h0=h0: e.dma_start(
                        out=QT.ap[h0:h0 + 4, :, 128 * o:128 * o + 128].rearrange("h p t -> p h t"),
                        in_=eb.v("p (h t) -> p h t", t=128)), reads=[eb], writes=[QT.name], queue="pool")
                else:
                    P.dma(lambda e, eb=eb, i=i, h0=h0: e.dma_start(
                        out=KT.ap[h0:h0 + 4, :, 128 * i:128 * i + 128].rearrange("h p t -> p h t"),
                        in_=eb.v("p (h t) -> p h t", t=128)), reads=[eb], writes=[KT.name], queue="pool")
    wins = [(1 + 512 * w_, 512, 512 * w_) for w_ in range(4)] + [(2049, 128, 2048), (4098, 256, 2176), (4355, 256, 2432)]
    if (not only) or ("gb" in only):
        for g2 in range(2):
            gi = len(groups) + g2
            ws, wb = wst[gi % 2], wbf[gi % 2]
            cs = 7424 + 512 * g2
            P.dma(lambda e, ws=ws, cs=cs: e.dma_start(
                out=ws.v("p (k c) -> p k c", c=512), in_=w_in0[:, cs:cs + 512].rearrange("(k p) c -> p k c", p=128)), writes=[ws])
            P.op("pool", lambda e, ws=ws, wb=wb: e.tensor_copy(wb.ap, ws.ap), reads=[ws], writes=[wb])
            for q in range(4):
                for (c0, nw, o0) in wins:
                    cnt += 1
                    p1 = k.ps[4 + cnt % 4]
                    e1 = ev[cnt % 2]
                    for kk in range(8):
                        P.op("pe", lambda e, p1=p1, wb=wb, kk=kk, c0=c0, nw=nw, q=q: e.matmul(
                            p1.ap[:, 0:nw], wb.ap[:, kk * 512 + 128 * q:kk * 512 + 128 * q + 128], hT3[:, kk, c0:c0 + nw],
                            start=(kk == 0), stop=(kk == 7)), reads=[wb, hT], writes=[p1])
                    P.op("act", lambda e, p1=p1, e1=e1, nw=nw: e.activation(out=e1.ap[:, 0:nw], in_=p1.ap[:, 0:nw], func=AF.Silu),
                         reads=[p1], writes=[e1])
                    r0 = 512 * g2 + 128 * q
                    P.dma(lambda e, e1=e1, r0=r0, o0=o0, nw=nw: e.dma_start(out=GBT.ap[r0:r0 + 128, o0:o0 + nw], in_=e1.ap[:, 0:nw]),
                          reads=[e1], writes=[GBT.name])
    if (not only) or ("ctx" in only):
        for t_ in range(2):
            for which in range(2):
                cnt += 1
                e1a, e1b = ev[(cnt % 2) * 2], ev[(cnt % 2) * 2 + 1]
                src = ins["ctx_k"] if which == 0 else ins["ctx_v"]
                for hf in range(2):
                    ebuf = e1a if hf == 0 else e1b
                    P.dma(lambda e, ebuf=ebuf, src=src, t_=t_, hf=hf: e.dma_start(
                        out=ebuf.ap, in_=src[128 * t_:128 * t_ + 128, 512 * hf:512 * hf + 512]), writes=[ebuf])
                    tbb = tb[(cnt + hf) % 2]
                    P.op("dve", lambda e, ebuf=ebuf, tbb=tbb: e.tensor_copy(tbb.ap, ebuf.ap), reads=[ebuf], writes=[tbb])
                    if which == 1:
                        P.dma(lambda e, tbb=tbb, t_=t_, hf=hf: e.dma_start(
                            out=VS.ap[T_ALL + 128 * t_:T_ALL + 128 * t_ + 128, 512 * hf:512 * hf + 512], in_=tbb.ap),
                            reads=[tbb], writes=[VS.rows(T_ALL + 128 * t_, T_ALL + 128 * t_ + 128)])
                    else:
                        pt = k.ps[(cnt + hf) % 2]
                        ptb = pt.ap.bitcast(BF16)
                        for hh in range(4):
                            P.op("pe", lambda e, ptb=ptb, tbb=tbb, hh=hh: e.transpose(
                                ptb[:, 128 * hh:128 * hh + 128], tbb.ap[:, 128 * hh:128 * hh + 128], k.identb.ap),
                                reads=[tbb, k.identb], writes=[pt])
                        eb = evb[(cnt + hf) % 2]
                        P.op("act", lambda e, ptb=ptb, eb=eb: e.activation(out=eb.ap, in_=ptb[:, 0:512], func=AF.Identity),
                             reads=[pt], writes=[eb])
                        P.dma(lambda e, eb=eb, t_=t_, hf=hf: e.dma_start(
                            out=KT.ap[4 * hf:4 * hf + 4, :, T_ALL + 128 * t_:T_ALL + 128 * t_ + 128].rearrange("h p t -> p h t"),
                            in_=eb.v("p (h t) -> p h t", t=128)), reads=[eb], writes=[KT.name])
    A.release(m0)


def TT(P, eng, out, in0, in1, op, r, w):
    P.op(eng, lambda e: e.tensor_tensor(out=out, in0=in0, in1=in1, op=op), reads=r, writes=w)


def ACTF(P, out, in_, func, r, w, **kw):
    P.op("act", lambda e: e.activation(out=out, in_=in_, func=func, **kw), reads=r, writes=w)


def MM(P, out, lhsT, rhs, start, stop, r, w):
    P.op("pe", lambda e: e.matmul(out, lhsT, rhs, start=start, stop=stop), reads=r, writes=w)


def TR(P, out, in_, ident, r, w):
    P.op("pe", lambda e: e.transpose(out, in_, ident), reads=r, writes=w)


def CP(P, eng, out, in_, r, w):
    P.op(eng, lambda e: e.tensor_copy(out, in_), reads=r, writes=w)


def STT(P, out, in0, scalar, in1, op0, op1, r, w):
    P.op("dve", lambda e: e.scalar_tensor_tensor(out=out, in0=in0, scalar=scalar, in1=in1, op0=op0, op1=op1),
         reads=r, writes=w)


def RED(P, out, in_, r, w):
    P.op("dve", lambda e: e.tensor_reduce(out=out, in_=in_, axis=AX.X, op=ALU.add), reads=r, writes=w)


def g64(ap):
    return ap.rearrange("p (g d) -> p g d", d=64)


def b64(ap16):
    return ap16.unsqueeze(2).broadcast_to([128, 16, 64])


def emit_rwkv(k, ins, outs, PA, GA, OB, YT):
    P, A = k.P, k.A
    ps = k.ps
    psb = [p_.ap.bitcast(BF16) for p_ in ps]
    m0 = A.mark()
    prm = A.f32(5120)
    kkb, kab, rkb, lwb, lbb = [prm.cols(1024 * i, 1024 * i + 1024) for i in range(5)]
    wup = A.f32(4096)
    CM = A.f32(768)
    IND = A.f32(2)
    MSK = A.f32(1024)
    lT = A.f32(256)
    H = [A.f32(512) for _ in range(2)]
    Hb = [A.bf(512) for _ in range(2)]
    gC = A.f32(16)
    sbst = A.f32(21 * 16)
    P.dma(lambda e: e.dma_start(out=prm.ap, in_=ins["rw_prm"][:, :]), writes=[prm])
    P.dma(lambda e: e.dma_start(out=wup.ap[0:65, :], in_=ins["rw_up"][:, :]), writes=[wup])
    P.dma(lambda e: e.dma_start(out=CM.ap, in_=ins["rw_const"][:, 0:768]), writes=[CM])
    P.dma(lambda e: e.dma_start(out=IND.ap, in_=ins["rw_const"][:, 768:770]), writes=[IND])
    P.dma(lambda e: e.dma_start(out=MSK.ap, in_=ins["rw_const"][:, 770:1794]), writes=[MSK])
    P.op("pool", lambda e: e.memset(lT.ap, 1.0), writes=[lT])
    pa_one = A.f32(3328)
    pa2 = [pa_one, pa_one]
    LT = A.f32(128)
    SW, AA, KAP, KD, BE, X1, X2 = [A.f32(1024) for _ in range(7)]
    E = [A.f32(1024) for _ in range(2)]
    s16, sbc = A.f32(16), A.f32(16)
    RTIL, KATIL, BTIL, KTIL, BH, KH, VB = [A.bf(1024) for _ in range(7)]
    QRS, BKS = A.bf(2048), A.bf(2048)
    SC = A.bf(16 * 512)
    Nb = [A.bf(2048) for _ in range(2)]
    Ntb = [A.bf(2048) for _ in range(2)]
    Tb = [A.bf(2048) for _ in range(2)]
    Wsb = A.bf(1024)
    Usbc = [A.bf(1024) for _ in range(2)]
    VBc = [A.bf(1024) for _ in range(2)]
    Osb = A.f32(1024)
    obuf, gabuf = X2, E[0]
    for zb in (Wsb, Usbc[0], Usbc[1], VBc[0], VBc[1]):
        P.op("pool", lambda e, zb=zb: e.memset(zb.ap, 0.0), writes=[zb])
    ytb = A.bf(1024)
    lT3 = lT.v("p (a t) -> p a t", t=128)
    QRS4 = QRS.v("p (j x t) -> p j x t", x=2, t=128)
    BKS4 = BKS.v("p (j x t) -> p j x t", x=2, t=128)
    SC3 = SC.v("p (h c) -> p h c", c=512)

    def visit(i, d, state_only, chunks, finalize):
        n = visit.n
        visit.n += 1
        pa = pa2[n % 2]
        c_lo = 1024 if state_only else 0
        P.dma(lambda e: e.dma_start(out=pa.ap[:, c_lo:3328], in_=PA.ap[128 * i:128 * i + 128, c_lo:3328]),
              reads=[PA.rows(128 * i, 128 * i + 128)], writes=[pa])
        par, pak, pav = pa.ap[:, 0:1024], pa.ap[:, 1024:2048], pa.ap[:, 2048:3072]
        Hd, Hbd = H[d], Hb[d]
        Hd3 = Hd.v("p (j v) -> p j v", v=64)
        Hb3 = Hbd.v("p (j v) -> p j v", v=64)
        CP(P, "pool", VB.ap, pav, [pa], [VB])
        CP(P, "pool", VBc[0].ap[0:64, :], pav[0:64, :], [pa], [VBc[0]])
        CP(P, "pool", VBc[1].ap[64:128, :], pav[64:128, :], [pa], [VBc[1]])
        ACTF(P, LT.ap[:, 0:64], pa.ap[:, 3072 + 64 * d:3136 + 64 * d], AF.Tanh, [pa], [LT])
        ACTF(P, LT.ap[:, 64:128], pa.ap[:, 3200 + 64 * d:3264 + 64 * d], AF.Identity, [pa], [LT])
        TR(P, ps[0].ap[0:64, 0:128], LT.ap[:, 0:64], k.ident.ap, [LT, k.ident], [ps[0]])
        TR(P, ps[0].ap[0:64, 128:256], LT.ap[:, 64:128], k.ident.ap, [LT, k.ident], [ps[0]])
        ACTF(P, lT3[0:64, :, :], ps[0].ap[0:64, 0:256].rearrange("p (a t) -> p a t", t=128), AF.Identity, [ps[0]], [lT])
        for hf in range(2):
            MM(P, ps[hf].ap, lT3[0:65, 0, :], wup.ap[0:65, 1024 * d + 512 * hf:1024 * d + 512 * hf + 512], True, True,
               [lT, wup], [ps[hf]])
            MM(P, ps[2 + hf].ap, lT3[0:65, 1, :], wup.ap[0:65, 2048 + 1024 * d + 512 * hf:2048 + 1024 * d + 512 * hf + 512],
               True, True, [lT, wup], [ps[2 + hf]])
        for hf in range(2):
            ACTF(P, SW.ap[:, 512 * hf:512 * hf + 512], ps[hf].ap, AF.Sigmoid, [ps[hf]], [SW])
            ACTF(P, AA.ap[:, 512 * hf:512 * hf + 512], ps[2 + hf].ap, AF.Sigmoid, [ps[2 + hf]], [AA])
        TT(P, "dve", X1.ap, pak, kkb.ap, ALU.mult, [pa, kkb], [X1])
        ACTF(P, X2.ap, X1.ap, AF.Square, [X1], [X2])
        RED(P, s16.ap, g64(X2.ap), [X2], [s16])
        ACTF(P, s16.ap, s16.ap, AF.Ln, [s16], [s16], bias=1e-24)
        ACTF(P, s16.ap, s16.ap, AF.Exp, [s16], [s16], scale=-0.5)
        TT(P, "dve", g64(KAP.ap), g64(X1.ap), b64(s16.ap), ALU.mult, [X1, s16], [KAP])
        STT(P, X2.ap, AA.ap, -1.0, kab.ap, ALU.add, ALU.mult, [AA, kab], [X2])
        STT(P, KD.ap, X2.ap, 1.0, pak, ALU.add, ALU.mult, [X2, pa], [KD])
        TT(P, "pool", BE.ap, KAP.ap, AA.ap, ALU.mult, [KAP, AA], [BE])
        if not state_only:
            TT(P, "pool", X1.ap, par, rkb.ap, ALU.mult, [pa, rkb], [X1])
            TT(P, "pool", X1.ap, X1.ap, KD.ap, ALU.mult, [X1, KD], [X1])
            RED(P, sbc.ap, g64(X1.ap), [X1], [sbc])
        cm = CM.v("p (d m t) -> p d m t", d=2, m=3)
        for hf in range(2):
            MM(P, ps[4 + hf].ap, cm[:, d, 0, :], SW.ap[:, 512 * hf:512 * hf + 512], True, True, [CM, SW], [ps[4 + hf]])
            MM(P, ps[6 + hf].ap, cm[:, d, 1, :], SW.ap[:, 512 * hf:512 * hf + 512], True, True, [CM, SW], [ps[6 + hf]])
        for hf in range(2):
            ACTF(P, E[0].ap[:, 512 * hf:512 * hf + 512], ps[4 + hf].ap, AF.Exp, [ps[4 + hf]], [E[0]])
            ACTF(P, E[1].ap[:, 512 * hf:512 * hf + 512], ps[4 + hf].ap, AF.Exp, [ps[4 + hf]], [E[1]], scale=-1.0)
        if not state_only:
            TT(P, "dve", RTIL.ap, par, E[0].ap, ALU.mult, [pa, E[0]], [RTIL])
        TT(P, "pool", BTIL.ap, BE.ap, E[1].ap, ALU.mult, [BE, E[1]], [BTIL])
        TT(P, "dve", KTIL.ap, KD.ap, E[1].ap, ALU.mult, [KD, E[1]], [KTIL])
        for hf in range(2):
            MM(P, ps[hf].ap, cm[:, d, 2, :], SW.ap[:, 512 * hf:512 * hf + 512], True, True, [CM, SW], [ps[hf]])
        for hf in range(2):
            ACTF(P, E[0].ap[:, 512 * hf:512 * hf + 512], ps[6 + hf].ap, AF.Exp, [ps[6 + hf]], [E[0]])
            ACTF(P, E[1].ap[:, 512 * hf:512 * hf + 512], ps[hf].ap, AF.Exp, [ps[hf]], [E[1]])
        TT(P, "dve", KATIL.ap, KAP.ap, E[0].ap, ALU.mult, [KAP, E[0]], [KATIL])
        TT(P, "pool", BH.ap, BE.ap, E[1].ap, ALU.mult, [BE, E[1]], [BH])
        TT(P, "pool", KH.ap, KD.ap, E[1].ap, ALU.mult, [KD, E[1]], [KH])
        for j in range(8):
            MM(P, ps[2].ap[:, 2 * j:2 * j + 2], SW.ap[:, 128 * j:128 * j + 128], IND.ap, True, True, [SW, IND], [ps[2]])
        ACTF(P, gC.ap, ps[2].ap[:, 0:16], AF.Exp, [ps[2]], [gC])
        gC3 = gC.v("p (j c) -> p j c", c=2)
        quants = [(KATIL, QRS4, 0), (BTIL, BKS4, 0), (KTIL, BKS4, 1)]
        if not state_only:
            quants.append((RTIL, QRS4, 1))
        for qi, (src, dst4, x) in enumerate(quants):
            dstbuf = QRS if dst4 is QRS4 else BKS
            pi_ = 4 + qi % 4
            for j in range(8):
                TR(P, psb[pi_][:, 128 * j:128 * j + 128], src.ap[:, 128 * j:128 * j + 128], k.identb.ap,
                   [src, k.identb], [ps[pi_]])
            dview = dst4[:, :, x, :]
            sview = psb[pi_].rearrange("p (j t) -> p j t", t=128)
            if qi % 2 == 0:
                ACTF(P, dview, sview, AF.Identity, [ps[pi_]], [dstbuf])
            else:
                CP(P, "dve", dview, sview, [ps[pi_]], [dstbuf])
        msk = MSK.ap[:, 512 * d:512 * d + 512]
        ncol = 128 if state_only else 256
        for h in range(16):
            j, e = h // 2, h % 2
            pb = ps[h % 4]
            rhs = QRS4[64 * e:64 * e + 64, j, :, :] if not state_only else QRS4[64 * e:64 * e + 64, j, 0:1, :]
            MM(P, pb.ap[:, 0:ncol], BKS4[64 * e:64 * e + 64, j, 0, :], rhs, True, True, [BKS, QRS], [pb])
            MM(P, pb.ap[:, 256:256 + ncol], BKS4[64 * e:64 * e + 64, j, 1, :], rhs, True, True, [BKS, QRS], [pb])
            TT(P, "dve", SC3[:, h, :], pb.ap, msk, ALU.mult, [pb, MSK], [SC])
        CP(P, "pool", Nb[0].v("p (h t) -> p h t", t=128), SC3[:, :, 0:128], [SC], [Nb[0]])
        TT(P, "dve", Tb[0].v("p (h t) -> p h t", t=128),
           k.identb.ap.unsqueeze(1).broadcast_to([128, 16, 128]), SC3[:, :, 0:128], ALU.subtract, [SC, k.identb], [Tb[0]])
        for hh in range(2):
            pi_ = 4 + hh
            for q in range(8):
                h = 8 * hh + q
                TR(P, psb[pi_][:, 128 * q:128 * q + 128], SC3[:, h, 0:128], k.identb.ap, [SC, k.identb], [ps[pi_]])
            ACTF(P, Ntb[0].ap[:, 1024 * hh:1024 * hh + 1024], psb[pi_], AF.Identity, [ps[pi_]], [Ntb[0]])
        cur = 0
        for lvl in range(1, 6):
            nxt = 1 - cur
            last = lvl == 5
            for hq in range(4):
                sl = slice(512 * hq, 512 * hq + 512)
                pbt = ps[hq % 4]
                for q in range(4):
                    h = 4 * hq + q
                    c1 = slice(128 * h, 128 * h + 128)
                    MM(P, pbt.ap[:, 128 * q:128 * q + 128], Nb[cur].ap[:, c1], Ntb[cur].ap[:, c1], True, True,
                       [Nb[cur], Ntb[cur]], [pbt])
                ACTF(P, Ntb[nxt].ap[:, sl], pbt.ap, AF.Identity, [pbt], [Ntb[nxt]])
                if not last:
                    pbn = ps[4 + hq % 4]
                    for q in range(4):
                        h = 4 * hq + q
                        c1 = slice(128 * h, 128 * h + 128)
                        MM(P, pbn.ap[:, 128 * q:128 * q + 128], Ntb[cur].ap[:, c1], Nb[cur].ap[:, c1], True, True,
                           [Nb[cur], Ntb[cur]], [pbn])
                    CP(P, "dve", Nb[nxt].ap[:, sl], pbn.ap, [pbn], [Nb[nxt]])
            for hq in range(4):
                sl = slice(512 * hq, 512 * hq + 512)
                pb = ps[4 + hq % 4] if last else ps[hq % 4]
                for q in range(4):
                    h = 4 * hq + q
                    c1 = slice(128 * h, 128 * h + 128)
                    MM(P, pb.ap[:, 128 * q:128 * q + 128], Ntb[nxt].ap[:, c1], Tb[cur].ap[:, c1], True, True,
                       [Ntb[nxt], Tb[cur]], [pb])
                TT(P, "dve", Tb[nxt].ap[:, sl], pb.ap, Tb[cur].ap[:, sl], ALU.add, [pb, Tb[cur]], [Tb[nxt]])
            cur = nxt
        MTb = Tb[cur]
        MT3 = MTb.v("p (h t) -> p h t", t=128)
        for flag in ("xa", "xb", "xc", "xd"):
            if flag in k.debug:
                for h in range(16):
                    j, e = h // 2, h % 2
                    es = slice(64 * e, 64 * e + 64)
                    pb = ps[h // 8]
                    oc = slice(64 * (h % 8), 64 * (h % 8) + 64)
                    if flag == "xa":
                        MM(P, pb.ap[:, oc], BKS4[es, j, 0, :], BKS4[es, j, 1, 0:64], True, True, [BKS], [pb])
                    elif flag == "xb":
                        MM(P, pb.ap[:, oc], QRS4[es, j, 0, :], BKS4[es, j, 1, 0:64], True, True, [BKS, QRS], [pb])
                    elif flag == "xc":
                        MM(P, pb.ap[:, oc], QRS4[es, j, 0, :], Hb3[es, j, :], True, True, [QRS, Hbd], [pb])
                    elif flag == "xd":
                        MM(P, pb.ap[:, oc], BKS4[es, j, 0, :], Hb3[es, j, :], True, True, [BKS, Hbd], [pb])
                for hf in range(2):
                    ACTF(P, Wsb.ap[:, 512 * hf:512 * hf + 512], ps[hf].ap, AF.Identity, [ps[hf]], [Wsb])
        if "rwstop6" in k.debug:
            return
        for ch in chunks:
            rs = slice(64 * ch, 64 * ch + 64)
            Uc, Vc = Usbc[ch], VBc[ch]
            for h in range(16):
                j, e = h // 2, h % 2
                es = slice(64 * e, 64 * e + 64)
                pb = ps[h // 8]
                oc = slice(64 * (h % 8), 64 * (h % 8) + 64)
                MM(P, pb.ap[:, oc], QRS4[es, j, 0, :], Hb3[es, j, :], True, False, [QRS, Hbd], [pb])
                MM(P, pb.ap[:, oc], SC3[:, h, 256:384], VB.ap[:, 64 * h:64 * h + 64], False, True, [SC, VB], [pb])
            for hf in range(2):
                ACTF(P, Wsb.ap[rs, 512 * hf:512 * hf + 512], ps[hf].ap[rs, :], AF.Identity, [ps[hf]], [Wsb])
            if "rwstop7" in k.debug:
                return
            for h in range(16):
                pb = ps[2 + h // 8]
                oc = slice(64 * (h % 8), 64 * (h % 8) + 64)
                MM(P, pb.ap[:, oc], MT3[:, h, :], Wsb.ap[:, 64 * h:64 * h + 64], True, True, [MTb, Wsb], [pb])
            for hf in range(2):
                ACTF(P, Uc.ap[rs, 512 * hf:512 * hf + 512], ps[2 + hf].ap[rs, :], AF.Identity, [ps[2 + hf]], [Uc], scale=-1.0)
            if not state_only:
                for h in range(16):
                    j, e = h // 2, h % 2
                    es = slice(64 * e, 64 * e + 64)
                    pb = ps[4 + h // 8]
                    oc = slice(64 * (h % 8), 64 * (h % 8) + 64)
                    MM(P, pb.ap[:, oc], QRS4[es, j, 1, :], Hb3[es, j, :], True, False, [QRS, Hbd], [pb])
                    MM(P, pb.ap[:, oc], SC3[:, h, 128:256], Uc.ap[:, 64 * h:64 * h + 64], False, False, [SC, Uc], [pb])
                    MM(P, pb.ap[:, oc], SC3[:, h, 384:512], VB.ap[:, 64 * h:64 * h + 64], False, True, [SC, VB], [pb])
                for hf in range(2):
                    CP(P, "dve", Osb.ap[rs, 512 * hf:512 * hf + 512], ps[4 + hf].ap[rs, :], [ps[4 + hf]], [Osb])
            for j in range(8):
                pb = ps[6 + j // 4]
                oc = slice(128 * (j % 4), 128 * (j % 4) + 128)
                jc = slice(128 * j, 128 * j + 128)
                MM(P, pb.ap[:, oc], BH.ap[:, jc], Uc.ap[:, jc], True, False, [BH, Uc], [pb])
                MM(P, pb.ap[:, oc], KH.ap[:, jc], Vc.ap[:, jc], False, True, [KH, Vc], [pb])
            for e in range(2):
                es = slice(64 * e, 64 * e + 64)
                TT(P, "dve", Hd3[es, :, :], Hd3[es, :, :], gC3[es, :, ch].unsqueeze(2).broadcast_to([64, 8, 64]), ALU.mult,
                   [Hd, gC], [Hd])
                for hf in range(2):
                    src = ps[6 + hf].ap[es, :].rearrange("p (j e v) -> p j e v", e=2, v=64)[:, :, e, :]
                    TT(P, "dve", Hd3[es, 4 * hf:4 * hf + 4, :], Hd3[es, 4 * hf:4 * hf + 4, :], src, ALU.add,
                       [Hd, ps[6 + hf]], [Hd])
            CP(P, "pool", Hbd.ap, Hd.ap, [Hd], [Hbd])
        if state_only:
            return
        o = op_index(i)
        if not finalize:
            P.dma(lambda e: e.dma_start(out=OB.ap[128 * o:128 * o + 128, :], in_=Osb.ap), reads=[Osb],
                  writes=[OB.rows(128 * o, 128 * o + 128)])
            CP(P, "pool", sbst.ap[:, 16 * o:16 * o + 16], sbc.ap, [sbc], [sbst])
            return
        P.dma(lambda e: e.dma_start(out=obuf.ap, in_=OB.ap[128 * o:128 * o + 128, :]),
              reads=[OB.rows(128 * o, 128 * o + 128)], writes=[obuf])
        P.dma(lambda e: e.dma_start(out=gabuf.ap, in_=GA.ap[128 * o:128 * o + 128, :]),
              reads=[GA.rows(128 * o, 128 * o + 128)], writes=[gabuf])
        TT(P, "dve", X1.ap, Osb.ap, obuf.ap, ALU.add, [Osb, obuf], [X1])
        RED(P, s16.ap, g64(X1.ap), [X1], [s16])
        P.op("dve", lambda e: e.tensor_scalar(out=s16.ap, in0=s16.ap, scalar1=-1.0 / 64, scalar2=None, op0=ALU.mult),
             reads=[s16], writes=[s16])
        TT(P, "dve", g64(X1.ap), g64(X1.ap), b64(s16.ap), ALU.add, [X1, s16], [X1])
        ACTF(P, X2.ap, X1.ap, AF.Square, [X1], [X2])
        RED(P, s16.ap, g64(X2.ap), [X2], [s16])
        ACTF(P, s16.ap, s16.ap, AF.Ln, [s16], [s16], scale=1.0 / 64, bias=GN_EPS)
        ACTF(P, s16.ap, s16.ap, AF.Exp, [s16], [s16], scale=-0.5)
        TT(P, "dve", g64(X1.ap), g64(X1.ap), b64(s16.ap), ALU.mult, [X1, s16], [X1])
        TT(P, "pool", X1.ap, X1.ap, lwb.ap, ALU.mult, [X1, lwb], [X1])
        TT(P, "pool", X1.ap, X1.ap, lbb.ap, ALU.add, [X1, lbb], [X1])
        TT(P, "dve", sbc.ap, sbc.ap, sbst.ap[:, 16 * o:16 * o + 16], ALU.add, [sbc, sbst], [sbc])
        TT(P, "dve", g64(X2.ap), g64(pav), b64(sbc.ap), ALU.mult, [pa, sbc], [X2])
        TT(P, "pool", X1.ap, X1.ap, X2.ap, ALU.add, [X1, X2], [X1])
        TT(P, "dve", X1.ap, X1.ap, gabuf.ap, ALU.mult, [X1, gabuf], [X1])
        if "YA" in k.debug:
            P.dma(lambda e: e.dma_start(out=outs["YA"][128 * o:128 * o + 128, :], in_=X1.ap), reads=[X1], is_output=True)
        for hf in range(2):
            pb = ps[4 + hf]
            for q in range(4):
                kk_ = 4 * hf + q
                TR(P, pb.ap[:, 128 * q:128 * q + 128], X1.ap[:, 128 * kk_:128 * kk_ + 128], k.ident.ap, [X1, k.ident], [pb])
            ACTF(P, ytb.ap[:, 512 * hf:512 * hf + 512], pb.ap, AF.Identity, [pb], [ytb])
        P.dma(lambda e: e.dma_start(out=YT.ap[0:1024, 128 * o:128 * o + 128].rearrange("(kk p) t -> p kk t", p=128),
                                    in_=ytb.v("p (kk t) -> p kk t", t=128)), reads=[ytb], writes=[YT.name])

    visit.n = 0

    def init_state(d, from_input):
        if from_input:
            P.dma(lambda e: e.dma_start(out=H[d].ap, in_=ins["st_in"][:, 512 * d:512 * d + 512]), writes=[H[d]])
        else:
            P.op("pool", lambda e: e.memset(H[d].ap, 0.0), writes=[H[d]])
        CP(P, "pool", Hb[d].ap, H[d].ap, [H[d]], [Hb[d]])

    only = [x for x in k.debug if x.startswith("rw_")]
    do_sample = (not only) or ("rw_sample" in only)
    do_prompt = (not only) or ("rw_prompt" in only)
    if do_sample:
        init_state(1, True)
        for i in range(NT_S - 1, NT_OWN - 1, -1):
            visit(i, 1, True, (1, 0), False)
        for i in range(NT_OWN - 1, -1, -1):
            visit(i, 1, False, (1, 0), False)
        init_state(0, True)
        for i in range(NT_OWN):
            visit(i, 0, False, (0, 1), True)
    if do_prompt:
        for sq in range(2):
            t0 = NT_S + 2 * sq
            init_state(1, False)
            for i in (t0 + 1, t0):
                visit(i, 1, False, (1, 0), False)
            P.dma(lambda e, sq=sq: e.dma_start(out=outs["st_out"][sq, 1, :, :], in_=H[1].ap), reads=[H[1]], is_output=True)
            init_state(0, False)
            for i in (t0, t0 + 1):
                visit(i, 0, False, (0, 1), True)
            P.dma(lambda e, sq=sq: e.dma_start(out=outs["st_out"][sq, 0, :, :], in_=H[0].ap), reads=[H[0]], is_output=True)
    A.release(m0)


def emit_attn(k, ins, outs, QT, KT, VS, GBT, YT):
    P, A = k.P, k.A
    ps = k.ps
    m0 = A.mark()
    prm = A.f32(257)
    lam4 = A.f32(4)
    neglam = A.f32(1)
    subw = A.f32(1)
    onesb = A.bf(128)
    P.dma(lambda e: e.dma_start(out=prm.ap, in_=ins["at_prm"][:, :]), writes=[prm])
    P.op("pool", lambda e: e.memset(onesb.ap, 1.0), writes=[onesb])
    tmp = A.f32(128)
    l4 = prm.ap[:, 0:256].rearrange("p (a d) -> p a d", d=64)
    TT(P, "dve", tmp.v("p (a d) -> p a d", d=64), l4[:, 0:4:2, :], l4[:, 1:4:2, :], ALU.mult, [prm], [tmp])
    P.op("dve", lambda e: e.tensor_reduce(out=lam4.ap[:, 0:2], in_=tmp.v("p (a d) -> p a d", d=64), axis=AX.X, op=ALU.add),
         reads=[tmp], writes=[lam4])
    ACTF(P, lam4.ap[:, 0:2], lam4.ap[:, 0:2], AF.Exp, [lam4], [lam4])
    STT(P, neglam.ap, lam4.ap[:, 1:2], -LAM_INIT, lam4.ap[:, 0:1], ALU.add, ALU.subtract, [lam4], [neglam])
    P.op("dve", lambda e: e.tensor_scalar(out=subw.ap, in0=prm.ap[:, 256:257], scalar1=1.0 - LAM_INIT, scalar2=None, op0=ALU.mult),
         reads=[prm], writes=[subw])
    NKMAX = 34
    kT = [A.bf(NKMAX * 128) for _ in range(2)]
    vv = [A.bf(NKMAX * 128) for _ in range(2)]
    qT = [A.bf(2176) for _ in range(2)]
    pT = [A.bf(512) for _ in range(4)]
    r0b, t0b, t1b, gbt = A.f32(512), A.f32(512), A.f32(512), A.f32(512)
    sqb = A.bf(512)
    yb = A.bf(512)
    sample_keys = [128 * i for i in range(NT_S)] + [T_ALL, T_ALL + 128]
    groups = [(0, 128 * NT_OWN, sample_keys)]
    for sq in range(2):
        groups.append((128 * NT_OWN + 256 * sq, 256, [128 * NT_S + 256 * sq, 128 * NT_S + 256 * sq + 128]))
    only = [x for x in k.debug if x.startswith("at_")]
    if "at_prompt" in only:
        groups = groups[1:]
    if "at_sample" in only:
        groups = groups[:1]
    it = 0
    for (q0, nq, keys) in groups:
        nk = len(keys)
        for h in range(1 if "at_h1" in k.debug else 8):
            it += 1
            kTb, vb, qTb = kT[it % 2], vv[it % 2], qT[it % 2]
            runs = []
            for ki, kr in enumerate(keys):
                if runs and runs[-1][1] + runs[-1][2] == kr:
                    runs[-1][2] += 128
                else:
                    runs.append([ki, kr, 128])
            for (ki, kr, ln) in runs:
                P.dma(lambda e, kTb=kTb, ki=ki, kr=kr, ln=ln, h=h: e.dma_start(
                    out=kTb.ap[:, 128 * ki:128 * ki + ln], in_=KT.ap[h, :, kr:kr + ln]), reads=[KT.name], writes=[kTb])
                P.dma(lambda e, vb=vb, ki=ki, kr=kr, ln=ln, h=h: e.dma_start(
                    out=vb.ap[:, 128 * ki:128 * ki + ln].rearrange("p (t c) -> p t c", c=128),
                    in_=VS.ap[kr:kr + ln, 128 * h:128 * h + 128].rearrange("(t p) c -> p t c", p=128)),
                    reads=[VS.rows(kr, kr + ln)], writes=[vb])
            P.dma(lambda e, qTb=qTb, q0=q0, nq=nq, h=h: e.dma_start(out=qTb.ap[:, 0:nq], in_=QT.ap[h, :, q0:q0 + nq]),
                  reads=[QT.name], writes=[qTb])
            for qs in range(0, nq, 512):
                qn = min(512, nq - qs)
                def emit_qk(ki):
                    for c in range(2):
                        sb_ = ps[2 * (ki % 2) + c]
                        cs_ = slice(64 * c, 64 * c + 64)
                        MM(P, sb_.ap[:, 0:qn], kTb.ap[cs_, 128 * ki:128 * ki + 128], qTb.ap[cs_, qs:qs + qn], True, True,
                           [kTb, qTb], [sb_])

                emit_qk(0)
                for ki in range(nk):
                    if ki + 1 < nk:
                        emit_qk(ki + 1)
                    for c in range(2):
                        sb_ = ps[2 * (ki % 2) + c]
                        pb_ = pT[2 * (ki % 2) + c]
                        ACTF(P, pb_.ap[:, 0:qn], sb_.ap[:, 0:qn], AF.Exp, [sb_], [pb_], scale=0.125)
                    for c in range(2):
                        pb_ = pT[2 * (ki % 2) + c]
                        MM(P, ps[4 + c].ap[:, 0:qn], vb.ap[:, 128 * ki:128 * ki + 128], pb_.ap[:, 0:qn], ki == 0, ki == nk - 1,
                           [vb, pb_], [ps[4 + c]])
                        MM(P, ps[6 + c].ap[:, 0:qn], onesb.ap, pb_.ap[:, 0:qn], ki == 0, ki == nk - 1, [onesb, pb_], [ps[6 + c]])
                P.op("dve", lambda e, qn=qn: e.reciprocal(out=r0b.ap[:, 0:qn], in_=ps[6].ap[:, 0:qn]), reads=[ps[6]], writes=[r0b])
                TT(P, "dve", t0b.ap[:, 0:qn], ps[4].ap[:, 0:qn], r0b.ap[:, 0:qn], ALU.mult, [ps[4], r0b], [t0b])
                P.op("dve", lambda e, qn=qn: e.reciprocal(out=r0b.ap[:, 0:qn], in_=ps[7].ap[:, 0:qn]), reads=[ps[7]], writes=[r0b])
                TT(P, "dve", t1b.ap[:, 0:qn], ps[5].ap[:, 0:qn], r0b.ap[:, 0:qn], ALU.mult, [ps[5], r0b], [t1b])
                STT(P, t0b.ap[:, 0:qn], t1b.ap[:, 0:qn], neglam.ap[:, 0:1], t0b.ap[:, 0:qn], ALU.mult, ALU.add,
                    [t1b, neglam, t0b], [t0b])
                ACTF(P, sqb.ap[:, 0:qn], t0b.ap[:, 0:qn], AF.Square, [t0b], [sqb])
                MM(P, ps[0].ap[:, 0:qn], onesb.ap, sqb.ap[:, 0:qn], True, True, [onesb, sqb], [ps[0]])
                ACTF(P, t1b.ap[:, 0:qn], ps[0].ap[:, 0:qn], AF.Ln, [ps[0]], [t1b], scale=1.0 / 128, bias=NORM_EPS)
                ACTF(P, t1b.ap[:, 0:qn], t1b.ap[:, 0:qn], AF.Exp, [t1b], [t1b], scale=-0.5)
                P.dma(lambda e, qs=qs, qn=qn, q0=q0, h=h: e.dma_start(
                    out=gbt.ap[:, 0:qn], in_=GBT.ap[128 * h:128 * h + 128, q0 + qs:q0 + qs + qn]), reads=[GBT.name], writes=[gbt])
                STT(P, t0b.ap[:, 0:qn], t0b.ap[:, 0:qn], subw.ap[:, 0:1], t1b.ap[:, 0:qn], ALU.mult, ALU.mult,
                    [t0b, subw, t1b], [t0b])
                TT(P, "dve", yb.ap[:, 0:qn], t0b.ap[:, 0:qn], gbt.ap[:, 0:qn], ALU.mult, [t0b, gbt], [yb])
                if "YB" in k.debug:
                    TT(P, "dve", t1b.ap[:, 0:qn], t0b.ap[:, 0:qn], gbt.ap[:, 0:qn], ALU.mult, [t0b, gbt], [t1b])
                    P.dma(lambda e, qs=qs, qn=qn, q0=q0, h=h: e.dma_start(
                        out=outs["YB"][128 * h:128 * h + 128, q0 + qs:q0 + qs + qn], in_=t1b.ap[:, 0:qn]), reads=[t1b], is_output=True)
                P.dma(lambda e, qs=qs, qn=qn, q0=q0, h=h: e.dma_start(
                    out=YT.ap[1024 + 128 * h:1024 + 128 * h + 128, q0 + qs:q0 + qs + qn], in_=yb.ap[:, 0:qn]),
                    reads=[yb], writes=[YT.name])
    A.release(m0)


L1_COLS = 2694


def l1_col0(o):
    if o < NT_OWN:
        return 1 + 128 * o
    if o < NT_OWN + 2:
        return 2179 + 128 * (o - NT_OWN)
    return 2436 + 128 * (o - NT_OWN - 2)


def emit_tail(k, ins, outs, YT, x_all):
    P, A, nc = k.P, k.A, k.nc
    ps = k.ps
    NO = NT_OWN + NT_PR
    m0 = A.mark()
    X1S = k.X1S
    hT1 = A.bf(8 * L1_COLS)
    hT13 = hT1.v("p (k c) -> p k c", c=L1_COLS)
    P.op("pool", lambda e: e.memset(hT1.ap, 0.0), writes=[hT1])
    m1 = A.mark()

    def outproj(w_dram, yt_cols, layer, x_get, x_put, tag):
        mm = A.mark()
        wst = [A.f32(1024) for _ in range(2)]
        wb = A.bf(16 * 1024)
        for kk in range(16):
            wsb = wst[kk % 2]
            P.dma(lambda e, wsb=wsb, kk=kk: e.dma_start(out=wsb.ap[:, 0:1024], in_=w_dram[128 * kk:128 * kk + 128, :]), writes=[wsb])
            CP(P, "pool" if kk % 2 else "dve", wb.ap[:, 1024 * kk:1024 * kk + 1024], wsb.ap[:, 0:1024], [wsb],
               [wb.cols(1024 * kk, 1024 * kk + 1024)])
        ytb = [A.bf(16 * 128) for _ in range(2)]
        xt = [A.f32(1024) for _ in range(2)]
        for o in range(NO):
            yb_ = ytb[o % 2]
            c0 = yt_cols(o)
            P.dma(lambda e, yb_=yb_, c0=c0: e.dma_start(
                out=yb_.v("p (kk t) -> p kk t", t=128), in_=YT.ap[:, c0:c0 + 128].rearrange("(kk p) t -> p kk t", p=128)),
                reads=[YT.name], writes=[yb_])
            v = 0 if o < NT_OWN else 1
            g = k.gB.ap[:, (layer * 2 + v) * 1024:(layer * 2 + v) * 1024 + 1024]
            xin = x_get(o, xt[o % 2])
            xo = x_put(o, xt[o % 2])
            for hf in range(2):
                pb = ps[2 * (o % 2) + hf]
                for kk in range(16):
                    MM(P, pb.ap, yb_.ap[:, 128 * kk:128 * kk + 128], wb.ap[:, 1024 * kk + 512 * hf:1024 * kk + 512 * hf + 512],
                       kk == 0, kk == 15, [yb_, wb], [pb])
                sl = slice(512 * hf, 512 * hf + 512)
                TT(P, "dve", xo.ap[:, sl], pb.ap, g[:, sl], ALU.mult, [pb, k.gB], [xo])
                TT(P, "pool" if hf else "dve", xo.ap[:, sl], xo.ap[:, sl], xin.ap[:, sl], ALU.add, [xo, xin], [xo])
            if tag == "final":
                P.dma(lambda e, xo=xo, o=o: e.dma_start(out=outs["y_out"][128 * o:128 * o + 128, :], in_=xo.ap), reads=[xo],
                      is_output=True)
            else:
                P.dma(lambda e, xo=xo, o=o: e.dma_start(out=X1S.ap[128 * o:128 * o + 128, :], in_=xo.ap), reads=[xo],
                      writes=[X1S.rows(128 * o, 128 * o + 128)])
            if tag != "final" and "X1" in k.debug:
                P.dma(lambda e, xo=xo, o=o: e.dma_start(out=outs["X1"][128 * o:128 * o + 128, :], in_=xo.ap), reads=[xo],
                      is_output=True)
        A.release(mm)

    def x_get0(o, buf):
        i = o if o < NT_OWN else NT_S + (o - NT_OWN)
        P.dma(lambda e, buf=buf, i=i: e.dma_start(out=buf.ap, in_=x_all[128 * i:128 * i + 128, :]), writes=[buf])
        return buf

    xo2 = [A.f32(1024) for _ in range(2)]
    outproj(ins["w_out0"], lambda o: 128 * o, 0, x_get0, lambda o, buf: xo2[o % 2], "l0")
    if "stopX1" in k.debug:
        A.release(m0)
        return
    mn = A.mark()
    emit_norm_phase(k, x_src=lambda o: X1S.ap[128 * o:128 * o + 128, :], tiles=list(range(NO)), layer=1, hT3=hT13, hT=hT1,
                    col0=l1_col0, variant=lambda o: 0 if o < NT_OWN else 1, cols_total=L1_COLS,
                    src_reg=lambda o: X1S.rows(128 * o, 128 * o + 128))
    A.release(mn)
    mm = A.mark()
    cvp = A.f32(64)
    P.dma(lambda e: e.dma_start(out=cvp.ap, in_=ins["cv_prm"][:, :]), writes=[cvp])
    wst = [A.f32(4 * 1024) for _ in range(2)]
    wbf = [A.bf(4 * 1024) for _ in range(2)]
    NB = L1_COLS + 2
    cgu = A.f32(NB)
    bgr = A.f32(NB)
    zs = A.f32(NB)
    cvb = A.f32(NB)
    cgt = A.f32(512)
    yrow = A.bf(NB)
    P.op("pool", lambda e: e.memset(cgu.ap, 0.0), writes=[cgu])
    w_in1 = ins["w_in1"]
    wins = [(c, min(512, L1_COLS - c)) for c in range(0, L1_COLS, 512)]
    for j in range(16):
        ws, wb = wst[j % 2], wbf[j % 2]
        for q in range(4):
            P.dma(lambda e, ws=ws, q=q, j=j: e.dma_start(
                out=ws.ap[:, 1024 * q:1024 * q + 1024].rearrange("p (k c) -> p k c", c=128),
                in_=w_in1[:, 2048 * q + 128 * j:2048 * q + 128 * j + 128].rearrange("(k p) c -> p k c", p=128)),
                writes=[ws.cols(1024 * q, 1024 * q + 1024)])
        CP(P, "pool", wb.ap, ws.ap, [ws], [wb])
        for (c0, nw) in wins:
            for q in range(4):
                pb = ps[4 + q]
                for kk in range(8):
                    MM(P, pb.ap[:, 0:nw], wb.ap[:, 1024 * q + 128 * kk:1024 * q + 128 * kk + 128], hT13[:, kk, c0:c0 + nw],
                       kk == 0, kk == 7, [wb, hT1], [pb])
            ACTF(P, bgr.ap[:, 1 + c0:1 + c0 + nw], ps[4].ap[:, 0:nw], AF.Identity, [ps[4]], [bgr])
            ACTF(P, cgt.ap[:, 0:nw], ps[5].ap[:, 0:nw], AF.Identity, [ps[5]], [cgt])
            TT(P, "dve", cgu.ap[:, 1 + c0:1 + c0 + nw], cgt.ap[:, 0:nw], ps[6].ap[:, 0:nw], ALU.mult, [cgt, ps[6]], [cgu])
            ACTF(P, zs.ap[:, 1 + c0:1 + c0 + nw], ps[7].ap[:, 0:nw], AF.Silu, [ps[7]], [zs])
        n = L1_COLS
        w0, w1, w2, bb = [cvp.ap[:, 16 * t_ + j:16 * t_ + j + 1] for t_ in range(4)]
        P.op("dve", lambda e, w1=w1, bb=bb, n=n: e.tensor_scalar(out=cvb.ap[:, 1:1 + n], in0=cgu.ap[:, 1:1 + n], scalar1=w1,
                                                                scalar2=bb, op0=ALU.mult, op1=ALU.add), reads=[cgu, cvp], writes=[cvb])
        STT(P, cvb.ap[:, 1:1 + n], cgu.ap[:, 0:n], w0, cvb.ap[:, 1:1 + n], ALU.mult, ALU.add, [cgu, cvp, cvb], [cvb])
        STT(P, cvb.ap[:, 1:1 + n], cgu.ap[:, 2:2 + n], w2, cvb.ap[:, 1:1 + n], ALU.mult, ALU.add, [cgu, cvp, cvb], [cvb])
        TT(P, "pool", cvb.ap[:, 1:1 + n], cvb.ap[:, 1:1 + n], bgr.ap[:, 1:1 + n], ALU.mult, [cvb, bgr], [cvb])
        TT(P, "dve", yrow.ap[:, 1:1 + n], cvb.ap[:, 1:1 + n], zs.ap[:, 1:1 + n], ALU.mult, [cvb, zs], [yrow])
        P.dma(lambda e, j=j, n=n: e.dma_start(out=YT.ap[128 * j:128 * j + 128, 0:n], in_=yrow.ap[:, 1:1 + n]), reads=[yrow],
              writes=[YT.name])
    A.release(mm)
    def x_get1(o, buf):
        P.dma(lambda e, buf=buf, o=o: e.dma_start(out=buf.ap, in_=X1S.ap[128 * o:128 * o + 128, :]),
              reads=[X1S.rows(128 * o, 128 * o + 128)], writes=[buf])
        return buf

    outproj(ins["w_out1"], l1_col0, 1, x_get1, lambda o, buf: xo2[o % 2], "final")
    A.release(m0)


def prep_core_inputs(q, inp):
    b, mir = q // 2, q % 2
    f = (lambda a: a[::-1]) if mir else (lambda a: a)
    d = {}
    xs = f(inp["x_sample"][b])
    xp0 = f(inp["x_prompt"][2 * q])
    xp1 = f(inp["x_prompt"][2 * q + 1])
    d["x_all"] = np.ascontiguousarray(np.concatenate([xs, xp0, xp1], 0))
    cvv = np.stack([inp["c"][b], inp["c_ctx"]], -1)
    d["cv"] = np.ascontiguousarray(cvv.reshape(8, 128, 2).transpose(1, 0, 2).reshape(128, 16))
    d["normw_fm"] = np.ascontiguousarray(inp["norm_w"].reshape(2, 8, 128).transpose(2, 0, 1).reshape(128, 16))
    d["ada_w"] = inp["ada_w"]
    d["adab_fm"] = np.ascontiguousarray(inp["ada_b"].reshape(2, 24, 128).transpose(2, 0, 1).reshape(128, 48))
    d["adab_g"] = np.ascontiguousarray(np.broadcast_to(inp["ada_b"][:, 2048:].reshape(1, 2048), (128, 2048)))
    w = inp["e_w_in"][0]
    mu = inp["e_mu"][0]
    if mir:
        perm = np.arange(EVEN_IN)
        perm[3072:3136], perm[3136:3200] = np.arange(3136, 3200), np.arange(3072, 3136)
        perm[3200:3264], perm[3264:3328] = np.arange(3264, 3328), np.arange(3200, 3264)
        w = w[:, perm]
        mu = mu[perm[:3328]]
    d["w_in0"] = np.ascontiguousarray(w)
    d["mu_b"] = np.ascontiguousarray(np.broadcast_to(mu.reshape(1, -1), (128, 3328)))
    qk = np.concatenate([np.tile(inp["e_q_norm"][0], 8), np.tile(inp["e_k_norm"][0], 8)])
    d["qkw_b"] = np.ascontiguousarray(np.broadcast_to(qk.reshape(1, -1), (128, 1024)))
    t = np.arange(4096)
    row = (t // 64).astype(np.float32)
    col = (t % 64).astype(np.float32)
    inv = (10000.0 ** (-np.arange(0, 32, 2, dtype=np.float32) / 32)).astype(np.float32)
    ar, ac = row[:, None] * inv, col[:, None] * inv
    cos = np.concatenate([np.cos(ar), np.cos(ar), np.cos(ac), np.cos(ac)], 1)
    sin = np.concatenate([-np.sin(ar), np.sin(ar), -np.sin(ac), np.sin(ac)], 1)
    d["rope_cs"] = np.ascontiguousarray(f(np.concatenate([cos, sin], 1).astype(np.float32)))
    dirs = (1, 0) if mir else (0, 1)
    rep = lambda v: np.broadcast_to(np.asarray(v, np.float32).reshape(1, -1), (128, v.size))
    d["rw_prm"] = np.ascontiguousarray(np.concatenate(
        [rep(inp["e_k_k"][0]), rep(inp["e_k_a"][0]), rep(inp["e_r_k"][0].reshape(-1)),
         rep(inp["e_lnx_w"][0]), rep(inp["e_lnx_b"][0])], 1))
    up = np.zeros((65, 4096), np.float32)
    for fd, td in enumerate(dirs):
        up[:64, 1024 * fd:1024 * fd + 1024] = inp["e_w_up"][0, td]
        up[64, 1024 * fd:1024 * fd + 1024] = inp["e_w0"][0, td]
        up[:64, 2048 + 1024 * fd:2048 + 1024 * fd + 1024] = inp["e_a_up"][0, td]
        up[64, 2048 + 1024 * fd:2048 + 1024 * fd + 1024] = inp["e_a0"][0, td]
    d["rw_up"] = up
    d["rw_const"] = rwkv_consts()
    sts = (inp["state_rwkv_fwd"][b, 0], inp["state_rwkv_bwd"][b, 0])
    st = np.zeros((128, 1024), np.float32)
    for fd, td in enumerate(dirs):
        hh = sts[td].reshape(8, 2, 64, 64).transpose(1, 3, 0, 2).reshape(128, 512)
        st[:, 512 * fd:512 * fd + 512] = hh
    d["st_in"] = st
    d["ctx_k"] = np.ascontiguousarray(inp["cache_diff_k"][b, 0].reshape(256, 1024))
    d["ctx_v"] = np.ascontiguousarray(inp["cache_diff_v"][b, 0].reshape(256, 1024))
    d["at_prm"] = np.ascontiguousarray(np.concatenate(
        [rep(inp["e_lambda"][0].reshape(-1)), inp["e_subln"][0].reshape(128, 1)], 1).astype(np.float32))
    d["w_out0"] = inp["e_w_out"][0]
    d["w_in1"] = inp["o_w_in"][0]
    d["w_out1"] = inp["o_w_out"][0]
    cw = inp["o_conv_w"][0]
    if mir:
        cw = cw[::-1]
    cvp = np.concatenate([cw.reshape(3, 16, 128), inp["o_conv_b"][0].reshape(1, 16, 128)], 0)
    d["cv_prm"] = np.ascontiguousarray(cvp.transpose(2, 0, 1).reshape(128, 64))
    return d


def rwkv_consts():
    idx = np.arange(128)
    ch, pos = idx // 64, idx % 64
    same = ch[:, None] == ch[None, :]
    out = np.zeros((128, 2050), np.float32)
    for dd in range(2):
        before = (pos[:, None] < pos[None, :]) if dd == 0 else (pos[:, None] > pos[None, :])
        eq = pos[:, None] == pos[None, :]
        incl = same & (before | eq)
        excl = same & before
        suf = same & before.T
        for m, mat in enumerate((incl, excl, suf)):
            out[:, (dd * 3 + m) * 128:(dd * 3 + m) * 128 + 128] = DECAY_C * mat
        msk = np.concatenate([excl, incl], 1).astype(np.float32)
        out[:, 770 + 512 * dd:770 + 512 * dd + 256] = msk
        out[:, 770 + 512 * dd + 256:770 + 512 * dd + 512] = msk
        out[:, 1794 + 128 * dd:1794 + 128 * dd + 128] = excl.T
    out[:, 768] = DECAY_C * (ch == 0)
    out[:, 769] = DECAY_C * (ch == 1)
    return out


def assemble(q, r, outs6):
    y_p, y_s, n_sf, n_sb, n_k, n_v = outs6
    b, mir = q // 2, q % 2
    S = y_p.shape[1]
    yo = np.asarray(r["y_out"])
    if mir:
        y_s[b, 2048:] = yo[:2048][::-1]
    else:
        y_s[b, :2048] = yo[:2048]
    so = np.asarray(r["st_out"])
    for j in range(2):
        yp = yo[2176 + 256 * j:2176 + 256 * j + 256]
        kc = np.asarray(r["kc_out"])[256 * j:256 * j + 256]
        vc = np.asarray(r["vc_out"])[256 * j:256 * j + 256]
        if mir:
            yp, kc, vc = yp[::-1], kc[::-1], vc[::-1]
        y_p[2 * q + j] = yp
        n_k[2 * q + j, 0] = kc.reshape(S, 8, 2, 64)
        n_v[2 * q + j, 0] = vc.reshape(S, 8, 128)
        for fd in range(2):
            st = so[j, fd].reshape(2, 64, 8, 64).transpose(2, 0, 3, 1).reshape(16, 64, 64)
            td = (1 - fd) if mir else fd
            (n_sf if td == 0 else n_sb)[2 * q + j, 0] = st


def kernel(**inputs):
    inp = {k_: np.asarray(v) for k_, v in inputs.items()}
    nc, ins, outs = build_program()
    in_maps = []
    for q in range(8):
        d = prep_core_inputs(q, inp)
        in_maps.append({n: d[n] for n in ins})
    res = run_bass_kernel_spmd(nc, in_maps, core_ids=list(range(8)))
    B, S = inp["x_prompt"].shape[0], inp["x_prompt"].shape[1]
    outs6 = (np.zeros(inp["x_prompt"].shape, np.float32), np.zeros(inp["x_sample"].shape, np.float32),
             np.zeros((B, 1, 16, 64, 64), np.float32), np.zeros((B, 1, 16, 64, 64), np.float32),
             np.zeros((B, 1, S, 8, 2, 64), np.float32), np.zeros((B, 1, S, 8, 128), np.float32))
    for q in range(8):
        assemble(q, res.results[q], outs6)
    return outs6
```

```python
import contextlib
import numpy as np
import concourse.bass as bass
import concourse.mybir as mybir
from concourse.bass_utils import run_bass_kernel_spmd

F32 = mybir.dt.float32
BF16 = mybir.dt.bfloat16
AF = mybir.ActivationFunctionType
ALU = mybir.AluOpType
AX = mybir.AxisListType

ENGS = ("pe", "act", "dve", "pool", "sp")
N_DMA_SEMS = 32
LOADS_ON_ACT = True
BIG = 1 << 40

D = 1024
NT_OWN, NT_OTH, NT_PR = 17, 15, 4
NT_S = NT_OWN + NT_OTH
NT = NT_S + NT_PR
T_ALL = NT * 128
T_OP = (NT_OWN + NT_PR) * 128
HT_COLS = 4612
EVEN_IN = 8448
NORM_EPS = 1e-6
GN_EPS = 64e-5
LAM_INIT = 0.8 - 0.6
DECAY_C = -0.6065306597126334


def tile_col0(i):
    if i < NT_S:
        return 1 + 128 * i
    if i < NT_S + 2:
        return 4098 + 128 * (i - NT_S)
    return 4355 + 128 * (i - NT_S - 2)


def op_index(i):
    return i if i < NT_OWN else NT_OWN + (i - NT_S)


OUT_TILES = list(range(NT_OWN)) + list(range(NT_S, NT))


class Prog:
    def __init__(self, nc):
        self.nc = nc
        self.streams = {e: [] for e in ENGS}
        self.count = {e: 0 for e in ENGS}
        self.seen = {e: {} for e in ENGS}
        self.regions = {}
        self.dma_cnt = [0] * N_DMA_SEMS
        self.dma_rr = 0
        self.out_tokens = []

    def _deps(self, reads, writes):
        deps = []
        for (name, lo, hi) in reads:
            for rec in self.regions.get(name, ()):
                if rec[0] < hi and lo < rec[1]:
                    if rec[2] is not None:
                        deps.append(rec[2])
                    if name.startswith("ps"):
                        deps.extend(rec[3].values())
        for (name, lo, hi) in writes:
            for rec in self.regions.get(name, ()):
                if rec[0] < hi and lo < rec[1]:
                    if rec[2] is not None:
                        deps.append(rec[2])
                    deps.extend(rec[3].values())
        return deps

    def _commit(self, reads, writes, token, rkey):
        for (name, lo, hi) in reads:
            lst = self.regions.setdefault(name, [])
            hit = False
            for rec in lst:
                if rec[0] < hi and lo < rec[1]:
                    rec[3][rkey] = token
                    hit = True
            if not hit:
                lst.append([lo, hi, None, {rkey: token}])
        for (name, lo, hi) in writes:
            lst = self.regions.setdefault(name, [])
            keep = []
            for rec in lst:
                if rec[1] <= lo or hi <= rec[0]:
                    keep.append(rec)
                    continue
                if rec[0] < lo:
                    keep.append([rec[0], lo, rec[2], dict(rec[3])])
                if hi < rec[1]:
                    keep.append([hi, rec[1], rec[2], dict(rec[3])])
            keep.append([lo, hi, token, {}])
            self.regions[name] = keep

    def _waits(self, eng, deps):
        best = {}
        for (k, v) in deps:
            if v > best.get(k, 0):
                best[k] = v
        out = []
        for k, v in best.items():
            if eng == "pe" and k == "pe":
                continue
            if self.seen[eng].get(k, 0) >= v:
                continue
            self.seen[eng][k] = v
            out.append((k, v))
        return out

    @staticmethod
    def _norm(regs):
        out = []
        for r in regs:
            if isinstance(r, str):
                out.append((r, 0, BIG))
            elif isinstance(r, Buf):
                out.append(r.reg)
            else:
                out.append(r)
        return out

    def op(self, eng, fn, reads=(), writes=()):
        reads = self._norm(reads)
        writes = self._norm(writes)
        waits = self._waits(eng, self._deps(reads, writes))
        self.count[eng] += 1
        token = (eng, self.count[eng])
        self.streams[eng].append(("op", waits, fn))
        self._commit(reads, writes, token, eng)
        return token

    def dma(self, fn, reads=(), writes=(), is_output=False, queue="sp"):
        reads = self._norm(reads)
        writes = self._norm(writes)
        queue = "act" if (LOADS_ON_ACT and any(w[0] == "ar" for w in writes)) else "sp"
        deps = self._deps(reads, writes)
        s = self.dma_rr
        self.dma_rr = (self.dma_rr + 1) % N_DMA_SEMS
        key = "dma%d" % s
        if self.dma_cnt[s] > 0:
            deps.append((key, self.dma_cnt[s]))
        waits = self._waits(queue, deps)
        self.dma_cnt[s] += 16
        token = (key, self.dma_cnt[s])
        self.streams[queue].append(("dma", waits, fn, s))
        self._commit(reads, writes, token, key)
        if is_output:
            self.out_tokens.append(token)
        return token

    def run(self):
        nc = self.nc
        fin = self._waits("sp", list(self.out_tokens))
        self.streams["sp"].append(("wait", fin))
        with contextlib.ExitStack() as st:
            sems = {}
            for e in ENGS:
                sems[e] = st.enter_context(nc.semaphore("s_" + e))
            for i in range(N_DMA_SEMS):
                sems["dma%d" % i] = st.enter_context(nc.semaphore("s_dma%d" % i))
            block = st.enter_context(nc.Block())

            def player(ename):
                def play(eng):
                    for item in self.streams[ename]:
                        for (k, v) in item[1]:
                            eng.wait_ge(sems[k], v)
                        if item[0] == "op":
                            item[2](eng).then_inc(sems[ename], 1)
                        elif item[0] == "dma":
                            item[2](eng).then_inc(sems["dma%d" % item[3]], 16)
                return play

            block.tensor(player("pe"))
            block.scalar(player("act"))
            block.vector(player("dve"))
            block.gpsimd(player("pool"))
            block.sync(player("sp"))


class Buf:
    def __init__(self, ap, reg):
        self.ap = ap
        self.reg = reg

    def cols(self, a, b):
        name, lo, hi = self.reg
        if name == "ar":
            if self.ap.dtype == BF16:
                r = (name, lo + a // 2, lo + (b + 1) // 2)
            else:
                r = (name, lo + a, lo + b)
        else:
            r = self.reg
        return Buf(self.ap[:, a:b], r)

    def v(self, s, **kw):
        return self.ap.rearrange(s, **kw)


class Arena:
    def __init__(self, ap, size):
        self.ap = ap
        self.size = size
        self.top = 0

    def f32(self, n):
        lo = self.top
        self.top += n
        assert self.top <= self.size, ("arena overflow", self.top, self.size)
        return Buf(self.ap[:, lo:lo + n], ("ar", lo, lo + n))

    def bf(self, n):
        nf = (n + 1) // 2
        lo = self.top
        self.top += nf
        assert self.top <= self.size, ("arena overflow", self.top, self.size)
        return Buf(self.ap[:, lo:lo + nf].bitcast(BF16)[:, 0:n], ("ar", lo, lo + nf))

    def mark(self):
        return self.top

    def release(self, m):
        self.top = m


class DramBuf:
    def __init__(self, name, ap):
        self.name = name
        self.ap = ap

    def rows(self, a, b):
        return (self.name, a, b)


ARENA_F32 = 52800


class K:
    pass


def build_program(debug=()):
    nc = bass.Bass("TRN2", target_bir_lowering=False)
    k = K()
    k.nc = nc
    k.debug = debug
    ins = {}
    outs = {}

    def din(name, shape, dt=F32):
        ins[name] = nc.dram_tensor(name, list(shape), dt, kind="ExternalInput").ap()
        return ins[name]

    def dout(name, shape, dt=F32):
        outs[name] = nc.dram_tensor(name, list(shape), dt, kind="ExternalOutput").ap()
        return outs[name]

    def dscr(name, shape, dt=F32):
        kind = "ExternalOutput" if name in debug else "Internal"
        t = nc.dram_tensor(name, list(shape), dt, kind=kind).ap()
        if name in debug:
            outs[name] = t
        return DramBuf(name, t)

    x_all = din("x_all", [T_ALL, D])
    cv = din("cv", [128, 16])
    normw_fm = din("normw_fm", [128, 16])
    ada_w = din("ada_w", [2, D, 3072])
    adab_fm = din("adab_fm", [128, 48])
    adab_g = din("adab_g", [128, 2048])
    w_in0 = din("w_in0", [D, EVEN_IN])
    mu_b = din("mu_b", [128, 3328])
    qkw_b = din("qkw_b", [128, 1024])
    rope_cs = din("rope_cs", [4096, 128])
    rw_prm = din("rw_prm", [128, 5120])
    rw_up = din("rw_up", [65, 4096])
    rw_const = din("rw_const", [128, 2050])
    st_in = din("st_in", [128, 1024])
    ctx_k = din("ctx_k", [256, 1024])
    ctx_v = din("ctx_v", [256, 1024])
    at_prm = din("at_prm", [128, 257])
    w_out0 = din("w_out0", [2048, 1024])
    w_in1 = din("w_in1", [D, 8192])
    w_out1 = din("w_out1", [2048, 1024])
    cv_prm = din("cv_prm", [128, 64])
    OB = dscr("OB", [T_OP, 1024])
    k.X1S = dscr("X1S", [T_OP, 1024])
    YT = dscr("YT", [2048, 2816], BF16)
    PA = dscr("PA", [T_ALL, 3328])
    GA = dscr("GA", [T_OP, 1024])
    QT = dscr("QT", [8, 128, T_OP], BF16)
    KT = dscr("KT", [8, 128, T_ALL + 256], BF16)
    VS = dscr("VS", [T_ALL + 256, 1024], BF16)
    GBT = dscr("GBT", [1024, T_OP])
    kc_out = dout("kc_out", [512, 1024])
    vc_out = dout("vc_out", [512, 1024])
    st_out = dout("st_out", [2, 2, 128, 512])
    y_out = dout("y_out", [T_OP, 1024])
    if "YB" in debug:
        dout("YB", [1024, T_OP])
    if "X1" in debug:
        dout("X1", [T_OP, 1024])
    if "YA" in debug:
        dout("YA", [T_OP, 1024])
    if "dumpQ" in debug:
        dout("dQ", [128, 12288])

    st = contextlib.ExitStack()
    with st:
        ar_t = st.enter_context(nc.sbuf_tensor("ar", [128, ARENA_F32], F32))
        psb = [st.enter_context(nc.psum_tensor("ps%d" % i, [128, 512], F32)) for i in range(8)]
        k.ps = [Buf(psb[i][:, :], ("ps%d" % i, 0, BIG)) for i in range(8)]
        A = Arena(ar_t, ARENA_F32)
        P = Prog(nc)
        k.P, k.A = P, A

        ident = A.f32(128)
        identb = A.bf(128)
        P.op("pool", lambda e: e.memset(ident.ap, 1.0), writes=[ident])
        P.op("pool", lambda e: e.affine_select(out=ident.ap, in_=ident.ap, pattern=[[-1, 128]],
                                               compare_op=ALU.is_equal, fill=0.0, base=0, channel_multiplier=1),
             reads=[ident], writes=[ident])
        P.op("dve", lambda e: e.tensor_copy(identb.ap, ident.ap), reads=[ident], writes=[identb])
        k.ident, k.identb = ident, identb

        cvt = A.f32(16)
        nwt = A.f32(16)
        abf = A.f32(48)
        modS = A.f32(32)
        modB = A.f32(32)
        gB = A.f32(4096)
        k.modS, k.modB, k.gB = modS, modB, gB
        m0 = A.mark()
        abg = A.f32(2048)
        screp = A.f32(16 * 128)
        P.dma(lambda e: e.dma_start(out=cvt.ap, in_=cv[:, :]), writes=[cvt])
        P.dma(lambda e: e.dma_start(out=nwt.ap, in_=normw_fm[:, :]), writes=[nwt])
        P.dma(lambda e: e.dma_start(out=abf.ap, in_=adab_fm[:, :]), writes=[abf])
        P.dma(lambda e: e.dma_start(out=abg.ap, in_=adab_g[:, :]), writes=[abg])
        P.op("act", lambda e: e.activation(out=cvt.ap, in_=cvt.ap, func=AF.Silu), reads=[cvt], writes=[cvt])
        P.op("dve", lambda e: e.tensor_copy(screp.v("p (a b) -> p a b", b=128),
                                            cvt.ap.unsqueeze(2).broadcast_to([128, 16, 128])),
             reads=[cvt], writes=[screp])
        wst = [A.f32(8 * 512) for _ in range(2)]
        for l in range(2 if "noada" not in debug else 0):
            for g in range(6):
                wb = wst[(l * 6 + g) % 2]
                P.dma(lambda e, wb=wb, l=l, g=g: e.dma_start(
                    out=wb.v("p (k c) -> p k c", c=512),
                    in_=ada_w[l, :, 512 * g:512 * g + 512].rearrange("(k p) c -> p k c", p=128)), writes=[wb])
                if g < 4:
                    pb = k.ps[g % 2]
                    for j in range(4):
                        for kk in range(8):
                            P.op("pe", lambda e, pb=pb, wb=wb, j=j, kk=kk: e.matmul(
                                pb.ap[:, 2 * j:2 * j + 2], wb.ap[:, kk * 512 + 128 * j: kk * 512 + 128 * j + 128],
                                cvt.ap[:, 2 * kk:2 * kk + 2], start=(kk == 0), stop=(kk == 7)),
                                reads=[wb, cvt], writes=[pb])
                    for v in range(2):
                        if g < 2:
                            dst = modB.ap[:, l * 16 + v * 8 + 4 * g: l * 16 + v * 8 + 4 * g + 4]
                            P.op("dve", lambda e, pb=pb, dst=dst, v=v, l=l, g=g: e.tensor_tensor(
                                out=dst, in0=pb.ap[:, 0:8].rearrange("p (j v) -> p j v", v=2)[:, :, v],
                                in1=abf.ap[:, l * 24 + 4 * g: l * 24 + 4 * g + 4], op=ALU.add),
                                reads=[pb, abf], writes=[modB])
                        else:
                            gg = g - 2
                            dst = modS.ap[:, l * 16 + v * 8 + 4 * gg: l * 16 + v * 8 + 4 * gg + 4]
                            P.op("dve", lambda e, pb=pb, dst=dst, v=v, l=l, g=g: e.tensor_tensor(
                                out=dst, in0=pb.ap[:, 0:8].rearrange("p (j v) -> p j v", v=2)[:, :, v],
                                in1=abf.ap[:, l * 24 + 4 * g: l * 24 + 4 * g + 4], op=ALU.add),
                                reads=[pb, abf], writes=[modS])
                            P.op("dve", lambda e, dst=dst, l=l, gg=gg: e.scalar_tensor_tensor(
                                out=dst, in0=dst, scalar=1.0, in1=nwt.ap[:, l * 8 + 4 * gg: l * 8 + 4 * gg + 4],
                                op0=ALU.add, op1=ALU.mult), reads=[modS, nwt], writes=[modS])
                else:
                    gg = g - 4
                    for v in range(2):
                        pb = k.ps[2 + v]
                        for kk in range(8):
                            P.op("pe", lambda e, pb=pb, wb=wb, v=v, kk=kk: e.matmul(
                                pb.ap, screp.ap[:, (2 * kk + v) * 128:(2 * kk + v) * 128 + 128],
                                wb.ap[:, kk * 512: kk * 512 + 512], start=(kk == 0), stop=(kk == 7)),
                                reads=[wb, screp], writes=[pb])
                        dst = gB.cols((l * 2 + v) * 1024 + 512 * gg, (l * 2 + v) * 1024 + 512 * gg + 512)
                        P.op("dve", lambda e, pb=pb, dst=dst, l=l, gg=gg: e.tensor_tensor(
                            out=dst.ap, in0=pb.ap, in1=abg.ap[:, l * 1024 + 512 * gg: l * 1024 + 512 * gg + 512],
                            op=ALU.add), reads=[pb, abg], writes=[dst])
        A.release(m0)

        mH = A.mark()
        hT = A.bf(8 * HT_COLS)
        k.hT = hT
        hT3 = hT.v("p (k c) -> p k c", c=HT_COLS)
        for zc in (0, 4097, 4354, 4611):
            P.op("pool", lambda e, zc=zc: e.memset(hT3[:, :, zc:zc + 1], 0.0), writes=[hT])
        mA = A.mark()
        emit_norm_phase(k, x_src=lambda i: x_all[128 * i:128 * i + 128, :], tiles=(list(range(NT)) if "noA1" not in debug else []), layer=0,
                        hT3=hT3, hT=hT, col0=tile_col0, variant=lambda i: 0 if i < NT_S else 1)
        A.release(mA)

        if "noA2" not in debug:
            emit_inproj0(k, ins, outs, PA, GA, QT, KT, VS, GBT, hT3)

        if "hT" in debug:
            hdbg = dout("hT_dbg", [128, 8 * HT_COLS], BF16)
            P.dma(lambda e: e.dma_start(out=hdbg[:, :], in_=hT.ap), reads=[hT], is_output=True)
        A.release(mH)
        if "noR" not in debug:
            emit_rwkv(k, ins, outs, PA, GA, OB, YT)
        if "noT" not in debug:
            emit_attn(k, ins, outs, QT, KT, VS, GBT, YT)
        if "noO" not in debug:
            emit_tail(k, ins, outs, YT, x_all)
        P.run()
    return nc, ins, outs


def emit_norm_phase(k, x_src, tiles, layer, hT3, hT, col0, variant, x_keep=None, cols_total=HT_COLS, src_reg=None):
    P, A = k.P, k.A
    xt = [A.f32(1024) for _ in range(2)]
    xn = [A.f32(1024) for _ in range(2)]
    junk = A.bf(1024)
    ss = [A.f32(1) for _ in range(2)]
    for n, i in enumerate(tiles):
        xb, xnb, ssb = xt[n % 2], xn[n % 2], ss[n % 2]
        src = x_src(i)
        if src is not None:
            P.dma(lambda e, xb=xb, src=src: e.dma_start(out=xb.ap, in_=src), writes=[xb],
                  reads=([src_reg(i)] if src_reg is not None else []))
        else:
            xb = x_keep(i)
        P.op("act", lambda e, xb=xb, ssb=ssb: e.activation(out=junk.ap, in_=xb.ap, func=AF.Square, accum_out=ssb.ap),
             reads=[xb], writes=[junk, ssb])
        P.op("act", lambda e, ssb=ssb: e.activation(out=ssb.ap, in_=ssb.ap, func=AF.Ln, scale=1.0 / D, bias=NORM_EPS),
             reads=[ssb], writes=[ssb])
        P.op("act", lambda e, ssb=ssb: e.activation(out=ssb.ap, in_=ssb.ap, func=AF.Exp, scale=-0.5),
             reads=[ssb], writes=[ssb])
        P.op("dve", lambda e, xb=xb, xnb=xnb, ssb=ssb: e.tensor_scalar(
            out=xnb.ap, in0=xb.ap, scalar1=ssb.ap[:, 0:1], scalar2=None, op0=ALU.mult), reads=[xb, ssb], writes=[xnb])
        v = variant(i)
        c0 = col0(i)
        for half in range(2):
            pb = k.ps[(2 * n + half) % 4]
            for q in range(4):
                kk = 4 * half + q
                P.op("pe", lambda e, pb=pb, xnb=xnb, kk=kk, q=q: e.transpose(
                    pb.ap[:, 128 * q:128 * q + 128], xnb.ap[:, 128 * kk:128 * kk + 128], k.ident.ap),
                    reads=[xnb, k.ident], writes=[pb])
            for q in range(4):
                kk = 4 * half + q
                mi = layer * 16 + v * 8 + kk
                hreg = ("ar", hT.reg[1] + (kk * cols_total + c0) // 2, hT.reg[1] + (kk * cols_total + c0 + 129) // 2)
                P.op("act", lambda e, pb=pb, kk=kk, q=q, mi=mi, c0=c0: e.activation(
                    out=hT3[:, kk, c0:c0 + 128], in_=pb.ap[:, 128 * q:128 * q + 128], func=AF.Identity,
                    scale=k.modS.ap[:, mi:mi + 1], bias=k.modB.ap[:, mi:mi + 1]),
                    reads=[pb, k.modS, k.modB], writes=[hreg])


def emit_inproj0(k, ins, outs, PA, GA, QT, KT, VS, GBT, hT3):
    P, A, nc = k.P, k.A, k.nc
    hT = k.hT
    w_in0 = ins["w_in0"]
    m0 = A.mark()
    mub = A.f32(3328)
    qkw = A.f32(1024)
    P.dma(lambda e: e.dma_start(out=mub.ap, in_=ins["mu_b"][:, :]), writes=[mub])
    P.dma(lambda e: e.dma_start(out=qkw.ap, in_=ins["qkw_b"][:, :]), writes=[qkw])
    wst = [A.f32(8 * 512) for _ in range(2)]
    wbf = [A.bf(8 * 512) for _ in range(2)]
    hd = [A.bf(8 * 128) for _ in range(2)]
    ev = [A.f32(512) for _ in range(6)]
    evs = [A.f32(512) for _ in range(4)]
    rs8 = [A.f32(8) for _ in range(4)]
    ropet = [A.f32(128) for _ in range(4)]
    tb = [A.bf(512) for _ in range(4)]
    evb = [A.bf(512) for _ in range(4)]

    def hreg(i):
        c0 = tile_col0(i)
        return ("ar", hT.reg[1], hT.reg[2])

    groups = []
    for c in range(0, 3072, 512):
        groups.append((c, 512, "A"))
    groups.append((3072, 256, "A"))
    for c in range(3328, 4352, 512):
        groups.append((c, 512, "ga"))
    for c in range(4352, 5376, 512):
        groups.append((c, 512, "pq"))
    for c in range(5376, 6400, 512):
        groups.append((c, 512, "pk"))
    for c in range(6400, 7424, 512):
        groups.append((c, 512, "pv"))
    cnt = 0
    only = [d[5:] for d in k.debug if d.startswith("only_")]
    if only:
        groups = [g for g in groups if g[2] in only]
    for gi, (cs, wdt, kind) in enumerate(groups):
        ws, wb = wst[gi % 2], wbf[gi % 2]
        P.dma(lambda e, ws=ws, cs=cs, wdt=wdt: e.dma_start(
            out=ws.v("p (k c) -> p k c", c=512)[:, :, 0:wdt],
            in_=w_in0[:, cs:cs + wdt].rearrange("(k p) c -> p k c", p=128)), writes=[ws])
        P.op("pool", lambda e, ws=ws, wb=wb: e.tensor_copy(wb.ap, ws.ap), reads=[ws], writes=[wb])
        if kind == "A":
            tiles = list(range(NT)) if cs >= 1024 else OUT_TILES
        elif kind in ("ga", "pq"):
            tiles = OUT_TILES
        else:
            tiles = list(range(NT))
        def emit_hd(i_, slot):
            hdb_ = hd[slot % 2]
            hd3_ = hdb_.v("p (k c) -> p k c", c=128)
            c0_ = tile_col0(i_)
            P.op("pool", lambda e, hd3_=hd3_, c0_=c0_: e.tensor_tensor(
                out=hd3_, in0=hT3[:, :, c0_ - 1:c0_ + 127], in1=hT3[:, :, c0_ + 1:c0_ + 129], op=ALU.add),
                reads=[hT], writes=[hdb_])
            P.op("dve", lambda e, hd3_=hd3_, c0_=c0_: e.scalar_tensor_tensor(
                out=hd3_, in0=hd3_, scalar=0.5, in1=hT3[:, :, c0_:c0_ + 128], op0=ALU.mult, op1=ALU.subtract),
                reads=[hT, hdb_], writes=[hdb_])

        if kind == "A":
            emit_hd(tiles[0], 0)
        for ti, i in enumerate(tiles):
            c0 = tile_col0(i)
            sample = i < NT_S
            cnt += 1
            if kind == "A":
                p1 = k.ps[4 + (cnt % 2) * 2]
                p2 = k.ps[5 + (cnt % 2) * 2]
            else:
                p1 = k.ps[4 + cnt % 4]
                p2 = None
            if kind == "A" and ti + 1 < len(tiles):
                emit_hd(tiles[ti + 1], ti + 1)
            for kk in range(8):
                P.op("pe", lambda e, p1=p1, wb=wb, kk=kk, c0=c0, wdt=wdt: e.matmul(
                    p1.ap[:, 0:wdt], hT3[:, kk, c0:c0 + 128], wb.ap[:, kk * 512:kk * 512 + wdt],
                    start=(kk == 0), stop=(kk == 7)), reads=[wb, hT], writes=[p1])
            if kind == "A":
                hdb = hd[ti % 2]
                hd3 = hdb.v("p (k c) -> p k c", c=128)
                for kk in range(8):
                    P.op("pe", lambda e, p2=p2, wb=wb, kk=kk, hd3=hd3, wdt=wdt: e.matmul(
                        p2.ap[:, 0:wdt], hd3[:, kk, :], wb.ap[:, kk * 512:kk * 512 + wdt],
                        start=(kk == 0), stop=(kk == 7)), reads=[wb, hdb], writes=[p2])
                e1, e2 = ev[cnt % 2], evs[cnt % 4]
                P.op("act", lambda e, p2=p2, e1=e1, wdt=wdt: e.activation(out=e1.ap[:, 0:wdt], in_=p2.ap[:, 0:wdt], func=AF.Identity),
                     reads=[p2], writes=[e1])
                P.op("pool", lambda e, e1=e1, cs=cs, wdt=wdt: e.tensor_tensor(
                    out=e1.ap[:, 0:wdt], in0=e1.ap[:, 0:wdt], in1=mub.ap[:, cs:cs + wdt], op=ALU.mult),
                    reads=[e1, mub], writes=[e1])
                P.op("dve", lambda e, e1=e1, e2=e2, p1=p1, wdt=wdt: e.tensor_tensor(
                    out=e2.ap[:, 0:wdt], in0=e1.ap[:, 0:wdt], in1=p1.ap[:, 0:wdt], op=ALU.add),
                    reads=[e1, p1], writes=[e2])
                P.dma(lambda e, e2=e2, i=i, cs=cs, wdt=wdt: e.dma_start(
                    out=PA.ap[128 * i:128 * i + 128, cs:cs + wdt], in_=e2.ap[:, 0:wdt]),
                    reads=[e2], writes=[PA.rows(128 * i, 128 * i + 128)], queue="pool")
            elif kind == "ga":
                e1 = evs[cnt % 4]
                o = op_index(i)
                P.op("act", lambda e, p1=p1, e1=e1: e.activation(out=e1.ap, in_=p1.ap, func=AF.Silu), reads=[p1], writes=[e1])
                P.dma(lambda e, e1=e1, o=o, cs=cs: e.dma_start(
                    out=GA.ap[128 * o:128 * o + 128, cs - 3328:cs - 3328 + 512], in_=e1.ap),
                    reads=[e1], writes=[GA.rows(128 * o, 128 * o + 128)], queue="pool")
            elif kind == "pv":
                e1, eb = evs[cnt % 4], evb[cnt % 4]
                P.op("act", lambda e, p1=p1, e1=e1: e.activation(out=e1.ap, in_=p1.ap, func=AF.Identity), reads=[p1], writes=[e1])
                P.op("dve", lambda e, e1=e1, eb=eb: e.tensor_copy(eb.ap, e1.ap), reads=[e1], writes=[eb])
                P.dma(lambda e, eb=eb, i=i, cs=cs: e.dma_start(
                    out=VS.ap[128 * i:128 * i + 128, cs - 6400:cs - 6400 + 512], in_=eb.ap),
                    reads=[eb], writes=[VS.rows(128 * i, 128 * i + 128)], queue="pool")
                if not sample:
                    pi = i - NT_S
                    P.dma(lambda e, e1=e1, pi=pi, cs=cs: e.dma_start(
                        out=outs["vc_out"][128 * pi:128 * pi + 128, cs - 6400:cs - 6400 + 512], in_=e1.ap),
                        reads=[e1], is_output=True, queue="pool")
            else:
                isq = kind == "pq"
                base = 4352 if isq else 5376
                h0 = (cs - base) // 128
                wof = 0 if isq else 512
                e1, e2, e3 = ev[(cnt % 2) * 3], ev[(cnt % 2) * 3 + 1], ev[(cnt % 2) * 3 + 2]
                if cnt % 4 >= 2:
                    e1, e3 = evs[0], evs[1]
                    e2 = evs[2 + cnt % 2]
                r8 = rs8[cnt % 4]
                P.op("act", lambda e, p1=p1, e1=e1: e.activation(out=e1.ap, in_=p1.ap, func=AF.Square), reads=[p1], writes=[e1])
                P.op("dve", lambda e, e1=e1, r8=r8: e.tensor_reduce(
                    out=r8.ap, in_=e1.v("p (g d) -> p g d", d=64), axis=AX.X, op=ALU.add), reads=[e1], writes=[r8])
                P.op("act", lambda e, r8=r8: e.activation(out=r8.ap, in_=r8.ap, func=AF.Ln, scale=1.0 / 64, bias=NORM_EPS),
                     reads=[r8], writes=[r8])
                P.op("act", lambda e, r8=r8: e.activation(out=r8.ap, in_=r8.ap, func=AF.Exp, scale=-0.5), reads=[r8], writes=[r8])
                P.op("dve", lambda e, p1=p1, e2=e2, r8=r8: e.tensor_tensor(
                    out=e2.v("p (g d) -> p g d", d=64), in0=p1.v("p (g d) -> p g d", d=64),
                    in1=r8.ap.unsqueeze(2).broadcast_to([128, 8, 64]), op=ALU.mult), reads=[p1, r8], writes=[e2])
                P.op("dve", lambda e, e2=e2, wof=wof: e.tensor_tensor(
                    out=e2.ap, in0=e2.ap, in1=qkw.ap[:, wof:wof + 512], op=ALU.mult), reads=[e2, qkw], writes=[e2])
                tbb = tb[cnt % 4]
                if (not sample) and (not isq):
                    pi = i - NT_S
                    P.dma(lambda e, e2=e2, pi=pi, cs=cs: e.dma_start(
                        out=outs["kc_out"][128 * pi:128 * pi + 128, cs - 5376:cs - 5376 + 512], in_=e2.ap),
                        reads=[e2], is_output=True, queue="pool")
                if sample:
                    rt = ropet[cnt % 4]
                    P.dma(lambda e, rt=rt, i=i: e.dma_start(out=rt.ap, in_=ins["rope_cs"][128 * i:128 * i + 128, :]), writes=[rt])
                    cosb = rt.ap[:, 0:64].unsqueeze(1).broadcast_to([128, 8, 64])
                    P.op("dve", lambda e, e1=e1, e2=e2, cosb=cosb: e.tensor_tensor(
                        out=e1.v("p (g d) -> p g d", d=64), in0=e2.v("p (g d) -> p g d", d=64), in1=cosb, op=ALU.mult),
                        reads=[e2, rt], writes=[e1])
                    for hf in range(2):
                        sinb = rt.ap[:, 64:128].rearrange("p (c h d) -> p c h d", c=2, h=2)[:, :, hf, :] \
                            .unsqueeze(1).broadcast_to([128, 8, 2, 16])
                        P.op("pool", lambda e, e2=e2, e3=e3, hf=hf, sinb=sinb: e.tensor_tensor(
                            out=e3.v("p (g c h d) -> p g c h d", c=2, h=2, d=16)[:, :, :, hf, :],
                            in0=e2.v("p (g c h d) -> p g c h d", c=2, h=2, d=16)[:, :, :, 1 - hf, :],
                            in1=sinb, op=ALU.mult), reads=[e2, rt], writes=[e3])
                    P.op("dve", lambda e, e1=e1, e3=e3, tbb=tbb: e.tensor_tensor(out=tbb.ap, in0=e1.ap, in1=e3.ap, op=ALU.add),
                         reads=[e1, e3], writes=[tbb])
                else:
                    P.op("dve", lambda e, e2=e2, tbb=tbb: e.tensor_copy(tbb.ap, e2.ap), reads=[e2], writes=[tbb])
                pt = k.ps[cnt % 4]
                ptb = pt.ap.bitcast(BF16)
                for hh in range(4):
                    P.op("pe", lambda e, ptb=ptb, tbb=tbb, hh=hh: e.transpose(
                        ptb[:, 128 * hh:128 * hh + 128], tbb.ap[:, 128 * hh:128 * hh + 128], k.identb.ap),
                        reads=[tbb, k.identb], writes=[pt])
                eb = evb[cnt % 4]
                P.op("act", lambda e, ptb=ptb, eb=eb: e.activation(out=eb.ap, in_=ptb[:, 0:512], func=AF.Identity),
                     reads=[pt], writes=[eb])
                if isq:
                    o = op_index(i)
                    P.dma(lambda e, eb=eb, o=o, h0=h0: e.dma_start(
                        out=QT.ap[h0:h0 + 4, :, 128 * o:128 * o + 128].rearrange("h p t -> p h t"),
                        in_=eb.v("p (h t) -> p h t", t=128)), reads=[eb], writes=[QT.name], queue="pool")
                else:
                    P.dma(lambda e, eb=eb, i=i, h0=h0: e.dma_start(
                        out=KT.ap[h0:h0 + 4, :, 128 * i:128 * i + 128].rearrange("h p t -> p h t"),
                        in_=eb.v("p (h t) -> p h t", t=128)), reads=[eb], writes=[KT.name], queue="pool")
    wins = [(1 + 512 * w_, 512, 512 * w_) for w_ in range(4)] + [(2049, 128, 2048), (4098, 256, 2176), (4355, 256, 2432)]
    if (not only) or ("gb" in only):
        for g2 in range(2):
            gi = len(groups) + g2
            ws, wb = wst[gi % 2], wbf[gi % 2]
            cs = 7424 + 512 * g2
            P.dma(lambda e, ws=ws, cs=cs: e.dma_start(
                out=ws.v("p (k c) -> p k c", c=512), in_=w_in0[:, cs:cs + 512].rearrange("(k p) c -> p k c", p=128)), writes=[ws])
            P.op("pool", lambda e, ws=ws, wb=wb: e.tensor_copy(wb.ap, ws.ap), reads=[ws], writes=[wb])
            for q in range(4):
                for (c0, nw, o0) in wins:
                    cnt += 1
                    p1 = k.ps[4 + cnt % 4]
                    e1 = ev[cnt % 2]
                    for kk in range(8):
                        P.op("pe", lambda e, p1=p1, wb=wb, kk=kk, c0=c0, nw=nw, q=q: e.matmul(
                            p1.ap[:, 0:nw], wb.ap[:, kk * 512 + 128 * q:kk * 512 + 128 * q + 128], hT3[:, kk, c0:c0 + nw],
                            start=(kk == 0), stop=(kk == 7)), reads=[wb, hT], writes=[p1])
                    P.op("act", lambda e, p1=p1, e1=e1, nw=nw: e.activation(out=e1.ap[:, 0:nw], in_=p1.ap[:, 0:nw], func=AF.Silu),
                         reads=[p1], writes=[e1])
                    r0 = 512 * g2 + 128 * q
                    P.dma(lambda e, e1=e1, r0=r0, o0=o0, nw=nw: e.dma_start(out=GBT.ap[r0:r0 + 128, o0:o0 + nw], in_=e1.ap[:, 0:nw]),
                          reads=[e1], writes=[GBT.name])
    if (not only) or ("ctx" in only):
        for t_ in range(2):
            for which in range(2):
                cnt += 1
                e1a, e1b = ev[(cnt % 2) * 2], ev[(cnt % 2) * 2 + 1]
                src = ins["ctx_k"] if which == 0 else ins["ctx_v"]
                for hf in range(2):
                    ebuf = e1a if hf == 0 else e1b
                    P.dma(lambda e, ebuf=ebuf, src=src, t_=t_, hf=hf: e.dma_start(
                        out=ebuf.ap, in_=src[128 * t_:128 * t_ + 128, 512 * hf:512 * hf + 512]), writes=[ebuf])
                    tbb = tb[(cnt + hf) % 2]
                    P.op("dve", lambda e, ebuf=ebuf, tbb=tbb: e.tensor_copy(tbb.ap, ebuf.ap), reads=[ebuf], writes=[tbb])
                    if which == 1:
                        P.dma(lambda e, tbb=tbb, t_=t_, hf=hf: e.dma_start(
                            out=VS.ap[T_ALL + 128 * t_:T_ALL + 128 * t_ + 128, 512 * hf:512 * hf + 512], in_=tbb.ap),
                            reads=[tbb], writes=[VS.rows(T_ALL + 128 * t_, T_ALL + 128 * t_ + 128)])
                    else:
                        pt = k.ps[(cnt + hf) % 2]
                        ptb = pt.ap.bitcast(BF16)
                        for hh in range(4):
                            P.op("pe", lambda e, ptb=ptb, tbb=tbb, hh=hh: e.transpose(
                                ptb[:, 128 * hh:128 * hh + 128], tbb.ap[:, 128 * hh:128 * hh + 128], k.identb.ap),
                                reads=[tbb, k.identb], writes=[pt])
                        eb = evb[(cnt + hf) % 2]
                        P.op("act", lambda e, ptb=ptb, eb=eb: e.activation(out=eb.ap, in_=ptb[:, 0:512], func=AF.Identity),
                             reads=[pt], writes=[eb])
                        P.dma(lambda e, eb=eb, t_=t_, hf=hf: e.dma_start(
                            out=KT.ap[4 * hf:4 * hf + 4, :, T_ALL + 128 * t_:T_ALL + 128 * t_ + 128].rearrange("h p t -> p h t"),
                            in_=eb.v("p (h t) -> p h t", t=128)), reads=[eb], writes=[KT.name])
    A.release(m0)


def TT(P, eng, out, in0, in1, op, r, w):
    P.op(eng, lambda e: e.tensor_tensor(out=out, in0=in0, in1=in1, op=op), reads=r, writes=w)


def ACTF(P, out, in_, func, r, w, **kw):
    P.op("act", lambda e: e.activation(out=out, in_=in_, func=func, **kw), reads=r, writes=w)


def MM(P, out, lhsT, rhs, start, stop, r, w):
    P.op("pe", lambda e: e.matmul(out, lhsT, rhs, start=start, stop=stop), reads=r, writes=w)


def TR(P, out, in_, ident, r, w):
    P.op("pe", lambda e: e.transpose(out, in_, ident), reads=r, writes=w)


def CP(P, eng, out, in_, r, w):
    P.op(eng, lambda e: e.tensor_copy(out, in_), reads=r, writes=w)


def STT(P, out, in0, scalar, in1, op0, op1, r, w):
    P.op("dve", lambda e: e.scalar_tensor_tensor(out=out, in0=in0, scalar=scalar, in1=in1, op0=op0, op1=op1),
         reads=r, writes=w)


def RED(P, out, in_, r, w):
    P.op("dve", lambda e: e.tensor_reduce(out=out, in_=in_, axis=AX.X, op=ALU.add), reads=r, writes=w)


def g64(ap):
    return ap.rearrange("p (g d) -> p g d", d=64)


def b64(ap16):
    return ap16.unsqueeze(2).broadcast_to([128, 16, 64])


def emit_rwkv(k, ins, outs, PA, GA, OB, YT):
    P, A = k.P, k.A
    ps = k.ps
    psb = [p_.ap.bitcast(BF16) for p_ in ps]
    m0 = A.mark()
    prm = A.f32(5120)
    kkb, kab, rkb, lwb, lbb = [prm.cols(1024 * i, 1024 * i + 1024) for i in range(5)]
    wup = A.f32(4096)
    CM = A.f32(768)
    IND = A.f32(2)
    MSK = A.f32(1024)
    lT = A.f32(256)
    H = [A.f32(512) for _ in range(2)]
    Hb = [A.bf(512) for _ in range(2)]
    gC = A.f32(16)
    sbst = A.f32(21 * 16)
    P.dma(lambda e: e.dma_start(out=prm.ap, in_=ins["rw_prm"][:, :]), writes=[prm])
    P.dma(lambda e: e.dma_start(out=wup.ap[0:65, :], in_=ins["rw_up"][:, :]), writes=[wup])
    P.dma(lambda e: e.dma_start(out=CM.ap, in_=ins["rw_const"][:, 0:768]), writes=[CM])
    P.dma(lambda e: e.dma_start(out=IND.ap, in_=ins["rw_const"][:, 768:770]), writes=[IND])
    P.dma(lambda e: e.dma_start(out=MSK.ap, in_=ins["rw_const"][:, 770:1794]), writes=[MSK])
    P.op("pool", lambda e: e.memset(lT.ap, 1.0), writes=[lT])
    pa_one = A.f32(3328)
    pa2 = [pa_one, pa_one]
    LT = A.f32(128)
    SW, AA, KAP, KD, BE, X1, X2 = [A.f32(1024) for _ in range(7)]
    E = [A.f32(1024) for _ in range(2)]
    s16, sbc = A.f32(16), A.f32(16)
    RTIL, KATIL, BTIL, KTIL, BH, KH, VB = [A.bf(1024) for _ in range(7)]
    QRS, BKS = A.bf(2048), A.bf(2048)
    SC = A.bf(16 * 512)
    Nb = [A.bf(2048) for _ in range(2)]
    Ntb = [A.bf(2048) for _ in range(2)]
    Tb = [A.bf(2048) for _ in range(2)]
    Wsb = A.bf(1024)
    Usbc = [A.bf(1024) for _ in range(2)]
    VBc = [A.bf(1024) for _ in range(2)]
    Osb = A.f32(1024)
    obuf, gabuf = X2, E[0]
    for zb in (Wsb, Usbc[0], Usbc[1], VBc[0], VBc[1]):
        P.op("pool", lambda e, zb=zb: e.memset(zb.ap, 0.0), writes=[zb])
    ytb = A.bf(1024)
    lT3 = lT.v("p (a t) -> p a t", t=128)
    QRS4 = QRS.v("p (j x t) -> p j x t", x=2, t=128)
    BKS4 = BKS.v("p (j x t) -> p j x t", x=2, t=128)
    SC3 = SC.v("p (h c) -> p h c", c=512)

    def visit(i, d, state_only, chunks, finalize):
        n = visit.n
        visit.n += 1
        pa = pa2[n % 2]
        c_lo = 1024 if state_only else 0
        P.dma(lambda e: e.dma_start(out=pa.ap[:, c_lo:3328], in_=PA.ap[128 * i:128 * i + 128, c_lo:3328]),
              reads=[PA.rows(128 * i, 128 * i + 128)], writes=[pa])
        par, pak, pav = pa.ap[:, 0:1024], pa.ap[:, 1024:2048], pa.ap[:, 2048:3072]
        Hd, Hbd = H[d], Hb[d]
        Hd3 = Hd.v("p (j v) -> p j v", v=64)
        Hb3 = Hbd.v("p (j v) -> p j v", v=64)
        CP(P, "pool", VB.ap, pav, [pa], [VB])
        CP(P, "pool", VBc[0].ap[0:64, :], pav[0:64, :], [pa], [VBc[0]])
        CP(P, "pool", VBc[1].ap[64:128, :], pav[64:128, :], [pa], [VBc[1]])
        ACTF(P, LT.ap[:, 0:64], pa.ap[:, 3072 + 64 * d:3136 + 64 * d], AF.Tanh, [pa], [LT])
        ACTF(P, LT.ap[:, 64:128], pa.ap[:, 3200 + 64 * d:3264 + 64 * d], AF.Identity, [pa], [LT])
        TR(P, ps[0].ap[0:64, 0:128], LT.ap[:, 0:64], k.ident.ap, [LT, k.ident], [ps[0]])
        TR(P, ps[0].ap[0:64, 128:256], LT.ap[:, 64:128], k.ident.ap, [LT, k.ident], [ps[0]])
        ACTF(P, lT3[0:64, :, :], ps[0].ap[0:64, 0:256].rearrange("p (a t) -> p a t", t=128), AF.Identity, [ps[0]], [lT])
        for hf in range(2):
            MM(P, ps[hf].ap, lT3[0:65, 0, :], wup.ap[0:65, 1024 * d + 512 * hf:1024 * d + 512 * hf + 512], True, True,
               [lT, wup], [ps[hf]])
            MM(P, ps[2 + hf].ap, lT3[0:65, 1, :], wup.ap[0:65, 2048 + 1024 * d + 512 * hf:2048 + 1024 * d + 512 * hf + 512],
               True, True, [lT, wup], [ps[2 + hf]])
        for hf in range(2):
            ACTF(P, SW.ap[:, 512 * hf:512 * hf + 512], ps[hf].ap, AF.Sigmoid, [ps[hf]], [SW])
            ACTF(P, AA.ap[:, 512 * hf:512 * hf + 512], ps[2 + hf].ap, AF.Sigmoid, [ps[2 + hf]], [AA])
        TT(P, "dve", X1.ap, pak, kkb.ap, ALU.mult, [pa, kkb], [X1])
        ACTF(P, X2.ap, X1.ap, AF.Square, [X1], [X2])
        RED(P, s16.ap, g64(X2.ap), [X2], [s16])
        ACTF(P, s16.ap, s16.ap, AF.Ln, [s16], [s16], bias=1e-24)
        ACTF(P, s16.ap, s16.ap, AF.Exp, [s16], [s16], scale=-0.5)
        TT(P, "dve", g64(KAP.ap), g64(X1.ap), b64(s16.ap), ALU.mult, [X1, s16], [KAP])
        STT(P, X2.ap, AA.ap, -1.0, kab.ap, ALU.add, ALU.mult, [AA, kab], [X2])
        STT(P, KD.ap, X2.ap, 1.0, pak, ALU.add, ALU.mult, [X2, pa], [KD])
        TT(P, "pool", BE.ap, KAP.ap, AA.ap, ALU.mult, [KAP, AA], [BE])
        if not state_only:
            TT(P, "pool", X1.ap, par, rkb.ap, ALU.mult, [pa, rkb], [X1])
            TT(P, "pool", X1.ap, X1.ap, KD.ap, ALU.mult, [X1, KD], [X1])
            RED(P, sbc.ap, g64(X1.ap), [X1], [sbc])
        cm = CM.v("p (d m t) -> p d m t", d=2, m=3)
        for hf in range(2):
            MM(P, ps[4 + hf].ap, cm[:, d, 0, :], SW.ap[:, 512 * hf:512 * hf + 512], True, True, [CM, SW], [ps[4 + hf]])
            MM(P, ps[6 + hf].ap, cm[:, d, 1, :], SW.ap[:, 512 * hf:512 * hf + 512], True, True, [CM, SW], [ps[6 + hf]])
        for hf in range(2):
            ACTF(P, E[0].ap[:, 512 * hf:512 * hf + 512], ps[4 + hf].ap, AF.Exp, [ps[4 + hf]], [E[0]])
            ACTF(P, E[1].ap[:, 512 * hf:512 * hf + 512], ps[4 + hf].ap, AF.Exp, [ps[4 + hf]], [E[1]], scale=-1.0)
        if not state_only:
            TT(P, "dve", RTIL.ap, par, E[0].ap, ALU.mult, [pa, E[0]], [RTIL])
        TT(P, "pool", BTIL.ap, BE.ap, E[1].ap, ALU.mult, [BE, E[1]], [BTIL])
        TT(P, "dve", KTIL.ap, KD.ap, E[1].ap, ALU.mult, [KD, E[1]], [KTIL])
        for hf in range(2):
            MM(P, ps[hf].ap, cm[:, d, 2, :], SW.ap[:, 512 * hf:512 * hf + 512], True, True, [CM, SW], [ps[hf]])
        for hf in range(2):
            ACTF(P, E[0].ap[:, 512 * hf:512 * hf + 512], ps[6 + hf].ap, AF.Exp, [ps[6 + hf]], [E[0]])
            ACTF(P, E[1].ap[:, 512 * hf:512 * hf + 512], ps[hf].ap, AF.Exp, [ps[hf]], [E[1]])
        TT(P, "dve", KATIL.ap, KAP.ap, E[0].ap, ALU.mult, [KAP, E[0]], [KATIL])
        TT(P, "pool", BH.ap, BE.ap, E[1].ap, ALU.mult, [BE, E[1]], [BH])
        TT(P, "pool", KH.ap, KD.ap, E[1].ap, ALU.mult, [KD, E[1]], [KH])
        for j in range(8):
            MM(P, ps[2].ap[:, 2 * j:2 * j + 2], SW.ap[:, 128 * j:128 * j + 128], IND.ap, True, True, [SW, IND], [ps[2]])
        ACTF(P, gC.ap, ps[2].ap[:, 0:16], AF.Exp, [ps[2]], [gC])
        gC3 = gC.v("p (j c) -> p j c", c=2)
        quants = [(KATIL, QRS4, 0), (BTIL, BKS4, 0), (KTIL, BKS4, 1)]
        if not state_only:
            quants.append((RTIL, QRS4, 1))
        for qi, (src, dst4, x) in enumerate(quants):
            dstbuf = QRS if dst4 is QRS4 else BKS
            pi_ = 4 + qi % 4
            for j in range(8):
                TR(P, psb[pi_][:, 128 * j:128 * j + 128], src.ap[:, 128 * j:128 * j + 128], k.identb.ap,
                   [src, k.identb], [ps[pi_]])
            dview = dst4[:, :, x, :]
            sview = psb[pi_].rearrange("p (j t) -> p j t", t=128)
            if qi % 2 == 0:
                ACTF(P, dview, sview, AF.Identity, [ps[pi_]], [dstbuf])
            else:
                CP(P, "dve", dview, sview, [ps[pi_]], [dstbuf])
        msk = MSK.ap[:, 512 * d:512 * d + 512]
        ncol = 128 if state_only else 256
        for h in range(16):
            j, e = h // 2, h % 2
            pb = ps[h % 4]
            rhs = QRS4[64 * e:64 * e + 64, j, :, :] if not state_only else QRS4[64 * e:64 * e + 64, j, 0:1, :]
            MM(P, pb.ap[:, 0:ncol], BKS4[64 * e:64 * e + 64, j, 0, :], rhs, True, True, [BKS, QRS], [pb])
            MM(P, pb.ap[:, 256:256 + ncol], BKS4[64 * e:64 * e + 64, j, 1, :], rhs, True, True, [BKS, QRS], [pb])
            TT(P, "dve", SC3[:, h, :], pb.ap, msk, ALU.mult, [pb, MSK], [SC])
        CP(P, "pool", Nb[0].v("p (h t) -> p h t", t=128), SC3[:, :, 0:128], [SC], [Nb[0]])
        TT(P, "dve", Tb[0].v("p (h t) -> p h t", t=128),
           k.identb.ap.unsqueeze(1).broadcast_to([128, 16, 128]), SC3[:, :, 0:128], ALU.subtract, [SC, k.identb], [Tb[0]])
        for hh in range(2):
            pi_ = 4 + hh
            for q in range(8):
                h = 8 * hh + q
                TR(P, psb[pi_][:, 128 * q:128 * q + 128], SC3[:, h, 0:128], k.identb.ap, [SC, k.identb], [ps[pi_]])
            ACTF(P, Ntb[0].ap[:, 1024 * hh:1024 * hh + 1024], psb[pi_], AF.Identity, [ps[pi_]], [Ntb[0]])
        cur = 0
        for lvl in range(1, 6):
            nxt = 1 - cur
            last = lvl == 5
            for hq in range(4):
                sl = slice(512 * hq, 512 * hq + 512)
                pbt = ps[hq % 4]
                for q in range(4):
                    h = 4 * hq + q
                    c1 = slice(128 * h, 128 * h + 128)
                    MM(P, pbt.ap[:, 128 * q:128 * q + 128], Nb[cur].ap[:, c1], Ntb[cur].ap[:, c1], True, True,
                       [Nb[cur], Ntb[cur]], [pbt])
                ACTF(P, Ntb[nxt].ap[:, sl], pbt.ap, AF.Identity, [pbt], [Ntb[nxt]])
                if not last:
                    pbn = ps[4 + hq % 4]
                    for q in range(4):
                        h = 4 * hq + q
                        c1 = slice(128 * h, 128 * h + 128)
                        MM(P, pbn.ap[:, 128 * q:128 * q + 128], Ntb[cur].ap[:, c1], Nb[cur].ap[:, c1], True, True,
                           [Nb[cur], Ntb[cur]], [pbn])
                    ACTF(P, Nb[nxt].ap[:, sl], pbn.ap, AF.Identity, [pbn], [Nb[nxt]])
            for hq in range(4):
                sl = slice(512 * hq, 512 * hq + 512)
                pb = ps[4 + hq % 4] if last else ps[hq % 4]
                for q in range(4):
                    h = 4 * hq + q
                    c1 = slice(128 * h, 128 * h + 128)
                    MM(P, pb.ap[:, 128 * q:128 * q + 128], Ntb[nxt].ap[:, c1], Tb[cur].ap[:, c1], True, True,
                       [Ntb[nxt], Tb[cur]], [pb])
                TT(P, "dve", Tb[nxt].ap[:, sl], pb.ap, Tb[cur].ap[:, sl], ALU.add, [pb, Tb[cur]], [Tb[nxt]])
            cur = nxt
        MTb = Tb[cur]
        MT3 = MTb.v("p (h t) -> p h t", t=128)
        for flag in ("xa", "xb", "xc", "xd"):
            if flag in k.debug:
                for h in range(16):
                    j, e = h // 2, h % 2
                    es = slice(64 * e, 64 * e + 64)
                    pb = ps[h // 8]
                    oc = slice(64 * (h % 8), 64 * (h % 8) + 64)
                    if flag == "xa":
                        MM(P, pb.ap[:, oc], BKS4[es, j, 0, :], BKS4[es, j, 1, 0:64], True, True, [BKS], [pb])
                    elif flag == "xb":
                        MM(P, pb.ap[:, oc], QRS4[es, j, 0, :], BKS4[es, j, 1, 0:64], True, True, [BKS, QRS], [pb])
                    elif flag == "xc":
                        MM(P, pb.ap[:, oc], QRS4[es, j, 0, :], Hb3[es, j, :], True, True, [QRS, Hbd], [pb])
                    elif flag == "xd":
                        MM(P, pb.ap[:, oc], BKS4[es, j, 0, :], Hb3[es, j, :], True, True, [BKS, Hbd], [pb])
                for hf in range(2):
                    ACTF(P, Wsb.ap[:, 512 * hf:512 * hf + 512], ps[hf].ap, AF.Identity, [ps[hf]], [Wsb])
        if "rwstop6" in k.debug:
            return
        for ch in chunks:
            rs = slice(64 * ch, 64 * ch + 64)
            Uc, Vc = Usbc[ch], VBc[ch]
            for h in range(16):
                j, e = h // 2, h % 2
                es = slice(64 * e, 64 * e + 64)
                pb = ps[h // 8]
                oc = slice(64 * (h % 8), 64 * (h % 8) + 64)
                MM(P, pb.ap[:, oc], QRS4[es, j, 0, :], Hb3[es, j, :], True, False, [QRS, Hbd], [pb])
                MM(P, pb.ap[:, oc], SC3[:, h, 256:384], VB.ap[:, 64 * h:64 * h + 64], False, True, [SC, VB], [pb])
            for hf in range(2):
                ACTF(P, Wsb.ap[rs, 512 * hf:512 * hf + 512], ps[hf].ap[rs, :], AF.Identity, [ps[hf]], [Wsb])
            if "rwstop7" in k.debug:
                return
            for h in range(16):
                pb = ps[2 + h // 8]
                oc = slice(64 * (h % 8), 64 * (h % 8) + 64)
                MM(P, pb.ap[:, oc], MT3[:, h, :], Wsb.ap[:, 64 * h:64 * h + 64], True, True, [MTb, Wsb], [pb])
            for hf in range(2):
                ACTF(P, Uc.ap[rs, 512 * hf:512 * hf + 512], ps[2 + hf].ap[rs, :], AF.Identity, [ps[2 + hf]], [Uc], scale=-1.0)
            if not state_only:
                for h in range(16):
                    j, e = h // 2, h % 2
                    es = slice(64 * e, 64 * e + 64)
                    pb = ps[4 + h // 8]
                    oc = slice(64 * (h % 8), 64 * (h % 8) + 64)
                    MM(P, pb.ap[:, oc], QRS4[es, j, 1, :], Hb3[es, j, :], True, False, [QRS, Hbd], [pb])
                    MM(P, pb.ap[:, oc], SC3[:, h, 128:256], Uc.ap[:, 64 * h:64 * h + 64], False, False, [SC, Uc], [pb])
                    MM(P, pb.ap[:, oc], SC3[:, h, 384:512], VB.ap[:, 64 * h:64 * h + 64], False, True, [SC, VB], [pb])
                for hf in range(2):
                    CP(P, "dve", Osb.ap[rs, 512 * hf:512 * hf + 512], ps[4 + hf].ap[rs, :], [ps[4 + hf]], [Osb])
            for j in range(8):
                pb = ps[6 + j // 4]
                oc = slice(128 * (j % 4), 128 * (j % 4) + 128)
                jc = slice(128 * j, 128 * j + 128)
                MM(P, pb.ap[:, oc], BH.ap[:, jc], Uc.ap[:, jc], True, False, [BH, Uc], [pb])
                MM(P, pb.ap[:, oc], KH.ap[:, jc], Vc.ap[:, jc], False, True, [KH, Vc], [pb])
            for e in range(2):
                es = slice(64 * e, 64 * e + 64)
                TT(P, "dve", Hd3[es, :, :], Hd3[es, :, :], gC3[es, :, ch].unsqueeze(2).broadcast_to([64, 8, 64]), ALU.mult,
                   [Hd, gC], [Hd])
                for hf in range(2):
                    src = ps[6 + hf].ap[es, :].rearrange("p (j e v) -> p j e v", e=2, v=64)[:, :, e, :]
                    TT(P, "dve", Hd3[es, 4 * hf:4 * hf + 4, :], Hd3[es, 4 * hf:4 * hf + 4, :], src, ALU.add,
                       [Hd, ps[6 + hf]], [Hd])
            CP(P, "pool", Hbd.ap, Hd.ap, [Hd], [Hbd])
        if state_only:
            return
        o = op_index(i)
        if not finalize:
            P.dma(lambda e: e.dma_start(out=OB.ap[128 * o:128 * o + 128, :], in_=Osb.ap), reads=[Osb],
                  writes=[OB.rows(128 * o, 128 * o + 128)])
            CP(P, "pool", sbst.ap[:, 16 * o:16 * o + 16], sbc.ap, [sbc], [sbst])
            return
        P.dma(lambda e: e.dma_start(out=obuf.ap, in_=OB.ap[128 * o:128 * o + 128, :]),
              reads=[OB.rows(128 * o, 128 * o + 128)], writes=[obuf])
        P.dma(lambda e: e.dma_start(out=gabuf.ap, in_=GA.ap[128 * o:128 * o + 128, :]),
              reads=[GA.rows(128 * o, 128 * o + 128)], writes=[gabuf])
        TT(P, "dve", X1.ap, Osb.ap, obuf.ap, ALU.add, [Osb, obuf], [X1])
        RED(P, s16.ap, g64(X1.ap), [X1], [s16])
        P.op("dve", lambda e: e.tensor_scalar(out=s16.ap, in0=s16.ap, scalar1=-1.0 / 64, scalar2=None, op0=ALU.mult),
             reads=[s16], writes=[s16])
        TT(P, "dve", g64(X1.ap), g64(X1.ap), b64(s16.ap), ALU.add, [X1, s16], [X1])
        ACTF(P, X2.ap, X1.ap, AF.Square, [X1], [X2])
        RED(P, s16.ap, g64(X2.ap), [X2], [s16])
        ACTF(P, s16.ap, s16.ap, AF.Ln, [s16], [s16], scale=1.0 / 64, bias=GN_EPS)
        ACTF(P, s16.ap, s16.ap, AF.Exp, [s16], [s16], scale=-0.5)
        TT(P, "dve", g64(X1.ap), g64(X1.ap), b64(s16.ap), ALU.mult, [X1, s16], [X1])
        TT(P, "pool", X1.ap, X1.ap, lwb.ap, ALU.mult, [X1, lwb], [X1])
        TT(P, "pool", X1.ap, X1.ap, lbb.ap, ALU.add, [X1, lbb], [X1])
        TT(P, "dve", sbc.ap, sbc.ap, sbst.ap[:, 16 * o:16 * o + 16], ALU.add, [sbc, sbst], [sbc])
        TT(P, "dve", g64(X2.ap), g64(pav), b64(sbc.ap), ALU.mult, [pa, sbc], [X2])
        TT(P, "pool", X1.ap, X1.ap, X2.ap, ALU.add, [X1, X2], [X1])
        TT(P, "dve", X1.ap, X1.ap, gabuf.ap, ALU.mult, [X1, gabuf], [X1])
        if "YA" in k.debug:
            P.dma(lambda e: e.dma_start(out=outs["YA"][128 * o:128 * o + 128, :], in_=X1.ap), reads=[X1], is_output=True)
        for hf in range(2):
            pb = ps[4 + hf]
            for q in range(4):
                kk_ = 4 * hf + q
                TR(P, pb.ap[:, 128 * q:128 * q + 128], X1.ap[:, 128 * kk_:128 * kk_ + 128], k.ident.ap, [X1, k.ident], [pb])
            ACTF(P, ytb.ap[:, 512 * hf:512 * hf + 512], pb.ap, AF.Identity, [pb], [ytb])
        P.dma(lambda e: e.dma_start(out=YT.ap[0:1024, 128 * o:128 * o + 128].rearrange("(kk p) t -> p kk t", p=128),
                                    in_=ytb.v("p (kk t) -> p kk t", t=128)), reads=[ytb], writes=[YT.name])

    visit.n = 0

    def init_state(d, from_input):
        if from_input:
            P.dma(lambda e: e.dma_start(out=H[d].ap, in_=ins["st_in"][:, 512 * d:512 * d + 512]), writes=[H[d]])
        else:
            P.op("pool", lambda e: e.memset(H[d].ap, 0.0), writes=[H[d]])
        CP(P, "pool", Hb[d].ap, H[d].ap, [H[d]], [Hb[d]])

    only = [x for x in k.debug if x.startswith("rw_")]
    do_sample = (not only) or ("rw_sample" in only)
    do_prompt = (not only) or ("rw_prompt" in only)
    if do_sample:
        init_state(1, True)
        for i in range(NT_S - 1, NT_OWN - 1, -1):
            visit(i, 1, True, (1, 0), False)
        for i in range(NT_OWN - 1, -1, -1):
            visit(i, 1, False, (1, 0), False)
        init_state(0, True)
        for i in range(NT_OWN):
            visit(i, 0, False, (0, 1), True)
    if do_prompt:
        for sq in range(2):
            t0 = NT_S + 2 * sq
            init_state(1, False)
            for i in (t0 + 1, t0):
                visit(i, 1, False, (1, 0), False)
            P.dma(lambda e, sq=sq: e.dma_start(out=outs["st_out"][sq, 1, :, :], in_=H[1].ap), reads=[H[1]], is_output=True)
            init_state(0, False)
            for i in (t0, t0 + 1):
                visit(i, 0, False, (0, 1), True)
            P.dma(lambda e, sq=sq: e.dma_start(out=outs["st_out"][sq, 0, :, :], in_=H[0].ap), reads=[H[0]], is_output=True)
    A.release(m0)


def emit_attn(k, ins, outs, QT, KT, VS, GBT, YT):
    P, A = k.P, k.A
    ps = k.ps
    m0 = A.mark()
    prm = A.f32(257)
    lam4 = A.f32(4)
    neglam = A.f32(1)
    subw = A.f32(1)
    onesb = A.bf(128)
    P.dma(lambda e: e.dma_start(out=prm.ap, in_=ins["at_prm"][:, :]), writes=[prm])
    P.op("pool", lambda e: e.memset(onesb.ap, 1.0), writes=[onesb])
    tmp = A.f32(128)
    l4 = prm.ap[:, 0:256].rearrange("p (a d) -> p a d", d=64)
    TT(P, "dve", tmp.v("p (a d) -> p a d", d=64), l4[:, 0:4:2, :], l4[:, 1:4:2, :], ALU.mult, [prm], [tmp])
    P.op("dve", lambda e: e.tensor_reduce(out=lam4.ap[:, 0:2], in_=tmp.v("p (a d) -> p a d", d=64), axis=AX.X, op=ALU.add),
         reads=[tmp], writes=[lam4])
    ACTF(P, lam4.ap[:, 0:2], lam4.ap[:, 0:2], AF.Exp, [lam4], [lam4])
    STT(P, neglam.ap, lam4.ap[:, 1:2], -LAM_INIT, lam4.ap[:, 0:1], ALU.add, ALU.subtract, [lam4], [neglam])
    P.op("dve", lambda e: e.tensor_scalar(out=subw.ap, in0=prm.ap[:, 256:257], scalar1=1.0 - LAM_INIT, scalar2=None, op0=ALU.mult),
         reads=[prm], writes=[subw])
    NKMAX = 34
    kT = [A.bf(NKMAX * 128) for _ in range(2)]
    vv = [A.bf(NKMAX * 128) for _ in range(2)]
    qT = [A.bf(2176) for _ in range(2)]
    pT = [A.bf(512) for _ in range(4)]
    r0b, t0b, t1b, gbt = A.f32(512), A.f32(512), A.f32(512), A.f32(512)
    sqb = A.bf(512)
    yb = A.bf(512)
    sample_keys = [128 * i for i in range(NT_S)] + [T_ALL, T_ALL + 128]
    groups = [(0, 128 * NT_OWN, sample_keys)]
    for sq in range(2):
        groups.append((128 * NT_OWN + 256 * sq, 256, [128 * NT_S + 256 * sq, 128 * NT_S + 256 * sq + 128]))
    only = [x for x in k.debug if x.startswith("at_")]
    if "at_prompt" in only:
        groups = groups[1:]
    if "at_sample" in only:
        groups = groups[:1]
    it = 0
    for (q0, nq, keys) in groups:
        nk = len(keys)
        for h in range(1 if "at_h1" in k.debug else 8):
            it += 1
            kTb, vb, qTb = kT[it % 2], vv[it % 2], qT[it % 2]
            runs = []
            for ki, kr in enumerate(keys):
                if runs and runs[-1][1] + runs[-1][2] == kr:
                    runs[-1][2] += 128
                else:
                    runs.append([ki, kr, 128])
            for (ki, kr, ln) in runs:
                P.dma(lambda e, kTb=kTb, ki=ki, kr=kr, ln=ln, h=h: e.dma_start(
                    out=kTb.ap[:, 128 * ki:128 * ki + ln], in_=KT.ap[h, :, kr:kr + ln]), reads=[KT.name], writes=[kTb])
                P.dma(lambda e, vb=vb, ki=ki, kr=kr, ln=ln, h=h: e.dma_start(
                    out=vb.ap[:, 128 * ki:128 * ki + ln].rearrange("p (t c) -> p t c", c=128),
                    in_=VS.ap[kr:kr + ln, 128 * h:128 * h + 128].rearrange("(t p) c -> p t c", p=128)),
                    reads=[VS.rows(kr, kr + ln)], writes=[vb])
            P.dma(lambda e, qTb=qTb, q0=q0, nq=nq, h=h: e.dma_start(out=qTb.ap[:, 0:nq], in_=QT.ap[h, :, q0:q0 + nq]),
                  reads=[QT.name], writes=[qTb])
            for qs in range(0, nq, 512):
                qn = min(512, nq - qs)
                def emit_qk(ki):
                    for c in range(2):
                        sb_ = ps[2 * (ki % 2) + c]
                        cs_ = slice(64 * c, 64 * c + 64)
                        MM(P, sb_.ap[:, 0:qn], kTb.ap[cs_, 128 * ki:128 * ki + 128], qTb.ap[cs_, qs:qs + qn], True, True,
                           [kTb, qTb], [sb_])

                emit_qk(0)
                for ki in range(nk):
                    if ki + 1 < nk:
                        emit_qk(ki + 1)
                    for c in range(2):
                        sb_ = ps[2 * (ki % 2) + c]
                        pb_ = pT[2 * (ki % 2) + c]
                        ACTF(P, pb_.ap[:, 0:qn], sb_.ap[:, 0:qn], AF.Exp, [sb_], [pb_], scale=0.125)
                    for c in range(2):
                        pb_ = pT[2 * (ki % 2) + c]
                        MM(P, ps[4 + c].ap[:, 0:qn], vb.ap[:, 128 * ki:128 * ki + 128], pb_.ap[:, 0:qn], ki == 0, ki == nk - 1,
                           [vb, pb_], [ps[4 + c]])
                        MM(P, ps[6 + c].ap[:, 0:qn], onesb.ap, pb_.ap[:, 0:qn], ki == 0, ki == nk - 1, [onesb, pb_], [ps[6 + c]])
                P.op("dve", lambda e, qn=qn: e.reciprocal(out=r0b.ap[:, 0:qn], in_=ps[6].ap[:, 0:qn]), reads=[ps[6]], writes=[r0b])
                TT(P, "dve", t0b.ap[:, 0:qn], ps[4].ap[:, 0:qn], r0b.ap[:, 0:qn], ALU.mult, [ps[4], r0b], [t0b])
                P.op("dve", lambda e, qn=qn: e.reciprocal(out=r0b.ap[:, 0:qn], in_=ps[7].ap[:, 0:qn]), reads=[ps[7]], writes=[r0b])
                TT(P, "dve", t1b.ap[:, 0:qn], ps[5].ap[:, 0:qn], r0b.ap[:, 0:qn], ALU.mult, [ps[5], r0b], [t1b])
                STT(P, t0b.ap[:, 0:qn], t1b.ap[:, 0:qn], neglam.ap[:, 0:1], t0b.ap[:, 0:qn], ALU.mult, ALU.add,
                    [t1b, neglam, t0b], [t0b])
                ACTF(P, sqb.ap[:, 0:qn], t0b.ap[:, 0:qn], AF.Square, [t0b], [sqb])
                MM(P, ps[0].ap[:, 0:qn], onesb.ap, sqb.ap[:, 0:qn], True, True, [onesb, sqb], [ps[0]])
                ACTF(P, t1b.ap[:, 0:qn], ps[0].ap[:, 0:qn], AF.Ln, [ps[0]], [t1b], scale=1.0 / 128, bias=NORM_EPS)
                ACTF(P, t1b.ap[:, 0:qn], t1b.ap[:, 0:qn], AF.Exp, [t1b], [t1b], scale=-0.5)
                P.dma(lambda e, qs=qs, qn=qn, q0=q0, h=h: e.dma_start(
                    out=gbt.ap[:, 0:qn], in_=GBT.ap[128 * h:128 * h + 128, q0 + qs:q0 + qs + qn]), reads=[GBT.name], writes=[gbt])
                STT(P, t0b.ap[:, 0:qn], t0b.ap[:, 0:qn], subw.ap[:, 0:1], t1b.ap[:, 0:qn], ALU.mult, ALU.mult,
                    [t0b, subw, t1b], [t0b])
                TT(P, "dve", yb.ap[:, 0:qn], t0b.ap[:, 0:qn], gbt.ap[:, 0:qn], ALU.mult, [t0b, gbt], [yb])
                if "YB" in k.debug:
                    TT(P, "dve", t1b.ap[:, 0:qn], t0b.ap[:, 0:qn], gbt.ap[:, 0:qn], ALU.mult, [t0b, gbt], [t1b])
                    P.dma(lambda e, qs=qs, qn=qn, q0=q0, h=h: e.dma_start(
                        out=outs["YB"][128 * h:128 * h + 128, q0 + qs:q0 + qs + qn], in_=t1b.ap[:, 0:qn]), reads=[t1b], is_output=True)
                P.dma(lambda e, qs=qs, qn=qn, q0=q0, h=h: e.dma_start(
                    out=YT.ap[1024 + 128 * h:1024 + 128 * h + 128, q0 + qs:q0 + qs + qn], in_=yb.ap[:, 0:qn]),
                    reads=[yb], writes=[YT.name])
    A.release(m0)


L1_COLS = 2694


def l1_col0(o):
    if o < NT_OWN:
        return 1 + 128 * o
    if o < NT_OWN + 2:
        return 2179 + 128 * (o - NT_OWN)
    return 2436 + 128 * (o - NT_OWN - 2)


def emit_tail(k, ins, outs, YT, x_all):
    P, A, nc = k.P, k.A, k.nc
    ps = k.ps
    NO = NT_OWN + NT_PR
    m0 = A.mark()
    X1S = k.X1S
    hT1 = A.bf(8 * L1_COLS)
    hT13 = hT1.v("p (k c) -> p k c", c=L1_COLS)
    P.op("pool", lambda e: e.memset(hT1.ap, 0.0), writes=[hT1])
    m1 = A.mark()

    def outproj(w_dram, yt_cols, layer, x_get, x_put, tag):
        mm = A.mark()
        wst = [A.f32(1024) for _ in range(2)]
        wb = A.bf(16 * 1024)
        for kk in range(16):
            wsb = wst[kk % 2]
            P.dma(lambda e, wsb=wsb, kk=kk: e.dma_start(out=wsb.ap[:, 0:1024], in_=w_dram[128 * kk:128 * kk + 128, :]), writes=[wsb])
            CP(P, "pool" if kk % 2 else "dve", wb.ap[:, 1024 * kk:1024 * kk + 1024], wsb.ap[:, 0:1024], [wsb],
               [wb.cols(1024 * kk, 1024 * kk + 1024)])
        ytb = [A.bf(16 * 128) for _ in range(2)]
        xt = [A.f32(1024) for _ in range(2)]
        for o in range(NO):
            yb_ = ytb[o % 2]
            c0 = yt_cols(o)
            P.dma(lambda e, yb_=yb_, c0=c0: e.dma_start(
                out=yb_.v("p (kk t) -> p kk t", t=128), in_=YT.ap[:, c0:c0 + 128].rearrange("(kk p) t -> p kk t", p=128)),
                reads=[YT.name], writes=[yb_])
            v = 0 if o < NT_OWN else 1
            g = k.gB.ap[:, (layer * 2 + v) * 1024:(layer * 2 + v) * 1024 + 1024]
            xin = x_get(o, xt[o % 2])
            xo = x_put(o, xt[o % 2])
            for hf in range(2):
                pb = ps[2 * (o % 2) + hf]
                for kk in range(16):
                    MM(P, pb.ap, yb_.ap[:, 128 * kk:128 * kk + 128], wb.ap[:, 1024 * kk + 512 * hf:1024 * kk + 512 * hf + 512],
                       kk == 0, kk == 15, [yb_, wb], [pb])
                sl = slice(512 * hf, 512 * hf + 512)
                TT(P, "dve", xo.ap[:, sl], pb.ap, g[:, sl], ALU.mult, [pb, k.gB], [xo])
                TT(P, "pool" if hf else "dve", xo.ap[:, sl], xo.ap[:, sl], xin.ap[:, sl], ALU.add, [xo, xin], [xo])
            if tag == "final":
                P.dma(lambda e, xo=xo, o=o: e.dma_start(out=outs["y_out"][128 * o:128 * o + 128, :], in_=xo.ap), reads=[xo],
                      is_output=True)
            else:
                P.dma(lambda e, xo=xo, o=o: e.dma_start(out=X1S.ap[128 * o:128 * o + 128, :], in_=xo.ap), reads=[xo],
                      writes=[X1S.rows(128 * o, 128 * o + 128)])
            if tag != "final" and "X1" in k.debug:
                P.dma(lambda e, xo=xo, o=o: e.dma_start(out=outs["X1"][128 * o:128 * o + 128, :], in_=xo.ap), reads=[xo],
                      is_output=True)
        A.release(mm)

    def x_get0(o, buf):
        i = o if o < NT_OWN else NT_S + (o - NT_OWN)
        P.dma(lambda e, buf=buf, i=i: e.dma_start(out=buf.ap, in_=x_all[128 * i:128 * i + 128, :]), writes=[buf])
        return buf

    xo2 = [A.f32(1024) for _ in range(2)]
    outproj(ins["w_out0"], lambda o: 128 * o, 0, x_get0, lambda o, buf: xo2[o % 2], "l0")
    if "stopX1" in k.debug:
        A.release(m0)
        return
    mn = A.mark()
    emit_norm_phase(k, x_src=lambda o: X1S.ap[128 * o:128 * o + 128, :], tiles=list(range(NO)), layer=1, hT3=hT13, hT=hT1,
                    col0=l1_col0, variant=lambda o: 0 if o < NT_OWN else 1, cols_total=L1_COLS,
                    src_reg=lambda o: X1S.rows(128 * o, 128 * o + 128))
    A.release(mn)
    mm = A.mark()
    cvp = A.f32(64)
    P.dma(lambda e: e.dma_start(out=cvp.ap, in_=ins["cv_prm"][:, :]), writes=[cvp])
    wst = [A.f32(4 * 1024) for _ in range(2)]
    wbf = [A.bf(4 * 1024) for _ in range(2)]
    NB = L1_COLS + 2
    cgu = A.f32(NB)
    bgr = A.f32(NB)
    zs = A.f32(NB)
    cvb = A.f32(NB)
    cgt = A.f32(512)
    yrow = A.bf(NB)
    P.op("pool", lambda e: e.memset(cgu.ap, 0.0), writes=[cgu])
    w_in1 = ins["w_in1"]
    wins = [(c, min(512, L1_COLS - c)) for c in range(0, L1_COLS, 512)]
    for j in range(16):
        ws, wb = wst[j % 2], wbf[j % 2]
        for q in range(4):
            P.dma(lambda e, ws=ws, q=q, j=j: e.dma_start(
                out=ws.ap[:, 1024 * q:1024 * q + 1024].rearrange("p (k c) -> p k c", c=128),
                in_=w_in1[:, 2048 * q + 128 * j:2048 * q + 128 * j + 128].rearrange("(k p) c -> p k c", p=128)),
                writes=[ws.cols(1024 * q, 1024 * q + 1024)])
        CP(P, "pool", wb.ap, ws.ap, [ws], [wb])
        for (c0, nw) in wins:
            for q in range(4):
                pb = ps[4 + q]
                for kk in range(8):
                    MM(P, pb.ap[:, 0:nw], wb.ap[:, 1024 * q + 128 * kk:1024 * q + 128 * kk + 128], hT13[:, kk, c0:c0 + nw],
                       kk == 0, kk == 7, [wb, hT1], [pb])
            ACTF(P, bgr.ap[:, 1 + c0:1 + c0 + nw], ps[4].ap[:, 0:nw], AF.Identity, [ps[4]], [bgr])
            ACTF(P, cgt.ap[:, 0:nw], ps[5].ap[:, 0:nw], AF.Identity, [ps[5]], [cgt])
            TT(P, "dve", cgu.ap[:, 1 + c0:1 + c0 + nw], cgt.ap[:, 0:nw], ps[6].ap[:, 0:nw], ALU.mult, [cgt, ps[6]], [cgu])
            ACTF(P, zs.ap[:, 1 + c0:1 + c0 + nw], ps[7].ap[:, 0:nw], AF.Silu, [ps[7]], [zs])
        n = L1_COLS
        w0, w1, w2, bb = [cvp.ap[:, 16 * t_ + j:16 * t_ + j + 1] for t_ in range(4)]
        P.op("dve", lambda e, w1=w1, bb=bb, n=n: e.tensor_scalar(out=cvb.ap[:, 1:1 + n], in0=cgu.ap[:, 1:1 + n], scalar1=w1,
                                                                scalar2=bb, op0=ALU.mult, op1=ALU.add), reads=[cgu, cvp], writes=[cvb])
        STT(P, cvb.ap[:, 1:1 + n], cgu.ap[:, 0:n], w0, cvb.ap[:, 1:1 + n], ALU.mult, ALU.add, [cgu, cvp, cvb], [cvb])
        STT(P, cvb.ap[:, 1:1 + n], cgu.ap[:, 2:2 + n], w2, cvb.ap[:, 1:1 + n], ALU.mult, ALU.add, [cgu, cvp, cvb], [cvb])
        TT(P, "pool", cvb.ap[:, 1:1 + n], cvb.ap[:, 1:1 + n], bgr.ap[:, 1:1 + n], ALU.mult, [cvb, bgr], [cvb])
        TT(P, "dve", yrow.ap[:, 1:1 + n], cvb.ap[:, 1:1 + n], zs.ap[:, 1:1 + n], ALU.mult, [cvb, zs], [yrow])
        P.dma(lambda e, j=j, n=n: e.dma_start(out=YT.ap[128 * j:128 * j + 128, 0:n], in_=yrow.ap[:, 1:1 + n]), reads=[yrow],
              writes=[YT.name])
    A.release(mm)
    def x_get1(o, buf):
        P.dma(lambda e, buf=buf, o=o: e.dma_start(out=buf.ap, in_=X1S.ap[128 * o:128 * o + 128, :]),
              reads=[X1S.rows(128 * o, 128 * o + 128)], writes=[buf])
        return buf

    outproj(ins["w_out1"], l1_col0, 1, x_get1, lambda o, buf: xo2[o % 2], "final")
    A.release(m0)


def prep_core_inputs(q, inp):
    b, mir = q // 2, q % 2
    f = (lambda a: a[::-1]) if mir else (lambda a: a)
    d = {}
    xs = f(inp["x_sample"][b])
    xp0 = f(inp["x_prompt"][2 * q])
    xp1 = f(inp["x_prompt"][2 * q + 1])
    d["x_all"] = np.ascontiguousarray(np.concatenate([xs, xp0, xp1], 0))
    cvv = np.stack([inp["c"][b], inp["c_ctx"]], -1)
    d["cv"] = np.ascontiguousarray(cvv.reshape(8, 128, 2).transpose(1, 0, 2).reshape(128, 16))
    d["normw_fm"] = np.ascontiguousarray(inp["norm_w"].reshape(2, 8, 128).transpose(2, 0, 1).reshape(128, 16))
    d["ada_w"] = inp["ada_w"]
    d["adab_fm"] = np.ascontiguousarray(inp["ada_b"].reshape(2, 24, 128).transpose(2, 0, 1).reshape(128, 48))
    d["adab_g"] = np.ascontiguousarray(np.broadcast_to(inp["ada_b"][:, 2048:].reshape(1, 2048), (128, 2048)))
    w = inp["e_w_in"][0]
    mu = inp["e_mu"][0]
    if mir:
        perm = np.arange(EVEN_IN)
        perm[3072:3136], perm[3136:3200] = np.arange(3136, 3200), np.arange(3072, 3136)
        perm[3200:3264], perm[3264:3328] = np.arange(3264, 3328), np.arange(3200, 3264)
        w = w[:, perm]
        mu = mu[perm[:3328]]
    d["w_in0"] = np.ascontiguousarray(w)
    d["mu_b"] = np.ascontiguousarray(np.broadcast_to(mu.reshape(1, -1), (128, 3328)))
    qk = np.concatenate([np.tile(inp["e_q_norm"][0], 8), np.tile(inp["e_k_norm"][0], 8)])
    d["qkw_b"] = np.ascontiguousarray(np.broadcast_to(qk.reshape(1, -1), (128, 1024)))
    t = np.arange(4096)
    row = (t // 64).astype(np.float32)
    col = (t % 64).astype(np.float32)
    inv = (10000.0 ** (-np.arange(0, 32, 2, dtype=np.float32) / 32)).astype(np.float32)
    ar, ac = row[:, None] * inv, col[:, None] * inv
    cos = np.concatenate([np.cos(ar), np.cos(ar), np.cos(ac), np.cos(ac)], 1)
    sin = np.concatenate([-np.sin(ar), np.sin(ar), -np.sin(ac), np.sin(ac)], 1)
    d["rope_cs"] = np.ascontiguousarray(f(np.concatenate([cos, sin], 1).astype(np.float32)))
    dirs = (1, 0) if mir else (0, 1)
    rep = lambda v: np.broadcast_to(np.asarray(v, np.float32).reshape(1, -1), (128, v.size))
    d["rw_prm"] = np.ascontiguousarray(np.concatenate(
        [rep(inp["e_k_k"][0]), rep(inp["e_k_a"][0]), rep(inp["e_r_k"][0].reshape(-1)),
         rep(inp["e_lnx_w"][0]), rep(inp["e_lnx_b"][0])], 1))
    up = np.zeros((65, 4096), np.float32)
    for fd, td in enumerate(dirs):
        up[:64, 1024 * fd:1024 * fd + 1024] = inp["e_w_up"][0, td]
        up[64, 1024 * fd:1024 * fd + 1024] = inp["e_w0"][0, td]
        up[:64, 2048 + 1024 * fd:2048 + 1024 * fd + 1024] = inp["e_a_up"][0, td]
        up[64, 2048 + 1024 * fd:2048 + 1024 * fd + 1024] = inp["e_a0"][0, td]
    d["rw_up"] = up
    d["rw_const"] = rwkv_consts()
    sts = (inp["state_rwkv_fwd"][b, 0], inp["state_rwkv_bwd"][b, 0])
    st = np.zeros((128, 1024), np.float32)
    for fd, td in enumerate(dirs):
        hh = sts[td].reshape(8, 2, 64, 64).transpose(1, 3, 0, 2).reshape(128, 512)
        st[:, 512 * fd:512 * fd + 512] = hh
    d["st_in"] = st
    d["ctx_k"] = np.ascontiguousarray(inp["cache_diff_k"][b, 0].reshape(256, 1024))
    d["ctx_v"] = np.ascontiguousarray(inp["cache_diff_v"][b, 0].reshape(256, 1024))
    d["at_prm"] = np.ascontiguousarray(np.concatenate(
        [rep(inp["e_lambda"][0].reshape(-1)), inp["e_subln"][0].reshape(128, 1)], 1).astype(np.float32))
    d["w_out0"] = inp["e_w_out"][0]
    d["w_in1"] = inp["o_w_in"][0]
    d["w_out1"] = inp["o_w_out"][0]
    cw = inp["o_conv_w"][0]
    if mir:
        cw = cw[::-1]
    cvp = np.concatenate([cw.reshape(3, 16, 128), inp["o_conv_b"][0].reshape(1, 16, 128)], 0)
    d["cv_prm"] = np.ascontiguousarray(cvp.transpose(2, 0, 1).reshape(128, 64))
    return d


def rwkv_consts():
    idx = np.arange(128)
    ch, pos = idx // 64, idx % 64
    same = ch[:, None] == ch[None, :]
    out = np.zeros((128, 2050), np.float32)
    for dd in range(2):
        before = (pos[:, None] < pos[None, :]) if dd == 0 else (pos[:, None] > pos[None, :])
        eq = pos[:, None] == pos[None, :]
        incl = same & (before | eq)
        excl = same & before
        suf = same & before.T
        for m, mat in enumerate((incl, excl, suf)):
            out[:, (dd * 3 + m) * 128:(dd * 3 + m) * 128 + 128] = DECAY_C * mat
        msk = np.concatenate([excl, incl], 1).astype(np.float32)
        out[:, 770 + 512 * dd:770 + 512 * dd + 256] = msk
        out[:, 770 + 512 * dd + 256:770 + 512 * dd + 512] = msk
        out[:, 1794 + 128 * dd:1794 + 128 * dd + 128] = excl.T
    out[:, 768] = DECAY_C * (ch == 0)
    out[:, 769] = DECAY_C * (ch == 1)
    return out


def assemble(q, r, outs6):
    y_p, y_s, n_sf, n_sb, n_k, n_v = outs6
    b, mir = q // 2, q % 2
    S = y_p.shape[1]
    yo = np.asarray(r["y_out"])
    if mir:
        y_s[b, 2048:] = yo[:2048][::-1]
    else:
        y_s[b, :2048] = yo[:2048]
    so = np.asarray(r["st_out"])
    for j in range(2):
        yp = yo[2176 + 256 * j:2176 + 256 * j + 256]
        kc = np.asarray(r["kc_out"])[256 * j:256 * j + 256]
        vc = np.asarray(r["vc_out"])[256 * j:256 * j + 256]
        if mir:
            yp, kc, vc = yp[::-1], kc[::-1], vc[::-1]
        y_p[2 * q + j] = yp
        n_k[2 * q + j, 0] = kc.reshape(S, 8, 2, 64)
        n_v[2 * q + j, 0] = vc.reshape(S, 8, 128)
        for fd in range(2):
            st = so[j, fd].reshape(2, 64, 8, 64).transpose(2, 0, 3, 1).reshape(16, 64, 64)
            td = (1 - fd) if mir else fd
            (n_sf if td == 0 else n_sb)[2 * q + j, 0] = st


def kernel(**inputs):
    inp = {k_: np.asarray(v) for k_, v in inputs.items()}
    nc, ins, outs = build_program()
    in_maps = []
    for q in range(8):
        d = prep_core_inputs(q, inp)
        in_maps.append({n: d[n] for n in ins})
    res = run_bass_kernel_spmd(nc, in_maps, core_ids=list(range(8)))
    B, S = inp["x_prompt"].shape[0], inp["x_prompt"].shape[1]
    outs6 = (np.zeros(inp["x_prompt"].shape, np.float32), np.zeros(inp["x_sample"].shape, np.float32),
             np.zeros((B, 1, 16, 64, 64), np.float32), np.zeros((B, 1, 16, 64, 64), np.float32),
             np.zeros((B, 1, S, 8, 2, 64), np.float32), np.zeros((B, 1, S, 8, 128), np.float32))
    for q in range(8):
        assemble(q, res.results[q], outs6)
    return outs6
```

```python
import contextlib
import numpy as np
import concourse.bass as bass
import concourse.mybir as mybir
from concourse.bass_utils import run_bass_kernel_spmd

F32 = mybir.dt.float32
BF16 = mybir.dt.bfloat16
AF = mybir.ActivationFunctionType
ALU = mybir.AluOpType
AX = mybir.AxisListType

ENGS = ("pe", "act", "dve", "pool", "sp")
N_DMA_SEMS = 32
LOADS_ON_ACT = True
BIG = 1 << 40

D = 1024
NT_OWN, NT_OTH, NT_PR = 17, 15, 4
NT_S = NT_OWN + NT_OTH
NT = NT_S + NT_PR
T_ALL = NT * 128
T_OP = (NT_OWN + NT_PR) * 128
HT_COLS = 4612
EVEN_IN = 8448
NORM_EPS = 1e-6
GN_EPS = 64e-5
LAM_INIT = 0.8 - 0.6
DECAY_C = -0.6065306597126334


def tile_col0(i):
    if i < NT_S:
        return 1 + 128 * i
    if i < NT_S + 2:
        return 4098 + 128 * (i - NT_S)
    return 4355 + 128 * (i - NT_S - 2)


def op_index(i):
    return i if i < NT_OWN else NT_OWN + (i - NT_S)


OUT_TILES = list(range(NT_OWN)) + list(range(NT_S, NT))


class Prog:
    def __init__(self, nc):
        self.nc = nc
        self.streams = {e: [] for e in ENGS}
        self.count = {e: 0 for e in ENGS}
        self.seen = {e: {} for e in ENGS}
        self.regions = {}
        self.dma_cnt = [0] * N_DMA_SEMS
        self.dma_rr = 0
        self.out_tokens = []

    def _deps(self, reads, writes):
        deps = []
        for (name, lo, hi) in reads:
            for rec in self.regions.get(name, ()):
                if rec[0] < hi and lo < rec[1]:
                    deps.extend(rec[2].items())
                    if name.startswith("ps"):
                        deps.extend(rec[3].values())
        for (name, lo, hi) in writes:
            for rec in self.regions.get(name, ()):
                if rec[0] < hi and lo < rec[1]:
                    deps.extend(rec[2].items())
                    deps.extend(rec[3].values())
        return deps

    def _commit(self, reads, writes, token, rkey):
        for (name, lo, hi) in reads:
            lst = self.regions.setdefault(name, [])
            hit = False
            for rec in lst:
                if rec[0] < hi and lo < rec[1]:
                    rec[3][rkey] = token
                    hit = True
            if not hit:
                lst.append([lo, hi, {}, {rkey: token}])
        for (name, lo, hi) in writes:
            lst = self.regions.setdefault(name, [])
            keep = []
            wr = {token[0]: token[1]}
            dram = name not in ("ar",) and not name.startswith("ps")
            for rec in lst:
                if rec[1] <= lo or hi <= rec[0]:
                    keep.append(rec)
                    continue
                if dram:
                    for k_, v_ in rec[2].items():
                        if v_ > wr.get(k_, 0):
                            wr[k_] = v_
                if rec[0] < lo:
                    keep.append([rec[0], lo, dict(rec[2]), dict(rec[3])])
                if hi < rec[1]:
                    keep.append([hi, rec[1], dict(rec[2]), dict(rec[3])])
            keep.append([lo, hi, wr, {}])
            self.regions[name] = keep

    def _waits(self, eng, deps):
        best = {}
        for (k, v) in deps:
            if v > best.get(k, 0):
                best[k] = v
        out = []
        for k, v in best.items():
            if eng == "pe" and k == "pe":
                continue
            if self.seen[eng].get(k, 0) >= v:
                continue
            self.seen[eng][k] = v
            out.append((k, v))
        return out

    @staticmethod
    def _norm(regs):
        out = []
        for r in regs:
            if isinstance(r, str):
                out.append((r, 0, BIG))
            elif isinstance(r, Buf):
                out.append(r.reg)
            else:
                out.append(r)
        return out

    def op(self, eng, fn, reads=(), writes=()):
        reads = self._norm(reads)
        writes = self._norm(writes)
        waits = self._waits(eng, self._deps(reads, writes))
        self.count[eng] += 1
        token = (eng, self.count[eng])
        self.streams[eng].append(("op", waits, fn))
        self._commit(reads, writes, token, eng)
        return token

    def dma(self, fn, reads=(), writes=(), is_output=False, queue="sp"):
        reads = self._norm(reads)
        writes = self._norm(writes)
        queue = "act" if (LOADS_ON_ACT and any(w[0] == "ar" for w in writes)) else "sp"
        deps = self._deps(reads, writes)
        s = self.dma_rr
        self.dma_rr = (self.dma_rr + 1) % N_DMA_SEMS
        key = "dma%d" % s
        if self.dma_cnt[s] > 0:
            deps.append((key, self.dma_cnt[s]))
        waits = self._waits(queue, deps)
        self.dma_cnt[s] += 16
        token = (key, self.dma_cnt[s])
        self.streams[queue].append(("dma", waits, fn, s))
        self._commit(reads, writes, token, key)
        if is_output:
            self.out_tokens.append(token)
        return token

    def run(self):
        nc = self.nc
        fin = self._waits("sp", list(self.out_tokens))
        self.streams["sp"].append(("wait", fin))
        with contextlib.ExitStack() as st:
            sems = {}
            for e in ENGS:
                sems[e] = st.enter_context(nc.semaphore("s_" + e))
            for i in range(N_DMA_SEMS):
                sems["dma%d" % i] = st.enter_context(nc.semaphore("s_dma%d" % i))
            block = st.enter_context(nc.Block())

            def player(ename):
                def play(eng):
                    for item in self.streams[ename]:
                        for (k, v) in item[1]:
                            eng.wait_ge(sems[k], v)
                        if item[0] == "op":
                            item[2](eng).then_inc(sems[ename], 1)
                        elif item[0] == "dma":
                            item[2](eng).then_inc(sems["dma%d" % item[3]], 16)
                return play

            block.tensor(player("pe"))
            block.scalar(player("act"))
            block.vector(player("dve"))
            block.gpsimd(player("pool"))
            block.sync(player("sp"))


class Buf:
    def __init__(self, ap, reg):
        self.ap = ap
        self.reg = reg

    def cols(self, a, b):
        name, lo, hi = self.reg
        if name == "ar":
            if self.ap.dtype == BF16:
                r = (name, lo + a // 2, lo + (b + 1) // 2)
            else:
                r = (name, lo + a, lo + b)
        else:
            r = self.reg
        return Buf(self.ap[:, a:b], r)

    def v(self, s, **kw):
        return self.ap.rearrange(s, **kw)


class Arena:
    def __init__(self, ap, size):
        self.ap = ap
        self.size = size
        self.top = 0

    def f32(self, n):
        lo = self.top
        self.top += n
        assert self.top <= self.size, ("arena overflow", self.top, self.size)
        return Buf(self.ap[:, lo:lo + n], ("ar", lo, lo + n))

    def bf(self, n):
        nf = (n + 1) // 2
        lo = self.top
        self.top += nf
        assert self.top <= self.size, ("arena overflow", self.top, self.size)
        return Buf(self.ap[:, lo:lo + nf].bitcast(BF16)[:, 0:n], ("ar", lo, lo + nf))

    def mark(self):
        return self.top

    def release(self, m):
        self.top = m


class DramBuf:
    def __init__(self, name, ap):
        self.name = name
        self.ap = ap

    def rows(self, a, b):
        return (self.name, a, b)


ARENA_F32 = 52800


class K:
    pass


def build_program(debug=()):
    nc = bass.Bass("TRN2", target_bir_lowering=False)
    k = K()
    k.nc = nc
    k.debug = debug
    ins = {}
    outs = {}

    def din(name, shape, dt=F32):
        ins[name] = nc.dram_tensor(name, list(shape), dt, kind="ExternalInput").ap()
        return ins[name]

    def dout(name, shape, dt=F32):
        outs[name] = nc.dram_tensor(name, list(shape), dt, kind="ExternalOutput").ap()
        return outs[name]

    def dscr(name, shape, dt=F32):
        kind = "ExternalOutput" if name in debug else "Internal"
        t = nc.dram_tensor(name, list(shape), dt, kind=kind).ap()
        if name in debug:
            outs[name] = t
        return DramBuf(name, t)

    x_all = din("x_all", [T_ALL, D])
    cv = din("cv", [128, 16])
    normw_fm = din("normw_fm", [128, 16])
    ada_w = din("ada_w", [2, D, 3072])
    adab_fm = din("adab_fm", [128, 48])
    adab_g = din("adab_g", [128, 2048])
    w_in0 = din("w_in0", [D, EVEN_IN])
    mu_b = din("mu_b", [128, 3328])
    qkw_b = din("qkw_b", [128, 1024])
    rope_cs = din("rope_cs", [4096, 128])
    rw_prm = din("rw_prm", [128, 5120])
    rw_up = din("rw_up", [65, 4096])
    rw_const = din("rw_const", [128, 2050])
    st_in = din("st_in", [128, 1024])
    ctx_k = din("ctx_k", [256, 1024])
    ctx_v = din("ctx_v", [256, 1024])
    at_prm = din("at_prm", [128, 257])
    w_out0 = din("w_out0", [2048, 1024])
    w_in1 = din("w_in1", [D, 8192])
    w_out1 = din("w_out1", [2048, 1024])
    cv_prm = din("cv_prm", [128, 64])
    OB = dscr("OB", [T_OP, 1024])
    k.X1S = dscr("X1S", [T_OP, 1024])
    YT = dscr("YT", [2048, 2816], BF16)
    PA = dscr("PA", [T_ALL, 3328])
    GA = dscr("GA", [T_OP, 1024])
    QT = dscr("QT", [8, 128, T_OP], BF16)
    KT = dscr("KT", [8, 128, T_ALL + 256], BF16)
    VS = dscr("VS", [T_ALL + 256, 1024], BF16)
    GBT = dscr("GBT", [1024, T_OP])
    kc_out = dout("kc_out", [512, 1024])
    vc_out = dout("vc_out", [512, 1024])
    st_out = dout("st_out", [2, 2, 128, 512])
    y_out = dout("y_out", [T_OP, 1024])
    if "YB" in debug:
        dout("YB", [1024, T_OP])
    if "X1" in debug:
        dout("X1", [T_OP, 1024])
    if "YA" in debug:
        dout("YA", [T_OP, 1024])
    if "dumpQ" in debug:
        dout("dQ", [128, 12288])

    st = contextlib.ExitStack()
    with st:
        ar_t = st.enter_context(nc.sbuf_tensor("ar", [128, ARENA_F32], F32))
        psb = [st.enter_context(nc.psum_tensor("ps%d" % i, [128, 512], F32)) for i in range(8)]
        k.ps = [Buf(psb[i][:, :], ("ps%d" % i, 0, BIG)) for i in range(8)]
        A = Arena(ar_t, ARENA_F32)
        P = Prog(nc)
        k.P, k.A = P, A

        ident = A.f32(128)
        identb = A.bf(128)
        P.op("pool", lambda e: e.memset(ident.ap, 1.0), writes=[ident])
        P.op("pool", lambda e: e.affine_select(out=ident.ap, in_=ident.ap, pattern=[[-1, 128]],
                                               compare_op=ALU.is_equal, fill=0.0, base=0, channel_multiplier=1),
             reads=[ident], writes=[ident])
        P.op("dve", lambda e: e.tensor_copy(identb.ap, ident.ap), reads=[ident], writes=[identb])
        k.ident, k.identb = ident, identb

        cvt = A.f32(16)
        nwt = A.f32(16)
        abf = A.f32(48)
        modS = A.f32(32)
        modB = A.f32(32)
        gB = A.f32(4096)
        k.modS, k.modB, k.gB = modS, modB, gB
        m0 = A.mark()
        abg = A.f32(2048)
        screp = A.f32(16 * 128)
        P.dma(lambda e: e.dma_start(out=cvt.ap, in_=cv[:, :]), writes=[cvt])
        P.dma(lambda e: e.dma_start(out=nwt.ap, in_=normw_fm[:, :]), writes=[nwt])
        P.dma(lambda e: e.dma_start(out=abf.ap, in_=adab_fm[:, :]), writes=[abf])
        P.dma(lambda e: e.dma_start(out=abg.ap, in_=adab_g[:, :]), writes=[abg])
        P.op("act", lambda e: e.activation(out=cvt.ap, in_=cvt.ap, func=AF.Silu), reads=[cvt], writes=[cvt])
        P.op("dve", lambda e: e.tensor_copy(screp.v("p (a b) -> p a b", b=128),
                                            cvt.ap.unsqueeze(2).broadcast_to([128, 16, 128])),
             reads=[cvt], writes=[screp])
        wst = [A.f32(8 * 512) for _ in range(2)]
        for l in range(2 if "noada" not in debug else 0):
            for g in range(6):
                wb = wst[(l * 6 + g) % 2]
                P.dma(lambda e, wb=wb, l=l, g=g: e.dma_start(
                    out=wb.v("p (k c) -> p k c", c=512),
                    in_=ada_w[l, :, 512 * g:512 * g + 512].rearrange("(k p) c -> p k c", p=128)), writes=[wb])
                if g < 4:
                    pb = k.ps[g % 2]
                    for j in range(4):
                        for kk in range(8):
                            P.op("pe", lambda e, pb=pb, wb=wb, j=j, kk=kk: e.matmul(
                                pb.ap[:, 2 * j:2 * j + 2], wb.ap[:, kk * 512 + 128 * j: kk * 512 + 128 * j + 128],
                                cvt.ap[:, 2 * kk:2 * kk + 2], start=(kk == 0), stop=(kk == 7)),
                                reads=[wb, cvt], writes=[pb])
                    for v in range(2):
                        if g < 2:
                            dst = modB.ap[:, l * 16 + v * 8 + 4 * g: l * 16 + v * 8 + 4 * g + 4]
                            P.op("dve", lambda e, pb=pb, dst=dst, v=v, l=l, g=g: e.tensor_tensor(
                                out=dst, in0=pb.ap[:, 0:8].rearrange("p (j v) -> p j v", v=2)[:, :, v],
                                in1=abf.ap[:, l * 24 + 4 * g: l * 24 + 4 * g + 4], op=ALU.add),
                                reads=[pb, abf], writes=[modB])
                        else:
                            gg = g - 2
                            dst = modS.ap[:, l * 16 + v * 8 + 4 * gg: l * 16 + v * 8 + 4 * gg + 4]
                            P.op("dve", lambda e, pb=pb, dst=dst, v=v, l=l, g=g: e.tensor_tensor(
                                out=dst, in0=pb.ap[:, 0:8].rearrange("p (j v) -> p j v", v=2)[:, :, v],
                                in1=abf.ap[:, l * 24 + 4 * g: l * 24 + 4 * g + 4], op=ALU.add),
                                reads=[pb, abf], writes=[modS])
                            P.op("dve", lambda e, dst=dst, l=l, gg=gg: e.scalar_tensor_tensor(
                                out=dst, in0=dst, scalar=1.0, in1=nwt.ap[:, l * 8 + 4 * gg: l * 8 + 4 * gg + 4],
                                op0=ALU.add, op1=ALU.mult), reads=[modS, nwt], writes=[modS])
                else:
                    gg = g - 4
                    for v in range(2):
                        pb = k.ps[2 + v]
                        for kk in range(8):
                            P.op("pe", lambda e, pb=pb, wb=wb, v=v, kk=kk: e.matmul(
                                pb.ap, screp.ap[:, (2 * kk + v) * 128:(2 * kk + v) * 128 + 128],
                                wb.ap[:, kk * 512: kk * 512 + 512], start=(kk == 0), stop=(kk == 7)),
                                reads=[wb, screp], writes=[pb])
                        dst = gB.cols((l * 2 + v) * 1024 + 512 * gg, (l * 2 + v) * 1024 + 512 * gg + 512)
                        P.op("dve", lambda e, pb=pb, dst=dst, l=l, gg=gg: e.tensor_tensor(
                            out=dst.ap, in0=pb.ap, in1=abg.ap[:, l * 1024 + 512 * gg: l * 1024 + 512 * gg + 512],
                            op=ALU.add), reads=[pb, abg], writes=[dst])
        A.release(m0)

        mH = A.mark()
        hT = A.bf(8 * HT_COLS)
        k.hT = hT
        hT3 = hT.v("p (k c) -> p k c", c=HT_COLS)
        for zc in (0, 4097, 4354, 4611):
            P.op("pool", lambda e, zc=zc: e.memset(hT3[:, :, zc:zc + 1], 0.0), writes=[hT])
        mA = A.mark()
        emit_norm_phase(k, x_src=lambda i: x_all[128 * i:128 * i + 128, :], tiles=(list(range(NT)) if "noA1" not in debug else []), layer=0,
                        hT3=hT3, hT=hT, col0=tile_col0, variant=lambda i: 0 if i < NT_S else 1)
        A.release(mA)

        if "noA2" not in debug:
            emit_inproj0(k, ins, outs, PA, GA, QT, KT, VS, GBT, hT3)

        if "hT" in debug:
            hdbg = dout("hT_dbg", [128, 8 * HT_COLS], BF16)
            P.dma(lambda e: e.dma_start(out=hdbg[:, :], in_=hT.ap), reads=[hT], is_output=True)
        A.release(mH)
        if "noR" not in debug:
            emit_rwkv(k, ins, outs, PA, GA, OB, YT)
        if "noT" not in debug:
            emit_attn(k, ins, outs, QT, KT, VS, GBT, YT)
        if "noO" not in debug:
            emit_tail(k, ins, outs, YT, x_all)
        P.run()
    return nc, ins, outs


def emit_norm_phase(k, x_src, tiles, layer, hT3, hT, col0, variant, x_keep=None, cols_total=HT_COLS, src_reg=None):
    P, A = k.P, k.A
    xt = [A.f32(1024) for _ in range(2)]
    xn = [A.f32(1024) for _ in range(2)]
    junk = A.bf(1024)
    ss = [A.f32(1) for _ in range(2)]
    pend = [None]
    for n, i in enumerate(tiles):
        xb, xnb, ssb = xt[n % 2], xn[n % 2], ss[n % 2]
        src = x_src(i)
        if src is not None:
            P.dma(lambda e, xb=xb, src=src: e.dma_start(out=xb.ap, in_=src), writes=[xb],
                  reads=([src_reg(i)] if src_reg is not None else []))
        else:
            xb = x_keep(i)
        P.op("act", lambda e, xb=xb, ssb=ssb: e.activation(out=junk.ap, in_=xb.ap, func=AF.Square, accum_out=ssb.ap),
             reads=[xb], writes=[junk, ssb])
        P.op("act", lambda e, ssb=ssb: e.activation(out=ssb.ap, in_=ssb.ap, func=AF.Ln, scale=1.0 / D, bias=NORM_EPS),
             reads=[ssb], writes=[ssb])
        P.op("act", lambda e, ssb=ssb: e.activation(out=ssb.ap, in_=ssb.ap, func=AF.Exp, scale=-0.5),
             reads=[ssb], writes=[ssb])
        P.op("dve", lambda e, xb=xb, xnb=xnb, ssb=ssb: e.tensor_scalar(
            out=xnb.ap, in0=xb.ap, scalar1=ssb.ap[:, 0:1], scalar2=None, op0=ALU.mult), reads=[xb, ssb], writes=[xnb])
        v = variant(i)
        c0 = col0(i)

        def stage2(n=n, xnb=xnb, v=v, c0=c0):
            for half in range(2):
                pb = k.ps[(2 * n + half) % 4]
                for q in range(4):
                    kk = 4 * half + q
                    P.op("pe", lambda e, pb=pb, xnb=xnb, kk=kk, q=q: e.transpose(
                        pb.ap[:, 128 * q:128 * q + 128], xnb.ap[:, 128 * kk:128 * kk + 128], k.ident.ap),
                        reads=[xnb, k.ident], writes=[pb])
                for q in range(4):
                    kk = 4 * half + q
                    mi = layer * 16 + v * 8 + kk
                    hreg = ("ar", hT.reg[1] + (kk * cols_total + c0) // 2, hT.reg[1] + (kk * cols_total + c0 + 129) // 2)
                    P.op("act", lambda e, pb=pb, kk=kk, q=q, mi=mi, c0=c0: e.activation(
                        out=hT3[:, kk, c0:c0 + 128], in_=pb.ap[:, 128 * q:128 * q + 128], func=AF.Identity,
                        scale=k.modS.ap[:, mi:mi + 1], bias=k.modB.ap[:, mi:mi + 1]),
                        reads=[pb, k.modS, k.modB], writes=[hreg])

        if pend[0] is not None:
            pend[0]()
        pend[0] = stage2
    if pend[0] is not None:
        pend[0]()


def emit_inproj0(k, ins, outs, PA, GA, QT, KT, VS, GBT, hT3):
    P, A, nc = k.P, k.A, k.nc
    hT = k.hT
    w_in0 = ins["w_in0"]
    m0 = A.mark()
    mub = A.f32(3328)
    qkw = A.f32(1024)
    P.dma(lambda e: e.dma_start(out=mub.ap, in_=ins["mu_b"][:, :]), writes=[mub])
    P.dma(lambda e: e.dma_start(out=qkw.ap, in_=ins["qkw_b"][:, :]), writes=[qkw])
    wst = [A.f32(8 * 512) for _ in range(2)]
    wbf = [A.bf(8 * 512) for _ in range(2)]
    hd = [A.bf(8 * 128) for _ in range(2)]
    ev = [A.f32(512) for _ in range(6)]
    evs = [A.f32(512) for _ in range(4)]
    rs8 = [A.f32(8) for _ in range(4)]
    ropet = [A.f32(128) for _ in range(4)]
    tb = [A.bf(512) for _ in range(4)]
    evb = [A.bf(512) for _ in range(4)]

    def hreg(i):
        c0 = tile_col0(i)
        return ("ar", hT.reg[1], hT.reg[2])

    groups = []
    for c in range(0, 3072, 512):
        groups.append((c, 512, "A"))
    groups.append((3072, 256, "A"))
    for c in range(3328, 4352, 512):
        groups.append((c, 512, "ga"))
    for c in range(4352, 5376, 512):
        groups.append((c, 512, "pq"))
    for c in range(5376, 6400, 512):
        groups.append((c, 512, "pk"))
    for c in range(6400, 7424, 512):
        groups.append((c, 512, "pv"))
    cnt = 0
    pending = [None]
    only = [d[5:] for d in k.debug if d.startswith("only_")]
    if only:
        groups = [g for g in groups if g[2] in only]
    for gi, (cs, wdt, kind) in enumerate(groups):
        ws, wb = wst[gi % 2], wbf[gi % 2]
        P.dma(lambda e, ws=ws, cs=cs, wdt=wdt: e.dma_start(
            out=ws.v("p (k c) -> p k c", c=512)[:, :, 0:wdt],
            in_=w_in0[:, cs:cs + wdt].rearrange("(k p) c -> p k c", p=128)), writes=[ws])
        P.op("pool", lambda e, ws=ws, wb=wb: e.tensor_copy(wb.ap, ws.ap), reads=[ws], writes=[wb])
        if kind == "A":
            tiles = list(range(NT)) if cs >= 1024 else OUT_TILES
        elif kind in ("ga", "pq"):
            tiles = OUT_TILES
        else:
            tiles = list(range(NT))
        def emit_hd(i_, slot):
            hdb_ = hd[slot % 2]
            hd3_ = hdb_.v("p (k c) -> p k c", c=128)
            c0_ = tile_col0(i_)
            P.op("pool", lambda e, hd3_=hd3_, c0_=c0_: e.tensor_tensor(
                out=hd3_, in0=hT3[:, :, c0_ - 1:c0_ + 127], in1=hT3[:, :, c0_ + 1:c0_ + 129], op=ALU.add),
                reads=[hT], writes=[hdb_])
            P.op("dve", lambda e, hd3_=hd3_, c0_=c0_: e.scalar_tensor_tensor(
                out=hd3_, in0=hd3_, scalar=0.5, in1=hT3[:, :, c0_:c0_ + 128], op0=ALU.mult, op1=ALU.subtract),
                reads=[hT, hdb_], writes=[hdb_])

        if kind == "A":
            emit_hd(tiles[0], 0)
        for ti, i in enumerate(tiles):
            c0 = tile_col0(i)
            sample = i < NT_S
            cnt += 1
            if kind == "A":
                p1 = k.ps[4 + (cnt % 2) * 2]
                p2 = k.ps[5 + (cnt % 2) * 2]
            else:
                p1 = k.ps[4 + cnt % 4]
                p2 = None
            if kind == "A" and ti + 1 < len(tiles):
                emit_hd(tiles[ti + 1], ti + 1)
            for kk in range(8):
                P.op("pe", lambda e, p1=p1, wb=wb, kk=kk, c0=c0, wdt=wdt: e.matmul(
                    p1.ap[:, 0:wdt], hT3[:, kk, c0:c0 + 128], wb.ap[:, kk * 512:kk * 512 + wdt],
                    start=(kk == 0), stop=(kk == 7)), reads=[wb, hT], writes=[p1])
            if kind == "A":
                hdb = hd[ti % 2]
                hd3 = hdb.v("p (k c) -> p k c", c=128)
                for kk in range(8):
                    P.op("pe", lambda e, p2=p2, wb=wb, kk=kk, hd3=hd3, wdt=wdt: e.matmul(
                        p2.ap[:, 0:wdt], hd3[:, kk, :], wb.ap[:, kk * 512:kk * 512 + wdt],
                        start=(kk == 0), stop=(kk == 7)), reads=[wb, hdb], writes=[p2])
                e1, e2 = ev[cnt % 2], evs[cnt % 4]
                P.op("act", lambda e, p2=p2, e1=e1, wdt=wdt: e.activation(out=e1.ap[:, 0:wdt], in_=p2.ap[:, 0:wdt], func=AF.Identity),
                     reads=[p2], writes=[e1])
                P.op("pool", lambda e, e1=e1, cs=cs, wdt=wdt: e.tensor_tensor(
                    out=e1.ap[:, 0:wdt], in0=e1.ap[:, 0:wdt], in1=mub.ap[:, cs:cs + wdt], op=ALU.mult),
                    reads=[e1, mub], writes=[e1])
                P.op("dve", lambda e, e1=e1, e2=e2, p1=p1, wdt=wdt: e.tensor_tensor(
                    out=e2.ap[:, 0:wdt], in0=e1.ap[:, 0:wdt], in1=p1.ap[:, 0:wdt], op=ALU.add),
                    reads=[e1, p1], writes=[e2])
                P.dma(lambda e, e2=e2, i=i, cs=cs, wdt=wdt: e.dma_start(
                    out=PA.ap[128 * i:128 * i + 128, cs:cs + wdt], in_=e2.ap[:, 0:wdt]),
                    reads=[e2], writes=[PA.rows(128 * i, 128 * i + 128)], queue="pool")
            elif kind == "ga":
                e1 = evs[cnt % 4]
                o = op_index(i)
                P.op("act", lambda e, p1=p1, e1=e1: e.activation(out=e1.ap, in_=p1.ap, func=AF.Silu), reads=[p1], writes=[e1])
                P.dma(lambda e, e1=e1, o=o, cs=cs: e.dma_start(
                    out=GA.ap[128 * o:128 * o + 128, cs - 3328:cs - 3328 + 512], in_=e1.ap),
                    reads=[e1], writes=[GA.rows(128 * o, 128 * o + 128)], queue="pool")
            elif kind == "pv":
                e1, eb = evs[cnt % 4], evb[cnt % 4]
                P.op("act", lambda e, p1=p1, e1=e1: e.activation(out=e1.ap, in_=p1.ap, func=AF.Identity), reads=[p1], writes=[e1])
                P.op("dve", lambda e, e1=e1, eb=eb: e.tensor_copy(eb.ap, e1.ap), reads=[e1], writes=[eb])
                P.dma(lambda e, eb=eb, i=i, cs=cs: e.dma_start(
                    out=VS.ap[128 * i:128 * i + 128, cs - 6400:cs - 6400 + 512], in_=eb.ap),
                    reads=[eb], writes=[VS.rows(128 * i, 128 * i + 128)], queue="pool")
                if not sample:
                    pi = i - NT_S
                    P.dma(lambda e, e1=e1, pi=pi, cs=cs: e.dma_start(
                        out=outs["vc_out"][128 * pi:128 * pi + 128, cs - 6400:cs - 6400 + 512], in_=e1.ap),
                        reads=[e1], is_output=True, queue="pool")
            else:
                isq = kind == "pq"
                base = 4352 if isq else 5376
                h0 = (cs - base) // 128
                wof = 0 if isq else 512
                e1, e2, e3 = ev[(cnt % 2) * 3], ev[(cnt % 2) * 3 + 1], ev[(cnt % 2) * 3 + 2]
                if cnt % 4 >= 2:
                    e1, e3 = evs[0], evs[1]
                    e2 = evs[2 + cnt % 2]
                r8 = rs8[cnt % 4]
                P.op("act", lambda e, p1=p1, e1=e1: e.activation(out=e1.ap, in_=p1.ap, func=AF.Square), reads=[p1], writes=[e1])
                P.op("dve", lambda e, e1=e1, r8=r8: e.tensor_reduce(
                    out=r8.ap, in_=e1.v("p (g d) -> p g d", d=64), axis=AX.X, op=ALU.add), reads=[e1], writes=[r8])
                P.op("act", lambda e, r8=r8: e.activation(out=r8.ap, in_=r8.ap, func=AF.Ln, scale=1.0 / 64, bias=NORM_EPS),
                     reads=[r8], writes=[r8])
                P.op("act", lambda e, r8=r8: e.activation(out=r8.ap, in_=r8.ap, func=AF.Exp, scale=-0.5), reads=[r8], writes=[r8])
                P.op("dve", lambda e, p1=p1, e2=e2, r8=r8: e.tensor_tensor(
                    out=e2.v("p (g d) -> p g d", d=64), in0=p1.v("p (g d) -> p g d", d=64),
                    in1=r8.ap.unsqueeze(2).broadcast_to([128, 8, 64]), op=ALU.mult), reads=[p1, r8], writes=[e2])
                P.op("dve", lambda e, e2=e2, wof=wof: e.tensor_tensor(
                    out=e2.ap, in0=e2.ap, in1=qkw.ap[:, wof:wof + 512], op=ALU.mult), reads=[e2, qkw], writes=[e2])
                tbb = tb[cnt % 4]
                if (not sample) and (not isq):
                    pi = i - NT_S
                    P.dma(lambda e, e2=e2, pi=pi, cs=cs: e.dma_start(
                        out=outs["kc_out"][128 * pi:128 * pi + 128, cs - 5376:cs - 5376 + 512], in_=e2.ap),
                        reads=[e2], is_output=True, queue="pool")
                if sample:
                    rt = ropet[cnt % 4]
                    P.dma(lambda e, rt=rt, i=i: e.dma_start(out=rt.ap, in_=ins["rope_cs"][128 * i:128 * i + 128, :]), writes=[rt])
                    cosb = rt.ap[:, 0:64].unsqueeze(1).broadcast_to([128, 8, 64])
                    P.op("dve", lambda e, e1=e1, e2=e2, cosb=cosb: e.tensor_tensor(
                        out=e1.v("p (g d) -> p g d", d=64), in0=e2.v("p (g d) -> p g d", d=64), in1=cosb, op=ALU.mult),
                        reads=[e2, rt], writes=[e1])
                    for hf in range(2):
                        sinb = rt.ap[:, 64:128].rearrange("p (c h d) -> p c h d", c=2, h=2)[:, :, hf, :] \
                            .unsqueeze(1).broadcast_to([128, 8, 2, 16])
                        P.op("pool", lambda e, e2=e2, e3=e3, hf=hf, sinb=sinb: e.tensor_tensor(
                            out=e3.v("p (g c h d) -> p g c h d", c=2, h=2, d=16)[:, :, :, hf, :],
                            in0=e2.v("p (g c h d) -> p g c h d", c=2, h=2, d=16)[:, :, :, 1 - hf, :],
                            in1=sinb, op=ALU.mult), reads=[e2, rt], writes=[e3])
                    P.op("dve", lambda e, e1=e1, e3=e3, tbb=tbb: e.tensor_tensor(out=tbb.ap, in0=e1.ap, in1=e3.ap, op=ALU.add),
                         reads=[e1, e3], writes=[tbb])
                else:
                    P.op("dve", lambda e, e2=e2, tbb=tbb: e.tensor_copy(tbb.ap, e2.ap), reads=[e2], writes=[tbb])
                def stage2(cnt=cnt, tbb=tbb, isq=isq, i=i, h0=h0):
                    pt = k.ps[cnt % 4]
                    ptb = pt.ap.bitcast(BF16)
                    for hh in range(4):
                        P.op("pe", lambda e, ptb=ptb, tbb=tbb, hh=hh: e.transpose(
                            ptb[:, 128 * hh:128 * hh + 128], tbb.ap[:, 128 * hh:128 * hh + 128], k.identb.ap),
                            reads=[tbb, k.identb], writes=[pt])
                    eb = evb[cnt % 4]
                    P.op("act", lambda e, ptb=ptb, eb=eb: e.activation(out=eb.ap, in_=ptb[:, 0:512], func=AF.Identity),
                         reads=[pt], writes=[eb])
                    if isq:
                        o = op_index(i)
                        P.dma(lambda e, eb=eb, o=o, h0=h0: e.dma_start(
                            out=QT.ap[h0:h0 + 4, :, 128 * o:128 * o + 128].rearrange("h p t -> p h t"),
                            in_=eb.v("p (h t) -> p h t", t=128)), reads=[eb], writes=[QT.name])
                    else:
                        P.dma(lambda e, eb=eb, i=i, h0=h0: e.dma_start(
                            out=KT.ap[h0:h0 + 4, :, 128 * i:128 * i + 128].rearrange("h p t -> p h t"),
                            in_=eb.v("p (h t) -> p h t", t=128)), reads=[eb], writes=[KT.name])

                if pending[0] is not None:
                    pending[0]()
                pending[0] = stage2
    if pending[0] is not None:
        pending[0]()
        pending[0] = None
    wins = [(1 + 512 * w_, 512, 512 * w_) for w_ in range(4)] + [(2049, 128, 2048), (4098, 256, 2176), (4355, 256, 2432)]
    if (not only) or ("gb" in only):
        for g2 in range(2):
            gi = len(groups) + g2
            ws, wb = wst[gi % 2], wbf[gi % 2]
            cs = 7424 + 512 * g2
            P.dma(lambda e, ws=ws, cs=cs: e.dma_start(
                out=ws.v("p (k c) -> p k c", c=512), in_=w_in0[:, cs:cs + 512].rearrange("(k p) c -> p k c", p=128)), writes=[ws])
            P.op("pool", lambda e, ws=ws, wb=wb: e.tensor_copy(wb.ap, ws.ap), reads=[ws], writes=[wb])
            for q in range(4):
                for (c0, nw, o0) in wins:
                    cnt += 1
                    p1 = k.ps[4 + cnt % 4]
                    e1 = ev[cnt % 2]
                    for kk in range(8):
                        P.op("pe", lambda e, p1=p1, wb=wb, kk=kk, c0=c0, nw=nw, q=q: e.matmul(
                            p1.ap[:, 0:nw], wb.ap[:, kk * 512 + 128 * q:kk * 512 + 128 * q + 128], hT3[:, kk, c0:c0 + nw],
                            start=(kk == 0), stop=(kk == 7)), reads=[wb, hT], writes=[p1])
                    P.op("act", lambda e, p1=p1, e1=e1, nw=nw: e.activation(out=e1.ap[:, 0:nw], in_=p1.ap[:, 0:nw], func=AF.Silu),
                         reads=[p1], writes=[e1])
                    r0 = 512 * g2 + 128 * q
                    P.dma(lambda e, e1=e1, r0=r0, o0=o0, nw=nw: e.dma_start(out=GBT.ap[r0:r0 + 128, o0:o0 + nw], in_=e1.ap[:, 0:nw]),
                          reads=[e1], writes=[GBT.name])
    if (not only) or ("ctx" in only):
        for t_ in range(2):
            for which in range(2):
                cnt += 1
                e1a, e1b = ev[(cnt % 2) * 2], ev[(cnt % 2) * 2 + 1]
                src = ins["ctx_k"] if which == 0 else ins["ctx_v"]
                for hf in range(2):
                    ebuf = e1a if hf == 0 else e1b
                    P.dma(lambda e, ebuf=ebuf, src=src, t_=t_, hf=hf: e.dma_start(
                        out=ebuf.ap, in_=src[128 * t_:128 * t_ + 128, 512 * hf:512 * hf + 512]), writes=[ebuf])
                    tbb = tb[(cnt + hf) % 2]
                    P.op("dve", lambda e, ebuf=ebuf, tbb=tbb: e.tensor_copy(tbb.ap, ebuf.ap), reads=[ebuf], writes=[tbb])
                    if which == 1:
                        P.dma(lambda e, tbb=tbb, t_=t_, hf=hf: e.dma_start(
                            out=VS.ap[T_ALL + 128 * t_:T_ALL + 128 * t_ + 128, 512 * hf:512 * hf + 512], in_=tbb.ap),
                            reads=[tbb], writes=[VS.rows(T_ALL + 128 * t_, T_ALL + 128 * t_ + 128)])
                    else:
                        pt = k.ps[(cnt + hf) % 2]
                        ptb = pt.ap.bitcast(BF16)
                        for hh in range(4):
                            P.op("pe", lambda e, ptb=ptb, tbb=tbb, hh=hh: e.transpose(
                                ptb[:, 128 * hh:128 * hh + 128], tbb.ap[:, 128 * hh:128 * hh + 128], k.identb.ap),
                                reads=[tbb, k.identb], writes=[pt])
                        eb = evb[(cnt + hf) % 2]
                        P.op("act", lambda e, ptb=ptb, eb=eb: e.activation(out=eb.ap, in_=ptb[:, 0:512], func=AF.Identity),
                             reads=[pt], writes=[eb])
                        P.dma(lambda e, eb=eb, t_=t_, hf=hf: e.dma_start(
                            out=KT.ap[4 * hf:4 * hf + 4, :, T_ALL + 128 * t_:T_ALL + 128 * t_ + 128].rearrange("h p t -> p h t"),
                            in_=eb.v("p (h t) -> p h t", t=128)), reads=[eb], writes=[KT.name])
    A.release(m0)


def TT(P, eng, out, in0, in1, op, r, w):
    P.op(eng, lambda e: e.tensor_tensor(out=out, in0=in0, in1=in1, op=op), reads=r, writes=w)


def ACTF(P, out, in_, func, r, w, **kw):
    P.op("act", lambda e: e.activation(out=out, in_=in_, func=func, **kw), reads=r, writes=w)


def MM(P, out, lhsT, rhs, start, stop, r, w):
    P.op("pe", lambda e: e.matmul(out, lhsT, rhs, start=start, stop=stop), reads=r, writes=w)


def TR(P, out, in_, ident, r, w):
    P.op("pe", lambda e: e.transpose(out, in_, ident), reads=r, writes=w)


def CP(P, eng, out, in_, r, w):
    P.op(eng, lambda e: e.tensor_copy(out, in_), reads=r, writes=w)


def STT(P, out, in0, scalar, in1, op0, op1, r, w):
    P.op("dve", lambda e: e.scalar_tensor_tensor(out=out, in0=in0, scalar=scalar, in1=in1, op0=op0, op1=op1),
         reads=r, writes=w)


def RED(P, out, in_, r, w):
    P.op("dve", lambda e: e.tensor_reduce(out=out, in_=in_, axis=AX.X, op=ALU.add), reads=r, writes=w)


def g64(ap):
    return ap.rearrange("p (g d) -> p g d", d=64)


def b64(ap16):
    return ap16.unsqueeze(2).broadcast_to([128, 16, 64])


def emit_rwkv(k, ins, outs, PA, GA, OB, YT):
    P, A = k.P, k.A
    ps = k.ps
    psb = [p_.ap.bitcast(BF16) for p_ in ps]
    m0 = A.mark()
    prm = A.f32(5120)
    kkb, kab, rkb, lwb, lbb = [prm.cols(1024 * i, 1024 * i + 1024) for i in range(5)]
    wup = A.f32(4096)
    CM = A.f32(768)
    IND = A.f32(2)
    MSK = A.f32(1024)
    lT = A.f32(256)
    H = [A.f32(512) for _ in range(2)]
    Hb = [A.bf(512) for _ in range(2)]
    gC = A.f32(16)
    sbst = A.f32(21 * 16)
    P.dma(lambda e: e.dma_start(out=prm.ap, in_=ins["rw_prm"][:, :]), writes=[prm])
    P.dma(lambda e: e.dma_start(out=wup.ap[0:65, :], in_=ins["rw_up"][:, :]), writes=[wup])
    P.dma(lambda e: e.dma_start(out=CM.ap, in_=ins["rw_const"][:, 0:768]), writes=[CM])
    P.dma(lambda e: e.dma_start(out=IND.ap, in_=ins["rw_const"][:, 768:770]), writes=[IND])
    P.dma(lambda e: e.dma_start(out=MSK.ap, in_=ins["rw_const"][:, 770:1794]), writes=[MSK])
    P.op("pool", lambda e: e.memset(lT.ap, 1.0), writes=[lT])
    pa_one = A.f32(3328)
    pa2 = [pa_one, pa_one]
    LT = A.f32(128)
    SW, AA, KAP, KD, BE, X1, X2 = [A.f32(1024) for _ in range(7)]
    E = [A.f32(1024) for _ in range(2)]
    s16, sbc = A.f32(16), A.f32(16)
    RTIL, KATIL, BTIL, KTIL, BH, KH, VB = [A.bf(1024) for _ in range(7)]
    QRS, BKS = A.bf(2048), A.bf(2048)
    SC = A.bf(16 * 512)
    Nb = [A.bf(2048) for _ in range(2)]
    Ntb = [A.bf(2048) for _ in range(2)]
    Tb = [A.bf(2048) for _ in range(2)]
    Wsb = A.bf(1024)
    Usbc = [A.bf(1024) for _ in range(2)]
    VBc = [A.bf(1024) for _ in range(2)]
    Osb = A.f32(1024)
    obuf, gabuf = X2, E[0]
    for zb in (Wsb, Usbc[0], Usbc[1], VBc[0], VBc[1]):
        P.op("pool", lambda e, zb=zb: e.memset(zb.ap, 0.0), writes=[zb])
    ytb = A.bf(1024)
    lT3 = lT.v("p (a t) -> p a t", t=128)
    QRS4 = QRS.v("p (j x t) -> p j x t", x=2, t=128)
    BKS4 = BKS.v("p (j x t) -> p j x t", x=2, t=128)
    SC3 = SC.v("p (h c) -> p h c", c=512)

    def visit(i, d, state_only, chunks, finalize):
        n = visit.n
        visit.n += 1
        pa = pa2[n % 2]
        c_lo = 1024 if state_only else 0
        P.dma(lambda e: e.dma_start(out=pa.ap[:, c_lo:3328], in_=PA.ap[128 * i:128 * i + 128, c_lo:3328]),
              reads=[PA.rows(128 * i, 128 * i + 128)], writes=[pa])
        par, pak, pav = pa.ap[:, 0:1024], pa.ap[:, 1024:2048], pa.ap[:, 2048:3072]
        Hd, Hbd = H[d], Hb[d]
        Hd3 = Hd.v("p (j v) -> p j v", v=64)
        Hb3 = Hbd.v("p (j v) -> p j v", v=64)
        CP(P, "pool", VB.ap, pav, [pa], [VB])
        CP(P, "pool", VBc[0].ap[0:64, :], pav[0:64, :], [pa], [VBc[0]])
        CP(P, "pool", VBc[1].ap[64:128, :], pav[64:128, :], [pa], [VBc[1]])
        ACTF(P, LT.ap[:, 0:64], pa.ap[:, 3072 + 64 * d:3136 + 64 * d], AF.Tanh, [pa], [LT])
        ACTF(P, LT.ap[:, 64:128], pa.ap[:, 3200 + 64 * d:3264 + 64 * d], AF.Identity, [pa], [LT])
        TR(P, ps[0].ap[0:64, 0:128], LT.ap[:, 0:64], k.ident.ap, [LT, k.ident], [ps[0]])
        TR(P, ps[0].ap[0:64, 128:256], LT.ap[:, 64:128], k.ident.ap, [LT, k.ident], [ps[0]])
        ACTF(P, lT3[0:64, :, :], ps[0].ap[0:64, 0:256].rearrange("p (a t) -> p a t", t=128), AF.Identity, [ps[0]], [lT])
        for hf in range(2):
            MM(P, ps[hf].ap, lT3[0:65, 0, :], wup.ap[0:65, 1024 * d + 512 * hf:1024 * d + 512 * hf + 512], True, True,
               [lT, wup], [ps[hf]])
            MM(P, ps[2 + hf].ap, lT3[0:65, 1, :], wup.ap[0:65, 2048 + 1024 * d + 512 * hf:2048 + 1024 * d + 512 * hf + 512],
               True, True, [lT, wup], [ps[2 + hf]])
        for hf in range(2):
            ACTF(P, SW.ap[:, 512 * hf:512 * hf + 512], ps[hf].ap, AF.Sigmoid, [ps[hf]], [SW])
            ACTF(P, AA.ap[:, 512 * hf:512 * hf + 512], ps[2 + hf].ap, AF.Sigmoid, [ps[2 + hf]], [AA])
        TT(P, "dve", X1.ap, pak, kkb.ap, ALU.mult, [pa, kkb], [X1])
        ACTF(P, X2.ap, X1.ap, AF.Square, [X1], [X2])
        RED(P, s16.ap, g64(X2.ap), [X2], [s16])
        ACTF(P, s16.ap, s16.ap, AF.Ln, [s16], [s16], bias=1e-24)
        ACTF(P, s16.ap, s16.ap, AF.Exp, [s16], [s16], scale=-0.5)
        TT(P, "dve", g64(KAP.ap), g64(X1.ap), b64(s16.ap), ALU.mult, [X1, s16], [KAP])
        STT(P, X2.ap, AA.ap, -1.0, kab.ap, ALU.add, ALU.mult, [AA, kab], [X2])
        STT(P, KD.ap, X2.ap, 1.0, pak, ALU.add, ALU.mult, [X2, pa], [KD])
        TT(P, "pool", BE.ap, KAP.ap, AA.ap, ALU.mult, [KAP, AA], [BE])
        if not state_only:
            TT(P, "pool", X1.ap, par, rkb.ap, ALU.mult, [pa, rkb], [X1])
            TT(P, "pool", X1.ap, X1.ap, KD.ap, ALU.mult, [X1, KD], [X1])
            RED(P, sbc.ap, g64(X1.ap), [X1], [sbc])
        cm = CM.v("p (d m t) -> p d m t", d=2, m=3)
        for hf in range(2):
            MM(P, ps[4 + hf].ap, cm[:, d, 0, :], SW.ap[:, 512 * hf:512 * hf + 512], True, True, [CM, SW], [ps[4 + hf]])
            MM(P, ps[6 + hf].ap, cm[:, d, 1, :], SW.ap[:, 512 * hf:512 * hf + 512], True, True, [CM, SW], [ps[6 + hf]])
        for hf in range(2):
            ACTF(P, E[0].ap[:, 512 * hf:512 * hf + 512], ps[4 + hf].ap, AF.Exp, [ps[4 + hf]], [E[0]])
            ACTF(P, E[1].ap[:, 512 * hf:512 * hf + 512], ps[4 + hf].ap, AF.Exp, [ps[4 + hf]], [E[1]], scale=-1.0)
        if not state_only:
            TT(P, "dve", RTIL.ap, par, E[0].ap, ALU.mult, [pa, E[0]], [RTIL])
        TT(P, "pool", BTIL.ap, BE.ap, E[1].ap, ALU.mult, [BE, E[1]], [BTIL])
        TT(P, "dve", KTIL.ap, KD.ap, E[1].ap, ALU.mult, [KD, E[1]], [KTIL])
        for hf in range(2):
            MM(P, ps[hf].ap, cm[:, d, 2, :], SW.ap[:, 512 * hf:512 * hf + 512], True, True, [CM, SW], [ps[hf]])
        for hf in range(2):
            ACTF(P, E[0].ap[:, 512 * hf:512 * hf + 512], ps[6 + hf].ap, AF.Exp, [ps[6 + hf]], [E[0]])
            ACTF(P, E[1].ap[:, 512 * hf:512 * hf + 512], ps[hf].ap, AF.Exp, [ps[hf]], [E[1]])
        TT(P, "dve", KATIL.ap, KAP.ap, E[0].ap, ALU.mult, [KAP, E[0]], [KATIL])
        TT(P, "pool", BH.ap, BE.ap, E[1].ap, ALU.mult, [BE, E[1]], [BH])
        TT(P, "pool", KH.ap, KD.ap, E[1].ap, ALU.mult, [KD, E[1]], [KH])
        for j in range(8):
            MM(P, ps[2].ap[:, 2 * j:2 * j + 2], SW.ap[:, 128 * j:128 * j + 128], IND.ap, True, True, [SW, IND], [ps[2]])
        ACTF(P, gC.ap, ps[2].ap[:, 0:16], AF.Exp, [ps[2]], [gC])
        gC3 = gC.v("p (j c) -> p j c", c=2)
        quants = [(KATIL, QRS4, 0), (BTIL, BKS4, 0), (KTIL, BKS4, 1)]
        if not state_only:
            quants.append((RTIL, QRS4, 1))
        for qi, (src, dst4, x) in enumerate(quants):
            dstbuf = QRS if dst4 is QRS4 else BKS
            pi_ = 4 + qi % 4
            for j in range(8):
                TR(P, psb[pi_][:, 128 * j:128 * j + 128], src.ap[:, 128 * j:128 * j + 128], k.identb.ap,
                   [src, k.identb], [ps[pi_]])
            dview = dst4[:, :, x, :]
            sview = psb[pi_].rearrange("p (j t) -> p j t", t=128)
            if qi % 2 == 0:
                ACTF(P, dview, sview, AF.Identity, [ps[pi_]], [dstbuf])
            else:
                CP(P, "dve", dview, sview, [ps[pi_]], [dstbuf])
        msk = MSK.ap[:, 512 * d:512 * d + 512]
        ncol = 128 if state_only else 256
        for h in range(16):
            j, e = h // 2, h % 2
            pb = ps[h % 4]
            rhs = QRS4[64 * e:64 * e + 64, j, :, :] if not state_only else QRS4[64 * e:64 * e + 64, j, 0:1, :]
            MM(P, pb.ap[:, 0:ncol], BKS4[64 * e:64 * e + 64, j, 0, :], rhs, True, True, [BKS, QRS], [pb])
            MM(P, pb.ap[:, 256:256 + ncol], BKS4[64 * e:64 * e + 64, j, 1, :], rhs, True, True, [BKS, QRS], [pb])
            TT(P, "dve", SC3[:, h, :], pb.ap, msk, ALU.mult, [pb, MSK], [SC])
        CP(P, "pool", Nb[0].v("p (h t) -> p h t", t=128), SC3[:, :, 0:128], [SC], [Nb[0]])
        TT(P, "dve", Tb[0].v("p (h t) -> p h t", t=128),
           k.identb.ap.unsqueeze(1).broadcast_to([128, 16, 128]), SC3[:, :, 0:128], ALU.subtract, [SC, k.identb], [Tb[0]])
        for hh in range(2):
            pi_ = 4 + hh
            for q in range(8):
                h = 8 * hh + q
                TR(P, psb[pi_][:, 128 * q:128 * q + 128], SC3[:, h, 0:128], k.identb.ap, [SC, k.identb], [ps[pi_]])
            ACTF(P, Ntb[0].ap[:, 1024 * hh:1024 * hh + 1024], psb[pi_], AF.Identity, [ps[pi_]], [Ntb[0]])
        cur = 0
        for lvl in range(1, 6):
            nxt = 1 - cur
            last = lvl == 5
            for hq in range(4):
                sl = slice(512 * hq, 512 * hq + 512)
                pbt = ps[hq % 4]
                for q in range(4):
                    h = 4 * hq + q
                    c1 = slice(128 * h, 128 * h + 128)
                    MM(P, pbt.ap[:, 128 * q:128 * q + 128], Nb[cur].ap[:, c1], Ntb[cur].ap[:, c1], True, True,
                       [Nb[cur], Ntb[cur]], [pbt])
                ACTF(P, Ntb[nxt].ap[:, sl], pbt.ap, AF.Identity, [pbt], [Ntb[nxt]])
                if not last:
                    pbn = ps[4 + hq % 4]
                    for q in range(4):
                        h = 4 * hq + q
                        c1 = slice(128 * h, 128 * h + 128)
                        MM(P, pbn.ap[:, 128 * q:128 * q + 128], Ntb[cur].ap[:, c1], Nb[cur].ap[:, c1], True, True,
                           [Nb[cur], Ntb[cur]], [pbn])
                    ACTF(P, Nb[nxt].ap[:, sl], pbn.ap, AF.Identity, [pbn], [Nb[nxt]])
            for hq in range(4):
                sl = slice(512 * hq, 512 * hq + 512)
                pb = ps[4 + hq % 4] if last else ps[hq % 4]
                for q in range(4):
                    h = 4 * hq + q
                    c1 = slice(128 * h, 128 * h + 128)
                    MM(P, pb.ap[:, 128 * q:128 * q + 128], Ntb[nxt].ap[:, c1], Tb[cur].ap[:, c1], True, True,
                       [Ntb[nxt], Tb[cur]], [pb])
                TT(P, "dve", Tb[nxt].ap[:, sl], pb.ap, Tb[cur].ap[:, sl], ALU.add, [pb, Tb[cur]], [Tb[nxt]])
            cur = nxt
        MTb = Tb[cur]
        MT3 = MTb.v("p (h t) -> p h t", t=128)
        for flag in ("xa", "xb", "xc", "xd"):
            if flag in k.debug:
                for h in range(16):
                    j, e = h // 2, h % 2
                    es = slice(64 * e, 64 * e + 64)
                    pb = ps[h // 8]
                    oc = slice(64 * (h % 8), 64 * (h % 8) + 64)
                    if flag == "xa":
                        MM(P, pb.ap[:, oc], BKS4[es, j, 0, :], BKS4[es, j, 1, 0:64], True, True, [BKS], [pb])
                    elif flag == "xb":
                        MM(P, pb.ap[:, oc], QRS4[es, j, 0, :], BKS4[es, j, 1, 0:64], True, True, [BKS, QRS], [pb])
                    elif flag == "xc":
                        MM(P, pb.ap[:, oc], QRS4[es, j, 0, :], Hb3[es, j, :], True, True, [QRS, Hbd], [pb])
                    elif flag == "xd":
                        MM(P, pb.ap[:, oc], BKS4[es, j, 0, :], Hb3[es, j, :], True, True, [BKS, Hbd], [pb])
                for hf in range(2):
                    ACTF(P, Wsb.ap[:, 512 * hf:512 * hf + 512], ps[hf].ap, AF.Identity, [ps[hf]], [Wsb])
        if "rwstop6" in k.debug:
            return
        for ch in chunks:
            rs = slice(64 * ch, 64 * ch + 64)
            Uc, Vc = Usbc[ch], VBc[ch]
            for h in range(16):
                j, e = h // 2, h % 2
                es = slice(64 * e, 64 * e + 64)
                pb = ps[h // 8]
                oc = slice(64 * (h % 8), 64 * (h % 8) + 64)
                MM(P, pb.ap[:, oc], QRS4[es, j, 0, :], Hb3[es, j, :], True, False, [QRS, Hbd], [pb])
                MM(P, pb.ap[:, oc], SC3[:, h, 256:384], VB.ap[:, 64 * h:64 * h + 64], False, True, [SC, VB], [pb])
            for hf in range(2):
                ACTF(P, Wsb.ap[rs, 512 * hf:512 * hf + 512], ps[hf].ap[rs, :], AF.Identity, [ps[hf]], [Wsb])
            if "rwstop7" in k.debug:
                return
            for h in range(16):
                pb = ps[2 + h // 8]
                oc = slice(64 * (h % 8), 64 * (h % 8) + 64)
                MM(P, pb.ap[:, oc], MT3[:, h, :], Wsb.ap[:, 64 * h:64 * h + 64], True, True, [MTb, Wsb], [pb])
            for hf in range(2):
                ACTF(P, Uc.ap[rs, 512 * hf:512 * hf + 512], ps[2 + hf].ap[rs, :], AF.Identity, [ps[2 + hf]], [Uc], scale=-1.0)
            if not state_only:
                for h in range(16):
                    j, e = h // 2, h % 2
                    es = slice(64 * e, 64 * e + 64)
                    pb = ps[4 + h // 8]
                    oc = slice(64 * (h % 8), 64 * (h % 8) + 64)
                    MM(P, pb.ap[:, oc], QRS4[es, j, 1, :], Hb3[es, j, :], True, False, [QRS, Hbd], [pb])
                    MM(P, pb.ap[:, oc], SC3[:, h, 128:256], Uc.ap[:, 64 * h:64 * h + 64], False, False, [SC, Uc], [pb])
                    MM(P, pb.ap[:, oc], SC3[:, h, 384:512], VB.ap[:, 64 * h:64 * h + 64], False, True, [SC, VB], [pb])
                for hf in range(2):
                    CP(P, "dve", Osb.ap[rs, 512 * hf:512 * hf + 512], ps[4 + hf].ap[rs, :], [ps[4 + hf]], [Osb])
            for j in range(8):
                pb = ps[6 + j // 4]
                oc = slice(128 * (j % 4), 128 * (j % 4) + 128)
                jc = slice(128 * j, 128 * j + 128)
                MM(P, pb.ap[:, oc], BH.ap[:, jc], Uc.ap[:, jc], True, False, [BH, Uc], [pb])
                MM(P, pb.ap[:, oc], KH.ap[:, jc], Vc.ap[:, jc], False, True, [KH, Vc], [pb])
            for e in range(2):
                es = slice(64 * e, 64 * e + 64)
                TT(P, "dve", Hd3[es, :, :], Hd3[es, :, :], gC3[es, :, ch].unsqueeze(2).broadcast_to([64, 8, 64]), ALU.mult,
                   [Hd, gC], [Hd])
                for hf in range(2):
                    src = ps[6 + hf].ap[es, :].rearrange("p (j e v) -> p j e v", e=2, v=64)[:, :, e, :]
                    TT(P, "dve", Hd3[es, 4 * hf:4 * hf + 4, :], Hd3[es, 4 * hf:4 * hf + 4, :], src, ALU.add,
                       [Hd, ps[6 + hf]], [Hd])
            CP(P, "pool", Hbd.ap, Hd.ap, [Hd], [Hbd])
        if state_only:
            return
        o = op_index(i)
        if not finalize:
            P.dma(lambda e: e.dma_start(out=OB.ap[128 * o:128 * o + 128, :], in_=Osb.ap), reads=[Osb],
                  writes=[OB.rows(128 * o, 128 * o + 128)])
            CP(P, "pool", sbst.ap[:, 16 * o:16 * o + 16], sbc.ap, [sbc], [sbst])
            return
        P.dma(lambda e: e.dma_start(out=obuf.ap, in_=OB.ap[128 * o:128 * o + 128, :]),
              reads=[OB.rows(128 * o, 128 * o + 128)], writes=[obuf])
        P.dma(lambda e: e.dma_start(out=gabuf.ap, in_=GA.ap[128 * o:128 * o + 128, :]),
              reads=[GA.rows(128 * o, 128 * o + 128)], writes=[gabuf])
        TT(P, "dve", X1.ap, Osb.ap, obuf.ap, ALU.add, [Osb, obuf], [X1])
        RED(P, s16.ap, g64(X1.ap), [X1], [s16])
        P.op("dve", lambda e: e.tensor_scalar(out=s16.ap, in0=s16.ap, scalar1=-1.0 / 64, scalar2=None, op0=ALU.mult),
             reads=[s16], writes=[s16])
        TT(P, "dve", g64(X1.ap), g64(X1.ap), b64(s16.ap), ALU.add, [X1, s16], [X1])
        ACTF(P, X2.ap, X1.ap, AF.Square, [X1], [X2])
        RED(P, s16.ap, g64(X2.ap), [X2], [s16])
        ACTF(P, s16.ap, s16.ap, AF.Ln, [s16], [s16], scale=1.0 / 64, bias=GN_EPS)
        ACTF(P, s16.ap, s16.ap, AF.Exp, [s16], [s16], scale=-0.5)
        TT(P, "dve", g64(X1.ap), g64(X1.ap), b64(s16.ap), ALU.mult, [X1, s16], [X1])
        TT(P, "pool", X1.ap, X1.ap, lwb.ap, ALU.mult, [X1, lwb], [X1])
        TT(P, "pool", X1.ap, X1.ap, lbb.ap, ALU.add, [X1, lbb], [X1])
        TT(P, "dve", sbc.ap, sbc.ap, sbst.ap[:, 16 * o:16 * o + 16], ALU.add, [sbc, sbst], [sbc])
        TT(P, "dve", g64(X2.ap), g64(pav), b64(sbc.ap), ALU.mult, [pa, sbc], [X2])
        TT(P, "pool", X1.ap, X1.ap, X2.ap, ALU.add, [X1, X2], [X1])
        TT(P, "dve", X1.ap, X1.ap, gabuf.ap, ALU.mult, [X1, gabuf], [X1])
        if "YA" in k.debug:
            P.dma(lambda e: e.dma_start(out=outs["YA"][128 * o:128 * o + 128, :], in_=X1.ap), reads=[X1], is_output=True)
        for hf in range(2):
            pb = ps[4 + hf]
            for q in range(4):
                kk_ = 4 * hf + q
                TR(P, pb.ap[:, 128 * q:128 * q + 128], X1.ap[:, 128 * kk_:128 * kk_ + 128], k.ident.ap, [X1, k.ident], [pb])
            ACTF(P, ytb.ap[:, 512 * hf:512 * hf + 512], pb.ap, AF.Identity, [pb], [ytb])
        P.dma(lambda e: e.dma_start(out=YT.ap[0:1024, 128 * o:128 * o + 128].rearrange("(kk p) t -> p kk t", p=128),
                                    in_=ytb.v("p (kk t) -> p kk t", t=128)), reads=[ytb], writes=[YT.name])

    visit.n = 0

    def init_state(d, from_input):
        if from_input:
            P.dma(lambda e: e.dma_start(out=H[d].ap, in_=ins["st_in"][:, 512 * d:512 * d + 512]), writes=[H[d]])
        else:
            P.op("pool", lambda e: e.memset(H[d].ap, 0.0), writes=[H[d]])
        CP(P, "pool", Hb[d].ap, H[d].ap, [H[d]], [Hb[d]])

    only = [x for x in k.debug if x.startswith("rw_")]
    do_sample = (not only) or ("rw_sample" in only)
    do_prompt = (not only) or ("rw_prompt" in only)
    if do_sample:
        init_state(1, True)
        for i in range(NT_S - 1, NT_OWN - 1, -1):
            visit(i, 1, True, (1, 0), False)
        for i in range(NT_OWN - 1, -1, -1):
            visit(i, 1, False, (1, 0), False)
        init_state(0, True)
        for i in range(NT_OWN):
            visit(i, 0, False, (0, 1), True)
    if do_prompt:
        for sq in range(2):
            t0 = NT_S + 2 * sq
            init_state(1, False)
            for i in (t0 + 1, t0):
                visit(i, 1, False, (1, 0), False)
            P.dma(lambda e, sq=sq: e.dma_start(out=outs["st_out"][sq, 1, :, :], in_=H[1].ap), reads=[H[1]], is_output=True)
            init_state(0, False)
            for i in (t0, t0 + 1):
                visit(i, 0, False, (0, 1), True)
            P.dma(lambda e, sq=sq: e.dma_start(out=outs["st_out"][sq, 0, :, :], in_=H[0].ap), reads=[H[0]], is_output=True)
    A.release(m0)


def emit_attn(k, ins, outs, QT, KT, VS, GBT, YT):
    P, A = k.P, k.A
    ps = k.ps
    m0 = A.mark()
    prm = A.f32(257)
    lam4 = A.f32(4)
    neglam = A.f32(1)
    subw = A.f32(1)
    onesb = A.bf(128)
    P.dma(lambda e: e.dma_start(out=prm.ap, in_=ins["at_prm"][:, :]), writes=[prm])
    P.op("pool", lambda e: e.memset(onesb.ap, 1.0), writes=[onesb])
    tmp = A.f32(128)
    l4 = prm.ap[:, 0:256].rearrange("p (a d) -> p a d", d=64)
    TT(P, "dve", tmp.v("p (a d) -> p a d", d=64), l4[:, 0:4:2, :], l4[:, 1:4:2, :], ALU.mult, [prm], [tmp])
    P.op("dve", lambda e: e.tensor_reduce(out=lam4.ap[:, 0:2], in_=tmp.v("p (a d) -> p a d", d=64), axis=AX.X, op=ALU.add),
         reads=[tmp], writes=[lam4])
    ACTF(P, lam4.ap[:, 0:2], lam4.ap[:, 0:2], AF.Exp, [lam4], [lam4])
    STT(P, neglam.ap, lam4.ap[:, 1:2], -LAM_INIT, lam4.ap[:, 0:1], ALU.add, ALU.subtract, [lam4], [neglam])
    P.op("dve", lambda e: e.tensor_scalar(out=subw.ap, in0=prm.ap[:, 256:257], scalar1=1.0 - LAM_INIT, scalar2=None, op0=ALU.mult),
         reads=[prm], writes=[subw])
    NKMAX = 34
    kT = [A.bf(NKMAX * 128) for _ in range(2)]
    vv = [A.bf(NKMAX * 128) for _ in range(2)]
    qT = [A.bf(2176) for _ in range(2)]
    pT = [A.bf(512) for _ in range(4)]
    r0b, t0b, t1b, gbt = A.f32(512), A.f32(512), A.f32(512), A.f32(512)
    sqb = A.bf(512)
    yb = A.bf(512)
    sample_keys = [128 * i for i in range(NT_S)] + [T_ALL, T_ALL + 128]
    groups = [(0, 128 * NT_OWN, sample_keys)]
    for sq in range(2):
        groups.append((128 * NT_OWN + 256 * sq, 256, [128 * NT_S + 256 * sq, 128 * NT_S + 256 * sq + 128]))
    only = [x for x in k.debug if x.startswith("at_")]
    if "at_prompt" in only:
        groups = groups[1:]
    if "at_sample" in only:
        groups = groups[:1]
    it = 0
    for (q0, nq, keys) in groups:
        nk = len(keys)
        for h in range(1 if "at_h1" in k.debug else 8):
            it += 1
            kTb, vb, qTb = kT[it % 2], vv[it % 2], qT[it % 2]
            runs = []
            for ki, kr in enumerate(keys):
                if runs and runs[-1][1] + runs[-1][2] == kr:
                    runs[-1][2] += 128
                else:
                    runs.append([ki, kr, 128])
            for (ki, kr, ln) in runs:
                P.dma(lambda e, kTb=kTb, ki=ki, kr=kr, ln=ln, h=h: e.dma_start(
                    out=kTb.ap[:, 128 * ki:128 * ki + ln], in_=KT.ap[h, :, kr:kr + ln]), reads=[KT.name], writes=[kTb])
                P.dma(lambda e, vb=vb, ki=ki, kr=kr, ln=ln, h=h: e.dma_start(
                    out=vb.ap[:, 128 * ki:128 * ki + ln].rearrange("p (t c) -> p t c", c=128),
                    in_=VS.ap[kr:kr + ln, 128 * h:128 * h + 128].rearrange("(t p) c -> p t c", p=128)),
                    reads=[VS.rows(kr, kr + ln)], writes=[vb])
            P.dma(lambda e, qTb=qTb, q0=q0, nq=nq, h=h: e.dma_start(out=qTb.ap[:, 0:nq], in_=QT.ap[h, :, q0:q0 + nq]),
                  reads=[QT.name], writes=[qTb])
            for qs in range(0, nq, 512):
                qn = min(512, nq - qs)
                def emit_qk(ki):
                    for c in range(2):
                        sb_ = ps[2 * (ki % 2) + c]
                        cs_ = slice(64 * c, 64 * c + 64)
                        MM(P, sb_.ap[:, 0:qn], kTb.ap[cs_, 128 * ki:128 * ki + 128], qTb.ap[cs_, qs:qs + qn], True, True,
                           [kTb, qTb], [sb_])

                emit_qk(0)
                for ki in range(nk):
                    if ki + 1 < nk:
                        emit_qk(ki + 1)
                    for c in range(2):
                        sb_ = ps[2 * (ki % 2) + c]
                        pb_ = pT[2 * (ki % 2) + c]
                        ACTF(P, pb_.ap[:, 0:qn], sb_.ap[:, 0:qn], AF.Exp, [sb_], [pb_], scale=0.125)
                    for c in range(2):
                        pb_ = pT[2 * (ki % 2) + c]
                        MM(P, ps[4 + c].ap[:, 0:qn], vb.ap[:, 128 * ki:128 * ki + 128], pb_.ap[:, 0:qn], ki == 0, ki == nk - 1,
                           [vb, pb_], [ps[4 + c]])
                        MM(P, ps[6 + c].ap[:, 0:qn], onesb.ap, pb_.ap[:, 0:qn], ki == 0, ki == nk - 1, [onesb, pb_], [ps[6 + c]])
                P.op("dve", lambda e, qn=qn: e.reciprocal(out=r0b.ap[:, 0:qn], in_=ps[6].ap[:, 0:qn]), reads=[ps[6]], writes=[r0b])
                TT(P, "dve", t0b.ap[:, 0:qn], ps[4].ap[:, 0:qn], r0b.ap[:, 0:qn], ALU.mult, [ps[4], r0b], [t0b])
                P.op("dve", lambda e, qn=qn: e.reciprocal(out=r0b.ap[:, 0:qn], in_=ps[7].ap[:, 0:qn]), reads=[ps[7]], writes=[r0b])
                TT(P, "dve", t1b.ap[:, 0:qn], ps[5].ap[:, 0:qn], r0b.ap[:, 0:qn], ALU.mult, [ps[5], r0b], [t1b])
                STT(P, t0b.ap[:, 0:qn], t1b.ap[:, 0:qn], neglam.ap[:, 0:1], t0b.ap[:, 0:qn], ALU.mult, ALU.add,
                    [t1b, neglam, t0b], [t0b])
                ACTF(P, sqb.ap[:, 0:qn], t0b.ap[:, 0:qn], AF.Square, [t0b], [sqb])
                MM(P, ps[0].ap[:, 0:qn], onesb.ap, sqb.ap[:, 0:qn], True, True, [onesb, sqb], [ps[0]])
                ACTF(P, t1b.ap[:, 0:qn], ps[0].ap[:, 0:qn], AF.Ln, [ps[0]], [t1b], scale=1.0 / 128, bias=NORM_EPS)
                ACTF(P, t1b.ap[:, 0:qn], t1b.ap[:, 0:qn], AF.Exp, [t1b], [t1b], scale=-0.5)
                P.dma(lambda e, qs=qs, qn=qn, q0=q0, h=h: e.dma_start(
                    out=gbt.ap[:, 0:qn], in_=GBT.ap[128 * h:128 * h + 128, q0 + qs:q0 + qs + qn]), reads=[GBT.name], writes=[gbt])
                STT(P, t0b.ap[:, 0:qn], t0b.ap[:, 0:qn], subw.ap[:, 0:1], t1b.ap[:, 0:qn], ALU.mult, ALU.mult,
                    [t0b, subw, t1b], [t0b])
                TT(P, "dve", yb.ap[:, 0:qn], t0b.ap[:, 0:qn], gbt.ap[:, 0:qn], ALU.mult, [t0b, gbt], [yb])
                if "YB" in k.debug:
                    TT(P, "dve", t1b.ap[:, 0:qn], t0b.ap[:, 0:qn], gbt.ap[:, 0:qn], ALU.mult, [t0b, gbt], [t1b])
                    P.dma(lambda e, qs=qs, qn=qn, q0=q0, h=h: e.dma_start(
                        out=outs["YB"][128 * h:128 * h + 128, q0 + qs:q0 + qs + qn], in_=t1b.ap[:, 0:qn]), reads=[t1b], is_output=True)
                P.dma(lambda e, qs=qs, qn=qn, q0=q0, h=h: e.dma_start(
                    out=YT.ap[1024 + 128 * h:1024 + 128 * h + 128, q0 + qs:q0 + qs + qn], in_=yb.ap[:, 0:qn]),
                    reads=[yb], writes=[YT.name])
    A.release(m0)


L1_COLS = 2694


def l1_col0(o):
    if o < NT_OWN:
        return 1 + 128 * o
    if o < NT_OWN + 2:
        return 2179 + 128 * (o - NT_OWN)
    return 2436 + 128 * (o - NT_OWN - 2)


def emit_tail(k, ins, outs, YT, x_all):
    P, A, nc = k.P, k.A, k.nc
    ps = k.ps
    NO = NT_OWN + NT_PR
    m0 = A.mark()
    X1S = k.X1S
    hT1 = A.bf(8 * L1_COLS)
    hT13 = hT1.v("p (k c) -> p k c", c=L1_COLS)
    P.op("pool", lambda e: e.memset(hT1.ap, 0.0), writes=[hT1])
    m1 = A.mark()

    def outproj(w_dram, yt_cols, layer, x_get, x_put, tag):
        mm = A.mark()
        wst = [A.f32(1024) for _ in range(2)]
        wb = A.bf(16 * 1024)
        for kk in range(16):
            wsb = wst[kk % 2]
            P.dma(lambda e, wsb=wsb, kk=kk: e.dma_start(out=wsb.ap[:, 0:1024], in_=w_dram[128 * kk:128 * kk + 128, :]), writes=[wsb])
            CP(P, "pool" if kk % 2 else "dve", wb.ap[:, 1024 * kk:1024 * kk + 1024], wsb.ap[:, 0:1024], [wsb],
               [wb.cols(1024 * kk, 1024 * kk + 1024)])
        ytb = [A.bf(16 * 128) for _ in range(2)]
        xt = [A.f32(1024) for _ in range(2)]
        for o in range(NO):
            yb_ = ytb[o % 2]
            c0 = yt_cols(o)
            P.dma(lambda e, yb_=yb_, c0=c0: e.dma_start(
                out=yb_.v("p (kk t) -> p kk t", t=128), in_=YT.ap[:, c0:c0 + 128].rearrange("(kk p) t -> p kk t", p=128)),
                reads=[YT.name], writes=[yb_])
            v = 0 if o < NT_OWN else 1
            g = k.gB.ap[:, (layer * 2 + v) * 1024:(layer * 2 + v) * 1024 + 1024]
            xin = x_get(o, xt[o % 2])
            xo = x_put(o, xt[o % 2])
            for hf in range(2):
                pb = ps[2 * (o % 2) + hf]
                for kk in range(16):
                    MM(P, pb.ap, yb_.ap[:, 128 * kk:128 * kk + 128], wb.ap[:, 1024 * kk + 512 * hf:1024 * kk + 512 * hf + 512],
                       kk == 0, kk == 15, [yb_, wb], [pb])
                sl = slice(512 * hf, 512 * hf + 512)
                TT(P, "dve", xo.ap[:, sl], pb.ap, g[:, sl], ALU.mult, [pb, k.gB], [xo])
                TT(P, "pool" if hf else "dve", xo.ap[:, sl], xo.ap[:, sl], xin.ap[:, sl], ALU.add, [xo, xin], [xo])
            if tag == "final":
                P.dma(lambda e, xo=xo, o=o: e.dma_start(out=outs["y_out"][128 * o:128 * o + 128, :], in_=xo.ap), reads=[xo],
                      is_output=True)
            else:
                P.dma(lambda e, xo=xo, o=o: e.dma_start(out=X1S.ap[128 * o:128 * o + 128, :], in_=xo.ap), reads=[xo],
                      writes=[X1S.rows(128 * o, 128 * o + 128)])
            if tag != "final" and "X1" in k.debug:
                P.dma(lambda e, xo=xo, o=o: e.dma_start(out=outs["X1"][128 * o:128 * o + 128, :], in_=xo.ap), reads=[xo],
                      is_output=True)
        A.release(mm)

    def x_get0(o, buf):
        i = o if o < NT_OWN else NT_S + (o - NT_OWN)
        P.dma(lambda e, buf=buf, i=i: e.dma_start(out=buf.ap, in_=x_all[128 * i:128 * i + 128, :]), writes=[buf])
        return buf

    xo2 = [A.f32(1024) for _ in range(2)]
    outproj(ins["w_out0"], lambda o: 128 * o, 0, x_get0, lambda o, buf: xo2[o % 2], "l0")
    if "stopX1" in k.debug:
        A.release(m0)
        return
    mn = A.mark()
    emit_norm_phase(k, x_src=lambda o: X1S.ap[128 * o:128 * o + 128, :], tiles=list(range(NO)), layer=1, hT3=hT13, hT=hT1,
                    col0=l1_col0, variant=lambda o: 0 if o < NT_OWN else 1, cols_total=L1_COLS,
                    src_reg=lambda o: X1S.rows(128 * o, 128 * o + 128))
    A.release(mn)
    mm = A.mark()
    cvp = A.f32(64)
    P.dma(lambda e: e.dma_start(out=cvp.ap, in_=ins["cv_prm"][:, :]), writes=[cvp])
    wst = [A.f32(4 * 1024) for _ in range(2)]
    wbf = [A.bf(4 * 1024) for _ in range(2)]
    NB = L1_COLS + 2
    cgu = A.f32(NB)
    bgr = A.f32(NB)
    zs = A.f32(NB)
    cvb = A.f32(NB)
    cgt = A.f32(512)
    yrow = A.bf(NB)
    P.op("pool", lambda e: e.memset(cgu.ap, 0.0), writes=[cgu])
    w_in1 = ins["w_in1"]
    wins = [(c, min(512, L1_COLS - c)) for c in range(0, L1_COLS, 512)]
    for j in range(16):
        ws, wb = wst[j % 2], wbf[j % 2]
        for q in range(4):
            P.dma(lambda e, ws=ws, q=q, j=j: e.dma_start(
                out=ws.ap[:, 1024 * q:1024 * q + 1024].rearrange("p (k c) -> p k c", c=128),
                in_=w_in1[:, 2048 * q + 128 * j:2048 * q + 128 * j + 128].rearrange("(k p) c -> p k c", p=128)),
                writes=[ws.cols(1024 * q, 1024 * q + 1024)])
        CP(P, "pool", wb.ap, ws.ap, [ws], [wb])
        for (c0, nw) in wins:
            for q in range(4):
                pb = ps[4 + q]
                for kk in range(8):
                    MM(P, pb.ap[:, 0:nw], wb.ap[:, 1024 * q + 128 * kk:1024 * q + 128 * kk + 128], hT13[:, kk, c0:c0 + nw],
                       kk == 0, kk == 7, [wb, hT1], [pb])
            ACTF(P, bgr.ap[:, 1 + c0:1 + c0 + nw], ps[4].ap[:, 0:nw], AF.Identity, [ps[4]], [bgr])
            ACTF(P, cgt.ap[:, 0:nw], ps[5].ap[:, 0:nw], AF.Identity, [ps[5]], [cgt])
            TT(P, "dve", cgu.ap[:, 1 + c0:1 + c0 + nw], cgt.ap[:, 0:nw], ps[6].ap[:, 0:nw], ALU.mult, [cgt, ps[6]], [cgu])
            ACTF(P, zs.ap[:, 1 + c0:1 + c0 + nw], ps[7].ap[:, 0:nw], AF.Silu, [ps[7]], [zs])
        n = L1_COLS
        w0, w1, w2, bb = [cvp.ap[:, 16 * t_ + j:16 * t_ + j + 1] for t_ in range(4)]
        P.op("dve", lambda e, w1=w1, bb=bb, n=n: e.tensor_scalar(out=cvb.ap[:, 1:1 + n], in0=cgu.ap[:, 1:1 + n], scalar1=w1,
                                                                scalar2=bb, op0=ALU.mult, op1=ALU.add), reads=[cgu, cvp], writes=[cvb])
        STT(P, cvb.ap[:, 1:1 + n], cgu.ap[:, 0:n], w0, cvb.ap[:, 1:1 + n], ALU.mult, ALU.add, [cgu, cvp, cvb], [cvb])
        STT(P, cvb.ap[:, 1:1 + n], cgu.ap[:, 2:2 + n], w2, cvb.ap[:, 1:1 + n], ALU.mult, ALU.add, [cgu, cvp, cvb], [cvb])
        TT(P, "pool", cvb.ap[:, 1:1 + n], cvb.ap[:, 1:1 + n], bgr.ap[:, 1:1 + n], ALU.mult, [cvb, bgr], [cvb])
        TT(P, "dve", yrow.ap[:, 1:1 + n], cvb.ap[:, 1:1 + n], zs.ap[:, 1:1 + n], ALU.mult, [cvb, zs], [yrow])
        P.dma(lambda e, j=j, n=n: e.dma_start(out=YT.ap[128 * j:128 * j + 128, 0:n], in_=yrow.ap[:, 1:1 + n]), reads=[yrow],
              writes=[YT.name])
    A.release(mm)
    def x_get1(o, buf):
        P.dma(lambda e, buf=buf, o=o: e.dma_start(out=buf.ap, in_=X1S.ap[128 * o:128 * o + 128, :]),
              reads=[X1S.rows(128 * o, 128 * o + 128)], writes=[buf])
        return buf

    outproj(ins["w_out1"], l1_col0, 1, x_get1, lambda o, buf: xo2[o % 2], "final")
    A.release(m0)


def prep_core_inputs(q, inp):
    b, mir = q // 2, q % 2
    f = (lambda a: a[::-1]) if mir else (lambda a: a)
    d = {}
    xs = f(inp["x_sample"][b])
    xp0 = f(inp["x_prompt"][2 * q])
    xp1 = f(inp["x_prompt"][2 * q + 1])
    d["x_all"] = np.ascontiguousarray(np.concatenate([xs, xp0, xp1], 0))
    cvv = np.stack([inp["c"][b], inp["c_ctx"]], -1)
    d["cv"] = np.ascontiguousarray(cvv.reshape(8, 128, 2).transpose(1, 0, 2).reshape(128, 16))
    d["normw_fm"] = np.ascontiguousarray(inp["norm_w"].reshape(2, 8, 128).transpose(2, 0, 1).reshape(128, 16))
    d["ada_w"] = inp["ada_w"]
    d["adab_fm"] = np.ascontiguousarray(inp["ada_b"].reshape(2, 24, 128).transpose(2, 0, 1).reshape(128, 48))
    d["adab_g"] = np.ascontiguousarray(np.broadcast_to(inp["ada_b"][:, 2048:].reshape(1, 2048), (128, 2048)))
    w = inp["e_w_in"][0]
    mu = inp["e_mu"][0]
    if mir:
        perm = np.arange(EVEN_IN)
        perm[3072:3136], perm[3136:3200] = np.arange(3136, 3200), np.arange(3072, 3136)
        perm[3200:3264], perm[3264:3328] = np.arange(3264, 3328), np.arange(3200, 3264)
        w = w[:, perm]
        mu = mu[perm[:3328]]
    d["w_in0"] = np.ascontiguousarray(w)
    d["mu_b"] = np.ascontiguousarray(np.broadcast_to(mu.reshape(1, -1), (128, 3328)))
    qk = np.concatenate([np.tile(inp["e_q_norm"][0], 8), np.tile(inp["e_k_norm"][0], 8)])
    d["qkw_b"] = np.ascontiguousarray(np.broadcast_to(qk.reshape(1, -1), (128, 1024)))
    t = np.arange(4096)
    row = (t // 64).astype(np.float32)
    col = (t % 64).astype(np.float32)
    inv = (10000.0 ** (-np.arange(0, 32, 2, dtype=np.float32) / 32)).astype(np.float32)
    ar, ac = row[:, None] * inv, col[:, None] * inv
    cos = np.concatenate([np.cos(ar), np.cos(ar), np.cos(ac), np.cos(ac)], 1)
    sin = np.concatenate([-np.sin(ar), np.sin(ar), -np.sin(ac), np.sin(ac)], 1)
    d["rope_cs"] = np.ascontiguousarray(f(np.concatenate([cos, sin], 1).astype(np.float32)))
    dirs = (1, 0) if mir else (0, 1)
    rep = lambda v: np.broadcast_to(np.asarray(v, np.float32).reshape(1, -1), (128, v.size))
    d["rw_prm"] = np.ascontiguousarray(np.concatenate(
        [rep(inp["e_k_k"][0]), rep(inp["e_k_a"][0]), rep(inp["e_r_k"][0].reshape(-1)),
         rep(inp["e_lnx_w"][0]), rep(inp["e_lnx_b"][0])], 1))
    up = np.zeros((65, 4096), np.float32)
    for fd, td in enumerate(dirs):
        up[:64, 1024 * fd:1024 * fd + 1024] = inp["e_w_up"][0, td]
        up[64, 1024 * fd:1024 * fd + 1024] = inp["e_w0"][0, td]
        up[:64, 2048 + 1024 * fd:2048 + 1024 * fd + 1024] = inp["e_a_up"][0, td]
        up[64, 2048 + 1024 * fd:2048 + 1024 * fd + 1024] = inp["e_a0"][0, td]
    d["rw_up"] = up
    d["rw_const"] = rwkv_consts()
    sts = (inp["state_rwkv_fwd"][b, 0], inp["state_rwkv_bwd"][b, 0])
    st = np.zeros((128, 1024), np.float32)
    for fd, td in enumerate(dirs):
        hh = sts[td].reshape(8, 2, 64, 64).transpose(1, 3, 0, 2).reshape(128, 512)
        st[:, 512 * fd:512 * fd + 512] = hh
    d["st_in"] = st
    d["ctx_k"] = np.ascontiguousarray(inp["cache_diff_k"][b, 0].reshape(256, 1024))
    d["ctx_v"] = np.ascontiguousarray(inp["cache_diff_v"][b, 0].reshape(256, 1024))
    d["at_prm"] = np.ascontiguousarray(np.concatenate(
        [rep(inp["e_lambda"][0].reshape(-1)), inp["e_subln"][0].reshape(128, 1)], 1).astype(np.float32))
    d["w_out0"] = inp["e_w_out"][0]
    d["w_in1"] = inp["o_w_in"][0]
    d["w_out1"] = inp["o_w_out"][0]
    cw = inp["o_conv_w"][0]
    if mir:
        cw = cw[::-1]
    cvp = np.concatenate([cw.reshape(3, 16, 128), inp["o_conv_b"][0].reshape(1, 16, 128)], 0)
    d["cv_prm"] = np.ascontiguousarray(cvp.transpose(2, 0, 1).reshape(128, 64))
    return d


def rwkv_consts():
    idx = np.arange(128)
    ch, pos = idx // 64, idx % 64
    same = ch[:, None] == ch[None, :]
    out = np.zeros((128, 2050), np.float32)
    for dd in range(2):
        before = (pos[:, None] < pos[None, :]) if dd == 0 else (pos[:, None] > pos[None, :])
        eq = pos[:, None] == pos[None, :]
        incl = same & (before | eq)
        excl = same & before
        suf = same & before.T
        for m, mat in enumerate((incl, excl, suf)):
            out[:, (dd * 3 + m) * 128:(dd * 3 + m) * 128 + 128] = DECAY_C * mat
        msk = np.concatenate([excl, incl], 1).astype(np.float32)
        out[:, 770 + 512 * dd:770 + 512 * dd + 256] = msk
        out[:, 770 + 512 * dd + 256:770 + 512 * dd + 512] = msk
        out[:, 1794 + 128 * dd:1794 + 128 * dd + 128] = excl.T
    out[:, 768] = DECAY_C * (ch == 0)
    out[:, 769] = DECAY_C * (ch == 1)
    return out


def assemble(q, r, outs6):
    y_p, y_s, n_sf, n_sb, n_k, n_v = outs6
    b, mir = q // 2, q % 2
    S = y_p.shape[1]
    yo = np.asarray(r["y_out"])
    if mir:
        y_s[b, 2048:] = yo[:2048][::-1]
    else:
        y_s[b, :2048] = yo[:2048]
    so = np.asarray(r["st_out"])
    for j in range(2):
        yp = yo[2176 + 256 * j:2176 + 256 * j + 256]
        kc = np.asarray(r["kc_out"])[256 * j:256 * j + 256]
        vc = np.asarray(r["vc_out"])[256 * j:256 * j + 256]
        if mir:
            yp, kc, vc = yp[::-1], kc[::-1], vc[::-1]
        y_p[2 * q + j] = yp
        n_k[2 * q + j, 0] = kc.reshape(S, 8, 2, 64)
        n_v[2 * q + j, 0] = vc.reshape(S, 8, 128)
        for fd in range(2):
            st = so[j, fd].reshape(2, 64, 8, 64).transpose(2, 0, 3, 1).reshape(16, 64, 64)
            td = (1 - fd) if mir else fd
            (n_sf if td == 0 else n_sb)[2 * q + j, 0] = st


def kernel(**inputs):
    inp = {k_: np.asarray(v) for k_, v in inputs.items()}
    nc, ins, outs = build_program()
    in_maps = []
    for q in range(8):
        d = prep_core_inputs(q, inp)
        in_maps.append({n: d[n] for n in ins})
    res = run_bass_kernel_spmd(nc, in_maps, core_ids=list(range(8)))
    B, S = inp["x_prompt"].shape[0], inp["x_prompt"].shape[1]
    outs6 = (np.zeros(inp["x_prompt"].shape, np.float32), np.zeros(inp["x_sample"].shape, np.float32),
             np.zeros((B, 1, 16, 64, 64), np.float32), np.zeros((B, 1, 16, 64, 64), np.float32),
             np.zeros((B, 1, S, 8, 2, 64), np.float32), np.zeros((B, 1, S, 8, 128), np.float32))
    for q in range(8):
        assemble(q, res.results[q], outs6)
    return outs6
```

```python
import contextlib
import numpy as np
import concourse.bass as bass
import concourse.mybir as mybir
from concourse.bass_utils import run_bass_kernel_spmd

F32 = mybir.dt.float32
BF16 = mybir.dt.bfloat16
AF = mybir.ActivationFunctionType
ALU = mybir.AluOpType
AX = mybir.AxisListType

ENGS = ("pe", "act", "dve", "pool", "sp")
N_DMA_SEMS = 32
LOADS_ON_ACT = True
BIG = 1 << 40

D = 1024
NT_OWN, NT_OTH, NT_PR = 17, 15, 4
NT_S = NT_OWN + NT_OTH
NT = NT_S + NT_PR
T_ALL = NT * 128
T_OP = (NT_OWN + NT_PR) * 128
HT_COLS = 4612
EVEN_IN = 8448
NORM_EPS = 1e-6
GN_EPS = 64e-5
LAM_INIT = 0.8 - 0.6
DECAY_C = -0.6065306597126334


def tile_col0(i):
    if i < NT_S:
        return 1 + 128 * i
    if i < NT_S + 2:
        return 4098 + 128 * (i - NT_S)
    return 4355 + 128 * (i - NT_S - 2)


def op_index(i):
    return i if i < NT_OWN else NT_OWN + (i - NT_S)


OUT_TILES = list(range(NT_OWN)) + list(range(NT_S, NT))


class Prog:
    def __init__(self, nc):
        self.nc = nc
        self.streams = {e: [] for e in ENGS}
        self.count = {e: 0 for e in ENGS}
        self.seen = {e: {} for e in ENGS}
        self.regions = {}
        self.dma_cnt = [0] * N_DMA_SEMS
        self.dma_rr = 0
        self.out_tokens = []

    def _deps(self, reads, writes):
        deps = []
        for (name, lo, hi) in reads:
            for rec in self.regions.get(name, ()):
                if rec[0] < hi and lo < rec[1]:
                    deps.extend(rec[2].items())
                    if name.startswith("ps"):
                        deps.extend(rec[3].values())
        for (name, lo, hi) in writes:
            for rec in self.regions.get(name, ()):
                if rec[0] < hi and lo < rec[1]:
                    deps.extend(rec[2].items())
                    deps.extend(rec[3].values())
        return deps

    def _commit(self, reads, writes, token, rkey):
        for (name, lo, hi) in reads:
            lst = self.regions.setdefault(name, [])
            hit = False
            for rec in lst:
                if rec[0] < hi and lo < rec[1]:
                    rec[3][rkey] = token
                    hit = True
            if not hit:
                lst.append([lo, hi, {}, {rkey: token}])
        for (name, lo, hi) in writes:
            lst = self.regions.setdefault(name, [])
            keep = []
            wr = {token[0]: token[1]}
            dram = name not in ("ar",) and not name.startswith("ps")
            for rec in lst:
                if rec[1] <= lo or hi <= rec[0]:
                    keep.append(rec)
                    continue
                if dram:
                    for k_, v_ in rec[2].items():
                        if v_ > wr.get(k_, 0):
                            wr[k_] = v_
                if rec[0] < lo:
                    keep.append([rec[0], lo, dict(rec[2]), dict(rec[3])])
                if hi < rec[1]:
                    keep.append([hi, rec[1], dict(rec[2]), dict(rec[3])])
            keep.append([lo, hi, wr, {}])
            self.regions[name] = keep

    def _waits(self, eng, deps):
        best = {}
        for (k, v) in deps:
            if v > best.get(k, 0):
                best[k] = v
        out = []
        for k, v in best.items():
            if eng == "pe" and k == "pe":
                continue
            if self.seen[eng].get(k, 0) >= v:
                continue
            self.seen[eng][k] = v
            out.append((k, v))
        return out

    @staticmethod
    def _norm(regs):
        out = []
        for r in regs:
            if isinstance(r, str):
                out.append((r, 0, BIG))
            elif isinstance(r, Buf):
                out.append(r.reg)
            else:
                out.append(r)
        return out

    def op(self, eng, fn, reads=(), writes=()):
        reads = self._norm(reads)
        writes = self._norm(writes)
        waits = self._waits(eng, self._deps(reads, writes))
        self.count[eng] += 1
        token = (eng, self.count[eng])
        self.streams[eng].append(("op", waits, fn))
        self._commit(reads, writes, token, eng)
        return token

    def dma(self, fn, reads=(), writes=(), is_output=False, queue="sp"):
        reads = self._norm(reads)
        writes = self._norm(writes)
        queue = "act" if (LOADS_ON_ACT and any(w[0] == "ar" for w in writes)) else "sp"
        deps = self._deps(reads, writes)
        s = self.dma_rr
        self.dma_rr = (self.dma_rr + 1) % N_DMA_SEMS
        key = "dma%d" % s
        if self.dma_cnt[s] > 0:
            deps.append((key, self.dma_cnt[s]))
        waits = self._waits(queue, deps)
        self.dma_cnt[s] += 16
        token = (key, self.dma_cnt[s])
        self.streams[queue].append(("dma", waits, fn, s))
        self._commit(reads, writes, token, key)
        if is_output:
            self.out_tokens.append(token)
        return token

    def run(self):
        nc = self.nc
        fin = self._waits("sp", list(self.out_tokens))
        self.streams["sp"].append(("wait", fin))
        with contextlib.ExitStack() as st:
            sems = {}
            for e in ENGS:
                sems[e] = st.enter_context(nc.semaphore("s_" + e))
            for i in range(N_DMA_SEMS):
                sems["dma%d" % i] = st.enter_context(nc.semaphore("s_dma%d" % i))
            block = st.enter_context(nc.Block())

            def player(ename):
                def play(eng):
                    for item in self.streams[ename]:
                        for (k, v) in item[1]:
                            eng.wait_ge(sems[k], v)
                        if item[0] == "op":
                            item[2](eng).then_inc(sems[ename], 1)
                        elif item[0] == "dma":
                            item[2](eng).then_inc(sems["dma%d" % item[3]], 16)
                return play

            block.tensor(player("pe"))
            block.scalar(player("act"))
            block.vector(player("dve"))
            block.gpsimd(player("pool"))
            block.sync(player("sp"))


class Buf:
    def __init__(self, ap, reg):
        self.ap = ap
        self.reg = reg

    def cols(self, a, b):
        name, lo, hi = self.reg
        if name == "ar":
            if self.ap.dtype == BF16:
                r = (name, lo + a // 2, lo + (b + 1) // 2)
            else:
                r = (name, lo + a, lo + b)
        else:
            r = self.reg
        return Buf(self.ap[:, a:b], r)

    def v(self, s, **kw):
        return self.ap.rearrange(s, **kw)


class Arena:
    def __init__(self, ap, size):
        self.ap = ap
        self.size = size
        self.top = 0

    def f32(self, n):
        lo = self.top
        self.top += n
        assert self.top <= self.size, ("arena overflow", self.top, self.size)
        return Buf(self.ap[:, lo:lo + n], ("ar", lo, lo + n))

    def bf(self, n):
        nf = (n + 1) // 2
        lo = self.top
        self.top += nf
        assert self.top <= self.size, ("arena overflow", self.top, self.size)
        return Buf(self.ap[:, lo:lo + nf].bitcast(BF16)[:, 0:n], ("ar", lo, lo + nf))

    def mark(self):
        return self.top

    def release(self, m):
        self.top = m


class DramBuf:
    def __init__(self, name, ap):
        self.name = name
        self.ap = ap

    def rows(self, a, b):
        return (self.name, a, b)


ARENA_F32 = 52800


class K:
    pass


def build_program(debug=()):
    nc = bass.Bass("TRN2", target_bir_lowering=False)
    k = K()
    k.nc = nc
    k.debug = debug
    ins = {}
    outs = {}

    def din(name, shape, dt=F32):
        ins[name] = nc.dram_tensor(name, list(shape), dt, kind="ExternalInput").ap()
        return ins[name]

    def dout(name, shape, dt=F32):
        outs[name] = nc.dram_tensor(name, list(shape), dt, kind="ExternalOutput").ap()
        return outs[name]

    def dscr(name, shape, dt=F32):
        kind = "ExternalOutput" if name in debug else "Internal"
        t = nc.dram_tensor(name, list(shape), dt, kind=kind).ap()
        if name in debug:
            outs[name] = t
        return DramBuf(name, t)

    x_all = din("x_all", [T_ALL, D])
    cv = din("cv", [128, 16])
    normw_fm = din("normw_fm", [128, 16])
    ada_w = din("ada_w", [2, D, 3072])
    adab_fm = din("adab_fm", [128, 48])
    adab_g = din("adab_g", [128, 2048])
    w_in0 = din("w_in0", [D, EVEN_IN])
    mu_b = din("mu_b", [128, 3328])
    qkw_b = din("qkw_b", [128, 1024])
    rope_cs = din("rope_cs", [4096, 128])
    rw_prm = din("rw_prm", [128, 5120])
    rw_up = din("rw_up", [65, 4096])
    rw_const = din("rw_const", [128, 2050])
    st_in = din("st_in", [128, 1024])
    ctx_k = din("ctx_k", [256, 1024])
    ctx_v = din("ctx_v", [256, 1024])
    at_prm = din("at_prm", [128, 257])
    w_out0 = din("w_out0", [2048, 1024])
    w_in1 = din("w_in1", [D, 8192])
    w_out1 = din("w_out1", [2048, 1024])
    cv_prm = din("cv_prm", [128, 64])
    OB = dscr("OB", [T_OP, 1024])
    k.X1S = dscr("X1S", [T_OP, 1024])
    YT = dscr("YT", [2048, 2816], BF16)
    PA = dscr("PA", [T_ALL, 3328])
    GA = dscr("GA", [T_OP, 1024])
    QT = dscr("QT", [8, 128, T_OP], BF16)
    KT = dscr("KT", [8, 128, T_ALL + 256], BF16)
    VS = dscr("VS", [T_ALL + 256, 1024], BF16)
    GBT = dscr("GBT", [1024, T_OP])
    kc_out = dout("kc_out", [512, 1024])
    vc_out = dout("vc_out", [512, 1024])
    st_out = dout("st_out", [2, 2, 128, 512])
    y_out = dout("y_out", [T_OP, 1024])
    if "YB" in debug:
        dout("YB", [1024, T_OP])
    if "X1" in debug:
        dout("X1", [T_OP, 1024])
    if "YA" in debug:
        dout("YA", [T_OP, 1024])
    if "dumpQ" in debug:
        dout("dQ", [128, 12288])

    st = contextlib.ExitStack()
    with st:
        ar_t = st.enter_context(nc.sbuf_tensor("ar", [128, ARENA_F32], F32))
        psb = [st.enter_context(nc.psum_tensor("ps%d" % i, [128, 512], F32)) for i in range(8)]
        k.ps = [Buf(psb[i][:, :], ("ps%d" % i, 0, BIG)) for i in range(8)]
        A = Arena(ar_t, ARENA_F32)
        P = Prog(nc)
        k.P, k.A = P, A

        ident = A.f32(128)
        identb = A.bf(128)
        P.op("pool", lambda e: e.memset(ident.ap, 1.0), writes=[ident])
        P.op("pool", lambda e: e.affine_select(out=ident.ap, in_=ident.ap, pattern=[[-1, 128]],
                                               compare_op=ALU.is_equal, fill=0.0, base=0, channel_multiplier=1),
             reads=[ident], writes=[ident])
        P.op("dve", lambda e: e.tensor_copy(identb.ap, ident.ap), reads=[ident], writes=[identb])
        k.ident, k.identb = ident, identb

        cvt = A.f32(16)
        nwt = A.f32(16)
        abf = A.f32(48)
        modS = A.f32(32)
        modB = A.f32(32)
        gB = A.f32(4096)
        k.modS, k.modB, k.gB = modS, modB, gB
        m0 = A.mark()
        abg = A.f32(2048)
        screp = A.f32(16 * 128)
        P.dma(lambda e: e.dma_start(out=cvt.ap, in_=cv[:, :]), writes=[cvt])
        P.dma(lambda e: e.dma_start(out=nwt.ap, in_=normw_fm[:, :]), writes=[nwt])
        P.dma(lambda e: e.dma_start(out=abf.ap, in_=adab_fm[:, :]), writes=[abf])
        P.dma(lambda e: e.dma_start(out=abg.ap, in_=adab_g[:, :]), writes=[abg])
        P.op("act", lambda e: e.activation(out=cvt.ap, in_=cvt.ap, func=AF.Silu), reads=[cvt], writes=[cvt])
        P.op("dve", lambda e: e.tensor_copy(screp.v("p (a b) -> p a b", b=128),
                                            cvt.ap.unsqueeze(2).broadcast_to([128, 16, 128])),
             reads=[cvt], writes=[screp])
        wst = [A.f32(8 * 512) for _ in range(2)]
        for l in range(2 if "noada" not in debug else 0):
            for g in range(6):
                wb = wst[(l * 6 + g) % 2]
                P.dma(lambda e, wb=wb, l=l, g=g: e.dma_start(
                    out=wb.v("p (k c) -> p k c", c=512),
                    in_=ada_w[l, :, 512 * g:512 * g + 512].rearrange("(k p) c -> p k c", p=128)), writes=[wb])
                if g < 4:
                    pb = k.ps[g % 2]
                    for j in range(4):
                        for kk in range(8):
                            P.op("pe", lambda e, pb=pb, wb=wb, j=j, kk=kk: e.matmul(
                                pb.ap[:, 2 * j:2 * j + 2], wb.ap[:, kk * 512 + 128 * j: kk * 512 + 128 * j + 128],
                                cvt.ap[:, 2 * kk:2 * kk + 2], start=(kk == 0), stop=(kk == 7)),
                                reads=[wb, cvt], writes=[pb])
                    for v in range(2):
                        if g < 2:
                            dst = modB.ap[:, l * 16 + v * 8 + 4 * g: l * 16 + v * 8 + 4 * g + 4]
                            P.op("dve", lambda e, pb=pb, dst=dst, v=v, l=l, g=g: e.tensor_tensor(
                                out=dst, in0=pb.ap[:, 0:8].rearrange("p (j v) -> p j v", v=2)[:, :, v],
                                in1=abf.ap[:, l * 24 + 4 * g: l * 24 + 4 * g + 4], op=ALU.add),
                                reads=[pb, abf], writes=[modB])
                        else:
                            gg = g - 2
                            dst = modS.ap[:, l * 16 + v * 8 + 4 * gg: l * 16 + v * 8 + 4 * gg + 4]
                            P.op("dve", lambda e, pb=pb, dst=dst, v=v, l=l, g=g: e.tensor_tensor(
                                out=dst, in0=pb.ap[:, 0:8].rearrange("p (j v) -> p j v", v=2)[:, :, v],
                                in1=abf.ap[:, l * 24 + 4 * g: l * 24 + 4 * g + 4], op=ALU.add),
                                reads=[pb, abf], writes=[modS])
                            P.op("dve", lambda e, dst=dst, l=l, gg=gg: e.scalar_tensor_tensor(
                                out=dst, in0=dst, scalar=1.0, in1=nwt.ap[:, l * 8 + 4 * gg: l * 8 + 4 * gg + 4],
                                op0=ALU.add, op1=ALU.mult), reads=[modS, nwt], writes=[modS])
                else:
                    gg = g - 4
                    for v in range(2):
                        pb = k.ps[2 + v]
                        for kk in range(8):
                            P.op("pe", lambda e, pb=pb, wb=wb, v=v, kk=kk: e.matmul(
                                pb.ap, screp.ap[:, (2 * kk + v) * 128:(2 * kk + v) * 128 + 128],
                                wb.ap[:, kk * 512: kk * 512 + 512], start=(kk == 0), stop=(kk == 7)),
                                reads=[wb, screp], writes=[pb])
                        dst = gB.cols((l * 2 + v) * 1024 + 512 * gg, (l * 2 + v) * 1024 + 512 * gg + 512)
                        P.op("dve", lambda e, pb=pb, dst=dst, l=l, gg=gg: e.tensor_tensor(
                            out=dst.ap, in0=pb.ap, in1=abg.ap[:, l * 1024 + 512 * gg: l * 1024 + 512 * gg + 512],
                            op=ALU.add), reads=[pb, abg], writes=[dst])
        A.release(m0)

        mH = A.mark()
        hT = A.bf(8 * HT_COLS)
        k.hT = hT
        hT3 = hT.v("p (k c) -> p k c", c=HT_COLS)
        for zc in (0, 4097, 4354, 4611):
            P.op("pool", lambda e, zc=zc: e.memset(hT3[:, :, zc:zc + 1], 0.0), writes=[hT])
        mA = A.mark()
        emit_norm_phase(k, x_src=lambda i: x_all[128 * i:128 * i + 128, :], tiles=(list(range(NT)) if "noA1" not in debug else []), layer=0,
                        hT3=hT3, hT=hT, col0=tile_col0, variant=lambda i: 0 if i < NT_S else 1)
        A.release(mA)

        if "noA2" not in debug:
            emit_inproj0(k, ins, outs, PA, GA, QT, KT, VS, GBT, hT3)

        if "hT" in debug:
            hdbg = dout("hT_dbg", [128, 8 * HT_COLS], BF16)
            P.dma(lambda e: e.dma_start(out=hdbg[:, :], in_=hT.ap), reads=[hT], is_output=True)
        A.release(mH)
        if "noR" not in debug:
            emit_rwkv(k, ins, outs, PA, GA, OB, YT)
        if "noT" not in debug:
            emit_attn(k, ins, outs, QT, KT, VS, GBT, YT)
        if "noO" not in debug:
            emit_tail(k, ins, outs, YT, x_all)
        P.run()
    return nc, ins, outs


def emit_norm_phase(k, x_src, tiles, layer, hT3, hT, col0, variant, x_keep=None, cols_total=HT_COLS, src_reg=None):
    P, A = k.P, k.A
    xt = [A.f32(1024) for _ in range(2)]
    xn = [A.f32(1024) for _ in range(2)]
    junk = A.bf(1024)
    ss = [A.f32(1) for _ in range(2)]
    pend = [None]
    for n, i in enumerate(tiles):
        xb, xnb, ssb = xt[n % 2], xn[n % 2], ss[n % 2]
        src = x_src(i)
        if src is not None:
            P.dma(lambda e, xb=xb, src=src: e.dma_start(out=xb.ap, in_=src), writes=[xb],
                  reads=([src_reg(i)] if src_reg is not None else []))
        else:
            xb = x_keep(i)
        P.op("act", lambda e, xb=xb, ssb=ssb: e.activation(out=junk.ap, in_=xb.ap, func=AF.Square, accum_out=ssb.ap),
             reads=[xb], writes=[junk, ssb])
        P.op("act", lambda e, ssb=ssb: e.activation(out=ssb.ap, in_=ssb.ap, func=AF.Ln, scale=1.0 / D, bias=NORM_EPS),
             reads=[ssb], writes=[ssb])
        P.op("act", lambda e, ssb=ssb: e.activation(out=ssb.ap, in_=ssb.ap, func=AF.Exp, scale=-0.5),
             reads=[ssb], writes=[ssb])
        P.op("dve", lambda e, xb=xb, xnb=xnb, ssb=ssb: e.tensor_scalar(
            out=xnb.ap, in0=xb.ap, scalar1=ssb.ap[:, 0:1], scalar2=None, op0=ALU.mult), reads=[xb, ssb], writes=[xnb])
        v = variant(i)
        c0 = col0(i)

        def stage2(n=n, xnb=xnb, v=v, c0=c0):
            for half in range(2):
                pb = k.ps[(2 * n + half) % 4]
                for q in range(4):
                    kk = 4 * half + q
                    P.op("pe", lambda e, pb=pb, xnb=xnb, kk=kk, q=q: e.transpose(
                        pb.ap[:, 128 * q:128 * q + 128], xnb.ap[:, 128 * kk:128 * kk + 128], k.ident.ap),
                        reads=[xnb, k.ident], writes=[pb])
                for q in range(4):
                    kk = 4 * half + q
                    mi = layer * 16 + v * 8 + kk
                    hreg = ("ar", hT.reg[1] + (kk * cols_total + c0) // 2, hT.reg[1] + (kk * cols_total + c0 + 129) // 2)
                    P.op("act", lambda e, pb=pb, kk=kk, q=q, mi=mi, c0=c0: e.activation(
                        out=hT3[:, kk, c0:c0 + 128], in_=pb.ap[:, 128 * q:128 * q + 128], func=AF.Identity,
                        scale=k.modS.ap[:, mi:mi + 1], bias=k.modB.ap[:, mi:mi + 1]),
                        reads=[pb, k.modS, k.modB], writes=[hreg])

        if pend[0] is not None:
            pend[0]()
        pend[0] = stage2
    if pend[0] is not None:
        pend[0]()


def emit_inproj0(k, ins, outs, PA, GA, QT, KT, VS, GBT, hT3):
    P, A, nc = k.P, k.A, k.nc
    hT = k.hT
    w_in0 = ins["w_in0"]
    m0 = A.mark()
    mub = A.f32(3328)
    qkw = A.f32(1024)
    P.dma(lambda e: e.dma_start(out=mub.ap, in_=ins["mu_b"][:, :]), writes=[mub])
    P.dma(lambda e: e.dma_start(out=qkw.ap, in_=ins["qkw_b"][:, :]), writes=[qkw])
    wst = [A.f32(8 * 512) for _ in range(2)]
    wbf = [A.bf(8 * 512) for _ in range(2)]
    hd = [A.bf(8 * 128) for _ in range(2)]
    ev = [A.f32(512) for _ in range(6)]
    evs = [A.f32(512) for _ in range(4)]
    rs8 = [A.f32(8) for _ in range(4)]
    ropet = [A.f32(128) for _ in range(4)]
    tb = [A.bf(512) for _ in range(4)]
    evb = [A.bf(512) for _ in range(4)]

    def hreg(i):
        c0 = tile_col0(i)
        return ("ar", hT.reg[1], hT.reg[2])

    groups = []
    for c in range(0, 3072, 512):
        groups.append((c, 512, "A"))
    groups.append((3072, 256, "A"))
    for c in range(3328, 4352, 512):
        groups.append((c, 512, "ga"))
    for c in range(4352, 5376, 512):
        groups.append((c, 512, "pq"))
    for c in range(5376, 6400, 512):
        groups.append((c, 512, "pk"))
    for c in range(6400, 7424, 512):
        groups.append((c, 512, "pv"))
    cnt = 0
    pending = [None]
    only = [d[5:] for d in k.debug if d.startswith("only_")]
    if only:
        groups = [g for g in groups if g[2] in only]
    for gi, (cs, wdt, kind) in enumerate(groups):
        ws, wb = wst[gi % 2], wbf[gi % 2]
        P.dma(lambda e, ws=ws, cs=cs, wdt=wdt: e.dma_start(
            out=ws.v("p (k c) -> p k c", c=512)[:, :, 0:wdt],
            in_=w_in0[:, cs:cs + wdt].rearrange("(k p) c -> p k c", p=128)), writes=[ws])
        P.op("act", lambda e, ws=ws, wb=wb: e.activation(out=wb.ap, in_=ws.ap, func=AF.Identity), reads=[ws], writes=[wb])
        if kind == "A":
            tiles = list(range(NT)) if cs >= 1024 else OUT_TILES
        elif kind in ("ga", "pq"):
            tiles = OUT_TILES
        else:
            tiles = list(range(NT))
        def emit_hd(i_, slot):
            hdb_ = hd[slot % 2]
            hd3_ = hdb_.v("p (k c) -> p k c", c=128)
            c0_ = tile_col0(i_)
            P.op("pool", lambda e, hd3_=hd3_, c0_=c0_: e.tensor_tensor(
                out=hd3_, in0=hT3[:, :, c0_ - 1:c0_ + 127], in1=hT3[:, :, c0_ + 1:c0_ + 129], op=ALU.add),
                reads=[hT], writes=[hdb_])
            P.op("dve", lambda e, hd3_=hd3_, c0_=c0_: e.scalar_tensor_tensor(
                out=hd3_, in0=hd3_, scalar=0.5, in1=hT3[:, :, c0_:c0_ + 128], op0=ALU.mult, op1=ALU.subtract),
                reads=[hT, hdb_], writes=[hdb_])

        if kind == "A":
            emit_hd(tiles[0], 0)
        for ti, i in enumerate(tiles):
            c0 = tile_col0(i)
            sample = i < NT_S
            cnt += 1
            if kind == "A":
                p1 = k.ps[4 + (cnt % 2) * 2]
                p2 = k.ps[5 + (cnt % 2) * 2]
            else:
                p1 = k.ps[4 + cnt % 4]
                p2 = None
            if kind == "A" and ti + 1 < len(tiles):
                emit_hd(tiles[ti + 1], ti + 1)
            for kk in range(8):
                P.op("pe", lambda e, p1=p1, wb=wb, kk=kk, c0=c0, wdt=wdt: e.matmul(
                    p1.ap[:, 0:wdt], hT3[:, kk, c0:c0 + 128], wb.ap[:, kk * 512:kk * 512 + wdt],
                    start=(kk == 0), stop=(kk == 7)), reads=[wb, hT], writes=[p1])
            if kind == "A":
                hdb = hd[ti % 2]
                hd3 = hdb.v("p (k c) -> p k c", c=128)
                for kk in range(8):
                    P.op("pe", lambda e, p2=p2, wb=wb, kk=kk, hd3=hd3, wdt=wdt: e.matmul(
                        p2.ap[:, 0:wdt], hd3[:, kk, :], wb.ap[:, kk * 512:kk * 512 + wdt],
                        start=(kk == 0), stop=(kk == 7)), reads=[wb, hdb], writes=[p2])
                e1, e2 = ev[cnt % 2], evs[cnt % 4]
                P.op("act", lambda e, p2=p2, e1=e1, wdt=wdt: e.activation(out=e1.ap[:, 0:wdt], in_=p2.ap[:, 0:wdt], func=AF.Identity),
                     reads=[p2], writes=[e1])
                P.op("pool", lambda e, e1=e1, cs=cs, wdt=wdt: e.tensor_tensor(
                    out=e1.ap[:, 0:wdt], in0=e1.ap[:, 0:wdt], in1=mub.ap[:, cs:cs + wdt], op=ALU.mult),
                    reads=[e1, mub], writes=[e1])
                P.op("dve", lambda e, e1=e1, e2=e2, p1=p1, wdt=wdt: e.tensor_tensor(
                    out=e2.ap[:, 0:wdt], in0=e1.ap[:, 0:wdt], in1=p1.ap[:, 0:wdt], op=ALU.add),
                    reads=[e1, p1], writes=[e2])
                P.dma(lambda e, e2=e2, i=i, cs=cs, wdt=wdt: e.dma_start(
                    out=PA.ap[128 * i:128 * i + 128, cs:cs + wdt], in_=e2.ap[:, 0:wdt]),
                    reads=[e2], writes=[PA.rows(128 * i, 128 * i + 128)], queue="pool")
            elif kind == "ga":
                e1 = evs[cnt % 4]
                o = op_index(i)
                P.op("act", lambda e, p1=p1, e1=e1: e.activation(out=e1.ap, in_=p1.ap, func=AF.Silu), reads=[p1], writes=[e1])
                P.dma(lambda e, e1=e1, o=o, cs=cs: e.dma_start(
                    out=GA.ap[128 * o:128 * o + 128, cs - 3328:cs - 3328 + 512], in_=e1.ap),
                    reads=[e1], writes=[GA.rows(128 * o, 128 * o + 128)], queue="pool")
            elif kind == "pv":
                e1, eb = evs[cnt % 4], evb[cnt % 4]
                P.op("act", lambda e, p1=p1, e1=e1: e.activation(out=e1.ap, in_=p1.ap, func=AF.Identity), reads=[p1], writes=[e1])
                P.op("dve", lambda e, e1=e1, eb=eb: e.tensor_copy(eb.ap, e1.ap), reads=[e1], writes=[eb])
                P.dma(lambda e, eb=eb, i=i, cs=cs: e.dma_start(
                    out=VS.ap[128 * i:128 * i + 128, cs - 6400:cs - 6400 + 512], in_=eb.ap),
                    reads=[eb], writes=[VS.rows(128 * i, 128 * i + 128)], queue="pool")
                if not sample:
                    pi = i - NT_S
                    P.dma(lambda e, e1=e1, pi=pi, cs=cs: e.dma_start(
                        out=outs["vc_out"][128 * pi:128 * pi + 128, cs - 6400:cs - 6400 + 512], in_=e1.ap),
                        reads=[e1], is_output=True, queue="pool")
            else:
                isq = kind == "pq"
                base = 4352 if isq else 5376
                h0 = (cs - base) // 128
                wof = 0 if isq else 512
                e1, e2, e3 = ev[(cnt % 2) * 3], ev[(cnt % 2) * 3 + 1], ev[(cnt % 2) * 3 + 2]
                if cnt % 4 >= 2:
                    e1, e3 = evs[0], evs[1]
                    e2 = evs[2 + cnt % 2]
                r8 = rs8[cnt % 4]
                P.op("act", lambda e, p1=p1, e1=e1: e.activation(out=e1.ap, in_=p1.ap, func=AF.Square), reads=[p1], writes=[e1])
                P.op("dve", lambda e, e1=e1, r8=r8: e.tensor_reduce(
                    out=r8.ap, in_=e1.v("p (g d) -> p g d", d=64), axis=AX.X, op=ALU.add), reads=[e1], writes=[r8])
                P.op("act", lambda e, r8=r8: e.activation(out=r8.ap, in_=r8.ap, func=AF.Ln, scale=1.0 / 64, bias=NORM_EPS),
                     reads=[r8], writes=[r8])
                P.op("act", lambda e, r8=r8: e.activation(out=r8.ap, in_=r8.ap, func=AF.Exp, scale=-0.5), reads=[r8], writes=[r8])
                P.op("dve", lambda e, p1=p1, e2=e2, r8=r8: e.tensor_tensor(
                    out=e2.v("p (g d) -> p g d", d=64), in0=p1.v("p (g d) -> p g d", d=64),
                    in1=r8.ap.unsqueeze(2).broadcast_to([128, 8, 64]), op=ALU.mult), reads=[p1, r8], writes=[e2])
                P.op("dve", lambda e, e2=e2, wof=wof: e.tensor_tensor(
                    out=e2.ap, in0=e2.ap, in1=qkw.ap[:, wof:wof + 512], op=ALU.mult), reads=[e2, qkw], writes=[e2])
                tbb = tb[cnt % 4]
                if (not sample) and (not isq):
                    pi = i - NT_S
                    P.dma(lambda e, e2=e2, pi=pi, cs=cs: e.dma_start(
                        out=outs["kc_out"][128 * pi:128 * pi + 128, cs - 5376:cs - 5376 + 512], in_=e2.ap),
                        reads=[e2], is_output=True, queue="pool")
                if sample:
                    rt = ropet[cnt % 4]
                    P.dma(lambda e, rt=rt, i=i: e.dma_start(out=rt.ap, in_=ins["rope_cs"][128 * i:128 * i + 128, :]), writes=[rt])
                    cosb = rt.ap[:, 0:64].unsqueeze(1).broadcast_to([128, 8, 64])
                    P.op("dve", lambda e, e1=e1, e2=e2, cosb=cosb: e.tensor_tensor(
                        out=e1.v("p (g d) -> p g d", d=64), in0=e2.v("p (g d) -> p g d", d=64), in1=cosb, op=ALU.mult),
                        reads=[e2, rt], writes=[e1])
                    for hf in range(2):
                        sinb = rt.ap[:, 64:128].rearrange("p (c h d) -> p c h d", c=2, h=2)[:, :, hf, :] \
                            .unsqueeze(1).broadcast_to([128, 8, 2, 16])
                        P.op("pool", lambda e, e2=e2, e3=e3, hf=hf, sinb=sinb: e.tensor_tensor(
                            out=e3.v("p (g c h d) -> p g c h d", c=2, h=2, d=16)[:, :, :, hf, :],
                            in0=e2.v("p (g c h d) -> p g c h d", c=2, h=2, d=16)[:, :, :, 1 - hf, :],
                            in1=sinb, op=ALU.mult), reads=[e2, rt], writes=[e3])
                    P.op("dve", lambda e, e1=e1, e3=e3, tbb=tbb: e.tensor_tensor(out=tbb.ap, in0=e1.ap, in1=e3.ap, op=ALU.add),
                         reads=[e1, e3], writes=[tbb])
                else:
                    P.op("dve", lambda e, e2=e2, tbb=tbb: e.tensor_copy(tbb.ap, e2.ap), reads=[e2], writes=[tbb])
                def stage2(cnt=cnt, tbb=tbb, isq=isq, i=i, h0=h0):
                    pt = k.ps[cnt % 4]
                    ptb = pt.ap.bitcast(BF16)
                    for hh in range(4):
                        P.op("pe", lambda e, ptb=ptb, tbb=tbb, hh=hh: e.transpose(
                            ptb[:, 128 * hh:128 * hh + 128], tbb.ap[:, 128 * hh:128 * hh + 128], k.identb.ap),
                            reads=[tbb, k.identb], writes=[pt])
                    eb = evb[cnt % 4]
                    P.op("act", lambda e, ptb=ptb, eb=eb: e.activation(out=eb.ap, in_=ptb[:, 0:512], func=AF.Identity),
                         reads=[pt], writes=[eb])
                    if isq:
                        o = op_index(i)
                        P.dma(lambda e, eb=eb, o=o, h0=h0: e.dma_start(
                            out=QT.ap[h0:h0 + 4, :, 128 * o:128 * o + 128].rearrange("h p t -> p h t"),
                            in_=eb.v("p (h t) -> p h t", t=128)), reads=[eb], writes=[QT.name])
                    else:
                        P.dma(lambda e, eb=eb, i=i, h0=h0: e.dma_start(
                            out=KT.ap[h0:h0 + 4, :, 128 * i:128 * i + 128].rearrange("h p t -> p h t"),
                            in_=eb.v("p (h t) -> p h t", t=128)), reads=[eb], writes=[KT.name])

                if pending[0] is not None:
                    pending[0]()
                pending[0] = stage2
    if pending[0] is not None:
        pending[0]()
        pending[0] = None
    wins = [(1 + 512 * w_, 512, 512 * w_) for w_ in range(4)] + [(2049, 128, 2048), (4098, 256, 2176), (4355, 256, 2432)]
    if (not only) or ("gb" in only):
        for g2 in range(2):
            gi = len(groups) + g2
            ws, wb = wst[gi % 2], wbf[gi % 2]
            cs = 7424 + 512 * g2
            P.dma(lambda e, ws=ws, cs=cs: e.dma_start(
                out=ws.v("p (k c) -> p k c", c=512), in_=w_in0[:, cs:cs + 512].rearrange("(k p) c -> p k c", p=128)), writes=[ws])
            P.op("act", lambda e, ws=ws, wb=wb: e.activation(out=wb.ap, in_=ws.ap, func=AF.Identity), reads=[ws], writes=[wb])
            for q in range(4):
                for (c0, nw, o0) in wins:
                    cnt += 1
                    p1 = k.ps[4 + cnt % 4]
                    e1 = ev[cnt % 2]
                    for kk in range(8):
                        P.op("pe", lambda e, p1=p1, wb=wb, kk=kk, c0=c0, nw=nw, q=q: e.matmul(
                            p1.ap[:, 0:nw], wb.ap[:, kk * 512 + 128 * q:kk * 512 + 128 * q + 128], hT3[:, kk, c0:c0 + nw],
                            start=(kk == 0), stop=(kk == 7)), reads=[wb, hT], writes=[p1])
                    P.op("act", lambda e, p1=p1, e1=e1, nw=nw: e.activation(out=e1.ap[:, 0:nw], in_=p1.ap[:, 0:nw], func=AF.Silu),
                         reads=[p1], writes=[e1])
                    r0 = 512 * g2 + 128 * q
                    P.dma(lambda e, e1=e1, r0=r0, o0=o0, nw=nw: e.dma_start(out=GBT.ap[r0:r0 + 128, o0:o0 + nw], in_=e1.ap[:, 0:nw]),
                          reads=[e1], writes=[GBT.name])
    if (not only) or ("ctx" in only):
        for t_ in range(2):
            for which in range(2):
                cnt += 1
                e1a, e1b = ev[(cnt % 2) * 2], ev[(cnt % 2) * 2 + 1]
                src = ins["ctx_k"] if which == 0 else ins["ctx_v"]
                for hf in range(2):
                    ebuf = e1a if hf == 0 else e1b
                    P.dma(lambda e, ebuf=ebuf, src=src, t_=t_, hf=hf: e.dma_start(
                        out=ebuf.ap, in_=src[128 * t_:128 * t_ + 128, 512 * hf:512 * hf + 512]), writes=[ebuf])
                    tbb = tb[(cnt + hf) % 2]
                    P.op("dve", lambda e, ebuf=ebuf, tbb=tbb: e.tensor_copy(tbb.ap, ebuf.ap), reads=[ebuf], writes=[tbb])
                    if which == 1:
                        P.dma(lambda e, tbb=tbb, t_=t_, hf=hf: e.dma_start(
                            out=VS.ap[T_ALL + 128 * t_:T_ALL + 128 * t_ + 128, 512 * hf:512 * hf + 512], in_=tbb.ap),
                            reads=[tbb], writes=[VS.rows(T_ALL + 128 * t_, T_ALL + 128 * t_ + 128)])
                    else:
                        pt = k.ps[(cnt + hf) % 2]
                        ptb = pt.ap.bitcast(BF16)
                        for hh in range(4):
                            P.op("pe", lambda e, ptb=ptb, tbb=tbb, hh=hh: e.transpose(
                                ptb[:, 128 * hh:128 * hh + 128], tbb.ap[:, 128 * hh:128 * hh + 128], k.identb.ap),
                                reads=[tbb, k.identb], writes=[pt])
                        eb = evb[(cnt + hf) % 2]
                        P.op("act", lambda e, ptb=ptb, eb=eb: e.activation(out=eb.ap, in_=ptb[:, 0:512], func=AF.Identity),
                             reads=[pt], writes=[eb])
                        P.dma(lambda e, eb=eb, t_=t_, hf=hf: e.dma_start(
                            out=KT.ap[4 * hf:4 * hf + 4, :, T_ALL + 128 * t_:T_ALL + 128 * t_ + 128].rearrange("h p t -> p h t"),
                            in_=eb.v("p (h t) -> p h t", t=128)), reads=[eb], writes=[KT.name])
    A.release(m0)


def TT(P, eng, out, in0, in1, op, r, w):
    P.op(eng, lambda e: e.tensor_tensor(out=out, in0=in0, in1=in1, op=op), reads=r, writes=w)


def ACTF(P, out, in_, func, r, w, **kw):
    P.op("act", lambda e: e.activation(out=out, in_=in_, func=func, **kw), reads=r, writes=w)


def MM(P, out, lhsT, rhs, start, stop, r, w):
    P.op("pe", lambda e: e.matmul(out, lhsT, rhs, start=start, stop=stop), reads=r, writes=w)


def TR(P, out, in_, ident, r, w):
    P.op("pe", lambda e: e.transpose(out, in_, ident), reads=r, writes=w)


def CP(P, eng, out, in_, r, w):
    P.op(eng, lambda e: e.tensor_copy(out, in_), reads=r, writes=w)


def STT(P, out, in0, scalar, in1, op0, op1, r, w):
    P.op("dve", lambda e: e.scalar_tensor_tensor(out=out, in0=in0, scalar=scalar, in1=in1, op0=op0, op1=op1),
         reads=r, writes=w)


def RED(P, out, in_, r, w):
    P.op("dve", lambda e: e.tensor_reduce(out=out, in_=in_, axis=AX.X, op=ALU.add), reads=r, writes=w)


def g64(ap):
    return ap.rearrange("p (g d) -> p g d", d=64)


def b64(ap16):
    return ap16.unsqueeze(2).broadcast_to([128, 16, 64])


def emit_rwkv(k, ins, outs, PA, GA, OB, YT):
    P, A = k.P, k.A
    ps = k.ps
    psb = [p_.ap.bitcast(BF16) for p_ in ps]
    m0 = A.mark()
    prm = A.f32(5120)
    kkb, kab, rkb, lwb, lbb = [prm.cols(1024 * i, 1024 * i + 1024) for i in range(5)]
    wup = A.f32(4096)
    CM = A.f32(768)
    IND = A.f32(2)
    MSK = A.f32(1024)
    lT = A.f32(256)
    H = [A.f32(512) for _ in range(2)]
    Hb = [A.bf(512) for _ in range(2)]
    gC = A.f32(16)
    sbst = A.f32(21 * 16)
    P.dma(lambda e: e.dma_start(out=prm.ap, in_=ins["rw_prm"][:, :]), writes=[prm])
    P.dma(lambda e: e.dma_start(out=wup.ap[0:65, :], in_=ins["rw_up"][:, :]), writes=[wup])
    P.dma(lambda e: e.dma_start(out=CM.ap, in_=ins["rw_const"][:, 0:768]), writes=[CM])
    P.dma(lambda e: e.dma_start(out=IND.ap, in_=ins["rw_const"][:, 768:770]), writes=[IND])
    P.dma(lambda e: e.dma_start(out=MSK.ap, in_=ins["rw_const"][:, 770:1794]), writes=[MSK])
    P.op("pool", lambda e: e.memset(lT.ap, 1.0), writes=[lT])
    pa_one = A.f32(3328)
    pa2 = [pa_one, pa_one]
    LT = A.f32(128)
    SW, AA, KAP, KD, BE, X1, X2 = [A.f32(1024) for _ in range(7)]
    E = [A.f32(1024) for _ in range(2)]
    s16, sbc = A.f32(16), A.f32(16)
    RTIL, KATIL, BTIL, KTIL, BH, KH, VB = [A.bf(1024) for _ in range(7)]
    QRS, BKS = A.bf(2048), A.bf(2048)
    SC = A.bf(16 * 512)
    Nb = [A.bf(2048) for _ in range(2)]
    Ntb = [A.bf(2048) for _ in range(2)]
    Tb = [A.bf(2048) for _ in range(2)]
    Wsb = A.bf(1024)
    Usbc = [A.bf(1024) for _ in range(2)]
    VBc = [A.bf(1024) for _ in range(2)]
    Osb = A.f32(1024)
    obuf, gabuf = X2, E[0]
    for zb in (Wsb, Usbc[0], Usbc[1], VBc[0], VBc[1]):
        P.op("pool", lambda e, zb=zb: e.memset(zb.ap, 0.0), writes=[zb])
    ytb = A.bf(1024)
    lT3 = lT.v("p (a t) -> p a t", t=128)
    QRS4 = QRS.v("p (j x t) -> p j x t", x=2, t=128)
    BKS4 = BKS.v("p (j x t) -> p j x t", x=2, t=128)
    SC3 = SC.v("p (h c) -> p h c", c=512)

    def visit(i, d, state_only, chunks, finalize):
        n = visit.n
        visit.n += 1
        pa = pa2[n % 2]
        c_lo = 1024 if state_only else 0
        P.dma(lambda e: e.dma_start(out=pa.ap[:, c_lo:3328], in_=PA.ap[128 * i:128 * i + 128, c_lo:3328]),
              reads=[PA.rows(128 * i, 128 * i + 128)], writes=[pa])
        par, pak, pav = pa.ap[:, 0:1024], pa.ap[:, 1024:2048], pa.ap[:, 2048:3072]
        Hd, Hbd = H[d], Hb[d]
        Hd3 = Hd.v("p (j v) -> p j v", v=64)
        Hb3 = Hbd.v("p (j v) -> p j v", v=64)
        CP(P, "pool", VB.ap, pav, [pa], [VB])
        CP(P, "pool", VBc[0].ap[0:64, :], pav[0:64, :], [pa], [VBc[0]])
        CP(P, "pool", VBc[1].ap[64:128, :], pav[64:128, :], [pa], [VBc[1]])
        ACTF(P, LT.ap[:, 0:64], pa.ap[:, 3072 + 64 * d:3136 + 64 * d], AF.Tanh, [pa], [LT])
        ACTF(P, LT.ap[:, 64:128], pa.ap[:, 3200 + 64 * d:3264 + 64 * d], AF.Identity, [pa], [LT])
        TR(P, ps[0].ap[0:64, 0:128], LT.ap[:, 0:64], k.ident.ap, [LT, k.ident], [ps[0]])
        TR(P, ps[0].ap[0:64, 128:256], LT.ap[:, 64:128], k.ident.ap, [LT, k.ident], [ps[0]])
        ACTF(P, lT3[0:64, :, :], ps[0].ap[0:64, 0:256].rearrange("p (a t) -> p a t", t=128), AF.Identity, [ps[0]], [lT])
        for hf in range(2):
            MM(P, ps[hf].ap, lT3[0:65, 0, :], wup.ap[0:65, 1024 * d + 512 * hf:1024 * d + 512 * hf + 512], True, True,
               [lT, wup], [ps[hf]])
            MM(P, ps[2 + hf].ap, lT3[0:65, 1, :], wup.ap[0:65, 2048 + 1024 * d + 512 * hf:2048 + 1024 * d + 512 * hf + 512],
               True, True, [lT, wup], [ps[2 + hf]])
        for hf in range(2):
            ACTF(P, SW.ap[:, 512 * hf:512 * hf + 512], ps[hf].ap, AF.Sigmoid, [ps[hf]], [SW])
            ACTF(P, AA.ap[:, 512 * hf:512 * hf + 512], ps[2 + hf].ap, AF.Sigmoid, [ps[2 + hf]], [AA])
        TT(P, "dve", X1.ap, pak, kkb.ap, ALU.mult, [pa, kkb], [X1])
        ACTF(P, X2.ap, X1.ap, AF.Square, [X1], [X2])
        RED(P, s16.ap, g64(X2.ap), [X2], [s16])
        ACTF(P, s16.ap, s16.ap, AF.Ln, [s16], [s16], bias=1e-24)
        ACTF(P, s16.ap, s16.ap, AF.Exp, [s16], [s16], scale=-0.5)
        TT(P, "dve", g64(KAP.ap), g64(X1.ap), b64(s16.ap), ALU.mult, [X1, s16], [KAP])
        STT(P, X2.ap, AA.ap, -1.0, kab.ap, ALU.add, ALU.mult, [AA, kab], [X2])
        STT(P, KD.ap, X2.ap, 1.0, pak, ALU.add, ALU.mult, [X2, pa], [KD])
        TT(P, "pool", BE.ap, KAP.ap, AA.ap, ALU.mult, [KAP, AA], [BE])
        if not state_only:
            TT(P, "pool", X1.ap, par, rkb.ap, ALU.mult, [pa, rkb], [X1])
            TT(P, "pool", X1.ap, X1.ap, KD.ap, ALU.mult, [X1, KD], [X1])
            RED(P, sbc.ap, g64(X1.ap), [X1], [sbc])
        cm = CM.v("p (d m t) -> p d m t", d=2, m=3)
        for hf in range(2):
            MM(P, ps[4 + hf].ap, cm[:, d, 0, :], SW.ap[:, 512 * hf:512 * hf + 512], True, True, [CM, SW], [ps[4 + hf]])
            MM(P, ps[6 + hf].ap, cm[:, d, 1, :], SW.ap[:, 512 * hf:512 * hf + 512], True, True, [CM, SW], [ps[6 + hf]])
        for hf in range(2):
            ACTF(P, E[0].ap[:, 512 * hf:512 * hf + 512], ps[4 + hf].ap, AF.Exp, [ps[4 + hf]], [E[0]])
            ACTF(P, E[1].ap[:, 512 * hf:512 * hf + 512], ps[4 + hf].ap, AF.Exp, [ps[4 + hf]], [E[1]], scale=-1.0)
        if not state_only:
            TT(P, "dve", RTIL.ap, par, E[0].ap, ALU.mult, [pa, E[0]], [RTIL])
        TT(P, "pool", BTIL.ap, BE.ap, E[1].ap, ALU.mult, [BE, E[1]], [BTIL])
        TT(P, "dve", KTIL.ap, KD.ap, E[1].ap, ALU.mult, [KD, E[1]], [KTIL])
        for hf in range(2):
            MM(P, ps[hf].ap, cm[:, d, 2, :], SW.ap[:, 512 * hf:512 * hf + 512], True, True, [CM, SW], [ps[hf]])
        for hf in range(2):
            ACTF(P, E[0].ap[:, 512 * hf:512 * hf + 512], ps[6 + hf].ap, AF.Exp, [ps[6 + hf]], [E[0]])
            ACTF(P, E[1].ap[:, 512 * hf:512 * hf + 512], ps[hf].ap, AF.Exp, [ps[hf]], [E[1]])
        TT(P, "dve", KATIL.ap, KAP.ap, E[0].ap, ALU.mult, [KAP, E[0]], [KATIL])
        TT(P, "pool", BH.ap, BE.ap, E[1].ap, ALU.mult, [BE, E[1]], [BH])
        TT(P, "pool", KH.ap, KD.ap, E[1].ap, ALU.mult, [KD, E[1]], [KH])
        for j in range(8):
            MM(P, ps[2].ap[:, 2 * j:2 * j + 2], SW.ap[:, 128 * j:128 * j + 128], IND.ap, True, True, [SW, IND], [ps[2]])
        ACTF(P, gC.ap, ps[2].ap[:, 0:16], AF.Exp, [ps[2]], [gC])
        gC3 = gC.v("p (j c) -> p j c", c=2)
        quants = [(KATIL, QRS4, 0), (BTIL, BKS4, 0), (KTIL, BKS4, 1)]
        if not state_only:
            quants.append((RTIL, QRS4, 1))
        for qi, (src, dst4, x) in enumerate(quants):
            dstbuf = QRS if dst4 is QRS4 else BKS
            pi_ = 4 + qi % 4
            for j in range(8):
                TR(P, psb[pi_][:, 128 * j:128 * j + 128], src.ap[:, 128 * j:128 * j + 128], k.identb.ap,
                   [src, k.identb], [ps[pi_]])
            dview = dst4[:, :, x, :]
            sview = psb[pi_].rearrange("p (j t) -> p j t", t=128)
            if qi % 2 == 0:
                ACTF(P, dview, sview, AF.Identity, [ps[pi_]], [dstbuf])
            else:
                CP(P, "dve", dview, sview, [ps[pi_]], [dstbuf])
        msk = MSK.ap[:, 512 * d:512 * d + 512]
        ncol = 128 if state_only else 256
        for h in range(16):
            j, e = h // 2, h % 2
            pb = ps[h % 4]
            rhs = QRS4[64 * e:64 * e + 64, j, :, :] if not state_only else QRS4[64 * e:64 * e + 64, j, 0:1, :]
            MM(P, pb.ap[:, 0:ncol], BKS4[64 * e:64 * e + 64, j, 0, :], rhs, True, True, [BKS, QRS], [pb])
            MM(P, pb.ap[:, 256:256 + ncol], BKS4[64 * e:64 * e + 64, j, 1, :], rhs, True, True, [BKS, QRS], [pb])
            TT(P, "dve", SC3[:, h, :], pb.ap, msk, ALU.mult, [pb, MSK], [SC])
        CP(P, "pool", Nb[0].v("p (h t) -> p h t", t=128), SC3[:, :, 0:128], [SC], [Nb[0]])
        TT(P, "dve", Tb[0].v("p (h t) -> p h t", t=128),
           k.identb.ap.unsqueeze(1).broadcast_to([128, 16, 128]), SC3[:, :, 0:128], ALU.subtract, [SC, k.identb], [Tb[0]])
        for hh in range(2):
            pi_ = 4 + hh
            for q in range(8):
                h = 8 * hh + q
                TR(P, psb[pi_][:, 128 * q:128 * q + 128], SC3[:, h, 0:128], k.identb.ap, [SC, k.identb], [ps[pi_]])
            ACTF(P, Ntb[0].ap[:, 1024 * hh:1024 * hh + 1024], psb[pi_], AF.Identity, [ps[pi_]], [Ntb[0]])
        cur = 0
        for lvl in range(1, 6):
            nxt = 1 - cur
            last = lvl == 5
            for hq in range(4):
                sl = slice(512 * hq, 512 * hq + 512)
                pbt = ps[hq % 4]
                for q in range(4):
                    h = 4 * hq + q
                    c1 = slice(128 * h, 128 * h + 128)
                    MM(P, pbt.ap[:, 128 * q:128 * q + 128], Nb[cur].ap[:, c1], Ntb[cur].ap[:, c1], True, True,
                       [Nb[cur], Ntb[cur]], [pbt])
                ACTF(P, Ntb[nxt].ap[:, sl], pbt.ap, AF.Identity, [pbt], [Ntb[nxt]])
                if not last:
                    pbn = ps[4 + hq % 4]
                    for q in range(4):
                        h = 4 * hq + q
                        c1 = slice(128 * h, 128 * h + 128)
                        MM(P, pbn.ap[:, 128 * q:128 * q + 128], Ntb[cur].ap[:, c1], Nb[cur].ap[:, c1], True, True,
                           [Nb[cur], Ntb[cur]], [pbn])
                    ACTF(P, Nb[nxt].ap[:, sl], pbn.ap, AF.Identity, [pbn], [Nb[nxt]])
            for hq in range(4):
                sl = slice(512 * hq, 512 * hq + 512)
                pb = ps[4 + hq % 4] if last else ps[hq % 4]
                for q in range(4):
                    h = 4 * hq + q
                    c1 = slice(128 * h, 128 * h + 128)
                    MM(P, pb.ap[:, 128 * q:128 * q + 128], Ntb[nxt].ap[:, c1], Tb[cur].ap[:, c1], True, True,
                       [Ntb[nxt], Tb[cur]], [pb])
                TT(P, "dve", Tb[nxt].ap[:, sl], pb.ap, Tb[cur].ap[:, sl], ALU.add, [pb, Tb[cur]], [Tb[nxt]])
            cur = nxt
        MTb = Tb[cur]
        MT3 = MTb.v("p (h t) -> p h t", t=128)
        for flag in ("xa", "xb", "xc", "xd"):
            if flag in k.debug:
                for h in range(16):
                    j, e = h // 2, h % 2
                    es = slice(64 * e, 64 * e + 64)
                    pb = ps[h // 8]
                    oc = slice(64 * (h % 8), 64 * (h % 8) + 64)
                    if flag == "xa":
                        MM(P, pb.ap[:, oc], BKS4[es, j, 0, :], BKS4[es, j, 1, 0:64], True, True, [BKS], [pb])
                    elif flag == "xb":
                        MM(P, pb.ap[:, oc], QRS4[es, j, 0, :], BKS4[es, j, 1, 0:64], True, True, [BKS, QRS], [pb])
                    elif flag == "xc":
                        MM(P, pb.ap[:, oc], QRS4[es, j, 0, :], Hb3[es, j, :], True, True, [QRS, Hbd], [pb])
                    elif flag == "xd":
                        MM(P, pb.ap[:, oc], BKS4[es, j, 0, :], Hb3[es, j, :], True, True, [BKS, Hbd], [pb])
                for hf in range(2):
                    ACTF(P, Wsb.ap[:, 512 * hf:512 * hf + 512], ps[hf].ap, AF.Identity, [ps[hf]], [Wsb])
        if "rwstop6" in k.debug:
            return
        for ch in chunks:
            rs = slice(64 * ch, 64 * ch + 64)
            Uc, Vc = Usbc[ch], VBc[ch]
            for h in range(16):
                j, e = h // 2, h % 2
                es = slice(64 * e, 64 * e + 64)
                pb = ps[h // 8]
                oc = slice(64 * (h % 8), 64 * (h % 8) + 64)
                MM(P, pb.ap[:, oc], QRS4[es, j, 0, :], Hb3[es, j, :], True, False, [QRS, Hbd], [pb])
                MM(P, pb.ap[:, oc], SC3[:, h, 256:384], VB.ap[:, 64 * h:64 * h + 64], False, True, [SC, VB], [pb])
            for hf in range(2):
                ACTF(P, Wsb.ap[rs, 512 * hf:512 * hf + 512], ps[hf].ap[rs, :], AF.Identity, [ps[hf]], [Wsb])
            if "rwstop7" in k.debug:
                return
            for h in range(16):
                pb = ps[2 + h // 8]
                oc = slice(64 * (h % 8), 64 * (h % 8) + 64)
                MM(P, pb.ap[:, oc], MT3[:, h, :], Wsb.ap[:, 64 * h:64 * h + 64], True, True, [MTb, Wsb], [pb])
            for hf in range(2):
                ACTF(P, Uc.ap[rs, 512 * hf:512 * hf + 512], ps[2 + hf].ap[rs, :], AF.Identity, [ps[2 + hf]], [Uc], scale=-1.0)
            if not state_only:
                for h in range(16):
                    j, e = h // 2, h % 2
                    es = slice(64 * e, 64 * e + 64)
                    pb = ps[4 + h // 8]
                    oc = slice(64 * (h % 8), 64 * (h % 8) + 64)
                    MM(P, pb.ap[:, oc], QRS4[es, j, 1, :], Hb3[es, j, :], True, False, [QRS, Hbd], [pb])
                    MM(P, pb.ap[:, oc], SC3[:, h, 128:256], Uc.ap[:, 64 * h:64 * h + 64], False, False, [SC, Uc], [pb])
                    MM(P, pb.ap[:, oc], SC3[:, h, 384:512], VB.ap[:, 64 * h:64 * h + 64], False, True, [SC, VB], [pb])
                for hf in range(2):
                    CP(P, "dve", Osb.ap[rs, 512 * hf:512 * hf + 512], ps[4 + hf].ap[rs, :], [ps[4 + hf]], [Osb])
            for j in range(8):
                pb = ps[6 + j // 4]
                oc = slice(128 * (j % 4), 128 * (j % 4) + 128)
                jc = slice(128 * j, 128 * j + 128)
                MM(P, pb.ap[:, oc], BH.ap[:, jc], Uc.ap[:, jc], True, False, [BH, Uc], [pb])
                MM(P, pb.ap[:, oc], KH.ap[:, jc], Vc.ap[:, jc], False, True, [KH, Vc], [pb])
            for e in range(2):
                es = slice(64 * e, 64 * e + 64)
                TT(P, "dve", Hd3[es, :, :], Hd3[es, :, :], gC3[es, :, ch].unsqueeze(2).broadcast_to([64, 8, 64]), ALU.mult,
                   [Hd, gC], [Hd])
                for hf in range(2):
                    src = ps[6 + hf].ap[es, :].rearrange("p (j e v) -> p j e v", e=2, v=64)[:, :, e, :]
                    TT(P, "dve", Hd3[es, 4 * hf:4 * hf + 4, :], Hd3[es, 4 * hf:4 * hf + 4, :], src, ALU.add,
                       [Hd, ps[6 + hf]], [Hd])
            CP(P, "pool", Hbd.ap, Hd.ap, [Hd], [Hbd])
        if state_only:
            return
        o = op_index(i)
        if not finalize:
            P.dma(lambda e: e.dma_start(out=OB.ap[128 * o:128 * o + 128, :], in_=Osb.ap), reads=[Osb],
                  writes=[OB.rows(128 * o, 128 * o + 128)])
            CP(P, "pool", sbst.ap[:, 16 * o:16 * o + 16], sbc.ap, [sbc], [sbst])
            return
        P.dma(lambda e: e.dma_start(out=obuf.ap, in_=OB.ap[128 * o:128 * o + 128, :]),
              reads=[OB.rows(128 * o, 128 * o + 128)], writes=[obuf])
        P.dma(lambda e: e.dma_start(out=gabuf.ap, in_=GA.ap[128 * o:128 * o + 128, :]),
              reads=[GA.rows(128 * o, 128 * o + 128)], writes=[gabuf])
        TT(P, "dve", X1.ap, Osb.ap, obuf.ap, ALU.add, [Osb, obuf], [X1])
        RED(P, s16.ap, g64(X1.ap), [X1], [s16])
        P.op("dve", lambda e: e.tensor_scalar(out=s16.ap, in0=s16.ap, scalar1=-1.0 / 64, scalar2=None, op0=ALU.mult),
             reads=[s16], writes=[s16])
        TT(P, "dve", g64(X1.ap), g64(X1.ap), b64(s16.ap), ALU.add, [X1, s16], [X1])
        ACTF(P, X2.ap, X1.ap, AF.Square, [X1], [X2])
        RED(P, s16.ap, g64(X2.ap), [X2], [s16])
        ACTF(P, s16.ap, s16.ap, AF.Ln, [s16], [s16], scale=1.0 / 64, bias=GN_EPS)
        ACTF(P, s16.ap, s16.ap, AF.Exp, [s16], [s16], scale=-0.5)
        TT(P, "dve", g64(X1.ap), g64(X1.ap), b64(s16.ap), ALU.mult, [X1, s16], [X1])
        TT(P, "pool", X1.ap, X1.ap, lwb.ap, ALU.mult, [X1, lwb], [X1])
        TT(P, "pool", X1.ap, X1.ap, lbb.ap, ALU.add, [X1, lbb], [X1])
        TT(P, "dve", sbc.ap, sbc.ap, sbst.ap[:, 16 * o:16 * o + 16], ALU.add, [sbc, sbst], [sbc])
        TT(P, "dve", g64(X2.ap), g64(pav), b64(sbc.ap), ALU.mult, [pa, sbc], [X2])
        TT(P, "pool", X1.ap, X1.ap, X2.ap, ALU.add, [X1, X2], [X1])
        TT(P, "dve", X1.ap, X1.ap, gabuf.ap, ALU.mult, [X1, gabuf], [X1])
        if "YA" in k.debug:
            P.dma(lambda e: e.dma_start(out=outs["YA"][128 * o:128 * o + 128, :], in_=X1.ap), reads=[X1], is_output=True)
        for hf in range(2):
            pb = ps[4 + hf]
            for q in range(4):
                kk_ = 4 * hf + q
                TR(P, pb.ap[:, 128 * q:128 * q + 128], X1.ap[:, 128 * kk_:128 * kk_ + 128], k.ident.ap, [X1, k.ident], [pb])
            ACTF(P, ytb.ap[:, 512 * hf:512 * hf + 512], pb.ap, AF.Identity, [pb], [ytb])
        P.dma(lambda e: e.dma_start(out=YT.ap[0:1024, 128 * o:128 * o + 128].rearrange("(kk p) t -> p kk t", p=128),
                                    in_=ytb.v("p (kk t) -> p kk t", t=128)), reads=[ytb], writes=[YT.name])

    visit.n = 0

    def init_state(d, from_input):
        if from_input:
            P.dma(lambda e: e.dma_start(out=H[d].ap, in_=ins["st_in"][:, 512 * d:512 * d + 512]), writes=[H[d]])
        else:
            P.op("pool", lambda e: e.memset(H[d].ap, 0.0), writes=[H[d]])
        CP(P, "pool", Hb[d].ap, H[d].ap, [H[d]], [Hb[d]])

    only = [x for x in k.debug if x.startswith("rw_")]
    do_sample = (not only) or ("rw_sample" in only)
    do_prompt = (not only) or ("rw_prompt" in only)
    if do_sample:
        init_state(1, True)
        for i in range(NT_S - 1, NT_OWN - 1, -1):
            visit(i, 1, True, (1, 0), False)
        for i in range(NT_OWN - 1, -1, -1):
            visit(i, 1, False, (1, 0), False)
        init_state(0, True)
        for i in range(NT_OWN):
            visit(i, 0, False, (0, 1), True)
    if do_prompt:
        for sq in range(2):
            t0 = NT_S + 2 * sq
            init_state(1, False)
            for i in (t0 + 1, t0):
                visit(i, 1, False, (1, 0), False)
            P.dma(lambda e, sq=sq: e.dma_start(out=outs["st_out"][sq, 1, :, :], in_=H[1].ap), reads=[H[1]], is_output=True)
            init_state(0, False)
            for i in (t0, t0 + 1):
                visit(i, 0, False, (0, 1), True)
            P.dma(lambda e, sq=sq: e.dma_start(out=outs["st_out"][sq, 0, :, :], in_=H[0].ap), reads=[H[0]], is_output=True)
    A.release(m0)


def emit_attn(k, ins, outs, QT, KT, VS, GBT, YT):
    P, A = k.P, k.A
    ps = k.ps
    m0 = A.mark()
    prm = A.f32(257)
    lam4 = A.f32(4)
    neglam = A.f32(1)
    subw = A.f32(1)
    onesb = A.bf(128)
    P.dma(lambda e: e.dma_start(out=prm.ap, in_=ins["at_prm"][:, :]), writes=[prm])
    P.op("pool", lambda e: e.memset(onesb.ap, 1.0), writes=[onesb])
    tmp = A.f32(128)
    l4 = prm.ap[:, 0:256].rearrange("p (a d) -> p a d", d=64)
    TT(P, "dve", tmp.v("p (a d) -> p a d", d=64), l4[:, 0:4:2, :], l4[:, 1:4:2, :], ALU.mult, [prm], [tmp])
    P.op("dve", lambda e: e.tensor_reduce(out=lam4.ap[:, 0:2], in_=tmp.v("p (a d) -> p a d", d=64), axis=AX.X, op=ALU.add),
         reads=[tmp], writes=[lam4])
    ACTF(P, lam4.ap[:, 0:2], lam4.ap[:, 0:2], AF.Exp, [lam4], [lam4])
    STT(P, neglam.ap, lam4.ap[:, 1:2], -LAM_INIT, lam4.ap[:, 0:1], ALU.add, ALU.subtract, [lam4], [neglam])
    P.op("dve", lambda e: e.tensor_scalar(out=subw.ap, in0=prm.ap[:, 256:257], scalar1=1.0 - LAM_INIT, scalar2=None, op0=ALU.mult),
         reads=[prm], writes=[subw])
    NKMAX = 34
    kT = [A.bf(NKMAX * 128) for _ in range(2)]
    vv = [A.bf(NKMAX * 128) for _ in range(2)]
    qT = [A.bf(2176) for _ in range(2)]
    pT = [A.bf(512) for _ in range(4)]
    r0b, t0b, t1b, gbt = A.f32(512), A.f32(512), A.f32(512), A.f32(512)
    sqb = A.bf(512)
    yb = A.bf(512)
    sample_keys = [128 * i for i in range(NT_S)] + [T_ALL, T_ALL + 128]
    groups = [(0, 128 * NT_OWN, sample_keys)]
    for sq in range(2):
        groups.append((128 * NT_OWN + 256 * sq, 256, [128 * NT_S + 256 * sq, 128 * NT_S + 256 * sq + 128]))
    only = [x for x in k.debug if x.startswith("at_")]
    if "at_prompt" in only:
        groups = groups[1:]
    if "at_sample" in only:
        groups = groups[:1]
    it = 0
    for (q0, nq, keys) in groups:
        nk = len(keys)
        for h in range(1 if "at_h1" in k.debug else 8):
            it += 1
            kTb, vb, qTb = kT[it % 2], vv[it % 2], qT[it % 2]
            runs = []
            for ki, kr in enumerate(keys):
                if runs and runs[-1][1] + runs[-1][2] == kr:
                    runs[-1][2] += 128
                else:
                    runs.append([ki, kr, 128])
            for (ki, kr, ln) in runs:
                P.dma(lambda e, kTb=kTb, ki=ki, kr=kr, ln=ln, h=h: e.dma_start(
                    out=kTb.ap[:, 128 * ki:128 * ki + ln], in_=KT.ap[h, :, kr:kr + ln]), reads=[KT.name], writes=[kTb])
                P.dma(lambda e, vb=vb, ki=ki, kr=kr, ln=ln, h=h: e.dma_start(
                    out=vb.ap[:, 128 * ki:128 * ki + ln].rearrange("p (t c) -> p t c", c=128),
                    in_=VS.ap[kr:kr + ln, 128 * h:128 * h + 128].rearrange("(t p) c -> p t c", p=128)),
                    reads=[VS.rows(kr, kr + ln)], writes=[vb])
            P.dma(lambda e, qTb=qTb, q0=q0, nq=nq, h=h: e.dma_start(out=qTb.ap[:, 0:nq], in_=QT.ap[h, :, q0:q0 + nq]),
                  reads=[QT.name], writes=[qTb])
            for qs in range(0, nq, 512):
                qn = min(512, nq - qs)
                def emit_qk(ki):
                    for c in range(2):
                        sb_ = ps[2 * (ki % 2) + c]
                        cs_ = slice(64 * c, 64 * c + 64)
                        MM(P, sb_.ap[:, 0:qn], kTb.ap[cs_, 128 * ki:128 * ki + 128], qTb.ap[cs_, qs:qs + qn], True, True,
                           [kTb, qTb], [sb_])

                emit_qk(0)
                for ki in range(nk):
                    if ki + 1 < nk:
                        emit_qk(ki + 1)
                    for c in range(2):
                        sb_ = ps[2 * (ki % 2) + c]
                        pb_ = pT[2 * (ki % 2) + c]
                        ACTF(P, pb_.ap[:, 0:qn], sb_.ap[:, 0:qn], AF.Exp, [sb_], [pb_], scale=0.125)
                    for c in range(2):
                        pb_ = pT[2 * (ki % 2) + c]
                        MM(P, ps[4 + c].ap[:, 0:qn], vb.ap[:, 128 * ki:128 * ki + 128], pb_.ap[:, 0:qn], ki == 0, ki == nk - 1,
                           [vb, pb_], [ps[4 + c]])
                        MM(P, ps[6 + c].ap[:, 0:qn], onesb.ap, pb_.ap[:, 0:qn], ki == 0, ki == nk - 1, [onesb, pb_], [ps[6 + c]])
                P.op("dve", lambda e, qn=qn: e.reciprocal(out=r0b.ap[:, 0:qn], in_=ps[6].ap[:, 0:qn]), reads=[ps[6]], writes=[r0b])
                TT(P, "dve", t0b.ap[:, 0:qn], ps[4].ap[:, 0:qn], r0b.ap[:, 0:qn], ALU.mult, [ps[4], r0b], [t0b])
                P.op("dve", lambda e, qn=qn: e.reciprocal(out=r0b.ap[:, 0:qn], in_=ps[7].ap[:, 0:qn]), reads=[ps[7]], writes=[r0b])
                TT(P, "dve", t1b.ap[:, 0:qn], ps[5].ap[:, 0:qn], r0b.ap[:, 0:qn], ALU.mult, [ps[5], r0b], [t1b])
                STT(P, t0b.ap[:, 0:qn], t1b.ap[:, 0:qn], neglam.ap[:, 0:1], t0b.ap[:, 0:qn], ALU.mult, ALU.add,
                    [t1b, neglam, t0b], [t0b])
                ACTF(P, sqb.ap[:, 0:qn], t0b.ap[:, 0:qn], AF.Square, [t0b], [sqb])
                MM(P, ps[0].ap[:, 0:qn], onesb.ap, sqb.ap[:, 0:qn], True, True, [onesb, sqb], [ps[0]])
                ACTF(P, t1b.ap[:, 0:qn], ps[0].ap[:, 0:qn], AF.Ln, [ps[0]], [t1b], scale=1.0 / 128, bias=NORM_EPS)
                ACTF(P, t1b.ap[:, 0:qn], t1b.ap[:, 0:qn], AF.Exp, [t1b], [t1b], scale=-0.5)
                P.dma(lambda e, qs=qs, qn=qn, q0=q0, h=h: e.dma_start(
                    out=gbt.ap[:, 0:qn], in_=GBT.ap[128 * h:128 * h + 128, q0 + qs:q0 + qs + qn]), reads=[GBT.name], writes=[gbt])
                STT(P, t0b.ap[:, 0:qn], t0b.ap[:, 0:qn], subw.ap[:, 0:1], t1b.ap[:, 0:qn], ALU.mult, ALU.mult,
                    [t0b, subw, t1b], [t0b])
                TT(P, "dve", yb.ap[:, 0:qn], t0b.ap[:, 0:qn], gbt.ap[:, 0:qn], ALU.mult, [t0b, gbt], [yb])
                if "YB" in k.debug:
                    TT(P, "dve", t1b.ap[:, 0:qn], t0b.ap[:, 0:qn], gbt.ap[:, 0:qn], ALU.mult, [t0b, gbt], [t1b])
                    P.dma(lambda e, qs=qs, qn=qn, q0=q0, h=h: e.dma_start(
                        out=outs["YB"][128 * h:128 * h + 128, q0 + qs:q0 + qs + qn], in_=t1b.ap[:, 0:qn]), reads=[t1b], is_output=True)
                P.dma(lambda e, qs=qs, qn=qn, q0=q0, h=h: e.dma_start(
                    out=YT.ap[1024 + 128 * h:1024 + 128 * h + 128, q0 + qs:q0 + qs + qn], in_=yb.ap[:, 0:qn]),
                    reads=[yb], writes=[YT.name])
    A.release(m0)


L1_COLS = 2694


def l1_col0(o):
    if o < NT_OWN:
        return 1 + 128 * o
    if o < NT_OWN + 2:
        return 2179 + 128 * (o - NT_OWN)
    return 2436 + 128 * (o - NT_OWN - 2)


def emit_tail(k, ins, outs, YT, x_all):
    P, A, nc = k.P, k.A, k.nc
    ps = k.ps
    NO = NT_OWN + NT_PR
    m0 = A.mark()
    X1S = k.X1S
    hT1 = A.bf(8 * L1_COLS)
    hT13 = hT1.v("p (k c) -> p k c", c=L1_COLS)
    P.op("pool", lambda e: e.memset(hT1.ap, 0.0), writes=[hT1])
    m1 = A.mark()

    def outproj(w_dram, yt_cols, layer, x_get, x_put, tag):
        mm = A.mark()
        wst = [A.f32(1024) for _ in range(2)]
        wb = A.bf(16 * 1024)
        for kk in range(16):
            wsb = wst[kk % 2]
            P.dma(lambda e, wsb=wsb, kk=kk: e.dma_start(out=wsb.ap[:, 0:1024], in_=w_dram[128 * kk:128 * kk + 128, :]), writes=[wsb])
            CP(P, "pool" if kk % 2 else "dve", wb.ap[:, 1024 * kk:1024 * kk + 1024], wsb.ap[:, 0:1024], [wsb],
               [wb.cols(1024 * kk, 1024 * kk + 1024)])
        ytb = [A.bf(16 * 128) for _ in range(2)]
        xt = [A.f32(1024) for _ in range(2)]
        for o in range(NO):
            yb_ = ytb[o % 2]
            c0 = yt_cols(o)
            P.dma(lambda e, yb_=yb_, c0=c0: e.dma_start(
                out=yb_.v("p (kk t) -> p kk t", t=128), in_=YT.ap[:, c0:c0 + 128].rearrange("(kk p) t -> p kk t", p=128)),
                reads=[YT.name], writes=[yb_])
            v = 0 if o < NT_OWN else 1
            g = k.gB.ap[:, (layer * 2 + v) * 1024:(layer * 2 + v) * 1024 + 1024]
            xin = x_get(o, xt[o % 2])
            xo = x_put(o, xt[o % 2])
            for hf in range(2):
                pb = ps[2 * (o % 2) + hf]
                for kk in range(16):
                    MM(P, pb.ap, yb_.ap[:, 128 * kk:128 * kk + 128], wb.ap[:, 1024 * kk + 512 * hf:1024 * kk + 512 * hf + 512],
                       kk == 0, kk == 15, [yb_, wb], [pb])
                sl = slice(512 * hf, 512 * hf + 512)
                TT(P, "dve", xo.ap[:, sl], pb.ap, g[:, sl], ALU.mult, [pb, k.gB], [xo])
                TT(P, "pool" if hf else "dve", xo.ap[:, sl], xo.ap[:, sl], xin.ap[:, sl], ALU.add, [xo, xin], [xo])
            if tag == "final":
                P.dma(lambda e, xo=xo, o=o: e.dma_start(out=outs["y_out"][128 * o:128 * o + 128, :], in_=xo.ap), reads=[xo],
                      is_output=True)
            else:
                P.dma(lambda e, xo=xo, o=o: e.dma_start(out=X1S.ap[128 * o:128 * o + 128, :], in_=xo.ap), reads=[xo],
                      writes=[X1S.rows(128 * o, 128 * o + 128)])
            if tag != "final" and "X1" in k.debug:
                P.dma(lambda e, xo=xo, o=o: e.dma_start(out=outs["X1"][128 * o:128 * o + 128, :], in_=xo.ap), reads=[xo],
                      is_output=True)
        A.release(mm)

    def x_get0(o, buf):
        i = o if o < NT_OWN else NT_S + (o - NT_OWN)
        P.dma(lambda e, buf=buf, i=i: e.dma_start(out=buf.ap, in_=x_all[128 * i:128 * i + 128, :]), writes=[buf])
        return buf

    xo2 = [A.f32(1024) for _ in range(2)]
    outproj(ins["w_out0"], lambda o: 128 * o, 0, x_get0, lambda o, buf: xo2[o % 2], "l0")
    if "stopX1" in k.debug:
        A.release(m0)
        return
    mn = A.mark()
    emit_norm_phase(k, x_src=lambda o: X1S.ap[128 * o:128 * o + 128, :], tiles=list(range(NO)), layer=1, hT3=hT13, hT=hT1,
                    col0=l1_col0, variant=lambda o: 0 if o < NT_OWN else 1, cols_total=L1_COLS,
                    src_reg=lambda o: X1S.rows(128 * o, 128 * o + 128))
    A.release(mn)
    mm = A.mark()
    cvp = A.f32(64)
    P.dma(lambda e: e.dma_start(out=cvp.ap, in_=ins["cv_prm"][:, :]), writes=[cvp])
    wst = [A.f32(4 * 1024) for _ in range(2)]
    wbf = [A.bf(4 * 1024) for _ in range(2)]
    NB = L1_COLS + 2
    cgu2 = [A.f32(NB) for _ in range(2)]
    bgr2 = [A.f32(NB) for _ in range(2)]
    zs2 = [A.f32(NB) for _ in range(2)]
    cvb = A.f32(NB)
    cgt2 = [A.f32(512) for _ in range(2)]
    yrow = A.bf(NB)
    for cg_ in cgu2:
        P.op("pool", lambda e, cg_=cg_: e.memset(cg_.ap, 0.0), writes=[cg_])
    w_in1 = ins["w_in1"]
    wins = [(c, min(512, L1_COLS - c)) for c in range(0, L1_COLS, 512)]
    wcount = 0
    for j in range(16):
        cgu, bgr, zs = cgu2[j % 2], bgr2[j % 2], zs2[j % 2]
        ws, wb = wst[j % 2], wbf[j % 2]
        for q in range(4):
            P.dma(lambda e, ws=ws, q=q, j=j: e.dma_start(
                out=ws.ap[:, 1024 * q:1024 * q + 1024].rearrange("p (k c) -> p k c", c=128),
                in_=w_in1[:, 2048 * q + 128 * j:2048 * q + 128 * j + 128].rearrange("(k p) c -> p k c", p=128)),
                writes=[ws.cols(1024 * q, 1024 * q + 1024)])
        ACTF(P, wb.ap, ws.ap, AF.Identity, [ws], [wb])
        for (c0, nw) in wins:
            wcount += 1
            pbs = [ps[4 * (wcount % 2) + q] for q in range(4)]
            cgt = cgt2[wcount % 2]
            for q in range(4):
                pb = pbs[q]
                for kk in range(8):
                    MM(P, pb.ap[:, 0:nw], wb.ap[:, 1024 * q + 128 * kk:1024 * q + 128 * kk + 128], hT13[:, kk, c0:c0 + nw],
                       kk == 0, kk == 7, [wb, hT1], [pb])
            ACTF(P, bgr.ap[:, 1 + c0:1 + c0 + nw], pbs[0].ap[:, 0:nw], AF.Identity, [pbs[0]], [bgr.cols(1 + c0, 1 + c0 + nw)])
            ACTF(P, cgt.ap[:, 0:nw], pbs[1].ap[:, 0:nw], AF.Identity, [pbs[1]], [cgt])
            TT(P, "dve", cgu.ap[:, 1 + c0:1 + c0 + nw], cgt.ap[:, 0:nw], pbs[2].ap[:, 0:nw], ALU.mult, [cgt, pbs[2]],
               [cgu.cols(1 + c0, 1 + c0 + nw)])
            ACTF(P, zs.ap[:, 1 + c0:1 + c0 + nw], pbs[3].ap[:, 0:nw], AF.Silu, [pbs[3]], [zs.cols(1 + c0, 1 + c0 + nw)])
        n = L1_COLS
        w0, w1, w2, bb = [cvp.ap[:, 16 * t_ + j:16 * t_ + j + 1] for t_ in range(4)]
        P.op("dve", lambda e, w1=w1, bb=bb, n=n, cgu=cgu: e.tensor_scalar(out=cvb.ap[:, 1:1 + n], in0=cgu.ap[:, 1:1 + n], scalar1=w1,
                                                                scalar2=bb, op0=ALU.mult, op1=ALU.add), reads=[cgu, cvp], writes=[cvb])
        STT(P, cvb.ap[:, 1:1 + n], cgu.ap[:, 0:n], w0, cvb.ap[:, 1:1 + n], ALU.mult, ALU.add, [cgu, cvp, cvb], [cvb])
        STT(P, cvb.ap[:, 1:1 + n], cgu.ap[:, 2:2 + n], w2, cvb.ap[:, 1:1 + n], ALU.mult, ALU.add, [cgu, cvp, cvb], [cvb])
        TT(P, "dve", cvb.ap[:, 1:1 + n], cvb.ap[:, 1:1 + n], bgr.ap[:, 1:1 + n], ALU.mult, [cvb, bgr], [cvb])
        TT(P, "dve", yrow.ap[:, 1:1 + n], cvb.ap[:, 1:1 + n], zs.ap[:, 1:1 + n], ALU.mult, [cvb, zs], [yrow])
        P.dma(lambda e, j=j, n=n: e.dma_start(out=YT.ap[128 * j:128 * j + 128, 0:n], in_=yrow.ap[:, 1:1 + n]), reads=[yrow],
              writes=[YT.name])
    A.release(mm)
    def x_get1(o, buf):
        P.dma(lambda e, buf=buf, o=o: e.dma_start(out=buf.ap, in_=X1S.ap[128 * o:128 * o + 128, :]),
              reads=[X1S.rows(128 * o, 128 * o + 128)], writes=[buf])
        return buf

    outproj(ins["w_out1"], l1_col0, 1, x_get1, lambda o, buf: xo2[o % 2], "final")
    A.release(m0)


def prep_core_inputs(q, inp):
    b, mir = q // 2, q % 2
    f = (lambda a: a[::-1]) if mir else (lambda a: a)
    d = {}
    xs = f(inp["x_sample"][b])
    xp0 = f(inp["x_prompt"][2 * q])
    xp1 = f(inp["x_prompt"][2 * q + 1])
    d["x_all"] = np.ascontiguousarray(np.concatenate([xs, xp0, xp1], 0))
    cvv = np.stack([inp["c"][b], inp["c_ctx"]], -1)
    d["cv"] = np.ascontiguousarray(cvv.reshape(8, 128, 2).transpose(1, 0, 2).reshape(128, 16))
    d["normw_fm"] = np.ascontiguousarray(inp["norm_w"].reshape(2, 8, 128).transpose(2, 0, 1).reshape(128, 16))
    d["ada_w"] = inp["ada_w"]
    d["adab_fm"] = np.ascontiguousarray(inp["ada_b"].reshape(2, 24, 128).transpose(2, 0, 1).reshape(128, 48))
    d["adab_g"] = np.ascontiguousarray(np.broadcast_to(inp["ada_b"][:, 2048:].reshape(1, 2048), (128, 2048)))
    w = inp["e_w_in"][0]
    mu = inp["e_mu"][0]
    if mir:
        perm = np.arange(EVEN_IN)
        perm[3072:3136], perm[3136:3200] = np.arange(3136, 3200), np.arange(3072, 3136)
        perm[3200:3264], perm[3264:3328] = np.arange(3264, 3328), np.arange(3200, 3264)
        w = w[:, perm]
        mu = mu[perm[:3328]]
    d["w_in0"] = np.ascontiguousarray(w)
    d["mu_b"] = np.ascontiguousarray(np.broadcast_to(mu.reshape(1, -1), (128, 3328)))
    qk = np.concatenate([np.tile(inp["e_q_norm"][0], 8), np.tile(inp["e_k_norm"][0], 8)])
    d["qkw_b"] = np.ascontiguousarray(np.broadcast_to(qk.reshape(1, -1), (128, 1024)))
    t = np.arange(4096)
    row = (t // 64).astype(np.float32)
    col = (t % 64).astype(np.float32)
    inv = (10000.0 ** (-np.arange(0, 32, 2, dtype=np.float32) / 32)).astype(np.float32)
    ar, ac = row[:, None] * inv, col[:, None] * inv
    cos = np.concatenate([np.cos(ar), np.cos(ar), np.cos(ac), np.cos(ac)], 1)
    sin = np.concatenate([-np.sin(ar), np.sin(ar), -np.sin(ac), np.sin(ac)], 1)
    d["rope_cs"] = np.ascontiguousarray(f(np.concatenate([cos, sin], 1).astype(np.float32)))
    dirs = (1, 0) if mir else (0, 1)
    rep = lambda v: np.broadcast_to(np.asarray(v, np.float32).reshape(1, -1), (128, v.size))
    d["rw_prm"] = np.ascontiguousarray(np.concatenate(
        [rep(inp["e_k_k"][0]), rep(inp["e_k_a"][0]), rep(inp["e_r_k"][0].reshape(-1)),
         rep(inp["e_lnx_w"][0]), rep(inp["e_lnx_b"][0])], 1))
    up = np.zeros((65, 4096), np.float32)
    for fd, td in enumerate(dirs):
        up[:64, 1024 * fd:1024 * fd + 1024] = inp["e_w_up"][0, td]
        up[64, 1024 * fd:1024 * fd + 1024] = inp["e_w0"][0, td]
        up[:64, 2048 + 1024 * fd:2048 + 1024 * fd + 1024] = inp["e_a_up"][0, td]
        up[64, 2048 + 1024 * fd:2048 + 1024 * fd + 1024] = inp["e_a0"][0, td]
    d["rw_up"] = up
    d["rw_const"] = rwkv_consts()
    sts = (inp["state_rwkv_fwd"][b, 0], inp["state_rwkv_bwd"][b, 0])
    st = np.zeros((128, 1024), np.float32)
    for fd, td in enumerate(dirs):
        hh = sts[td].reshape(8, 2, 64, 64).transpose(1, 3, 0, 2).reshape(128, 512)
        st[:, 512 * fd:512 * fd + 512] = hh
    d["st_in"] = st
    d["ctx_k"] = np.ascontiguousarray(inp["cache_diff_k"][b, 0].reshape(256, 1024))
    d["ctx_v"] = np.ascontiguousarray(inp["cache_diff_v"][b, 0].reshape(256, 1024))
    d["at_prm"] = np.ascontiguousarray(np.concatenate(
        [rep(inp["e_lambda"][0].reshape(-1)), inp["e_subln"][0].reshape(128, 1)], 1).astype(np.float32))
    d["w_out0"] = inp["e_w_out"][0]
    d["w_in1"] = inp["o_w_in"][0]
    d["w_out1"] = inp["o_w_out"][0]
    cw = inp["o_conv_w"][0]
    if mir:
        cw = cw[::-1]
    cvp = np.concatenate([cw.reshape(3, 16, 128), inp["o_conv_b"][0].reshape(1, 16, 128)], 0)
    d["cv_prm"] = np.ascontiguousarray(cvp.transpose(2, 0, 1).reshape(128, 64))
    return d


def rwkv_consts():
    idx = np.arange(128)
    ch, pos = idx // 64, idx % 64
    same = ch[:, None] == ch[None, :]
    out = np.zeros((128, 2050), np.float32)
    for dd in range(2):
        before = (pos[:, None] < pos[None, :]) if dd == 0 else (pos[:, None] > pos[None, :])
        eq = pos[:, None] == pos[None, :]
        incl = same & (before | eq)
        excl = same & before
        suf = same & before.T
        for m, mat in enumerate((incl, excl, suf)):
            out[:, (dd * 3 + m) * 128:(dd * 3 + m) * 128 + 128] = DECAY_C * mat
        msk = np.concatenate([excl, incl], 1).astype(np.float32)
        out[:, 770 + 512 * dd:770 + 512 * dd + 256] = msk
        out[:, 770 + 512 * dd + 256:770 + 512 * dd + 512] = msk
        out[:, 1794 + 128 * dd:1794 + 128 * dd + 128] = excl.T
    out[:, 768] = DECAY_C * (ch == 0)
    out[:, 769] = DECAY_C * (ch == 1)
    return out


def assemble(q, r, outs6):
    y_p, y_s, n_sf, n_sb, n_k, n_v = outs6
    b, mir = q // 2, q % 2
    S = y_p.shape[1]
    yo = np.asarray(r["y_out"])
    if mir:
        y_s[b, 2048:] = yo[:2048][::-1]
    else:
        y_s[b, :2048] = yo[:2048]
    so = np.asarray(r["st_out"])
    for j in range(2):
        yp = yo[2176 + 256 * j:2176 + 256 * j + 256]
        kc = np.asarray(r["kc_out"])[256 * j:256 * j + 256]
        vc = np.asarray(r["vc_out"])[256 * j:256 * j + 256]
        if mir:
            yp, kc, vc = yp[::-1], kc[::-1], vc[::-1]
        y_p[2 * q + j] = yp
        n_k[2 * q + j, 0] = kc.reshape(S, 8, 2, 64)
        n_v[2 * q + j, 0] = vc.reshape(S, 8, 128)
        for fd in range(2):
            st = so[j, fd].reshape(2, 64, 8, 64).transpose(2, 0, 3, 1).reshape(16, 64, 64)
            td = (1 - fd) if mir else fd
            (n_sf if td == 0 else n_sb)[2 * q + j, 0] = st


def kernel(**inputs):
    inp = {k_: np.asarray(v) for k_, v in inputs.items()}
    nc, ins, outs = build_program()
    in_maps = []
    for q in range(8):
        d = prep_core_inputs(q, inp)
        in_maps.append({n: d[n] for n in ins})
    res = run_bass_kernel_spmd(nc, in_maps, core_ids=list(range(8)))
    B, S = inp["x_prompt"].shape[0], inp["x_prompt"].shape[1]
    outs6 = (np.zeros(inp["x_prompt"].shape, np.float32), np.zeros(inp["x_sample"].shape, np.float32),
             np.zeros((B, 1, 16, 64, 64), np.float32), np.zeros((B, 1, 16, 64, 64), np.float32),
             np.zeros((B, 1, S, 8, 2, 64), np.float32), np.zeros((B, 1, S, 8, 128), np.float32))
    for q in range(8):
        assemble(q, res.results[q], outs6)
    return outs6
```

```python
import contextlib
import numpy as np
import concourse.bass as bass
import concourse.mybir as mybir
from concourse.bass_utils import run_bass_kernel_spmd

F32 = mybir.dt.float32
BF16 = mybir.dt.bfloat16
AF = mybir.ActivationFunctionType
ALU = mybir.AluOpType
AX = mybir.AxisListType

ENGS = ("pe", "act", "dve", "pool", "sp")
N_DMA_SEMS = 32
LOADS_ON_ACT = True
BIG = 1 << 40

D = 1024
NT_OWN, NT_OTH, NT_PR = 17, 15, 4
NT_S = NT_OWN + NT_OTH
NT = NT_S + NT_PR
T_ALL = NT * 128
T_OP = (NT_OWN + NT_PR) * 128
HT_COLS = 4612
EVEN_IN = 8448
NORM_EPS = 1e-6
GN_EPS = 64e-5
LAM_INIT = 0.8 - 0.6
DECAY_C = -0.6065306597126334


def tile_col0(i):
    if i < NT_S:
        return 1 + 128 * i
    if i < NT_S + 2:
        return 4098 + 128 * (i - NT_S)
    return 4355 + 128 * (i - NT_S - 2)


def op_index(i):
    return i if i < NT_OWN else NT_OWN + (i - NT_S)


OUT_TILES = list(range(NT_OWN)) + list(range(NT_S, NT))


class Prog:
    def __init__(self, nc):
        self.nc = nc
        self.streams = {e: [] for e in ENGS}
        self.count = {e: 0 for e in ENGS}
        self.seen = {e: {} for e in ENGS}
        self.regions = {}
        self.dma_cnt = [0] * N_DMA_SEMS
        self.dma_rr = 0
        self.out_tokens = []

    def _deps(self, reads, writes):
        deps = []
        for (name, lo, hi) in reads:
            for rec in self.regions.get(name, ()):
                if rec[0] < hi and lo < rec[1]:
                    deps.extend(rec[2].items())
                    if name.startswith("ps"):
                        deps.extend(rec[3].values())
        for (name, lo, hi) in writes:
            for rec in self.regions.get(name, ()):
                if rec[0] < hi and lo < rec[1]:
                    deps.extend(rec[2].items())
                    deps.extend(rec[3].values())
        return deps

    def _commit(self, reads, writes, token, rkey):
        for (name, lo, hi) in reads:
            lst = self.regions.setdefault(name, [])
            hit = False
            for rec in lst:
                if rec[0] < hi and lo < rec[1]:
                    rec[3][rkey] = token
                    hit = True
            if not hit:
                lst.append([lo, hi, {}, {rkey: token}])
        for (name, lo, hi) in writes:
            lst = self.regions.setdefault(name, [])
            keep = []
            wr = {token[0]: token[1]}
            dram = name not in ("ar",) and not name.startswith("ps")
            for rec in lst:
                if rec[1] <= lo or hi <= rec[0]:
                    keep.append(rec)
                    continue
                if dram:
                    for k_, v_ in rec[2].items():
                        if v_ > wr.get(k_, 0):
                            wr[k_] = v_
                if rec[0] < lo:
                    keep.append([rec[0], lo, dict(rec[2]), dict(rec[3])])
                if hi < rec[1]:
                    keep.append([hi, rec[1], dict(rec[2]), dict(rec[3])])
            keep.append([lo, hi, wr, {}])
            self.regions[name] = keep

    def _waits(self, eng, deps):
        best = {}
        for (k, v) in deps:
            if v > best.get(k, 0):
                best[k] = v
        out = []
        for k, v in best.items():
            if eng == "pe" and k == "pe":
                continue
            if self.seen[eng].get(k, 0) >= v:
                continue
            self.seen[eng][k] = v
            out.append((k, v))
        return out

    @staticmethod
    def _norm(regs):
        out = []
        for r in regs:
            if isinstance(r, str):
                out.append((r, 0, BIG))
            elif isinstance(r, Buf):
                out.append(r.reg)
            else:
                out.append(r)
        return out

    def op(self, eng, fn, reads=(), writes=()):
        reads = self._norm(reads)
        writes = self._norm(writes)
        waits = self._waits(eng, self._deps(reads, writes))
        self.count[eng] += 1
        token = (eng, self.count[eng])
        self.streams[eng].append(("op", waits, fn))
        self._commit(reads, writes, token, eng)
        return token

    def dma(self, fn, reads=(), writes=(), is_output=False, queue="sp"):
        reads = self._norm(reads)
        writes = self._norm(writes)
        queue = "act" if (LOADS_ON_ACT and any(w[0] == "ar" for w in writes)) else "sp"
        deps = self._deps(reads, writes)
        s = self.dma_rr
        self.dma_rr = (self.dma_rr + 1) % N_DMA_SEMS
        key = "dma%d" % s
        if self.dma_cnt[s] > 0:
            deps.append((key, self.dma_cnt[s]))
        waits = self._waits(queue, deps)
        self.dma_cnt[s] += 16
        token = (key, self.dma_cnt[s])
        self.streams[queue].append(("dma", waits, fn, s))
        self._commit(reads, writes, token, key)
        if is_output:
            self.out_tokens.append(token)
        return token

    def run(self):
        nc = self.nc
        fin = self._waits("sp", list(self.out_tokens))
        self.streams["sp"].append(("wait", fin))
        with contextlib.ExitStack() as st:
            sems = {}
            for e in ENGS:
                sems[e] = st.enter_context(nc.semaphore("s_" + e))
            for i in range(N_DMA_SEMS):
                sems["dma%d" % i] = st.enter_context(nc.semaphore("s_dma%d" % i))
            block = st.enter_context(nc.Block())

            def player(ename):
                def play(eng):
                    for item in self.streams[ename]:
                        for (k, v) in item[1]:
                            eng.wait_ge(sems[k], v)
                        if item[0] == "op":
                            item[2](eng).then_inc(sems[ename], 1)
                        elif item[0] == "dma":
                            item[2](eng).then_inc(sems["dma%d" % item[3]], 16)
                return play

            block.tensor(player("pe"))
            block.scalar(player("act"))
            block.vector(player("dve"))
            block.gpsimd(player("pool"))
            block.sync(player("sp"))


class Buf:
    def __init__(self, ap, reg):
        self.ap = ap
        self.reg = reg

    def cols(self, a, b):
        name, lo, hi = self.reg
        if name == "ar":
            if self.ap.dtype == BF16:
                r = (name, lo + a // 2, lo + (b + 1) // 2)
            else:
                r = (name, lo + a, lo + b)
        else:
            r = self.reg
        return Buf(self.ap[:, a:b], r)

    def v(self, s, **kw):
        return self.ap.rearrange(s, **kw)


class Arena:
    def __init__(self, ap, size):
        self.ap = ap
        self.size = size
        self.top = 0

    def f32(self, n):
        lo = self.top
        self.top += n
        assert self.top <= self.size, ("arena overflow", self.top, self.size)
        return Buf(self.ap[:, lo:lo + n], ("ar", lo, lo + n))

    def bf(self, n):
        nf = (n + 1) // 2
        lo = self.top
        self.top += nf
        assert self.top <= self.size, ("arena overflow", self.top, self.size)
        return Buf(self.ap[:, lo:lo + nf].bitcast(BF16)[:, 0:n], ("ar", lo, lo + nf))

    def mark(self):
        return self.top

    def release(self, m):
        self.top = m


class DramBuf:
    def __init__(self, name, ap):
        self.name = name
        self.ap = ap

    def rows(self, a, b):
        return (self.name, a, b)


ARENA_F32 = 52800


class K:
    pass


def build_program(debug=()):
    nc = bass.Bass("TRN2", target_bir_lowering=False)
    k = K()
    k.nc = nc
    k.debug = debug
    ins = {}
    outs = {}

    def din(name, shape, dt=F32):
        ins[name] = nc.dram_tensor(name, list(shape), dt, kind="ExternalInput").ap()
        return ins[name]

    def dout(name, shape, dt=F32):
        outs[name] = nc.dram_tensor(name, list(shape), dt, kind="ExternalOutput").ap()
        return outs[name]

    def dscr(name, shape, dt=F32):
        kind = "ExternalOutput" if name in debug else "Internal"
        t = nc.dram_tensor(name, list(shape), dt, kind=kind).ap()
        if name in debug:
            outs[name] = t
        return DramBuf(name, t)

    x_all = din("x_all", [T_ALL, D])
    cv = din("cv", [128, 16])
    normw_fm = din("normw_fm", [128, 16])
    ada_w = din("ada_w", [2, D, 3072])
    adab_fm = din("adab_fm", [128, 48])
    adab_g = din("adab_g", [128, 2048])
    w_in0 = din("w_in0", [D, EVEN_IN])
    mu_b = din("mu_b", [128, 3328])
    qkw_b = din("qkw_b", [128, 1024])
    rope_cs = din("rope_cs", [4096, 128])
    rw_prm = din("rw_prm", [128, 5120])
    rw_up = din("rw_up", [65, 4096])
    rw_const = din("rw_const", [128, 2050])
    st_in = din("st_in", [128, 1024])
    ctx_k = din("ctx_k", [256, 1024])
    ctx_v = din("ctx_v", [256, 1024])
    at_prm = din("at_prm", [128, 257])
    w_out0 = din("w_out0", [2048, 1024])
    w_in1 = din("w_in1", [D, 8192])
    w_out1 = din("w_out1", [2048, 1024])
    cv_prm = din("cv_prm", [128, 64])
    OB = dscr("OB", [T_OP, 1024])
    k.X1S = dscr("X1S", [T_OP, 1024])
    YT = dscr("YT", [2048, 2816], BF16)
    PA = dscr("PA", [T_ALL, 3328])
    GA = dscr("GA", [T_OP, 1024])
    QT = dscr("QT", [8, 128, T_OP], BF16)
    KT = dscr("KT", [8, 128, T_ALL + 256], BF16)
    VS = dscr("VS", [T_ALL + 256, 1024], BF16)
    GBT = dscr("GBT", [1024, T_OP])
    kc_out = dout("kc_out", [512, 1024])
    vc_out = dout("vc_out", [512, 1024])
    st_out = dout("st_out", [2, 2, 128, 512])
    y_out = dout("y_out", [T_OP, 1024])
    if "YB" in debug:
        dout("YB", [1024, T_OP])
    if "X1" in debug:
        dout("X1", [T_OP, 1024])
    if "YA" in debug:
        dout("YA", [T_OP, 1024])
    if "dumpQ" in debug:
        dout("dQ", [128, 12288])

    st = contextlib.ExitStack()
    with st:
        ar_t = st.enter_context(nc.sbuf_tensor("ar", [128, ARENA_F32], F32))
        psb = [st.enter_context(nc.psum_tensor("ps%d" % i, [128, 512], F32)) for i in range(8)]
        k.ps = [Buf(psb[i][:, :], ("ps%d" % i, 0, BIG)) for i in range(8)]
        A = Arena(ar_t, ARENA_F32)
        P = Prog(nc)
        k.P, k.A = P, A

        ident = A.f32(128)
        identb = A.bf(128)
        P.op("pool", lambda e: e.memset(ident.ap, 1.0), writes=[ident])
        P.op("pool", lambda e: e.affine_select(out=ident.ap, in_=ident.ap, pattern=[[-1, 128]],
                                               compare_op=ALU.is_equal, fill=0.0, base=0, channel_multiplier=1),
             reads=[ident], writes=[ident])
        P.op("dve", lambda e: e.tensor_copy(identb.ap, ident.ap), reads=[ident], writes=[identb])
        k.ident, k.identb = ident, identb

        cvt = A.f32(16)
        nwt = A.f32(16)
        abf = A.f32(48)
        modS = A.f32(32)
        modB = A.f32(32)
        gB = A.f32(4096)
        k.modS, k.modB, k.gB = modS, modB, gB
        m0 = A.mark()
        abg = A.f32(2048)
        screp = A.f32(16 * 128)
        P.dma(lambda e: e.dma_start(out=cvt.ap, in_=cv[:, :]), writes=[cvt])
        P.dma(lambda e: e.dma_start(out=nwt.ap, in_=normw_fm[:, :]), writes=[nwt])
        P.dma(lambda e: e.dma_start(out=abf.ap, in_=adab_fm[:, :]), writes=[abf])
        P.dma(lambda e: e.dma_start(out=abg.ap, in_=adab_g[:, :]), writes=[abg])
        P.op("act", lambda e: e.activation(out=cvt.ap, in_=cvt.ap, func=AF.Silu), reads=[cvt], writes=[cvt])
        P.op("dve", lambda e: e.tensor_copy(screp.v("p (a b) -> p a b", b=128),
                                            cvt.ap.unsqueeze(2).broadcast_to([128, 16, 128])),
             reads=[cvt], writes=[screp])
        wst = [A.f32(8 * 512) for _ in range(2)]
        for l in range(2 if "noada" not in debug else 0):
            for g in range(6):
                wb = wst[(l * 6 + g) % 2]
                P.dma(lambda e, wb=wb, l=l, g=g: e.dma_start(
                    out=wb.v("p (k c) -> p k c", c=512),
                    in_=ada_w[l, :, 512 * g:512 * g + 512].rearrange("(k p) c -> p k c", p=128)), writes=[wb])
                if g < 4:
                    pb = k.ps[g % 2]
                    for j in range(4):
                        for kk in range(8):
                            P.op("pe", lambda e, pb=pb, wb=wb, j=j, kk=kk: e.matmul(
                                pb.ap[:, 2 * j:2 * j + 2], wb.ap[:, kk * 512 + 128 * j: kk * 512 + 128 * j + 128],
                                cvt.ap[:, 2 * kk:2 * kk + 2], start=(kk == 0), stop=(kk == 7)),
                                reads=[wb, cvt], writes=[pb])
                    for v in range(2):
                        if g < 2:
                            dst = modB.ap[:, l * 16 + v * 8 + 4 * g: l * 16 + v * 8 + 4 * g + 4]
                            P.op("dve", lambda e, pb=pb, dst=dst, v=v, l=l, g=g: e.tensor_tensor(
                                out=dst, in0=pb.ap[:, 0:8].rearrange("p (j v) -> p j v", v=2)[:, :, v],
                                in1=abf.ap[:, l * 24 + 4 * g: l * 24 + 4 * g + 4], op=ALU.add),
                                reads=[pb, abf], writes=[modB])
                        else:
                            gg = g - 2
                            dst = modS.ap[:, l * 16 + v * 8 + 4 * gg: l * 16 + v * 8 + 4 * gg + 4]
                            P.op("dve", lambda e, pb=pb, dst=dst, v=v, l=l, g=g: e.tensor_tensor(
                                out=dst, in0=pb.ap[:, 0:8].rearrange("p (j v) -> p j v", v=2)[:, :, v],
                                in1=abf.ap[:, l * 24 + 4 * g: l * 24 + 4 * g + 4], op=ALU.add),
                                reads=[pb, abf], writes=[modS])
                            P.op("dve", lambda e, dst=dst, l=l, gg=gg: e.scalar_tensor_tensor(
                                out=dst, in0=dst, scalar=1.0, in1=nwt.ap[:, l * 8 + 4 * gg: l * 8 + 4 * gg + 4],
                                op0=ALU.add, op1=ALU.mult), reads=[modS, nwt], writes=[modS])
                else:
                    gg = g - 4
                    for v in range(2):
                        pb = k.ps[2 + v]
                        for kk in range(8):
                            P.op("pe", lambda e, pb=pb, wb=wb, v=v, kk=kk: e.matmul(
                                pb.ap, screp.ap[:, (2 * kk + v) * 128:(2 * kk + v) * 128 + 128],
                                wb.ap[:, kk * 512: kk * 512 + 512], start=(kk == 0), stop=(kk == 7)),
                                reads=[wb, screp], writes=[pb])
                        dst = gB.cols((l * 2 + v) * 1024 + 512 * gg, (l * 2 + v) * 1024 + 512 * gg + 512)
                        P.op("dve", lambda e, pb=pb, dst=dst, l=l, gg=gg: e.tensor_tensor(
                            out=dst.ap, in0=pb.ap, in1=abg.ap[:, l * 1024 + 512 * gg: l * 1024 + 512 * gg + 512],
                            op=ALU.add), reads=[pb, abg], writes=[dst])
        A.release(m0)

        mH = A.mark()
        hT = A.bf(8 * HT_COLS)
        k.hT = hT
        hT3 = hT.v("p (k c) -> p k c", c=HT_COLS)
        for zc in (0, 4097, 4354, 4611):
            P.op("pool", lambda e, zc=zc: e.memset(hT3[:, :, zc:zc + 1], 0.0), writes=[hT])
        mA = A.mark()
        emit_norm_phase(k, x_src=lambda i: x_all[128 * i:128 * i + 128, :], tiles=(list(range(NT)) if "noA1" not in debug else []), layer=0,
                        hT3=hT3, hT=hT, col0=tile_col0, variant=lambda i: 0 if i < NT_S else 1)
        A.release(mA)

        if "noA2" not in debug:
            emit_inproj0(k, ins, outs, PA, GA, QT, KT, VS, GBT, hT3)

        if "hT" in debug:
            hdbg = dout("hT_dbg", [128, 8 * HT_COLS], BF16)
            P.dma(lambda e: e.dma_start(out=hdbg[:, :], in_=hT.ap), reads=[hT], is_output=True)
        A.release(mH)
        if "noR" not in debug:
            emit_rwkv(k, ins, outs, PA, GA, OB, YT)
        if "noT" not in debug:
            emit_attn(k, ins, outs, QT, KT, VS, GBT, YT)
        if "noO" not in debug:
            emit_tail(k, ins, outs, YT, x_all)
        P.run()
    return nc, ins, outs


def emit_norm_phase(k, x_src, tiles, layer, hT3, hT, col0, variant, x_keep=None, cols_total=HT_COLS, src_reg=None):
    P, A = k.P, k.A
    xt = [A.f32(1024) for _ in range(2)]
    xn = [A.f32(1024) for _ in range(2)]
    junk = A.bf(1024)
    ss = [A.f32(1) for _ in range(2)]
    pend = [None]
    for n, i in enumerate(tiles):
        xb, xnb, ssb = xt[n % 2], xn[n % 2], ss[n % 2]
        src = x_src(i)
        if src is not None:
            P.dma(lambda e, xb=xb, src=src: e.dma_start(out=xb.ap, in_=src), writes=[xb],
                  reads=([src_reg(i)] if src_reg is not None else []))
        else:
            xb = x_keep(i)
        P.op("act", lambda e, xb=xb, ssb=ssb: e.activation(out=junk.ap, in_=xb.ap, func=AF.Square, accum_out=ssb.ap),
             reads=[xb], writes=[junk, ssb])
        P.op("act", lambda e, ssb=ssb: e.activation(out=ssb.ap, in_=ssb.ap, func=AF.Ln, scale=1.0 / D, bias=NORM_EPS),
             reads=[ssb], writes=[ssb])
        P.op("act", lambda e, ssb=ssb: e.activation(out=ssb.ap, in_=ssb.ap, func=AF.Exp, scale=-0.5),
             reads=[ssb], writes=[ssb])
        P.op("dve", lambda e, xb=xb, xnb=xnb, ssb=ssb: e.tensor_scalar(
            out=xnb.ap, in0=xb.ap, scalar1=ssb.ap[:, 0:1], scalar2=None, op0=ALU.mult), reads=[xb, ssb], writes=[xnb])
        v = variant(i)
        c0 = col0(i)

        def stage2(n=n, xnb=xnb, v=v, c0=c0):
            for half in range(2):
                pb = k.ps[(2 * n + half) % 4]
                for q in range(4):
                    kk = 4 * half + q
                    P.op("pe", lambda e, pb=pb, xnb=xnb, kk=kk, q=q: e.transpose(
                        pb.ap[:, 128 * q:128 * q + 128], xnb.ap[:, 128 * kk:128 * kk + 128], k.ident.ap),
                        reads=[xnb, k.ident], writes=[pb])
                for q in range(4):
                    kk = 4 * half + q
                    mi = layer * 16 + v * 8 + kk
                    hreg = ("ar", hT.reg[1] + (kk * cols_total + c0) // 2, hT.reg[1] + (kk * cols_total + c0 + 129) // 2)
                    P.op("act", lambda e, pb=pb, kk=kk, q=q, mi=mi, c0=c0: e.activation(
                        out=hT3[:, kk, c0:c0 + 128], in_=pb.ap[:, 128 * q:128 * q + 128], func=AF.Identity,
                        scale=k.modS.ap[:, mi:mi + 1], bias=k.modB.ap[:, mi:mi + 1]),
                        reads=[pb, k.modS, k.modB], writes=[hreg])

        if pend[0] is not None:
            pend[0]()
        pend[0] = stage2
    if pend[0] is not None:
        pend[0]()


def emit_inproj0(k, ins, outs, PA, GA, QT, KT, VS, GBT, hT3):
    P, A, nc = k.P, k.A, k.nc
    hT = k.hT
    w_in0 = ins["w_in0"]
    m0 = A.mark()
    mub = A.f32(3328)
    qkw = A.f32(1024)
    P.dma(lambda e: e.dma_start(out=mub.ap, in_=ins["mu_b"][:, :]), writes=[mub])
    P.dma(lambda e: e.dma_start(out=qkw.ap, in_=ins["qkw_b"][:, :]), writes=[qkw])
    wst = [A.f32(8 * 512) for _ in range(2)]
    wbf = [A.bf(8 * 512) for _ in range(2)]
    hd = [A.bf(8 * 128) for _ in range(2)]
    ev = [A.f32(512) for _ in range(6)]
    evs = [A.f32(512) for _ in range(4)]
    rs8 = [A.f32(8) for _ in range(4)]
    ropet = [A.f32(128) for _ in range(4)]
    tb = [A.bf(512) for _ in range(4)]
    evb = [A.bf(512) for _ in range(4)]

    def hreg(i):
        c0 = tile_col0(i)
        return ("ar", hT.reg[1], hT.reg[2])

    groups = []
    for c in range(0, 3072, 512):
        groups.append((c, 512, "A"))
    groups.append((3072, 256, "A"))
    for c in range(3328, 4352, 512):
        groups.append((c, 512, "ga"))
    for c in range(4352, 5376, 512):
        groups.append((c, 512, "pq"))
    for c in range(5376, 6400, 512):
        groups.append((c, 512, "pk"))
    for c in range(6400, 7424, 512):
        groups.append((c, 512, "pv"))
    cnt = 0
    pending = [None]
    only = [d[5:] for d in k.debug if d.startswith("only_")]
    if only:
        groups = [g for g in groups if g[2] in only]
    for gi, (cs, wdt, kind) in enumerate(groups):
        ws, wb = wst[gi % 2], wbf[gi % 2]
        P.dma(lambda e, ws=ws, cs=cs, wdt=wdt: e.dma_start(
            out=ws.v("p (k c) -> p k c", c=512)[:, :, 0:wdt],
            in_=w_in0[:, cs:cs + wdt].rearrange("(k p) c -> p k c", p=128)), writes=[ws])
        P.op("act", lambda e, ws=ws, wb=wb: e.activation(out=wb.ap, in_=ws.ap, func=AF.Identity), reads=[ws], writes=[wb])
        if kind == "A":
            tiles = list(range(NT)) if cs >= 1024 else OUT_TILES
        elif kind in ("ga", "pq"):
            tiles = OUT_TILES
        else:
            tiles = list(range(NT))
        def emit_hd(i_, slot):
            hdb_ = hd[slot % 2]
            hd3_ = hdb_.v("p (k c) -> p k c", c=128)
            c0_ = tile_col0(i_)
            P.op("pool", lambda e, hd3_=hd3_, c0_=c0_: e.tensor_tensor(
                out=hd3_, in0=hT3[:, :, c0_ - 1:c0_ + 127], in1=hT3[:, :, c0_ + 1:c0_ + 129], op=ALU.add),
                reads=[hT], writes=[hdb_])
            P.op("dve", lambda e, hd3_=hd3_, c0_=c0_: e.scalar_tensor_tensor(
                out=hd3_, in0=hd3_, scalar=0.5, in1=hT3[:, :, c0_:c0_ + 128], op0=ALU.mult, op1=ALU.subtract),
                reads=[hT, hdb_], writes=[hdb_])

        if kind == "A":
            emit_hd(tiles[0], 0)
        for ti, i in enumerate(tiles):
            c0 = tile_col0(i)
            sample = i < NT_S
            cnt += 1
            if kind == "A":
                p1 = k.ps[4 + (cnt % 2) * 2]
                p2 = k.ps[5 + (cnt % 2) * 2]
            else:
                p1 = k.ps[4 + cnt % 4]
                p2 = None
            if kind == "A" and ti + 1 < len(tiles):
                emit_hd(tiles[ti + 1], ti + 1)
            for kk in range(8):
                P.op("pe", lambda e, p1=p1, wb=wb, kk=kk, c0=c0, wdt=wdt: e.matmul(
                    p1.ap[:, 0:wdt], hT3[:, kk, c0:c0 + 128], wb.ap[:, kk * 512:kk * 512 + wdt],
                    start=(kk == 0), stop=(kk == 7)), reads=[wb, hT], writes=[p1])
            if kind == "A":
                hdb = hd[ti % 2]
                hd3 = hdb.v("p (k c) -> p k c", c=128)
                for kk in range(8):
                    P.op("pe", lambda e, p2=p2, wb=wb, kk=kk, hd3=hd3, wdt=wdt: e.matmul(
                        p2.ap[:, 0:wdt], hd3[:, kk, :], wb.ap[:, kk * 512:kk * 512 + wdt],
                        start=(kk == 0), stop=(kk == 7)), reads=[wb, hdb], writes=[p2])
                e1, e2 = ev[cnt % 2], evs[cnt % 4]
                P.op("act", lambda e, p2=p2, e1=e1, wdt=wdt: e.activation(out=e1.ap[:, 0:wdt], in_=p2.ap[:, 0:wdt], func=AF.Identity),
                     reads=[p2], writes=[e1])
                P.op("pool", lambda e, e1=e1, cs=cs, wdt=wdt: e.tensor_tensor(
                    out=e1.ap[:, 0:wdt], in0=e1.ap[:, 0:wdt], in1=mub.ap[:, cs:cs + wdt], op=ALU.mult),
                    reads=[e1, mub], writes=[e1])
                P.op("dve", lambda e, e1=e1, e2=e2, p1=p1, wdt=wdt: e.tensor_tensor(
                    out=e2.ap[:, 0:wdt], in0=e1.ap[:, 0:wdt], in1=p1.ap[:, 0:wdt], op=ALU.add),
                    reads=[e1, p1], writes=[e2])
                P.dma(lambda e, e2=e2, i=i, cs=cs, wdt=wdt: e.dma_start(
                    out=PA.ap[128 * i:128 * i + 128, cs:cs + wdt], in_=e2.ap[:, 0:wdt]),
                    reads=[e2], writes=[PA.rows(128 * i, 128 * i + 128)], queue="pool")
            elif kind == "ga":
                e1 = evs[cnt % 4]
                o = op_index(i)
                P.op("act", lambda e, p1=p1, e1=e1: e.activation(out=e1.ap, in_=p1.ap, func=AF.Silu), reads=[p1], writes=[e1])
                P.dma(lambda e, e1=e1, o=o, cs=cs: e.dma_start(
                    out=GA.ap[128 * o:128 * o + 128, cs - 3328:cs - 3328 + 512], in_=e1.ap),
                    reads=[e1], writes=[GA.rows(128 * o, 128 * o + 128)], queue="pool")
            elif kind == "pv":
                e1, eb = evs[cnt % 4], evb[cnt % 4]
                P.op("act", lambda e, p1=p1, e1=e1: e.activation(out=e1.ap, in_=p1.ap, func=AF.Identity), reads=[p1], writes=[e1])
                P.op("dve", lambda e, e1=e1, eb=eb: e.tensor_copy(eb.ap, e1.ap), reads=[e1], writes=[eb])
                P.dma(lambda e, eb=eb, i=i, cs=cs: e.dma_start(
                    out=VS.ap[128 * i:128 * i + 128, cs - 6400:cs - 6400 + 512], in_=eb.ap),
                    reads=[eb], writes=[VS.rows(128 * i, 128 * i + 128)], queue="pool")
                if not sample:
                    pi = i - NT_S
                    P.dma(lambda e, e1=e1, pi=pi, cs=cs: e.dma_start(
                        out=outs["vc_out"][128 * pi:128 * pi + 128, cs - 6400:cs - 6400 + 512], in_=e1.ap),
                        reads=[e1], is_output=True, queue="pool")
            else:
                isq = kind == "pq"
                base = 4352 if isq else 5376
                h0 = (cs - base) // 128
                wof = 0 if isq else 512
                e1, e2, e3 = ev[(cnt % 2) * 3], ev[(cnt % 2) * 3 + 1], ev[(cnt % 2) * 3 + 2]
                if cnt % 4 >= 2:
                    e1, e3 = evs[0], evs[1]
                    e2 = evs[2 + cnt % 2]
                r8 = rs8[cnt % 4]
                P.op("act", lambda e, p1=p1, e1=e1: e.activation(out=e1.ap, in_=p1.ap, func=AF.Square), reads=[p1], writes=[e1])
                P.op("dve", lambda e, e1=e1, r8=r8: e.tensor_reduce(
                    out=r8.ap, in_=e1.v("p (g d) -> p g d", d=64), axis=AX.X, op=ALU.add), reads=[e1], writes=[r8])
                P.op("act", lambda e, r8=r8: e.activation(out=r8.ap, in_=r8.ap, func=AF.Ln, scale=1.0 / 64, bias=NORM_EPS),
                     reads=[r8], writes=[r8])
                P.op("act", lambda e, r8=r8: e.activation(out=r8.ap, in_=r8.ap, func=AF.Exp, scale=-0.5), reads=[r8], writes=[r8])
                P.op("dve", lambda e, p1=p1, e2=e2, r8=r8: e.tensor_tensor(
                    out=e2.v("p (g d) -> p g d", d=64), in0=p1.v("p (g d) -> p g d", d=64),
                    in1=r8.ap.unsqueeze(2).broadcast_to([128, 8, 64]), op=ALU.mult), reads=[p1, r8], writes=[e2])
                P.op("dve", lambda e, e2=e2, wof=wof: e.tensor_tensor(
                    out=e2.ap, in0=e2.ap, in1=qkw.ap[:, wof:wof + 512], op=ALU.mult), reads=[e2, qkw], writes=[e2])
                tbb = tb[cnt % 4]
                if (not sample) and (not isq):
                    pi = i - NT_S
                    P.dma(lambda e, e2=e2, pi=pi, cs=cs: e.dma_start(
                        out=outs["kc_out"][128 * pi:128 * pi + 128, cs - 5376:cs - 5376 + 512], in_=e2.ap),
                        reads=[e2], is_output=True, queue="pool")
                if sample:
                    rt = ropet[cnt % 4]
                    P.dma(lambda e, rt=rt, i=i: e.dma_start(out=rt.ap, in_=ins["rope_cs"][128 * i:128 * i + 128, :]), writes=[rt])
                    cosb = rt.ap[:, 0:64].unsqueeze(1).broadcast_to([128, 8, 64])
                    P.op("dve", lambda e, e1=e1, e2=e2, cosb=cosb: e.tensor_tensor(
                        out=e1.v("p (g d) -> p g d", d=64), in0=e2.v("p (g d) -> p g d", d=64), in1=cosb, op=ALU.mult),
                        reads=[e2, rt], writes=[e1])
                    for hf in range(2):
                        sinb = rt.ap[:, 64:128].rearrange("p (c h d) -> p c h d", c=2, h=2)[:, :, hf, :] \
                            .unsqueeze(1).broadcast_to([128, 8, 2, 16])
                        P.op("pool", lambda e, e2=e2, e3=e3, hf=hf, sinb=sinb: e.tensor_tensor(
                            out=e3.v("p (g c h d) -> p g c h d", c=2, h=2, d=16)[:, :, :, hf, :],
                            in0=e2.v("p (g c h d) -> p g c h d", c=2, h=2, d=16)[:, :, :, 1 - hf, :],
                            in1=sinb, op=ALU.mult), reads=[e2, rt], writes=[e3])
                    P.op("dve", lambda e, e1=e1, e3=e3, tbb=tbb: e.tensor_tensor(out=tbb.ap, in0=e1.ap, in1=e3.ap, op=ALU.add),
                         reads=[e1, e3], writes=[tbb])
                else:
                    P.op("dve", lambda e, e2=e2, tbb=tbb: e.tensor_copy(tbb.ap, e2.ap), reads=[e2], writes=[tbb])
                def stage2(cnt=cnt, tbb=tbb, isq=isq, i=i, h0=h0):
                    pt = k.ps[cnt % 4]
                    ptb = pt.ap.bitcast(BF16)
                    for hh in range(4):
                        P.op("pe", lambda e, ptb=ptb, tbb=tbb, hh=hh: e.transpose(
                            ptb[:, 128 * hh:128 * hh + 128], tbb.ap[:, 128 * hh:128 * hh + 128], k.identb.ap),
                            reads=[tbb, k.identb], writes=[pt])
                    eb = evb[cnt % 4]
                    P.op("act", lambda e, ptb=ptb, eb=eb: e.activation(out=eb.ap, in_=ptb[:, 0:512], func=AF.Identity),
                         reads=[pt], writes=[eb])
                    if isq:
                        o = op_index(i)
                        P.dma(lambda e, eb=eb, o=o, h0=h0: e.dma_start(
                            out=QT.ap[h0:h0 + 4, :, 128 * o:128 * o + 128].rearrange("h p t -> p h t"),
                            in_=eb.v("p (h t) -> p h t", t=128)), reads=[eb], writes=[QT.name])
                    else:
                        P.dma(lambda e, eb=eb, i=i, h0=h0: e.dma_start(
                            out=KT.ap[h0:h0 + 4, :, 128 * i:128 * i + 128].rearrange("h p t -> p h t"),
                            in_=eb.v("p (h t) -> p h t", t=128)), reads=[eb], writes=[KT.name])

                if pending[0] is not None:
                    pending[0]()
                pending[0] = stage2
    if pending[0] is not None:
        pending[0]()
        pending[0] = None
    wins = [(1 + 512 * w_, 512, 512 * w_) for w_ in range(4)] + [(2049, 128, 2048), (4098, 256, 2176), (4355, 256, 2432)]
    if (not only) or ("gb" in only):
        for g2 in range(2):
            gi = len(groups) + g2
            ws, wb = wst[gi % 2], wbf[gi % 2]
            cs = 7424 + 512 * g2
            P.dma(lambda e, ws=ws, cs=cs: e.dma_start(
                out=ws.v("p (k c) -> p k c", c=512), in_=w_in0[:, cs:cs + 512].rearrange("(k p) c -> p k c", p=128)), writes=[ws])
            P.op("act", lambda e, ws=ws, wb=wb: e.activation(out=wb.ap, in_=ws.ap, func=AF.Identity), reads=[ws], writes=[wb])
            for q in range(4):
                for (c0, nw, o0) in wins:
                    cnt += 1
                    p1 = k.ps[4 + cnt % 4]
                    e1 = ev[cnt % 2]
                    for kk in range(8):
                        P.op("pe", lambda e, p1=p1, wb=wb, kk=kk, c0=c0, nw=nw, q=q: e.matmul(
                            p1.ap[:, 0:nw], wb.ap[:, kk * 512 + 128 * q:kk * 512 + 128 * q + 128], hT3[:, kk, c0:c0 + nw],
                            start=(kk == 0), stop=(kk == 7)), reads=[wb, hT], writes=[p1])
                    P.op("act", lambda e, p1=p1, e1=e1, nw=nw: e.activation(out=e1.ap[:, 0:nw], in_=p1.ap[:, 0:nw], func=AF.Silu),
                         reads=[p1], writes=[e1])
                    r0 = 512 * g2 + 128 * q
                    P.dma(lambda e, e1=e1, r0=r0, o0=o0, nw=nw: e.dma_start(out=GBT.ap[r0:r0 + 128, o0:o0 + nw], in_=e1.ap[:, 0:nw]),
                          reads=[e1], writes=[GBT.name])
    if (not only) or ("ctx" in only):
        for t_ in range(2):
            for which in range(2):
                cnt += 1
                e1a, e1b = ev[(cnt % 2) * 2], ev[(cnt % 2) * 2 + 1]
                src = ins["ctx_k"] if which == 0 else ins["ctx_v"]
                for hf in range(2):
                    ebuf = e1a if hf == 0 else e1b
                    P.dma(lambda e, ebuf=ebuf, src=src, t_=t_, hf=hf: e.dma_start(
                        out=ebuf.ap, in_=src[128 * t_:128 * t_ + 128, 512 * hf:512 * hf + 512]), writes=[ebuf])
                    tbb = tb[(cnt + hf) % 2]
                    P.op("dve", lambda e, ebuf=ebuf, tbb=tbb: e.tensor_copy(tbb.ap, ebuf.ap), reads=[ebuf], writes=[tbb])
                    if which == 1:
                        P.dma(lambda e, tbb=tbb, t_=t_, hf=hf: e.dma_start(
                            out=VS.ap[T_ALL + 128 * t_:T_ALL + 128 * t_ + 128, 512 * hf:512 * hf + 512], in_=tbb.ap),
                            reads=[tbb], writes=[VS.rows(T_ALL + 128 * t_, T_ALL + 128 * t_ + 128)])
                    else:
                        pt = k.ps[(cnt + hf) % 2]
                        ptb = pt.ap.bitcast(BF16)
                        for hh in range(4):
                            P.op("pe", lambda e, ptb=ptb, tbb=tbb, hh=hh: e.transpose(
                                ptb[:, 128 * hh:128 * hh + 128], tbb.ap[:, 128 * hh:128 * hh + 128], k.identb.ap),
                                reads=[tbb, k.identb], writes=[pt])
                        eb = evb[(cnt + hf) % 2]
                        P.op("act", lambda e, ptb=ptb, eb=eb: e.activation(out=eb.ap, in_=ptb[:, 0:512], func=AF.Identity),
                             reads=[pt], writes=[eb])
                        P.dma(lambda e, eb=eb, t_=t_, hf=hf: e.dma_start(
                            out=KT.ap[4 * hf:4 * hf + 4, :, T_ALL + 128 * t_:T_ALL + 128 * t_ + 128].rearrange("h p t -> p h t"),
                            in_=eb.v("p (h t) -> p h t", t=128)), reads=[eb], writes=[KT.name])
    A.release(m0)


def TT(P, eng, out, in0, in1, op, r, w):
    P.op(eng, lambda e: e.tensor_tensor(out=out, in0=in0, in1=in1, op=op), reads=r, writes=w)


def ACTF(P, out, in_, func, r, w, **kw):
    P.op("act", lambda e: e.activation(out=out, in_=in_, func=func, **kw), reads=r, writes=w)


def MM(P, out, lhsT, rhs, start, stop, r, w):
    P.op("pe", lambda e: e.matmul(out, lhsT, rhs, start=start, stop=stop), reads=r, writes=w)


def TR(P, out, in_, ident, r, w):
    P.op("pe", lambda e: e.transpose(out, in_, ident), reads=r, writes=w)


def CP(P, eng, out, in_, r, w):
    P.op(eng, lambda e: e.tensor_copy(out, in_), reads=r, writes=w)


def STT(P, out, in0, scalar, in1, op0, op1, r, w):
    P.op("dve", lambda e: e.scalar_tensor_tensor(out=out, in0=in0, scalar=scalar, in1=in1, op0=op0, op1=op1),
         reads=r, writes=w)


def RED(P, out, in_, r, w):
    P.op("dve", lambda e: e.tensor_reduce(out=out, in_=in_, axis=AX.X, op=ALU.add), reads=r, writes=w)


def g64(ap):
    return ap.rearrange("p (g d) -> p g d", d=64)


def b64(ap16):
    return ap16.unsqueeze(2).broadcast_to([128, 16, 64])


def emit_rwkv(k, ins, outs, PA, GA, OB, YT):
    P, A = k.P, k.A
    ps = k.ps
    psb = [p_.ap.bitcast(BF16) for p_ in ps]
    m0 = A.mark()
    prm = A.f32(5120)
    kkb, kab, rkb, lwb, lbb = [prm.cols(1024 * i, 1024 * i + 1024) for i in range(5)]
    wup = A.f32(4096)
    CM = A.f32(768)
    IND = A.f32(2)
    MSK = A.f32(1024)
    lT = A.f32(256)
    H = [A.f32(512) for _ in range(2)]
    Hb = [A.bf(512) for _ in range(2)]
    gC = A.f32(16)
    sbst = A.f32(21 * 16)
    P.dma(lambda e: e.dma_start(out=prm.ap, in_=ins["rw_prm"][:, :]), writes=[prm])
    P.dma(lambda e: e.dma_start(out=wup.ap[0:65, :], in_=ins["rw_up"][:, :]), writes=[wup])
    P.dma(lambda e: e.dma_start(out=CM.ap, in_=ins["rw_const"][:, 0:768]), writes=[CM])
    P.dma(lambda e: e.dma_start(out=IND.ap, in_=ins["rw_const"][:, 768:770]), writes=[IND])
    P.dma(lambda e: e.dma_start(out=MSK.ap, in_=ins["rw_const"][:, 770:1794]), writes=[MSK])
    P.op("pool", lambda e: e.memset(lT.ap, 1.0), writes=[lT])
    pa2 = [A.f32(3328) for _ in range(2)]
    LT = A.f32(128)
    SW, AA, KAP, KD, BE, X1, X2 = [A.f32(1024) for _ in range(7)]
    E = [A.f32(1024) for _ in range(2)]
    s16, sbc = A.f32(16), A.f32(16)
    RTIL, KATIL, BTIL, KTIL, BH, KH, VB = [A.bf(1024) for _ in range(7)]
    QRS, BKS = A.bf(2048), A.bf(2048)
    SC = A.bf(16 * 512)
    Nb = [A.bf(2048) for _ in range(2)]
    Ntb = [A.bf(2048) for _ in range(2)]
    Tb = [A.bf(2048) for _ in range(2)]
    Wsb = A.bf(1024)
    Usbc = [A.bf(1024) for _ in range(2)]
    VBc = [A.bf(1024) for _ in range(2)]
    Osb = Buf(A.ap[:, RTIL.reg[1]:KATIL.reg[2]], ("ar", RTIL.reg[1], KATIL.reg[2]))
    assert KATIL.reg[2] - RTIL.reg[1] == 1024 and RTIL.reg[2] == KATIL.reg[1]
    obuf, gabuf = X2, E[0]
    for zb in (Wsb, Usbc[0], Usbc[1], VBc[0], VBc[1]):
        P.op("pool", lambda e, zb=zb: e.memset(zb.ap, 0.0), writes=[zb])
    ytb = A.bf(1024)
    lT3 = lT.v("p (a t) -> p a t", t=128)
    QRS4 = QRS.v("p (j x t) -> p j x t", x=2, t=128)
    BKS4 = BKS.v("p (j x t) -> p j x t", x=2, t=128)
    SC3 = SC.v("p (h c) -> p h c", c=512)

    def load_pa(n, i, state_only):
        pa = pa2[n % 2]
        c_lo = 1024 if state_only else 0
        P.dma(lambda e: e.dma_start(out=pa.ap[:, c_lo:3328], in_=PA.ap[128 * i:128 * i + 128, c_lo:3328]),
              reads=[PA.rows(128 * i, 128 * i + 128)], writes=[pa])

    def visit(i, d, state_only, chunks, finalize, nxt=None):
        n = visit.n
        visit.n += 1
        pa = pa2[n % 2]
        if n == 0:
            load_pa(0, i, state_only)
        if nxt is not None:
            load_pa(n + 1, nxt[0], nxt[2])
        par, pak, pav = pa.ap[:, 0:1024], pa.ap[:, 1024:2048], pa.ap[:, 2048:3072]
        Hd, Hbd = H[d], Hb[d]
        Hd3 = Hd.v("p (j v) -> p j v", v=64)
        Hb3 = Hbd.v("p (j v) -> p j v", v=64)
        CP(P, "pool", VB.ap, pav, [pa], [VB])
        CP(P, "pool", VBc[0].ap[0:64, :], pav[0:64, :], [pa], [VBc[0]])
        CP(P, "pool", VBc[1].ap[64:128, :], pav[64:128, :], [pa], [VBc[1]])
        ACTF(P, LT.ap[:, 0:64], pa.ap[:, 3072 + 64 * d:3136 + 64 * d], AF.Tanh, [pa], [LT])
        ACTF(P, LT.ap[:, 64:128], pa.ap[:, 3200 + 64 * d:3264 + 64 * d], AF.Identity, [pa], [LT])
        TR(P, ps[0].ap[0:64, 0:128], LT.ap[:, 0:64], k.ident.ap, [LT, k.ident], [ps[0]])
        TR(P, ps[0].ap[0:64, 128:256], LT.ap[:, 64:128], k.ident.ap, [LT, k.ident], [ps[0]])
        ACTF(P, lT3[0:64, :, :], ps[0].ap[0:64, 0:256].rearrange("p (a t) -> p a t", t=128), AF.Identity, [ps[0]], [lT])
        for hf in range(2):
            MM(P, ps[hf].ap, lT3[0:65, 0, :], wup.ap[0:65, 1024 * d + 512 * hf:1024 * d + 512 * hf + 512], True, True,
               [lT, wup], [ps[hf]])
            MM(P, ps[2 + hf].ap, lT3[0:65, 1, :], wup.ap[0:65, 2048 + 1024 * d + 512 * hf:2048 + 1024 * d + 512 * hf + 512],
               True, True, [lT, wup], [ps[2 + hf]])
        for hf in range(2):
            ACTF(P, SW.ap[:, 512 * hf:512 * hf + 512], ps[hf].ap, AF.Sigmoid, [ps[hf]], [SW])
            ACTF(P, AA.ap[:, 512 * hf:512 * hf + 512], ps[2 + hf].ap, AF.Sigmoid, [ps[2 + hf]], [AA])
        TT(P, "dve", X1.ap, pak, kkb.ap, ALU.mult, [pa, kkb], [X1])
        ACTF(P, X2.ap, X1.ap, AF.Square, [X1], [X2])
        RED(P, s16.ap, g64(X2.ap), [X2], [s16])
        ACTF(P, s16.ap, s16.ap, AF.Ln, [s16], [s16], bias=1e-24)
        ACTF(P, s16.ap, s16.ap, AF.Exp, [s16], [s16], scale=-0.5)
        TT(P, "dve", g64(KAP.ap), g64(X1.ap), b64(s16.ap), ALU.mult, [X1, s16], [KAP])
        STT(P, X2.ap, AA.ap, -1.0, kab.ap, ALU.add, ALU.mult, [AA, kab], [X2])
        STT(P, KD.ap, X2.ap, 1.0, pak, ALU.add, ALU.mult, [X2, pa], [KD])
        TT(P, "pool", BE.ap, KAP.ap, AA.ap, ALU.mult, [KAP, AA], [BE])
        if not state_only:
            TT(P, "pool", X1.ap, par, rkb.ap, ALU.mult, [pa, rkb], [X1])
            TT(P, "pool", X1.ap, X1.ap, KD.ap, ALU.mult, [X1, KD], [X1])
            RED(P, sbc.ap, g64(X1.ap), [X1], [sbc])
        cm = CM.v("p (d m t) -> p d m t", d=2, m=3)
        for hf in range(2):
            MM(P, ps[4 + hf].ap, cm[:, d, 0, :], SW.ap[:, 512 * hf:512 * hf + 512], True, True, [CM, SW], [ps[4 + hf]])
            MM(P, ps[6 + hf].ap, cm[:, d, 1, :], SW.ap[:, 512 * hf:512 * hf + 512], True, True, [CM, SW], [ps[6 + hf]])
        for hf in range(2):
            ACTF(P, E[0].ap[:, 512 * hf:512 * hf + 512], ps[4 + hf].ap, AF.Exp, [ps[4 + hf]], [E[0]])
            ACTF(P, E[1].ap[:, 512 * hf:512 * hf + 512], ps[4 + hf].ap, AF.Exp, [ps[4 + hf]], [E[1]], scale=-1.0)
        if not state_only:
            TT(P, "dve", RTIL.ap, par, E[0].ap, ALU.mult, [pa, E[0]], [RTIL])
        TT(P, "pool", BTIL.ap, BE.ap, E[1].ap, ALU.mult, [BE, E[1]], [BTIL])
        TT(P, "dve", KTIL.ap, KD.ap, E[1].ap, ALU.mult, [KD, E[1]], [KTIL])
        for hf in range(2):
            MM(P, ps[hf].ap, cm[:, d, 2, :], SW.ap[:, 512 * hf:512 * hf + 512], True, True, [CM, SW], [ps[hf]])
        for hf in range(2):
            ACTF(P, E[0].ap[:, 512 * hf:512 * hf + 512], ps[6 + hf].ap, AF.Exp, [ps[6 + hf]], [E[0]])
            ACTF(P, E[1].ap[:, 512 * hf:512 * hf + 512], ps[hf].ap, AF.Exp, [ps[hf]], [E[1]])
        TT(P, "dve", KATIL.ap, KAP.ap, E[0].ap, ALU.mult, [KAP, E[0]], [KATIL])
        TT(P, "pool", BH.ap, BE.ap, E[1].ap, ALU.mult, [BE, E[1]], [BH])
        TT(P, "pool", KH.ap, KD.ap, E[1].ap, ALU.mult, [KD, E[1]], [KH])
        for j in range(8):
            MM(P, ps[2].ap[:, 2 * j:2 * j + 2], SW.ap[:, 128 * j:128 * j + 128], IND.ap, True, True, [SW, IND], [ps[2]])
        ACTF(P, gC.ap, ps[2].ap[:, 0:16], AF.Exp, [ps[2]], [gC])
        gC3 = gC.v("p (j c) -> p j c", c=2)
        quants = [(KATIL, QRS4, 0), (BTIL, BKS4, 0), (KTIL, BKS4, 1)]
        if not state_only:
            quants.append((RTIL, QRS4, 1))
        for qi, (src, dst4, x) in enumerate(quants):
            dstbuf = QRS if dst4 is QRS4 else BKS
            pi_ = 4 + qi % 4
            for j in range(8):
                TR(P, psb[pi_][:, 128 * j:128 * j + 128], src.ap[:, 128 * j:128 * j + 128], k.identb.ap,
                   [src, k.identb], [ps[pi_]])
            dview = dst4[:, :, x, :]
            sview = psb[pi_].rearrange("p (j t) -> p j t", t=128)
            if qi % 2 == 0:
                ACTF(P, dview, sview, AF.Identity, [ps[pi_]], [dstbuf])
            else:
                CP(P, "dve", dview, sview, [ps[pi_]], [dstbuf])
        msk = MSK.ap[:, 512 * d:512 * d + 512]
        ncol = 128 if state_only else 256
        for h in range(16):
            j, e = h // 2, h % 2
            pb = ps[h % 4]
            rhs = QRS4[64 * e:64 * e + 64, j, :, :] if not state_only else QRS4[64 * e:64 * e + 64, j, 0:1, :]
            MM(P, pb.ap[:, 0:ncol], BKS4[64 * e:64 * e + 64, j, 0, :], rhs, True, True, [BKS, QRS], [pb])
            MM(P, pb.ap[:, 256:256 + ncol], BKS4[64 * e:64 * e + 64, j, 1, :], rhs, True, True, [BKS, QRS], [pb])
            TT(P, "dve", SC3[:, h, :], pb.ap, msk, ALU.mult, [pb, MSK], [SC])
        CP(P, "pool", Nb[0].v("p (h t) -> p h t", t=128), SC3[:, :, 0:128], [SC], [Nb[0]])
        TT(P, "dve", Tb[0].v("p (h t) -> p h t", t=128),
           k.identb.ap.unsqueeze(1).broadcast_to([128, 16, 128]), SC3[:, :, 0:128], ALU.subtract, [SC, k.identb], [Tb[0]])
        for hh in range(2):
            pi_ = 4 + hh
            for q in range(8):
                h = 8 * hh + q
                TR(P, psb[pi_][:, 128 * q:128 * q + 128], SC3[:, h, 0:128], k.identb.ap, [SC, k.identb], [ps[pi_]])
            ACTF(P, Ntb[0].ap[:, 1024 * hh:1024 * hh + 1024], psb[pi_], AF.Identity, [ps[pi_]], [Ntb[0]])
        cur = 0
        for lvl in range(1, 6):
            nxt = 1 - cur
            last = lvl == 5
            for hq in range(4):
                sl = slice(512 * hq, 512 * hq + 512)
                pbt = ps[hq % 4]
                for q in range(4):
                    h = 4 * hq + q
                    c1 = slice(128 * h, 128 * h + 128)
                    MM(P, pbt.ap[:, 128 * q:128 * q + 128], Nb[cur].ap[:, c1], Ntb[cur].ap[:, c1], True, True,
                       [Nb[cur], Ntb[cur]], [pbt])
                ACTF(P, Ntb[nxt].ap[:, sl], pbt.ap, AF.Identity, [pbt], [Ntb[nxt]])
                if not last:
                    pbn = ps[4 + hq % 4]
                    for q in range(4):
                        h = 4 * hq + q
                        c1 = slice(128 * h, 128 * h + 128)
                        MM(P, pbn.ap[:, 128 * q:128 * q + 128], Ntb[cur].ap[:, c1], Nb[cur].ap[:, c1], True, True,
                           [Nb[cur], Ntb[cur]], [pbn])
                    ACTF(P, Nb[nxt].ap[:, sl], pbn.ap, AF.Identity, [pbn], [Nb[nxt]])
            for hq in range(4):
                sl = slice(512 * hq, 512 * hq + 512)
                pb = ps[4 + hq % 4] if last else ps[hq % 4]
                for q in range(4):
                    h = 4 * hq + q
                    c1 = slice(128 * h, 128 * h + 128)
                    MM(P, pb.ap[:, 128 * q:128 * q + 128], Ntb[nxt].ap[:, c1], Tb[cur].ap[:, c1], True, True,
                       [Ntb[nxt], Tb[cur]], [pb])
                TT(P, "dve", Tb[nxt].ap[:, sl], pb.ap, Tb[cur].ap[:, sl], ALU.add, [pb, Tb[cur]], [Tb[nxt]])
            cur = nxt
        MTb = Tb[cur]
        MT3 = MTb.v("p (h t) -> p h t", t=128)
        for flag in ("xa", "xb", "xc", "xd"):
            if flag in k.debug:
                for h in range(16):
                    j, e = h // 2, h % 2
                    es = slice(64 * e, 64 * e + 64)
                    pb = ps[h // 8]
                    oc = slice(64 * (h % 8), 64 * (h % 8) + 64)
                    if flag == "xa":
                        MM(P, pb.ap[:, oc], BKS4[es, j, 0, :], BKS4[es, j, 1, 0:64], True, True, [BKS], [pb])
                    elif flag == "xb":
                        MM(P, pb.ap[:, oc], QRS4[es, j, 0, :], BKS4[es, j, 1, 0:64], True, True, [BKS, QRS], [pb])
                    elif flag == "xc":
                        MM(P, pb.ap[:, oc], QRS4[es, j, 0, :], Hb3[es, j, :], True, True, [QRS, Hbd], [pb])
                    elif flag == "xd":
                        MM(P, pb.ap[:, oc], BKS4[es, j, 0, :], Hb3[es, j, :], True, True, [BKS, Hbd], [pb])
                for hf in range(2):
                    ACTF(P, Wsb.ap[:, 512 * hf:512 * hf + 512], ps[hf].ap, AF.Identity, [ps[hf]], [Wsb])
        if "rwstop6" in k.debug:
            return
        for ch in chunks:
            rs = slice(64 * ch, 64 * ch + 64)
            Uc, Vc = Usbc[ch], VBc[ch]
            for h in range(16):
                j, e = h // 2, h % 2
                es = slice(64 * e, 64 * e + 64)
                pb = ps[h // 8]
                oc = slice(64 * (h % 8), 64 * (h % 8) + 64)
                MM(P, pb.ap[:, oc], QRS4[es, j, 0, :], Hb3[es, j, :], True, False, [QRS, Hbd], [pb])
                MM(P, pb.ap[:, oc], SC3[:, h, 256:384], VB.ap[:, 64 * h:64 * h + 64], False, True, [SC, VB], [pb])
            for hf in range(2):
                ACTF(P, Wsb.ap[rs, 512 * hf:512 * hf + 512], ps[hf].ap[rs, :], AF.Identity, [ps[hf]], [Wsb])
            if "rwstop7" in k.debug:
                return
            for h in range(16):
                pb = ps[2 + h // 8]
                oc = slice(64 * (h % 8), 64 * (h % 8) + 64)
                MM(P, pb.ap[:, oc], MT3[:, h, :], Wsb.ap[:, 64 * h:64 * h + 64], True, True, [MTb, Wsb], [pb])
            for hf in range(2):
                ACTF(P, Uc.ap[rs, 512 * hf:512 * hf + 512], ps[2 + hf].ap[rs, :], AF.Identity, [ps[2 + hf]], [Uc], scale=-1.0)
            if not state_only:
                for h in range(16):
                    j, e = h // 2, h % 2
                    es = slice(64 * e, 64 * e + 64)
                    pb = ps[4 + h // 8]
                    oc = slice(64 * (h % 8), 64 * (h % 8) + 64)
                    MM(P, pb.ap[:, oc], QRS4[es, j, 1, :], Hb3[es, j, :], True, False, [QRS, Hbd], [pb])
                    MM(P, pb.ap[:, oc], SC3[:, h, 128:256], Uc.ap[:, 64 * h:64 * h + 64], False, False, [SC, Uc], [pb])
                    MM(P, pb.ap[:, oc], SC3[:, h, 384:512], VB.ap[:, 64 * h:64 * h + 64], False, True, [SC, VB], [pb])
                for hf in range(2):
                    CP(P, "dve", Osb.ap[rs, 512 * hf:512 * hf + 512], ps[4 + hf].ap[rs, :], [ps[4 + hf]], [Osb])
            for j in range(8):
                pb = ps[6 + j // 4]
                oc = slice(128 * (j % 4), 128 * (j % 4) + 128)
                jc = slice(128 * j, 128 * j + 128)
                MM(P, pb.ap[:, oc], BH.ap[:, jc], Uc.ap[:, jc], True, False, [BH, Uc], [pb])
                MM(P, pb.ap[:, oc], KH.ap[:, jc], Vc.ap[:, jc], False, True, [KH, Vc], [pb])
            for e in range(2):
                es = slice(64 * e, 64 * e + 64)
                TT(P, "dve", Hd3[es, :, :], Hd3[es, :, :], gC3[es, :, ch].unsqueeze(2).broadcast_to([64, 8, 64]), ALU.mult,
                   [Hd, gC], [Hd])
                for hf in range(2):
                    src = ps[6 + hf].ap[es, :].rearrange("p (j e v) -> p j e v", e=2, v=64)[:, :, e, :]
                    TT(P, "dve", Hd3[es, 4 * hf:4 * hf + 4, :], Hd3[es, 4 * hf:4 * hf + 4, :], src, ALU.add,
                       [Hd, ps[6 + hf]], [Hd])
            CP(P, "pool", Hbd.ap, Hd.ap, [Hd], [Hbd])
        if state_only:
            return
        o = op_index(i)
        if not finalize:
            P.dma(lambda e: e.dma_start(out=OB.ap[128 * o:128 * o + 128, :], in_=Osb.ap), reads=[Osb],
                  writes=[OB.rows(128 * o, 128 * o + 128)])
            CP(P, "pool", sbst.ap[:, 16 * o:16 * o + 16], sbc.ap, [sbc], [sbst])
            return
        P.dma(lambda e: e.dma_start(out=obuf.ap, in_=OB.ap[128 * o:128 * o + 128, :]),
              reads=[OB.rows(128 * o, 128 * o + 128)], writes=[obuf])
        P.dma(lambda e: e.dma_start(out=gabuf.ap, in_=GA.ap[128 * o:128 * o + 128, :]),
              reads=[GA.rows(128 * o, 128 * o + 128)], writes=[gabuf])
        TT(P, "dve", X1.ap, Osb.ap, obuf.ap, ALU.add, [Osb, obuf], [X1])
        RED(P, s16.ap, g64(X1.ap), [X1], [s16])
        P.op("dve", lambda e: e.tensor_scalar(out=s16.ap, in0=s16.ap, scalar1=-1.0 / 64, scalar2=None, op0=ALU.mult),
             reads=[s16], writes=[s16])
        TT(P, "dve", g64(X1.ap), g64(X1.ap), b64(s16.ap), ALU.add, [X1, s16], [X1])
        ACTF(P, X2.ap, X1.ap, AF.Square, [X1], [X2])
        RED(P, s16.ap, g64(X2.ap), [X2], [s16])
        ACTF(P, s16.ap, s16.ap, AF.Ln, [s16], [s16], scale=1.0 / 64, bias=GN_EPS)
        ACTF(P, s16.ap, s16.ap, AF.Exp, [s16], [s16], scale=-0.5)
        TT(P, "dve", g64(X1.ap), g64(X1.ap), b64(s16.ap), ALU.mult, [X1, s16], [X1])
        TT(P, "pool", X1.ap, X1.ap, lwb.ap, ALU.mult, [X1, lwb], [X1])
        TT(P, "pool", X1.ap, X1.ap, lbb.ap, ALU.add, [X1, lbb], [X1])
        TT(P, "dve", sbc.ap, sbc.ap, sbst.ap[:, 16 * o:16 * o + 16], ALU.add, [sbc, sbst], [sbc])
        TT(P, "dve", g64(X2.ap), g64(pav), b64(sbc.ap), ALU.mult, [pa, sbc], [X2])
        TT(P, "pool", X1.ap, X1.ap, X2.ap, ALU.add, [X1, X2], [X1])
        TT(P, "dve", X1.ap, X1.ap, gabuf.ap, ALU.mult, [X1, gabuf], [X1])
        if "YA" in k.debug:
            P.dma(lambda e: e.dma_start(out=outs["YA"][128 * o:128 * o + 128, :], in_=X1.ap), reads=[X1], is_output=True)
        for hf in range(2):
            pb = ps[4 + hf]
            for q in range(4):
                kk_ = 4 * hf + q
                TR(P, pb.ap[:, 128 * q:128 * q + 128], X1.ap[:, 128 * kk_:128 * kk_ + 128], k.ident.ap, [X1, k.ident], [pb])
            ACTF(P, ytb.ap[:, 512 * hf:512 * hf + 512], pb.ap, AF.Identity, [pb], [ytb])
        P.dma(lambda e: e.dma_start(out=YT.ap[0:1024, 128 * o:128 * o + 128].rearrange("(kk p) t -> p kk t", p=128),
                                    in_=ytb.v("p (kk t) -> p kk t", t=128)), reads=[ytb], writes=[YT.name])

    visit.n = 0

    def init_state(d, from_input):
        if from_input:
            P.dma(lambda e: e.dma_start(out=H[d].ap, in_=ins["st_in"][:, 512 * d:512 * d + 512]), writes=[H[d]])
        else:
            P.op("pool", lambda e: e.memset(H[d].ap, 0.0), writes=[H[d]])
        CP(P, "pool", Hb[d].ap, H[d].ap, [H[d]], [Hb[d]])

    only = [x for x in k.debug if x.startswith("rw_")]
    do_sample = (not only) or ("rw_sample" in only)
    do_prompt = (not only) or ("rw_prompt" in only)
    acts = []
    if do_sample:
        acts.append(("init", 1, True))
        for i in range(NT_S - 1, NT_OWN - 1, -1):
            acts.append(("visit", i, 1, True, (1, 0), False))
        for i in range(NT_OWN - 1, -1, -1):
            acts.append(("visit", i, 1, False, (1, 0), False))
        acts.append(("init", 0, True))
        for i in range(NT_OWN):
            acts.append(("visit", i, 0, False, (0, 1), True))
    if do_prompt:
        for sq in range(2):
            t0 = NT_S + 2 * sq
            acts.append(("init", 1, False))
            for i in (t0 + 1, t0):
                acts.append(("visit", i, 1, False, (1, 0), False))
            acts.append(("stout", sq, 1))
            acts.append(("init", 0, False))
            for i in (t0, t0 + 1):
                acts.append(("visit", i, 0, False, (0, 1), True))
            acts.append(("stout", sq, 0))
    for ai, a_ in enumerate(acts):
        if a_[0] == "init":
            init_state(a_[1], a_[2])
        elif a_[0] == "stout":
            sq, d_ = a_[1], a_[2]
            P.dma(lambda e, sq=sq, d_=d_: e.dma_start(out=outs["st_out"][sq, d_, :, :], in_=H[d_].ap), reads=[H[d_]],
                  is_output=True)
        else:
            nxt = None
            for b_ in acts[ai + 1:]:
                if b_[0] == "visit":
                    nxt = (b_[1], b_[2], b_[3])
                    break
            visit(a_[1], a_[2], a_[3], a_[4], a_[5], nxt)
    A.release(m0)


def emit_attn(k, ins, outs, QT, KT, VS, GBT, YT):
    P, A = k.P, k.A
    ps = k.ps
    m0 = A.mark()
    prm = A.f32(257)
    lam4 = A.f32(4)
    neglam = A.f32(1)
    subw = A.f32(1)
    onesb = A.bf(128)
    P.dma(lambda e: e.dma_start(out=prm.ap, in_=ins["at_prm"][:, :]), writes=[prm])
    P.op("pool", lambda e: e.memset(onesb.ap, 1.0), writes=[onesb])
    tmp = A.f32(128)
    l4 = prm.ap[:, 0:256].rearrange("p (a d) -> p a d", d=64)
    TT(P, "dve", tmp.v("p (a d) -> p a d", d=64), l4[:, 0:4:2, :], l4[:, 1:4:2, :], ALU.mult, [prm], [tmp])
    P.op("dve", lambda e: e.tensor_reduce(out=lam4.ap[:, 0:2], in_=tmp.v("p (a d) -> p a d", d=64), axis=AX.X, op=ALU.add),
         reads=[tmp], writes=[lam4])
    ACTF(P, lam4.ap[:, 0:2], lam4.ap[:, 0:2], AF.Exp, [lam4], [lam4])
    STT(P, neglam.ap, lam4.ap[:, 1:2], -LAM_INIT, lam4.ap[:, 0:1], ALU.add, ALU.subtract, [lam4], [neglam])
    P.op("dve", lambda e: e.tensor_scalar(out=subw.ap, in0=prm.ap[:, 256:257], scalar1=1.0 - LAM_INIT, scalar2=None, op0=ALU.mult),
         reads=[prm], writes=[subw])
    NKMAX = 34
    kT = [A.bf(NKMAX * 128) for _ in range(2)]
    vv = [A.bf(NKMAX * 128) for _ in range(2)]
    qT = [A.bf(2176) for _ in range(2)]
    pT = [A.bf(512) for _ in range(4)]
    r0b, t0b, t1b, gbt = A.f32(512), A.f32(512), A.f32(512), A.f32(512)
    sqb = A.bf(512)
    yb = A.bf(512)
    sample_keys = [128 * i for i in range(NT_S)] + [T_ALL, T_ALL + 128]
    groups = [(0, 128 * NT_OWN, sample_keys)]
    for sq in range(2):
        groups.append((128 * NT_OWN + 256 * sq, 256, [128 * NT_S + 256 * sq, 128 * NT_S + 256 * sq + 128]))
    only = [x for x in k.debug if x.startswith("at_")]
    if "at_prompt" in only:
        groups = groups[1:]
    if "at_sample" in only:
        groups = groups[:1]
    it = 0
    for (q0, nq, keys) in groups:
        nk = len(keys)
        for h in range(1 if "at_h1" in k.debug else 8):
            it += 1
            kTb, vb, qTb = kT[it % 2], vv[it % 2], qT[it % 2]
            runs = []
            for ki, kr in enumerate(keys):
                if runs and runs[-1][1] + runs[-1][2] == kr:
                    runs[-1][2] += 128
                else:
                    runs.append([ki, kr, 128])
            for (ki, kr, ln) in runs:
                P.dma(lambda e, kTb=kTb, ki=ki, kr=kr, ln=ln, h=h: e.dma_start(
                    out=kTb.ap[:, 128 * ki:128 * ki + ln], in_=KT.ap[h, :, kr:kr + ln]), reads=[KT.name], writes=[kTb])
                P.dma(lambda e, vb=vb, ki=ki, kr=kr, ln=ln, h=h: e.dma_start(
                    out=vb.ap[:, 128 * ki:128 * ki + ln].rearrange("p (t c) -> p t c", c=128),
                    in_=VS.ap[kr:kr + ln, 128 * h:128 * h + 128].rearrange("(t p) c -> p t c", p=128)),
                    reads=[VS.rows(kr, kr + ln)], writes=[vb])
            P.dma(lambda e, qTb=qTb, q0=q0, nq=nq, h=h: e.dma_start(out=qTb.ap[:, 0:nq], in_=QT.ap[h, :, q0:q0 + nq]),
                  reads=[QT.name], writes=[qTb])
            for qs in range(0, nq, 512):
                qn = min(512, nq - qs)
                def emit_qk(ki):
                    for c in range(2):
                        sb_ = ps[2 * (ki % 2) + c]
                        cs_ = slice(64 * c, 64 * c + 64)
                        MM(P, sb_.ap[:, 0:qn], kTb.ap[cs_, 128 * ki:128 * ki + 128], qTb.ap[cs_, qs:qs + qn], True, True,
                           [kTb, qTb], [sb_])

                emit_qk(0)
                for ki in range(nk):
                    if ki + 1 < nk:
                        emit_qk(ki + 1)
                    for c in range(2):
                        sb_ = ps[2 * (ki % 2) + c]
                        pb_ = pT[2 * (ki % 2) + c]
                        ACTF(P, pb_.ap[:, 0:qn], sb_.ap[:, 0:qn], AF.Exp, [sb_], [pb_], scale=0.125)
                    for c in range(2):
                        pb_ = pT[2 * (ki % 2) + c]
                        MM(P, ps[4 + c].ap[:, 0:qn], vb.ap[:, 128 * ki:128 * ki + 128], pb_.ap[:, 0:qn], ki == 0, ki == nk - 1,
                           [vb, pb_], [ps[4 + c]])
                        MM(P, ps[6 + c].ap[:, 0:qn], onesb.ap, pb_.ap[:, 0:qn], ki == 0, ki == nk - 1, [onesb, pb_], [ps[6 + c]])
                P.op("dve", lambda e, qn=qn: e.reciprocal(out=r0b.ap[:, 0:qn], in_=ps[6].ap[:, 0:qn]), reads=[ps[6]], writes=[r0b])
                TT(P, "dve", t0b.ap[:, 0:qn], ps[4].ap[:, 0:qn], r0b.ap[:, 0:qn], ALU.mult, [ps[4], r0b], [t0b])
                P.op("dve", lambda e, qn=qn: e.reciprocal(out=r0b.ap[:, 0:qn], in_=ps[7].ap[:, 0:qn]), reads=[ps[7]], writes=[r0b])
                TT(P, "dve", t1b.ap[:, 0:qn], ps[5].ap[:, 0:qn], r0b.ap[:, 0:qn], ALU.mult, [ps[5], r0b], [t1b])
                STT(P, t0b.ap[:, 0:qn], t1b.ap[:, 0:qn], neglam.ap[:, 0:1], t0b.ap[:, 0:qn], ALU.mult, ALU.add,
                    [t1b, neglam, t0b], [t0b])
                ACTF(P, sqb.ap[:, 0:qn], t0b.ap[:, 0:qn], AF.Square, [t0b], [sqb])
                MM(P, ps[0].ap[:, 0:qn], onesb.ap, sqb.ap[:, 0:qn], True, True, [onesb, sqb], [ps[0]])
                ACTF(P, t1b.ap[:, 0:qn], ps[0].ap[:, 0:qn], AF.Ln, [ps[0]], [t1b], scale=1.0 / 128, bias=NORM_EPS)
                ACTF(P, t1b.ap[:, 0:qn], t1b.ap[:, 0:qn], AF.Exp, [t1b], [t1b], scale=-0.5)
                P.dma(lambda e, qs=qs, qn=qn, q0=q0, h=h: e.dma_start(
                    out=gbt.ap[:, 0:qn], in_=GBT.ap[128 * h:128 * h + 128, q0 + qs:q0 + qs + qn]), reads=[GBT.name], writes=[gbt])
                STT(P, t0b.ap[:, 0:qn], t0b.ap[:, 0:qn], subw.ap[:, 0:1], t1b.ap[:, 0:qn], ALU.mult, ALU.mult,
                    [t0b, subw, t1b], [t0b])
                TT(P, "dve", yb.ap[:, 0:qn], t0b.ap[:, 0:qn], gbt.ap[:, 0:qn], ALU.mult, [t0b, gbt], [yb])
                if "YB" in k.debug:
                    TT(P, "dve", t1b.ap[:, 0:qn], t0b.ap[:, 0:qn], gbt.ap[:, 0:qn], ALU.mult, [t0b, gbt], [t1b])
                    P.dma(lambda e, qs=qs, qn=qn, q0=q0, h=h: e.dma_start(
                        out=outs["YB"][128 * h:128 * h + 128, q0 + qs:q0 + qs + qn], in_=t1b.ap[:, 0:qn]), reads=[t1b], is_output=True)
                P.dma(lambda e, qs=qs, qn=qn, q0=q0, h=h: e.dma_start(
                    out=YT.ap[1024 + 128 * h:1024 + 128 * h + 128, q0 + qs:q0 + qs + qn], in_=yb.ap[:, 0:qn]),
                    reads=[yb], writes=[YT.name])
    A.release(m0)


L1_COLS = 2694


def l1_col0(o):
    if o < NT_OWN:
        return 1 + 128 * o
    if o < NT_OWN + 2:
        return 2179 + 128 * (o - NT_OWN)
    return 2436 + 128 * (o - NT_OWN - 2)


def emit_tail(k, ins, outs, YT, x_all):
    P, A, nc = k.P, k.A, k.nc
    ps = k.ps
    NO = NT_OWN + NT_PR
    m0 = A.mark()
    X1S = k.X1S
    hT1 = A.bf(8 * L1_COLS)
    hT13 = hT1.v("p (k c) -> p k c", c=L1_COLS)
    P.op("pool", lambda e: e.memset(hT1.ap, 0.0), writes=[hT1])
    m1 = A.mark()

    def outproj(w_dram, yt_cols, layer, x_get, x_put, tag):
        mm = A.mark()
        wst = [A.f32(1024) for _ in range(2)]
        wb = A.bf(16 * 1024)
        for kk in range(16):
            wsb = wst[kk % 2]
            P.dma(lambda e, wsb=wsb, kk=kk: e.dma_start(out=wsb.ap[:, 0:1024], in_=w_dram[128 * kk:128 * kk + 128, :]), writes=[wsb])
            CP(P, "pool" if kk % 2 else "dve", wb.ap[:, 1024 * kk:1024 * kk + 1024], wsb.ap[:, 0:1024], [wsb],
               [wb.cols(1024 * kk, 1024 * kk + 1024)])
        ytb = [A.bf(16 * 128) for _ in range(2)]
        xt = [A.f32(1024) for _ in range(2)]
        for o in range(NO):
            yb_ = ytb[o % 2]
            c0 = yt_cols(o)
            P.dma(lambda e, yb_=yb_, c0=c0: e.dma_start(
                out=yb_.v("p (kk t) -> p kk t", t=128), in_=YT.ap[:, c0:c0 + 128].rearrange("(kk p) t -> p kk t", p=128)),
                reads=[YT.name], writes=[yb_])
            v = 0 if o < NT_OWN else 1
            g = k.gB.ap[:, (layer * 2 + v) * 1024:(layer * 2 + v) * 1024 + 1024]
            xin = x_get(o, xt[o % 2])
            xo = x_put(o, xt[o % 2])
            for hf in range(2):
                pb = ps[2 * (o % 2) + hf]
                for kk in range(16):
                    MM(P, pb.ap, yb_.ap[:, 128 * kk:128 * kk + 128], wb.ap[:, 1024 * kk + 512 * hf:1024 * kk + 512 * hf + 512],
                       kk == 0, kk == 15, [yb_, wb], [pb])
                sl = slice(512 * hf, 512 * hf + 512)
                TT(P, "dve", xo.ap[:, sl], pb.ap, g[:, sl], ALU.mult, [pb, k.gB], [xo])
                TT(P, "pool" if hf else "dve", xo.ap[:, sl], xo.ap[:, sl], xin.ap[:, sl], ALU.add, [xo, xin], [xo])
            if tag == "final":
                P.dma(lambda e, xo=xo, o=o: e.dma_start(out=outs["y_out"][128 * o:128 * o + 128, :], in_=xo.ap), reads=[xo],
                      is_output=True)
            else:
                P.dma(lambda e, xo=xo, o=o: e.dma_start(out=X1S.ap[128 * o:128 * o + 128, :], in_=xo.ap), reads=[xo],
                      writes=[X1S.rows(128 * o, 128 * o + 128)])
            if tag != "final" and "X1" in k.debug:
                P.dma(lambda e, xo=xo, o=o: e.dma_start(out=outs["X1"][128 * o:128 * o + 128, :], in_=xo.ap), reads=[xo],
                      is_output=True)
        A.release(mm)

    def x_get0(o, buf):
        i = o if o < NT_OWN else NT_S + (o - NT_OWN)
        P.dma(lambda e, buf=buf, i=i: e.dma_start(out=buf.ap, in_=x_all[128 * i:128 * i + 128, :]), writes=[buf])
        return buf

    xo2 = [A.f32(1024) for _ in range(2)]
    outproj(ins["w_out0"], lambda o: 128 * o, 0, x_get0, lambda o, buf: xo2[o % 2], "l0")
    if "stopX1" in k.debug:
        A.release(m0)
        return
    mn = A.mark()
    emit_norm_phase(k, x_src=lambda o: X1S.ap[128 * o:128 * o + 128, :], tiles=list(range(NO)), layer=1, hT3=hT13, hT=hT1,
                    col0=l1_col0, variant=lambda o: 0 if o < NT_OWN else 1, cols_total=L1_COLS,
                    src_reg=lambda o: X1S.rows(128 * o, 128 * o + 128))
    A.release(mn)
    mm = A.mark()
    cvp = A.f32(64)
    P.dma(lambda e: e.dma_start(out=cvp.ap, in_=ins["cv_prm"][:, :]), writes=[cvp])
    wst = [A.f32(4 * 1024) for _ in range(2)]
    wbf = [A.bf(4 * 1024) for _ in range(2)]
    NB = L1_COLS + 2
    cgu2 = [A.f32(NB) for _ in range(2)]
    bgr2 = [A.f32(NB) for _ in range(2)]
    zs2 = [A.f32(NB) for _ in range(2)]
    cvb = A.f32(NB)
    cgt2 = [A.f32(512) for _ in range(2)]
    yrow = A.bf(NB)
    for cg_ in cgu2:
        P.op("pool", lambda e, cg_=cg_: e.memset(cg_.ap, 0.0), writes=[cg_])
    w_in1 = ins["w_in1"]
    wins = [(c, min(512, L1_COLS - c)) for c in range(0, L1_COLS, 512)]
    wcount = 0
    for j in range(16):
        cgu, bgr, zs = cgu2[j % 2], bgr2[j % 2], zs2[j % 2]
        ws, wb = wst[j % 2], wbf[j % 2]
        for q in range(4):
            P.dma(lambda e, ws=ws, q=q, j=j: e.dma_start(
                out=ws.ap[:, 1024 * q:1024 * q + 1024].rearrange("p (k c) -> p k c", c=128),
                in_=w_in1[:, 2048 * q + 128 * j:2048 * q + 128 * j + 128].rearrange("(k p) c -> p k c", p=128)),
                writes=[ws.cols(1024 * q, 1024 * q + 1024)])
        ACTF(P, wb.ap, ws.ap, AF.Identity, [ws], [wb])
        for (c0, nw) in wins:
            wcount += 1
            pbs = [ps[4 * (wcount % 2) + q] for q in range(4)]
            cgt = cgt2[wcount % 2]
            for q in range(4):
                pb = pbs[q]
                for kk in range(8):
                    MM(P, pb.ap[:, 0:nw], wb.ap[:, 1024 * q + 128 * kk:1024 * q + 128 * kk + 128], hT13[:, kk, c0:c0 + nw],
                       kk == 0, kk == 7, [wb, hT1], [pb])
            ACTF(P, bgr.ap[:, 1 + c0:1 + c0 + nw], pbs[0].ap[:, 0:nw], AF.Identity, [pbs[0]], [bgr.cols(1 + c0, 1 + c0 + nw)])
            ACTF(P, cgt.ap[:, 0:nw], pbs[1].ap[:, 0:nw], AF.Identity, [pbs[1]], [cgt])
            TT(P, "dve", cgu.ap[:, 1 + c0:1 + c0 + nw], cgt.ap[:, 0:nw], pbs[2].ap[:, 0:nw], ALU.mult, [cgt, pbs[2]],
               [cgu.cols(1 + c0, 1 + c0 + nw)])
            ACTF(P, zs.ap[:, 1 + c0:1 + c0 + nw], pbs[3].ap[:, 0:nw], AF.Silu, [pbs[3]], [zs.cols(1 + c0, 1 + c0 + nw)])
        n = L1_COLS
        w0, w1, w2, bb = [cvp.ap[:, 16 * t_ + j:16 * t_ + j + 1] for t_ in range(4)]
        P.op("dve", lambda e, w1=w1, bb=bb, n=n, cgu=cgu: e.tensor_scalar(out=cvb.ap[:, 1:1 + n], in0=cgu.ap[:, 1:1 + n], scalar1=w1,
                                                                scalar2=bb, op0=ALU.mult, op1=ALU.add), reads=[cgu, cvp], writes=[cvb])
        STT(P, cvb.ap[:, 1:1 + n], cgu.ap[:, 0:n], w0, cvb.ap[:, 1:1 + n], ALU.mult, ALU.add, [cgu, cvp, cvb], [cvb])
        STT(P, cvb.ap[:, 1:1 + n], cgu.ap[:, 2:2 + n], w2, cvb.ap[:, 1:1 + n], ALU.mult, ALU.add, [cgu, cvp, cvb], [cvb])
        TT(P, "dve", cvb.ap[:, 1:1 + n], cvb.ap[:, 1:1 + n], bgr.ap[:, 1:1 + n], ALU.mult, [cvb, bgr], [cvb])
        TT(P, "dve", yrow.ap[:, 1:1 + n], cvb.ap[:, 1:1 + n], zs.ap[:, 1:1 + n], ALU.mult, [cvb, zs], [yrow])
        P.dma(lambda e, j=j, n=n: e.dma_start(out=YT.ap[128 * j:128 * j + 128, 0:n], in_=yrow.ap[:, 1:1 + n]), reads=[yrow],
              writes=[YT.name])
    A.release(mm)
    def x_get1(o, buf):
        P.dma(lambda e, buf=buf, o=o: e.dma_start(out=buf.ap, in_=X1S.ap[128 * o:128 * o + 128, :]),
              reads=[X1S.rows(128 * o, 128 * o + 128)], writes=[buf])
        return buf

    outproj(ins["w_out1"], l1_col0, 1, x_get1, lambda o, buf: xo2[o % 2], "final")
    A.release(m0)


def prep_core_inputs(q, inp):
    b, mir = q // 2, q % 2
    f = (lambda a: a[::-1]) if mir else (lambda a: a)
    d = {}
    xs = f(inp["x_sample"][b])
    xp0 = f(inp["x_prompt"][2 * q])
    xp1 = f(inp["x_prompt"][2 * q + 1])
    d["x_all"] = np.ascontiguousarray(np.concatenate([xs, xp0, xp1], 0))
    cvv = np.stack([inp["c"][b], inp["c_ctx"]], -1)
    d["cv"] = np.ascontiguousarray(cvv.reshape(8, 128, 2).transpose(1, 0, 2).reshape(128, 16))
    d["normw_fm"] = np.ascontiguousarray(inp["norm_w"].reshape(2, 8, 128).transpose(2, 0, 1).reshape(128, 16))
    d["ada_w"] = inp["ada_w"]
    d["adab_fm"] = np.ascontiguousarray(inp["ada_b"].reshape(2, 24, 128).transpose(2, 0, 1).reshape(128, 48))
    d["adab_g"] = np.ascontiguousarray(np.broadcast_to(inp["ada_b"][:, 2048:].reshape(1, 2048), (128, 2048)))
    w = inp["e_w_in"][0]
    mu = inp["e_mu"][0]
    if mir:
        perm = np.arange(EVEN_IN)
        perm[3072:3136], perm[3136:3200] = np.arange(3136, 3200), np.arange(3072, 3136)
        perm[3200:3264], perm[3264:3328] = np.arange(3264, 3328), np.arange(3200, 3264)
        w = w[:, perm]
        mu = mu[perm[:3328]]
    d["w_in0"] = np.ascontiguousarray(w)
    d["mu_b"] = np.ascontiguousarray(np.broadcast_to(mu.reshape(1, -1), (128, 3328)))
    qk = np.concatenate([np.tile(inp["e_q_norm"][0], 8), np.tile(inp["e_k_norm"][0], 8)])
    d["qkw_b"] = np.ascontiguousarray(np.broadcast_to(qk.reshape(1, -1), (128, 1024)))
    t = np.arange(4096)
    row = (t // 64).astype(np.float32)
    col = (t % 64).astype(np.float32)
    inv = (10000.0 ** (-np.arange(0, 32, 2, dtype=np.float32) / 32)).astype(np.float32)
    ar, ac = row[:, None] * inv, col[:, None] * inv
    cos = np.concatenate([np.cos(ar), np.cos(ar), np.cos(ac), np.cos(ac)], 1)
    sin = np.concatenate([-np.sin(ar), np.sin(ar), -np.sin(ac), np.sin(ac)], 1)
    d["rope_cs"] = np.ascontiguousarray(f(np.concatenate([cos, sin], 1).astype(np.float32)))
    dirs = (1, 0) if mir else (0, 1)
    rep = lambda v: np.broadcast_to(np.asarray(v, np.float32).reshape(1, -1), (128, v.size))
    d["rw_prm"] = np.ascontiguousarray(np.concatenate(
        [rep(inp["e_k_k"][0]), rep(inp["e_k_a"][0]), rep(inp["e_r_k"][0].reshape(-1)),
         rep(inp["e_lnx_w"][0]), rep(inp["e_lnx_b"][0])], 1))
    up = np.zeros((65, 4096), np.float32)
    for fd, td in enumerate(dirs):
        up[:64, 1024 * fd:1024 * fd + 1024] = inp["e_w_up"][0, td]
        up[64, 1024 * fd:1024 * fd + 1024] = inp["e_w0"][0, td]
        up[:64, 2048 + 1024 * fd:2048 + 1024 * fd + 1024] = inp["e_a_up"][0, td]
        up[64, 2048 + 1024 * fd:2048 + 1024 * fd + 1024] = inp["e_a0"][0, td]
    d["rw_up"] = up
    d["rw_const"] = rwkv_consts()
    sts = (inp["state_rwkv_fwd"][b, 0], inp["state_rwkv_bwd"][b, 0])
    st = np.zeros((128, 1024), np.float32)
    for fd, td in enumerate(dirs):
        hh = sts[td].reshape(8, 2, 64, 64).transpose(1, 3, 0, 2).reshape(128, 512)
        st[:, 512 * fd:512 * fd + 512] = hh
    d["st_in"] = st
    d["ctx_k"] = np.ascontiguousarray(inp["cache_diff_k"][b, 0].reshape(256, 1024))
    d["ctx_v"] = np.ascontiguousarray(inp["cache_diff_v"][b, 0].reshape(256, 1024))
    d["at_prm"] = np.ascontiguousarray(np.concatenate(
        [rep(inp["e_lambda"][0].reshape(-1)), inp["e_subln"][0].reshape(128, 1)], 1).astype(np.float32))
    d["w_out0"] = inp["e_w_out"][0]
    d["w_in1"] = inp["o_w_in"][0]
    d["w_out1"] = inp["o_w_out"][0]
    cw = inp["o_conv_w"][0]
    if mir:
        cw = cw[::-1]
    cvp = np.concatenate([cw.reshape(3, 16, 128), inp["o_conv_b"][0].reshape(1, 16, 128)], 0)
    d["cv_prm"] = np.ascontiguousarray(cvp.transpose(2, 0, 1).reshape(128, 64))
    return d


def rwkv_consts():
    idx = np.arange(128)
    ch, pos = idx // 64, idx % 64
    same = ch[:, None] == ch[None, :]
    out = np.zeros((128, 2050), np.float32)
    for dd in range(2):
        before = (pos[:, None] < pos[None, :]) if dd == 0 else (pos[:, None] > pos[None, :])
        eq = pos[:, None] == pos[None, :]
        incl = same & (before | eq)
        excl = same & before
        suf = same & before.T
        for m, mat in enumerate((incl, excl, suf)):
            out[:, (dd * 3 + m) * 128:(dd * 3 + m) * 128 + 128] = DECAY_C * mat
        msk = np.concatenate([excl, incl], 1).astype(np.float32)
        out[:, 770 + 512 * dd:770 + 512 * dd + 256] = msk
        out[:, 770 + 512 * dd + 256:770 + 512 * dd + 512] = msk
        out[:, 1794 + 128 * dd:1794 + 128 * dd + 128] = excl.T
    out[:, 768] = DECAY_C * (ch == 0)
    out[:, 769] = DECAY_C * (ch == 1)
    return out


def assemble(q, r, outs6):
    y_p, y_s, n_sf, n_sb, n_k, n_v = outs6
    b, mir = q // 2, q % 2
    S = y_p.shape[1]
    yo = np.asarray(r["y_out"])
    if mir:
        y_s[b, 2048:] = yo[:2048][::-1]
    else:
        y_s[b, :2048] = yo[:2048]
    so = np.asarray(r["st_out"])
    for j in range(2):
        yp = yo[2176 + 256 * j:2176 + 256 * j + 256]
        kc = np.asarray(r["kc_out"])[256 * j:256 * j + 256]
        vc = np.asarray(r["vc_out"])[256 * j:256 * j + 256]
        if mir:
            yp, kc, vc = yp[::-1], kc[::-1], vc[::-1]
        y_p[2 * q + j] = yp
        n_k[2 * q + j, 0] = kc.reshape(S, 8, 2, 64)
        n_v[2 * q + j, 0] = vc.reshape(S, 8, 128)
        for fd in range(2):
            st = so[j, fd].reshape(2, 64, 8, 64).transpose(2, 0, 3, 1).reshape(16, 64, 64)
            td = (1 - fd) if mir else fd
            (n_sf if td == 0 else n_sb)[2 * q + j, 0] = st


def kernel(**inputs):
    inp = {k_: np.asarray(v) for k_, v in inputs.items()}
    nc, ins, outs = build_program()
    in_maps = []
    for q in range(8):
        d = prep_core_inputs(q, inp)
        in_maps.append({n: d[n] for n in ins})
    res = run_bass_kernel_spmd(nc, in_maps, core_ids=list(range(8)))
    B, S = inp["x_prompt"].shape[0], inp["x_prompt"].shape[1]
    outs6 = (np.zeros(inp["x_prompt"].shape, np.float32), np.zeros(inp["x_sample"].shape, np.float32),
             np.zeros((B, 1, 16, 64, 64), np.float32), np.zeros((B, 1, 16, 64, 64), np.float32),
             np.zeros((B, 1, S, 8, 2, 64), np.float32), np.zeros((B, 1, S, 8, 128), np.float32))
    for q in range(8):
        assemble(q, res.results[q], outs6)
    return outs6
```
